# Optimizing a Trainium2 kernel written in Bass

```python
import math
import jax, jax.numpy as jnp
from jax import lax
import numpy as np

D_MODEL = 1024
BATCH = 8
SEQ = 2048
DEPTH = 2
DEC_BATCH = 128
DEC_SEQ = 4
PAST_LEN = 16384
PAGE_SIZE = 128

GDN_HEADS = 4
GDN_DK = 128
GDN_DV = 128
MLSTM_HEADS = 4
MLSTM_DQK = 64
MLSTM_DV = 64
SSD_HEADS = 4
SSD_HEAD_DIM = 64
SSD_GROUPS = 2
SSD_STATE = 128
CONV_WIDTH = 4
FFN_CONV_WIDTH = 3
D_FF = 2816
CHUNK = 64
EPS = 1e-6

GDN_QK = GDN_HEADS * GDN_DK
GDN_V = GDN_HEADS * GDN_DV
MLSTM_QK = MLSTM_HEADS * MLSTM_DQK
MLSTM_V = MLSTM_HEADS * MLSTM_DV
SSD_INNER = SSD_HEADS * SSD_HEAD_DIM
SSD_BC = SSD_GROUPS * SSD_STATE
D_MIX = GDN_V + MLSTM_V + SSD_INNER
CONV_DIM = 2 * GDN_QK + GDN_V + SSD_INNER + 2 * SSD_BC
IN_COLS = CONV_DIM + GDN_V + 2 * GDN_HEADS + 2 * MLSTM_QK + 2 * MLSTM_V + 2 * MLSTM_HEADS + SSD_INNER + SSD_HEADS
CONV_SIZES = (GDN_QK, GDN_QK, GDN_V, SSD_INNER, SSD_BC, SSD_BC)
REST_SIZES = (GDN_V, GDN_HEADS, GDN_HEADS, MLSTM_QK, MLSTM_QK, MLSTM_V, MLSTM_V,
              MLSTM_HEADS, MLSTM_HEADS, SSD_INNER, SSD_HEADS)

kernel_name = "hymba_gdn_mlstm_ssd_convffn_step"


def rms_norm(x, w):
    xf = x.astype(jnp.float32)
    y = xf * lax.rsqrt(jnp.mean(xf * xf, axis=-1, keepdims=True) + EPS)
    return (y * w.astype(jnp.float32)).astype(x.dtype)


def l2norm(x):
    return x * lax.rsqrt(jnp.sum(x * x, axis=-1, keepdims=True) + EPS)


def _split(t, sizes):
    return jnp.split(t, np.cumsum(sizes)[:-1].tolist(), axis=-1)


def _to_chunks(t, n_chunks):
    b, l = t.shape[:2]
    return jnp.swapaxes(t.reshape((b, n_chunks, l // n_chunks) + t.shape[2:]), 0, 1)


def _from_chunks(t):
    t = jnp.swapaxes(t, 0, 1)
    return t.reshape((t.shape[0], t.shape[1] * t.shape[2]) + t.shape[3:])


def _masks(c):
    tril = jnp.tril(jnp.ones((c, c), dtype=bool))
    strict = jnp.tril(jnp.ones((c, c), dtype=bool), -1)
    return tril, strict


def causal_dwconv(x, buf, w, b):
    width = w.shape[0]
    L = x.shape[1]
    xp = jnp.concatenate([buf.astype(x.dtype), x], axis=1)
    y = b
    for j in range(width):
        y = y + w[j] * xp[:, j:j + L]
    return y, xp[:, L:]


def gdn_chunked(q, k, v, g, beta, s0):
    L = q.shape[1]
    c = math.gcd(L, CHUNK)
    n = L // c
    tril, strict = _masks(c)
    eye = jnp.eye(c, dtype=jnp.float32)

    def step(s, inp):
        qi, ki, vi, gi, bi = inp
        gam = jnp.cumsum(jnp.swapaxes(gi, 1, 2), axis=-1)
        bh = jnp.swapaxes(bi, 1, 2)
        decay = jnp.exp(jnp.where(tril, gam[..., :, None] - gam[..., None, :], -jnp.inf))
        kk = jnp.einsum('bihd,bjhd->bhij', ki, ki)
        m = jnp.where(strict, kk * decay * bh[..., :, None], 0.0)
        t = lax.linalg.triangular_solve(m + eye, jnp.broadcast_to(eye, m.shape), left_side=True,
                                        lower=True, unit_diagonal=True)
        u = jnp.einsum('bhij,bjhv,bhj->bhiv', t, vi, bh)
        wk = jnp.einsum('bhij,bjhd,bhj->bhid', t, ki, bh * jnp.exp(gam))
        v_new = u - jnp.einsum('bhid,bhdv->bhiv', wk, s)
        qk = jnp.einsum('bihd,bjhd->bhij', qi, ki) * decay
        o = (jnp.einsum('bihd,bhi,bhdv->bhiv', qi, jnp.exp(gam), s)
             + jnp.einsum('bhij,bhjv->bhiv', qk, v_new))
        g_last = gam[..., -1]
        s_new = (jnp.exp(g_last)[..., None, None] * s
                 + jnp.einsum('bjhd,bhj,bhjv->bhdv', ki, jnp.exp(g_last[..., None] - gam), v_new))
        return s_new, jnp.swapaxes(o, 1, 2)

    xs = tuple(_to_chunks(t, n) for t in (q, k, v, g, beta))
    s_fin, o = lax.scan(step, s0, xs)
    return _from_chunks(o), s_fin


def mlstm_chunked(q, k, v, log_i, log_f, c0, n0, m0):
    L = q.shape[1]
    c = math.gcd(L, CHUNK)
    n = L // c
    tril, _ = _masks(c)

    def step(carry, inp):
        cm, nv, m = carry
        qi, ki, vi, ii, fi = inp
        b = jnp.cumsum(jnp.swapaxes(fi, 1, 2), axis=-1)
        ih = jnp.swapaxes(ii, 1, 2)
        d = jnp.where(tril, b[..., :, None] - b[..., None, :] + ih[..., None, :], -jnp.inf)
        inter = b + m[..., None]
        m_t = jnp.maximum(inter, jnp.max(d, axis=-1))
        w_intra = jnp.exp(d - m_t[..., None])
        w_inter = jnp.exp(inter - m_t)
        s = jnp.einsum('bihd,bjhd->bhij', qi, ki) * w_intra
        num = (jnp.einsum('bihd,bhdv->bhiv', qi, cm) * w_inter[..., None]
               + jnp.einsum('bhij,bjhv->bhiv', s, vi))
        den = jnp.einsum('bihd,bhd->bhi', qi, nv) * w_inter + jnp.sum(s, axis=-1)
        den = jnp.maximum(jnp.abs(den), jnp.exp(-m_t))
        h = jnp.swapaxes(num / den[..., None], 1, 2)
        m_new = m_t[..., -1]
        b_last = b[..., -1]
        w_c = jnp.exp(b_last + m - m_new)
        w_k = jnp.exp(b_last[..., None] - b + ih - m_new[..., None])
        c_new = w_c[..., None, None] * cm + jnp.einsum('bjhd,bhj,bjhv->bhdv', ki, w_k, vi)
        n_new = w_c[..., None] * nv + jnp.einsum('bjhd,bhj->bhd', ki, w_k)
        return (c_new, n_new, m_new), h

    xs = tuple(_to_chunks(t, n) for t in (q, k, v, log_i, log_f))
    (c_fin, n_fin, m_fin), h = lax.scan(step, (c0, n0, m0), xs)
    return _from_chunks(h), c_fin, n_fin, m_fin


def ssd_chunked(x, dt, a, bm, cm, h0):
    L = x.shape[1]
    c = math.gcd(L, CHUNK)
    n = L // c
    tril, _ = _masks(c)
    rep = SSD_HEADS // SSD_GROUPS
    bh_ = jnp.repeat(bm, rep, axis=2)
    ch_ = jnp.repeat(cm, rep, axis=2)

    def step(h, inp):
        xi, dti, bi, ci = inp
        dth = jnp.swapaxes(dti, 1, 2)
        gam = jnp.cumsum(dth * a[:, None], axis=-1)
        decay = jnp.exp(jnp.where(tril, gam[..., :, None] - gam[..., None, :], -jnp.inf))
        cb = jnp.einsum('bihn,bjhn->bhij', ci, bi) * decay * dth[..., None, :]
        y = (jnp.einsum('bhij,bjhp->bihp', cb, xi)
             + jnp.einsum('bihn,bhi,bhpn->bihp', ci, jnp.exp(gam), h))
        g_last = gam[..., -1]
        h_new = (jnp.exp(g_last)[..., None, None] * h
                 + jnp.einsum('bjhp,bhj,bjhn->bhpn', xi, jnp.exp(g_last[..., None] - gam) * dth, bi))
        return h_new, y

    xs = tuple(_to_chunks(t, n) for t in (x, dt, bh_, ch_))
    h_fin, y = lax.scan(step, h0, xs)
    return _from_chunks(y), h_fin


def hybrid_mixer(h, conv_state, gdn_state, ml_c, ml_n, ml_m, ssd_state, w_in, conv_w, conv_b,
                 gdn_a_log, gdn_dt_bias, gdn_norm, mlstm_i_bias, mlstm_f_bias, mlstm_norm,
                 ssd_a_log, ssd_dt_bias, ssd_d, ssd_norm, w_out):
    B, L, _ = h.shape
    f32 = lambda t: t.astype(jnp.float32)
    proj = jnp.einsum('bld,de->ble', h, w_in)
    conv_in, rest = proj[..., :CONV_DIM], proj[..., CONV_DIM:]
    conv_out, new_conv = causal_dwconv(conv_in, conv_state, conv_w, conv_b)
    conv_out = jax.nn.silu(conv_out)
    gq, gk, gv, sx, sb, sc = _split(conv_out, CONV_SIZES)
    gg, ga, gb, mq, mk, mv, mo, mi, mf, sz, sdt = _split(rest, REST_SIZES)

    q = l2norm(f32(gq).reshape(B, L, GDN_HEADS, GDN_DK)) * (GDN_DK ** -0.5)
    k = l2norm(f32(gk).reshape(B, L, GDN_HEADS, GDN_DK))
    v = f32(gv).reshape(B, L, GDN_HEADS, GDN_DV)
    g = -jnp.exp(f32(gdn_a_log)) * jax.nn.softplus(f32(ga) + f32(gdn_dt_bias))
    beta = jax.nn.sigmoid(f32(gb))
    o_gdn, gdn_new = gdn_chunked(q, k, v, g, beta, f32(gdn_state))
    o_gdn = rms_norm(o_gdn, gdn_norm) * jax.nn.silu(f32(gg).reshape(B, L, GDN_HEADS, GDN_DV))
    o_gdn = o_gdn.reshape(B, L, GDN_V)

    mq_ = f32(mq).reshape(B, L, MLSTM_HEADS, MLSTM_DQK)
    mk_ = f32(mk).reshape(B, L, MLSTM_HEADS, MLSTM_DQK) * (MLSTM_DQK ** -0.5)
    mv_ = f32(mv).reshape(B, L, MLSTM_HEADS, MLSTM_DV)
    log_i = f32(mi) + f32(mlstm_i_bias)
    log_f = jax.nn.log_sigmoid(f32(mf) + f32(mlstm_f_bias))
    h_ml, c_new, n_new, m_new = mlstm_chunked(mq_, mk_, mv_, log_i, log_f, f32(ml_c), f32(ml_n), f32(ml_m))
    h_ml = jax.nn.sigmoid(f32(mo)).reshape(B, L, MLSTM_HEADS, MLSTM_DV) * h_ml
    h_ml = rms_norm(h_ml, mlstm_norm).reshape(B, L, MLSTM_V)

    xs = f32(sx).reshape(B, L, SSD_HEADS, SSD_HEAD_DIM)
    bm = f32(sb).reshape(B, L, SSD_GROUPS, SSD_STATE)
    cm = f32(sc).reshape(B, L, SSD_GROUPS, SSD_STATE)
    dt = jax.nn.softplus(f32(sdt) + f32(ssd_dt_bias))
    a = -jnp.exp(f32(ssd_a_log))
    y_ssd, ssd_new = ssd_chunked(xs, dt, a, bm, cm, f32(ssd_state))
    y_ssd = (y_ssd + f32(ssd_d)[:, None] * xs).reshape(B, L, SSD_INNER)
    y_ssd = rms_norm(y_ssd * jax.nn.silu(f32(sz)), ssd_norm)

    mix = jnp.concatenate([o_gdn, h_ml, y_ssd], axis=-1).astype(h.dtype)
    out = jnp.einsum('ble,ed->bld', mix, w_out)
    return (out, new_conv.astype(conv_state.dtype), gdn_new.astype(gdn_state.dtype),
            c_new.astype(ml_c.dtype), n_new.astype(ml_n.dtype), m_new.astype(ml_m.dtype),
            ssd_new.astype(ssd_state.dtype))


def conv_ffn(h, ffn_state, w_up, conv_w, conv_b, w_down):
    up = jnp.einsum('bld,df->blf', h, w_up)
    gate, val = up[..., :D_FF], up[..., D_FF:]
    gate, new_state = causal_dwconv(gate, ffn_state, conv_w, conv_b)
    out = jnp.einsum('blf,fd->bld', jax.nn.gelu(gate, approximate=True) * val, w_down)
    return out, new_state.astype(ffn_state.dtype)


def run_trunk(x, conv_s, gdn_s, mc_s, mn_s, mm_s, ssd_s, ffn_s, weights):
    (norm_mix_pre, norm_mix_post, norm_ffn_pre, norm_ffn_post, w_in, conv_w, conv_b,
     gdn_a_log, gdn_dt_bias, gdn_norm, mlstm_i_bias, mlstm_f_bias, mlstm_norm,
     ssd_a_log, ssd_dt_bias, ssd_d, ssd_norm, w_out,
     ffn_w_up, ffn_conv_w, ffn_conv_b, ffn_w_down) = weights
    per_layer = []
    for l in range(DEPTH):
        hn = rms_norm(x, norm_mix_pre[l])
        mix, nc, ng, nmc, nmn, nmm, nss = hybrid_mixer(
            hn, conv_s[l], gdn_s[l], mc_s[l], mn_s[l], mm_s[l], ssd_s[l], w_in[l], conv_w[l], conv_b[l],
            gdn_a_log[l], gdn_dt_bias[l], gdn_norm[l], mlstm_i_bias[l], mlstm_f_bias[l], mlstm_norm[l],
            ssd_a_log[l], ssd_dt_bias[l], ssd_d[l], ssd_norm[l], w_out[l])
        x = x + rms_norm(mix, norm_mix_post[l])
        hn = rms_norm(x, norm_ffn_pre[l])
        f, nf = conv_ffn(hn, ffn_s[l], ffn_w_up[l], ffn_conv_w[l], ffn_conv_b[l], ffn_w_down[l])
        x = x + rms_norm(f, norm_ffn_post[l])
        per_layer.append((nc, ng, nmc, nmn, nmm, nss, nf))
    new_states = [jnp.stack(s, axis=0) for s in zip(*per_layer)]
    return x, new_states


def _dt_bias(key, shape):
    dt = jnp.exp(jax.random.uniform(key, shape, minval=math.log(1e-3), maxval=math.log(1e-1)))
    return dt + jnp.log(-jnp.expm1(-dt))


def setup_inputs(seed: int = 0) -> dict:
    key = jax.random.key(seed)
    ks = iter(jax.random.split(key, 48))
    nrm = lambda shape, scale: scale * jax.random.normal(next(ks), shape, dtype=jnp.float32)
    gain = lambda shape: 1.0 + nrm(shape, 0.05)
    Dp = DEPTH
    inp = {}
    inp["x_prompt"] = nrm((BATCH, SEQ, D_MODEL), 1.0)
    inp["x_sample"] = nrm((DEC_BATCH, DEC_SEQ, D_MODEL), 1.0)
    inp["state_conv"] = nrm((Dp, DEC_BATCH, CONV_WIDTH - 1, CONV_DIM), 1.0)
    inp["state_gdn"] = nrm((Dp, DEC_BATCH, GDN_HEADS, GDN_DK, GDN_DV), 0.3)
    inp["state_mlstm_c"] = nrm((Dp, DEC_BATCH, MLSTM_HEADS, MLSTM_DQK, MLSTM_DV), 0.3)
    inp["state_mlstm_n"] = nrm((Dp, DEC_BATCH, MLSTM_HEADS, MLSTM_DQK), 0.3)
    inp["state_mlstm_m"] = nrm((Dp, DEC_BATCH, MLSTM_HEADS), 1.0)
    inp["state_ssd"] = nrm((Dp, DEC_BATCH, SSD_HEADS, SSD_HEAD_DIM, SSD_STATE), 0.3)
    inp["state_ffn_conv"] = nrm((Dp, DEC_BATCH, FFN_CONV_WIDTH - 1, D_FF), 1.0)
    inp["norm_mix_pre"] = gain((Dp, D_MODEL))
    inp["norm_mix_post"] = gain((Dp, D_MODEL))
    inp["norm_ffn_pre"] = gain((Dp, D_MODEL))
    inp["norm_ffn_post"] = gain((Dp, D_MODEL))
    inp["w_in"] = nrm((Dp, D_MODEL, IN_COLS), D_MODEL ** -0.5)
    inp["conv_w"] = nrm((Dp, CONV_WIDTH, CONV_DIM), CONV_WIDTH ** -0.5)
    inp["conv_b"] = nrm((Dp, CONV_DIM), 0.01)
    inp["gdn_a_log"] = jnp.log(jax.random.uniform(next(ks), (Dp, GDN_HEADS), minval=1.0, maxval=16.0))
    inp["gdn_dt_bias"] = _dt_bias(next(ks), (Dp, GDN_HEADS))
    inp["gdn_norm"] = gain((Dp, GDN_DV))
    inp["mlstm_i_bias"] = nrm((Dp, MLSTM_HEADS), 0.1)
    inp["mlstm_f_bias"] = jnp.linspace(3.0, 6.0, MLSTM_HEADS, dtype=jnp.float32)[None, :] + nrm((Dp, MLSTM_HEADS), 0.1)
    inp["mlstm_norm"] = gain((Dp, MLSTM_DV))
    inp["ssd_a_log"] = jnp.log(jax.random.uniform(next(ks), (Dp, SSD_HEADS), minval=1.0, maxval=16.0))
    inp["ssd_dt_bias"] = _dt_bias(next(ks), (Dp, SSD_HEADS))
    inp["ssd_d"] = gain((Dp, SSD_HEADS))
    inp["ssd_norm"] = gain((Dp, SSD_INNER))
    inp["w_out"] = nrm((Dp, D_MIX, D_MODEL), D_MIX ** -0.5)
    inp["ffn_w_up"] = nrm((Dp, D_MODEL, 2 * D_FF), D_MODEL ** -0.5)
    inp["ffn_conv_w"] = nrm((Dp, FFN_CONV_WIDTH, D_FF), FFN_CONV_WIDTH ** -0.5)
    inp["ffn_conv_b"] = nrm((Dp, D_FF), 0.01)
    inp["ffn_w_down"] = nrm((Dp, D_FF, D_MODEL), D_FF ** -0.5)
    return inp


def reference(x_prompt, x_sample, state_conv, state_gdn, state_mlstm_c, state_mlstm_n, state_mlstm_m,
              state_ssd, state_ffn_conv, norm_mix_pre, norm_mix_post, norm_ffn_pre, norm_ffn_post,
              w_in, conv_w, conv_b, gdn_a_log, gdn_dt_bias, gdn_norm, mlstm_i_bias, mlstm_f_bias,
              mlstm_norm, ssd_a_log, ssd_dt_bias, ssd_d, ssd_norm, w_out,
              ffn_w_up, ffn_conv_w, ffn_conv_b, ffn_w_down):
    weights = (norm_mix_pre, norm_mix_post, norm_ffn_pre, norm_ffn_post, w_in, conv_w, conv_b,
               gdn_a_log, gdn_dt_bias, gdn_norm, mlstm_i_bias, mlstm_f_bias, mlstm_norm,
               ssd_a_log, ssd_dt_bias, ssd_d, ssd_norm, w_out,
               ffn_w_up, ffn_conv_w, ffn_conv_b, ffn_w_down)
    bp = x_prompt.shape[0]
    dtp = x_prompt.dtype
    z_conv = jnp.zeros((DEPTH, bp, CONV_WIDTH - 1, CONV_DIM), dtp)
    z_gdn = jnp.zeros((DEPTH, bp, GDN_HEADS, GDN_DK, GDN_DV), dtp)
    z_mc = jnp.zeros((DEPTH, bp, MLSTM_HEADS, MLSTM_DQK, MLSTM_DV), dtp)
    z_mn = jnp.zeros((DEPTH, bp, MLSTM_HEADS, MLSTM_DQK), dtp)
    z_mm = jnp.zeros((DEPTH, bp, MLSTM_HEADS), dtp)
    z_ssd = jnp.zeros((DEPTH, bp, SSD_HEADS, SSD_HEAD_DIM, SSD_STATE), dtp)
    z_ffn = jnp.zeros((DEPTH, bp, FFN_CONV_WIDTH - 1, D_FF), dtp)
    y_prompt, (p_conv, p_gdn, p_mc, p_mn, p_mm, p_ssd, p_ffn) = run_trunk(
        x_prompt, z_conv, z_gdn, z_mc, z_mn, z_mm, z_ssd, z_ffn, weights)
    y_sample, (s_conv, s_gdn, s_mc, s_mn, s_mm, s_ssd, s_ffn) = run_trunk(
        x_sample, state_conv, state_gdn, state_mlstm_c, state_mlstm_n, state_mlstm_m, state_ssd,
        state_ffn_conv, weights)
    return (y_prompt, y_sample, p_conv, p_gdn, p_mc, p_mn, p_mm, p_ssd, p_ffn,
            s_conv, s_gdn, s_mc, s_mn, s_mm, s_ssd, s_ffn)
```

```python
import numpy as np
import concourse.bass as bass
import concourse.mybir as mybir

F32 = mybir.dt.float32
BF16 = mybir.dt.bfloat16
ALU = mybir.AluOpType
AF = mybir.ActivationFunctionType
AX = mybir.AxisListType


class T:
    def __init__(self, ap, name):
        self.ap = ap
        self.name = name
        self.we = None
        self.wd = []
        self.re = {}
        self.rd = []

    def __getitem__(self, idx):
        return V(self, self.ap[idx])

    @property
    def t(self):
        return self


class V:
    def __init__(self, t, ap):
        self.t = t
        self.ap = ap

    def __getitem__(self, idx):
        return V(self.t, self.ap[idx])

    def re(self, pattern_, **kw):
        return V(self.t, self.ap.rearrange(pattern_, **kw))

    def bc(self, shape):
        return V(self.t, self.ap.to_broadcast(shape))


def _ap(x):
    return x.ap if isinstance(x, (T, V)) else x


class KB:
    def __init__(self, nc, es, n_dma_sems=6):
        self.nc = nc
        self.es = es
        self.E = {"pe": nc.tensor, "act": nc.scalar, "dve": nc.vector, "pool": nc.gpsimd, "sp": nc.sync}
        self.sem = {e: es.enter_context(nc.semaphore("s_" + e)) for e in ("pe", "act", "dve", "pool")}
        self.cnt = {e: 0 for e in self.sem}
        self.known = {e: {} for e in self.E}
        self.knownd = {e: {} for e in self.E}
        self.pend = {e: [] for e in self.E}
        self.dsems = {}
        for q in ("sp", "pool", "act"):
            self.dsems[q] = [[es.enter_context(nc.semaphore("d_%s%d" % (q, i))), 0] for i in range(n_dma_sems if q != "act" else 4)]
        self.dnext = {q: 0 for q in self.dsems}
        self.nbank = 0
        self.out_deps = []

    def sb(self, name, shape, dt=F32):
        self.nbank += 1
        name = "sb%d_%s" % (self.nbank, name)
        return T(self.es.enter_context(self.nc.sbuf_tensor(name, list(shape), dt)).ap(), name)

    def ps(self, name, shape, dt=F32):
        self.nbank += 1
        name = "ps%d_%s" % (self.nbank, name)
        t = T(self.es.enter_context(self.nc.psum_tensor(name, list(shape), dt)).ap(), name)
        t.psum = True
        return t

    def _wait_e(self, eng, dep):
        e2, c = dep
        if self.known[eng].get(e2, 0) >= c:
            return
        self.E[eng].wait_ge(self.sem[e2], c)
        self.known[eng][e2] = c

    def _wait_d(self, eng, dep):
        key, sem, tgt = dep
        if self.knownd[eng].get(key, 0) >= tgt:
            return
        self.E[eng].wait_ge(sem, tgt)
        self.knownd[eng][key] = tgt

    def _pre(self, eng, outs, ins):
        for e2, pl in self.pend.items():
            if e2 == eng:
                continue
            for (po, pi) in pl:
                for v in outs:
                    assert all(v.t is not x.t for x in po + pi), ("pending hazard", v.t.name, e2)
                for v in ins:
                    assert all(v.t is not x.t for x in po), ("pending hazard", v.t.name, e2)
        for v in ins:
            t = v.t
            if t.we is not None and not (eng == "pe" and t.we[0] == "pe"):
                self._wait_e(eng, t.we)
            for d in t.wd:
                self._wait_d(eng, d)
            if getattr(t, "psum", False):
                for e2, c in t.re.items():
                    if e2 != eng:
                        self._wait_e(eng, (e2, c))
        for v in outs:
            t = v.t
            if t.we is not None and not (eng == "pe" and t.we[0] == "pe"):
                self._wait_e(eng, t.we)
            for d in t.wd:
                self._wait_d(eng, d)
            for e2, c in t.re.items():
                if not (e2 == "pe" and eng == "pe"):
                    self._wait_e(eng, (e2, c))
            for d in t.rd:
                self._wait_d(eng, d)

    def op(self, eng, fn, outs, ins, inc=True):
        outs = [o for o in outs if o is not None]
        ins = [i for i in ins if isinstance(i, (T, V))]
        self._pre(eng, outs, ins)
        inst = fn()
        self.pend[eng].append((outs, ins))
        if inc:
            self.cnt[eng] += 1
            c = self.cnt[eng]
            inst.then_inc(self.sem[eng], 1)
            for (po, pi) in self.pend[eng]:
                for v in pi:
                    v.t.re[eng] = c
                for v in po:
                    v.t.we = (eng, c)
                    v.t.wd = []
                    v.t.re = {}
                    v.t.rd = []
            self.pend[eng] = []
        return inst

    def dma(self, q, out, in_, out_dram=False, **kw):
        assert not self.pend[q] if q in self.pend else True
        pool = self.dsems[q]
        i = self.dnext[q]
        self.dnext[q] = (i + 1) % len(pool)
        sem, prev = pool[i]
        key = (q, i)
        if prev > 0:
            self._wait_d(q, (key, sem, prev))
        self._pre(q, [o for o in [out] if isinstance(o, (T, V))], [o for o in [in_] if isinstance(o, (T, V))])
        tgt = prev + 16
        pool[i][1] = tgt
        self.E[q].dma_start(out=_ap(out), in_=_ap(in_), **kw).then_inc(sem, 16)
        dep = (key, sem, tgt)
        if isinstance(in_, (T, V)):
            in_.t.rd.append(dep)
        if isinstance(out, (T, V)):
            out.t.wd.append(dep)
        if out_dram:
            self.out_deps.append(dep)
        return dep

    def finish(self):
        for d in self.out_deps:
            self._wait_d("sp", d)

    def mm(self, out, lhsT, rhs, start=True, stop=True, inc=True):
        return self.op("pe", lambda: self.nc.tensor.matmul(_ap(out), _ap(lhsT), _ap(rhs), start=start, stop=stop),
                       [out], [lhsT, rhs], inc=inc)

    def tr(self, out, in_, ident, inc=True):
        return self.op("pe", lambda: self.nc.tensor.transpose(_ap(out), _ap(in_), _ap(ident)), [out], [in_, ident], inc=inc)

    def act(self, out, in_, func, bias=0.0, scale=1.0, eng="act"):
        return self.op("act", lambda: self.nc.scalar.activation(_ap(out), _ap(in_), func, bias=_ap(bias), scale=_ap(scale)),
                       [out], [in_, bias, scale])

    def ts(self, out, in0, s1, s2, op0, op1=None, eng="dve"):
        e = self.E[eng]
        if op1 is None:
            return self.op(eng, lambda: e.tensor_scalar(_ap(out), _ap(in0), _ap(s1), None, op0), [out], [in0, s1])
        return self.op(eng, lambda: e.tensor_scalar(_ap(out), _ap(in0), _ap(s1), _ap(s2), op0, op1), [out], [in0, s1, s2])

    def stt(self, out, in0, s, in1, op0, op1):
        return self.op("dve", lambda: self.nc.vector.scalar_tensor_tensor(_ap(out), _ap(in0), _ap(s), _ap(in1), op0, op1),
                       [out], [in0, s, in1])

    def tt(self, out, in0, in1, op, eng="dve"):
        e = self.E[eng]
        return self.op(eng, lambda: e.tensor_tensor(_ap(out), _ap(in0), _ap(in1), op), [out], [in0, in1])

    def cp(self, out, in_, eng="dve"):
        if eng == "act":
            return self.op("act", lambda: self.nc.scalar.copy(_ap(out), _ap(in_)), [out], [in_])
        e = self.E[eng]
        return self.op(eng, lambda: e.tensor_copy(_ap(out), _ap(in_)), [out], [in_])

    def recip(self, out, in_):
        return self.op("dve", lambda: self.nc.vector.reciprocal(_ap(out), _ap(in_)), [out], [in_])

    def red(self, out, in_, op, axis=AX.X):
        return self.op("dve", lambda: self.nc.vector.tensor_reduce(_ap(out), _ap(in_), axis, op), [out], [in_])

    def memset(self, out, val, eng="dve"):
        e = self.E[eng]
        return self.op(eng, lambda: e.memset(_ap(out), val), [out], [])


from contextlib import ExitStack
from concourse.bass_utils import run_bass_kernel_spmd

D = 1024
KC = 8
CONV_DIM = 2304
IN_COLS = 4116
NREST = 1812
DFF = 2816
NFB = 22
EPS = 1e-6
NEG = -30000.0
LNQ = float(np.log(128.0 ** -0.5))
DEBUG = False
HOIST_SSD = True
ILV_ML = True


def _step(g):
    try:
        next(g)
        return True
    except StopIteration:
        return False
NWU = 4
NWD = 3
STOP = 7
CUT = 99


class _Stop(Exception):
    pass
R_GG, R_GA, R_GB, R_MQ, R_MK, R_MV, R_MO, R_MI, R_MF, R_SZ, R_SDT = 0, 512, 516, 520, 776, 1032, 1288, 1544, 1548, 1552, 1808
MUL, ADD, SUB, MAX = ALU.mult, ALU.add, ALU.subtract, ALU.max


def make_consts(NS):
    c = {}
    c["ident"] = np.eye(128, dtype=np.float32)
    c["ones"] = np.ones((128, 128), np.float32)
    i = np.arange(128)[:, None]
    j = np.arange(128)[None, :]
    cm = np.zeros((128, 6, 128), np.float32)
    cm[:, 0] = (i <= j)
    cm[:, 1] = np.where(i > j, 0.0, NEG)
    cm[:, 2] = np.where(j >= i, 0.0, NEG)
    cm[:, 3] = np.where(j <= i, 0.0, NEG)
    cm[:, 4] = 1.0
    cm[:, 5] = (i == 127)
    c["cm_p"] = cm
    ls = np.zeros((128, 1), np.float32)
    ls[127, 0] = 1
    c["lastsel_p"] = ls
    n = NS * 4
    i = np.arange(n)[:, None]
    j = np.arange(n)[None, :]
    same = (i // 4) == (j // 4)
    cs = np.zeros((n, 6, n), np.float32)
    cs[:, 0] = same & (i <= j)
    cs[:, 1] = np.where(same & (i > j), 0.0, NEG)
    cs[:, 2] = np.where(same & (j >= i), 0.0, NEG)
    cs[:, 3] = np.where(same & (j <= i), 0.0, NEG)
    cs[:, 4] = same
    cs[:, 5] = same & (i % 4 == 3)
    c["cm_s"] = cs
    s = np.arange(NS)[None, :]
    k = np.arange(n)[:, None]
    c["lastsel_s"] = (k == 4 * s + 3).astype(np.float32)
    c["rowmask"] = ((k // 4) == s).astype(np.float32)
    return c


def build(NPC, NS, DEPTH):
    nc = bass.Bass("TRN2", target_bir_lowering=False)
    TP = NPC * 128
    NST = NS * 4
    assert NST <= 64

    def din(name, shape):
        return nc.dram_tensor(name, list(shape), F32, kind="ExternalInput").ap()

    def dout(name, shape):
        return nc.dram_tensor(name, list(shape), F32, kind="ExternalOutput").ap()

    xp = din("xp", [TP, D])
    xs = din("xs", [NST, D])
    st_conv = din("st_conv", [DEPTH, NS, 3, CONV_DIM])
    st_gdn = din("st_gdn", [DEPTH, NS, 4, 128, 128])
    st_mc = din("st_mc", [DEPTH, NS, 4, 64, 64])
    st_mn = din("st_mn", [DEPTH, NS, 4, 64])
    st_mm = din("st_mm", [DEPTH, NS, 4])
    st_ssd = din("st_ssd", [DEPTH, NS, 4, 64, 128])
    st_ffn = din("st_ffn", [DEPTH, NS, 2, DFF])
    nrm = [din(n, [DEPTH, D]) for n in ("norm_mix_pre", "norm_mix_post", "norm_ffn_pre", "norm_ffn_post")]
    w_in = din("w_in", [DEPTH, D, IN_COLS])
    conv_w = din("conv_w", [DEPTH, 4, CONV_DIM])
    conv_b = din("conv_b", [DEPTH, CONV_DIM])
    hp_names = ["gdn_a_log", "gdn_dt_bias", "mlstm_i_bias", "mlstm_f_bias", "ssd_a_log", "ssd_dt_bias", "ssd_d"]
    hp_d = [din(n, [DEPTH, 4]) for n in hp_names]
    gdn_norm = din("gdn_norm", [DEPTH, 128])
    mlstm_norm = din("mlstm_norm", [DEPTH, 64])
    ssd_norm = din("ssd_norm", [DEPTH, 256])
    w_out = din("w_out", [DEPTH, D, D])
    w_up = din("ffn_w_up", [DEPTH, D, 2 * DFF])
    fconv_w = din("ffn_conv_w", [DEPTH, 3, DFF])
    fconv_b = din("ffn_conv_b", [DEPTH, DFF])
    w_down = din("ffn_w_down", [DEPTH, DFF, D])
    c_ident = din("ident", [128, 128])
    c_ones = din("ones", [128, 128])
    c_cm_p = din("cm_p", [128, 6, 128])
    c_ls_p = din("lastsel_p", [128, 1])
    c_cm_s = din("cm_s", [NST, 6, NST])
    c_ls_s = din("lastsel_s", [NST, NS])
    c_rowmask = din("rowmask", [NST, NS])

    y_p = dout("y_p", [TP, D])
    y_s = dout("y_s", [NST, D])
    o_pconv = dout("p_conv", [DEPTH, 3, CONV_DIM])
    o_pgdn = dout("p_gdn", [DEPTH, 4, 128, 128])
    o_pmc = dout("p_mc", [DEPTH, 4, 64, 64])
    o_pmn = dout("p_mn", [DEPTH, 4, 64])
    o_pmm = dout("p_mm", [DEPTH, 4])
    o_pssd = dout("p_ssd", [DEPTH, 4, 64, 128])
    o_pffn = dout("p_ffn", [DEPTH, 2, DFF])
    o_sconv = dout("s_conv", [DEPTH, NS, 3, CONV_DIM])
    o_sgdn = dout("s_gdn", [DEPTH, NS, 4, 128, 128])
    o_smc = dout("s_mc", [DEPTH, NS, 4, 64, 64])
    o_smn = dout("s_mn", [DEPTH, NS, 4, 64])
    o_smm = dout("s_mm", [DEPTH, NS, 4])
    o_sssd = dout("s_ssd", [DEPTH, NS, 4, 64, 128])
    o_sffn = dout("s_ffn", [DEPTH, NS, 2, DFF])

    NCH = NPC + 1
    dbg_mix = dout("dbg_mix", [NST, D]) if DEBUG else None
    NSL = True

    with ExitStack() as es0:
        kb = KB(nc, es0)
        PS = [kb.ps("psb%d" % i, [128, 512]) for i in range(8)]
        bstate = [0]

        held = set()

        def bank(hold=False):
            while (bstate[0] % 8) in held:
                bstate[0] += 1
            i = bstate[0] % 8
            bstate[0] += 1
            if hold:
                held.add(i)
            return PS[i]

        def release(*bs):
            for b in bs:
                held.discard(PS.index(b))

        def barrier():
            for e in ("pe", "act", "dve", "pool", "sp"):
                for e2 in ("pe", "act", "dve", "pool"):
                    if e2 != e and kb.cnt[e2] > 0:
                        kb._wait_e(e, (e2, kb.cnt[e2]))
                for q in kb.dsems:
                    for i, (sem, tgt) in enumerate(kb.dsems[q]):
                        if tgt > 0:
                            kb._wait_d(e, ((q, i), sem, tgt))

        xT = [kb.sb("xT%d" % c, [128, 8, 128 if c < NPC else NST]) for c in range(NCH)]
        ident = kb.sb("ident", [128, 128])
        ones = kb.sb("ones", [128, 128])
        cmp_ = kb.sb("cm_p", [128, 6, 128])
        lsp = kb.sb("ls_p", [128, 1])
        cms = kb.sb("cm_s", [NST, 6, NST])
        lss = kb.sb("ls_s", [NST, NS])
        rowmask = kb.sb("rowmask", [NST, NS])
        NW = kb.sb("NW", [128, 4 * DEPTH * 8])
        CW = kb.sb("CW", [128, DEPTH, 18, 4])
        CB = kb.sb("CB", [128, DEPTH, 18])
        FCW = kb.sb("FCW", [128, DEPTH, NFB, 3])
        FCB = kb.sb("FCB", [128, DEPTH, NFB])
        HP = kb.sb("HP", [128, 7, DEPTH, 4])
        GN = kb.sb("GN", [128, DEPTH, 128])
        MNb = kb.sb("MNb", [128, DEPTH, 64])
        SN = kb.sb("SN", [128, DEPTH, 256])
        RS = kb.sb("RS", [128, 128])
        SQa = kb.sb("SQa", [128, 8, 128])
        SQb = kb.sb("SQb", [128, 8, 128])

        kb.dma("sp", ident, c_ident)
        kb.dma("sp", ones, c_ones)
        kb.dma("sp", cmp_, c_cm_p)
        kb.dma("sp", lsp, c_ls_p)
        kb.dma("sp", cms, c_cm_s)
        kb.dma("sp", lss, c_ls_s)
        kb.dma("sp", rowmask, c_rowmask)
        for w in range(4):
            for l in range(DEPTH):
                o = (w * DEPTH + l) * 8
                kb.dma("sp", NW[:, o:o + 8], nrm[w][l].rearrange("(k p) -> p k", p=128), allow_slow_non_contiguous=True)
        for l in range(DEPTH):
            for j in range(4):
                kb.dma("sp", CW[:, l, :, j], conv_w[l, j].rearrange("(b p) -> p b", p=128), allow_slow_non_contiguous=True)
            kb.dma("sp", CB[:, l, :], conv_b[l].rearrange("(b p) -> p b", p=128), allow_slow_non_contiguous=True)
            for j in range(3):
                kb.dma("sp", FCW[:, l, :, j], fconv_w[l, j].rearrange("(b p) -> p b", p=128), allow_slow_non_contiguous=True)
            kb.dma("sp", FCB[:, l, :], fconv_b[l].rearrange("(b p) -> p b", p=128), allow_slow_non_contiguous=True)
        for i in range(7):
            kb.dma("sp", HP[:, i], hp_d[i].partition_broadcast(128))
        kb.dma("sp", GN, gdn_norm.partition_broadcast(128))
        kb.dma("sp", MNb, mlstm_norm.partition_broadcast(128))
        kb.dma("sp", SN, ssd_norm.partition_broadcast(128))
        for i in (0, 4):
            kb.act(HP[:, i], HP[:, i], AF.Exp)
            kb.ts(HP[:, i], HP[:, i], -1.0, None, MUL)

        def nw(w, l, k):
            o = (w * DEPTH + l) * 8 + k
            return NW[:, o:o + 1]

        def ctok(c):
            return 128 if c < NPC else NST

        def rmsnorm_to(dst, c, w, l, CT):
            xc = xT[c]
            kb.act(SQa[:, :, :CT], xc[:, :, :CT], AF.Square)
            ps = bank()
            for k in range(8):
                kb.mm(ps[:, :CT], ones, SQa[:, k, :CT], start=(k == 0), stop=(k == 7), inc=(k == 7))
            kb.act(RS[:, :CT], ps[:, :CT], AF.Ln, bias=EPS, scale=1.0 / D)
            kb.act(RS[:, :CT], RS[:, :CT], AF.Exp, scale=-0.5)
            for k in range(8):
                kb.stt(dst[:, k, :CT], xc[:, k, :CT], nw(w, l, k), RS[:, :CT], MUL, MUL)

        def postnorm_add(Y, c, w, l, CT):
            kb.act(SQb[:, :, :CT], Y[:, :, :CT], AF.Square)
            ps = bank()
            for k in range(8):
                kb.mm(ps[:, :CT], ones, SQb[:, k, :CT], start=(k == 0), stop=(k == 7), inc=(k == 7))
            kb.act(RS[:, :CT], ps[:, :CT], AF.Ln, bias=EPS, scale=1.0 / D)
            kb.act(RS[:, :CT], RS[:, :CT], AF.Exp, scale=-0.5)
            for k in range(8):
                kb.stt(Y[:, k, :CT], Y[:, k, :CT], nw(w, l, k), RS[:, :CT], MUL, MUL)
            kb.tt(xT[c][:, :, :CT], xT[c][:, :, :CT], Y[:, :, :CT], ADD)

        with ExitStack() as es1:
            kb.es = es1
            xin = [kb.sb("xin%d" % i, [128, D]) for i in range(2)]
            for c in range(NCH):
                CT = ctok(c)
                xi = xin[c % 2]
                src = xp[c * 128:(c + 1) * 128, :] if c < NPC else xs
                kb.dma("sp", xi[:CT, :], src)
                for half in range(2):
                    ps = bank()
                    for j in range(4):
                        k = half * 4 + j
                        kb.tr(ps[:, j * CT:(j + 1) * CT], xi[:CT, k * 128:(k + 1) * 128], ident[:CT, :CT], inc=(j == 3))
                    kb.cp(xT[c][:, half * 4:half * 4 + 4, :CT], ps[:, :4 * CT].re("p (k t) -> p k t", k=4), eng=("dve" if half == 0 else "act"))
            barrier()

        for l in range(DEPTH):
            with ExitStack() as es2:
                kb.es = es2
                WinR = kb.sb("WinR", [128, 8, NREST], BF16)
                Wout = kb.sb("Wout", [128, 8, D], BF16)
                for k in range(8):
                    kb.dma("pool", WinR[:, k, :], w_in[l, k * 128:(k + 1) * 128, CONV_DIM:IN_COLS])
                for k in range(8):
                    kb.dma("pool", Wout[:, k, :], w_out[l, k * 128:(k + 1) * 128, :])
                hnT = kb.sb("hnT", [128, 8, 128], BF16)
                R = kb.sb("R", [128, NREST])
                STGc = [kb.sb("STGc%d" % i, [64, 256]) for i in range(2)]
                stgi = [0]
                QKm = kb.sb("QKm", [64, 8, 128])
                CO = kb.sb("CO", [128, 12, 128])
                CO_main = CO
                TMP = {n: kb.sb("tmp" + n, [128, 512]) for n in ("E", "F", "G2", "H", "I", "X1")}
                SQa2 = SQa[:, :, :].re("p k t -> p (k t)")
                SQb2 = SQb[:, :, :].re("p k t -> p (k t)")
                TMP["A"] = SQa2[:, 0:512]
                TMP["B"] = SQa2[:, 512:1024]
                TMP["C"] = SQb2[:, 0:512]
                TMP["Dt"] = SQb2[:, 512:1024]
                SM = kb.sb("SM", [128, 160])
                EG = kb.sb("EG", [128, 4, 128])
                MIX = kb.sb("MIX", [128, D])
                mixT = kb.sb("mixT", [128, 8, 128], BF16)
                VE = kb.sb("VE", [128, 4, 65])
                KW = kb.sb("KW", [128, 4, 64])
                XT_ = kb.sb("XTs", [128, 256])
                kb.memset(VE[:, :, 64:65], 1.0)
                Z = SM[:, 0:12]
                SP = SM[:, 12:24]
                CS = SM[:, 24:36]
                GC = SM[:, 36:48]
                BETA = SM[:, 48:52]
                NBETA = SM[:, 52:56]
                LI = SM[:, 56:60]
                BE = SM[:, 60:64]
                KD = SM[:, 64:68]
                GL = SM[:, 68:80]
                CC = SM[:, 80:84]
                MX = SM[:, 84:88]
                INT = SM[:, 88:92]
                MT = SM[:, 92:96]
                NMT = SM[:, 96:100]
                WI = SM[:, 100:104]
                DEN = SM[:, 104:108]
                EM = SM[:, 108:112]
                WK = SM[:, 112:116]
                SS4 = SM[:, 116:120]
                RSTD4 = SM[:, 120:124]
                KDS = SM[:, 124:128]
                M0C = SM[:, 128:132]
                SS1 = SM[:, 132:133]

                def v4(t, CT, w=None):
                    w = CT if w is None else w
                    return t[:CT, 0:4 * w].re("p (h j) -> p h j", h=4)

                def bc4(v, CT, w):
                    return v[:CT].re("p (h o) -> p h o", o=1).bc([CT, 4, w])

                def mixer_chunk(c, P):
                    CT, nseq, sample = P["CT"], P["nseq"], P["sample"]
                    cm = P["cm"]
                    U, Ls, Um, Lm, SSm, LSm = (cm[:CT, i, :CT] for i in range(6))
                    lastsel = P["lastsel"]

                    def mbc(m):
                        return m.re("p (o j) -> p o j", o=1).bc([CT, 4, CT])

                    rmsnorm_to(hnT, c, 0, l, CT)

                    if CUT <= 1:
                        return
                    need_cs = sample or c == NPC - 1
                    rows = slice(0, CT) if sample else slice(CT - 3, CT)
                    nr = CT if sample else 3

                    def fm_pair(col0, conv=True):
                        ws = P["WS"][P["wsi"][0] % len(P["WS"])]
                        P["wsi"][0] += 1
                        kb.dma("pool", ws, w_in[l, :, col0:col0 + 256].rearrange("(k p) c -> p k c", p=128))
                        if conv and need_cs:
                            psr = bank()
                            for k in range(8):
                                kb.mm(psr[:nr, 0:256], hnT[:, k, rows], ws[:, k, :], start=(k == 0), stop=(k == 7), inc=(k == 7))
                            stg = STGc[stgi[0] % 2]
                            stgi[0] += 1
                            kb.cp(stg[:nr, :], psr[:nr, 0:256], eng="act")
                            if sample:
                                for r in range(3):
                                    kb.dma("sp", o_sconv[l, :, r, col0:col0 + 256], stg[1 + r:CT:4, :], out_dram=True)
                            else:
                                kb.dma("sp", o_pconv[l, :, col0:col0 + 256], stg[:3, :], out_dram=True)
                        return ws

                    for pc in range(4):
                        ps = bank()
                        c0 = pc * 453
                        for k in range(8):
                            kb.mm(ps[:CT, :453], hnT[:, k, :CT], WinR[:, k, c0:c0 + 453], start=(k == 0), stop=(k == 7), inc=(k == 7))
                        kb.cp(R[:CT, pc * 453:(pc + 1) * 453], ps[:CT, :453], eng=("act" if pc % 2 else "dve"))
                    for pr in range(2):
                        ws = fm_pair(CONV_DIM + R_MQ + pr * 256, conv=False)
                        ps = bank()
                        for hh in range(4):
                            for k in range(8):
                                kb.mm(ps[:64, hh * CT:(hh + 1) * CT], ws[:, k, hh * 64:(hh + 1) * 64], hnT[:, k, :CT], start=(k == 0), stop=(k == 7), inc=(k == 7 and hh == 3))
                        if pr == 0:
                            kb.cp(QKm[:, 0:4, :CT], ps[:64, 0:4 * CT].re("p (b t) -> p b t", b=4))
                        else:
                            kb.ts(QKm[:, 4:8, :CT], ps[:64, 0:4 * CT].re("p (b t) -> p b t", b=4), 0.125, None, MUL)

                    if CUT <= 2:
                        return
                    def hp(i):
                        return HP[:CT, i, l, :]
                    kb.act(R[:CT, R_GG:R_GG + 512], R[:CT, R_GG:R_GG + 512], AF.Silu)
                    kb.act(R[:CT, R_SZ:R_SZ + 256], R[:CT, R_SZ:R_SZ + 256], AF.Silu)
                    kb.act(R[:CT, R_MO:R_MO + 256], R[:CT, R_MO:R_MO + 256], AF.Sigmoid)
                    kb.act(BETA[:CT], R[:CT, R_GB:R_GB + 4], AF.Sigmoid)
                    kb.tt(Z[:CT, 0:4], R[:CT, R_GA:R_GA + 4], hp(1), ADD)
                    kb.tt(Z[:CT, 4:8], R[:CT, R_MF:R_MF + 4], hp(3), ADD)
                    kb.ts(Z[:CT, 4:8], Z[:CT, 4:8], -1.0, None, MUL)
                    kb.tt(Z[:CT, 8:12], R[:CT, R_SDT:R_SDT + 4], hp(5), ADD)
                    kb.act(SP[:CT], Z[:CT], AF.Exp)
                    kb.act(SP[:CT], SP[:CT], AF.Ln, bias=1.0)
                    kb.tt(CS[:CT, 0:4], SP[:CT, 0:4], hp(0), MUL)
                    kb.ts(CS[:CT, 4:8], SP[:CT, 4:8], -1.0, None, MUL)
                    kb.tt(CS[:CT, 8:12], SP[:CT, 8:12], hp(4), MUL)
                    kb.ts(NBETA[:CT], BETA[:CT], -1.0, None, MUL)
                    kb.tt(LI[:CT], R[:CT, R_MI:R_MI + 4], hp(2), ADD)
                    ps = bank()
                    kb.mm(ps[:CT, 0:12], U, CS[:CT, 0:12], inc=False)
                    kb.mm(ps[:CT, 12:24], SSm, CS[:CT, 0:12])
                    kb.cp(GC[:CT], ps[:CT, 0:12])
                    kb.cp(GL[:CT], ps[:CT, 12:24])

                    if CUT <= 3:
                        return
                    yield "front"

                    def rowbc(colv, mat):
                        t = TMP["X1"]
                        kb.tt(v4(t, CT), mbc(mat), bc4(colv, CT, CT), MUL)
                        ps = bank()
                        kb.mm(ps[:, :4 * CT], ones[:CT, :], t[:CT, :4 * CT])
                        return ps

                    A, B, C, Dt, E, F, G2, H, I, X1 = (TMP[n] for n in ("A", "B", "C", "Dt", "E", "F", "G2", "H", "I", "X1"))

                    def mlstm_section(mA, mB, mI, mF):
                        psR = rowbc(CS[:, 4:8], U)
                        BL = P["BL"]
                        if sample:
                            kb.cp(BL, psR[:, :4 * CT].re("p (h s t) -> p h s t", h=4, t=4)[:, :, :, 3])
                        else:
                            kb.cp(BL, psR[:, :4 * CT].re("p (h j) -> p h j", h=4)[:, :, CT - 1:CT])
                        kb.tt(CC[:CT], LI[:CT], GC[:CT, 4:8], SUB)
                        yield
                        psC = rowbc(CC, ident[:CT, :CT])
                        Dm = v4(mA, CT)
                        kb.tt(Dm, psC[:CT, :4 * CT].re("p (h j) -> p h j", h=4), bc4(GC[:, 4:8], CT, CT), ADD)
                        kb.tt(Dm, Dm, mbc(Lm), ADD)
                        kb.red(MX[:CT], Dm, MAX)
                        yield
                        if sample:
                            ps = bank()
                            kb.mm(ps[:CT, 0:4], P["RMT"], P["m0"])
                            kb.cp(M0C[:CT], ps[:CT, 0:4])
                            kb.tt(INT[:CT], GC[:CT, 4:8], M0C[:CT], ADD)
                        else:
                            kb.tt(INT[:CT], GC[:CT, 4:8], P["MB"][:CT, :], ADD)
                        kb.tt(MT[:CT], INT[:CT], MX[:CT], MAX)
                        kb.ts(NMT[:CT], MT[:CT], -1.0, None, MUL)
                        yield
                        for h in range(4):
                            kb.act(mA[:CT, h * CT:(h + 1) * CT], mA[:CT, h * CT:(h + 1) * CT], AF.Exp, bias=NMT[:CT, h:h + 1])
                        kb.tt(WI[:CT], INT[:CT], MT[:CT], SUB)
                        kb.act(WI[:CT], WI[:CT], AF.Exp)
                        yield
                        psS_ = bank()
                        for h in range(4):
                            kb.mm(psS_[:CT, h * CT:(h + 1) * CT], QKm[:, h, :CT], QKm[:, 4 + h, :CT], inc=(h == 3))
                        kb.tt(Dm, Dm, psS_[:CT, :4 * CT].re("p (h j) -> p h j", h=4), MUL)
                        yield
                        ps = bank()
                        for h in range(4):
                            kb.tr(ps[:CT, h * CT:(h + 1) * CT], mA[:CT, h * CT:(h + 1) * CT], ident[:CT, :CT], inc=(h == 3))
                        kb.cp(mB[:CT, :4 * CT], ps[:CT, :4 * CT], eng="act")
                        yield
                        kb.cp(VE[:CT, :, 0:64], R[:CT, R_MV:R_MV + 256].re("p (h d) -> p h d", h=4))
                        psB = bank(hold=True)
                        for h in range(4):
                            kb.mm(psB[:CT, h * 65:(h + 1) * 65], mB[:CT, h * CT:(h + 1) * CT], VE[:CT, h, :], inc=(h == 3))
                        Xs = X1[:CT, 0:4 * nseq].re("p (h s) -> p h s", h=4)
                        kb.tt(Xs, MT[:CT].re("p (h o) -> p h o", o=1).bc([CT, 4, nseq]), lastsel[:CT, :].re("p (o s) -> p o s", o=1).bc([CT, 4, nseq]), MUL)
                        psM = bank()
                        kb.mm(psM[:, 0:4 * nseq], ones[:CT, :], X1[:CT, 0:4 * nseq])
                        MN_ = P["MN"]
                        WC = P["WC"]
                        kb.cp(MN_, psM[:, 0:4 * nseq].re("p (h s) -> p h s", h=4))
                        yield
                        kb.tt(WC, BL, P["M0R"], ADD)
                        kb.tt(WC, WC, MN_, SUB)
                        kb.act(WC, WC, AF.Exp)
                        yield
                        psBM = bank()
                        kb.mm(psBM[:CT, 0:4], LSm, NMT[:CT, 0:4])
                        kb.tt(WK[:CT], psBM[:CT, 0:4], GL[:CT, 4:8], ADD)
                        kb.tt(WK[:CT], WK[:CT], CC[:CT], ADD)
                        kb.act(WK[:CT], WK[:CT], AF.Exp)
                        kb.ts(WK[:CT], WK[:CT], 0.125, None, MUL)
                        kb.tt(KW[:CT], R[:CT, R_MK:R_MK + 256].re("p (h d) -> p h d", h=4), bc4(WK, CT, 64), MUL)
                        yield
                        psA = bank(hold=True)
                        if not sample:
                            CME = P["CME"]
                            for h in range(4):
                                kb.mm(psA[:CT, h * 65:(h + 1) * 65], QKm[:, h, :CT], CME[:, h, :], inc=(h == 3))
                            psN = bank()
                            for h in range(4):
                                kb.mm(psN[:64, h * 65:(h + 1) * 65], KW[:CT, h, :], VE[:CT, h, :], inc=(h == 3))
                            kb.tt(CME[:, :, :], CME[:, :, :], WC[:64, :, 0:1].bc([64, 4, 65]), MUL)
                            kb.tt(CME[:, :, :], CME[:, :, :], psN[:64, 0:260].re("p (h d) -> p h d", h=4), ADD)
                        for h in range(4):
                            if not sample:
                                pass
                            else:
                                CMs = P["CMs"]
                                Zt = P["Zt"]
                                ZK = P["ZK"]
                                kb.dma("sp", CMs[:, :, 0:64], st_mc[l, :, h].rearrange("s k v -> k s v"))
                                kb.dma("sp", CMs[:, :, 64:65], st_mn[l, :, h, :].rearrange("s (k o) -> k s o", o=1), allow_slow_non_contiguous=True)
                                zd = Zt[:64, 0:nseq * (CT + 4)].re("p (s r) -> p s r", r=CT + 4)[:, :, 0:4]
                                kb.cp(zd, QKm[:, h, :CT].re("p (s t) -> p s t", t=4))
                                for s_ in range(nseq):
                                    kb.mm(psA[:CT, h * 65:(h + 1) * 65], Zt[:64, s_ * CT:(s_ + 1) * CT], CMs[:, s_, :],
                                          start=(s_ == 0), stop=(s_ == nseq - 1), inc=(s_ == nseq - 1))
                                for g in range(nseq // 4):
                                    kb.tt(ZK[:CT, :, 0:64], KW[:CT, h, :].re("p (o d) -> p o d", o=1).bc([CT, 4, 64]),
                                          rowmask[:CT, 4 * g:4 * g + 4].re("p (s o) -> p s o", o=1).bc([CT, 4, 64]), MUL)
                                    psN = bank()
                                    for s4 in range(4):
                                        kb.mm(psN[:64, s4 * 65:(s4 + 1) * 65], ZK[:CT, s4, 0:64], VE[:CT, h, :], inc=(s4 == 3))
                                    wcb = WC[:64, h, 4 * g:4 * g + 4].re("p (s o) -> p s o", o=1).bc([64, 4, 65])
                                    kb.tt(CMs[:, 4 * g:4 * g + 4, :], CMs[:, 4 * g:4 * g + 4, :], wcb, MUL)
                                    kb.tt(CMs[:, 4 * g:4 * g + 4, :], CMs[:, 4 * g:4 * g + 4, :],
                                          psN[:64, 0:260].re("p (s d) -> p s d", s=4), ADD)
                                kb.dma("sp", o_smc[l, :, h].rearrange("s k v -> k s v"), CMs[:, :, 0:64], out_dram=True)
                                kb.dma("sp", o_smn[l, :, h, :].rearrange("s (k o) -> k s o", o=1), CMs[:, :, 64:65], out_dram=True, allow_slow_non_contiguous=True)
                        yield
                        if sample:
                            kb.dma("sp", o_smm[l].rearrange("(o s) h -> o h s", o=1), MN_[0:1, :, :], out_dram=True, allow_slow_non_contiguous=True)
                        else:
                            kb.cp(P["MB"], MN_[:, :, 0])
                        TOT = v4(mI, CT, 65)
                        kb.tt(TOT, psA[:CT, 0:260].re("p (h d) -> p h d", h=4), bc4(WI, CT, 65), MUL)
                        kb.tt(TOT, TOT, psB[:CT, 0:260].re("p (h d) -> p h d", h=4), ADD)
                        release(psA, psB)
                        yield
                        kb.act(DEN[:CT], mI[:CT, 64:260:65], AF.Abs)
                        kb.act(EM[:CT], NMT[:CT], AF.Exp)
                        kb.tt(DEN[:CT], DEN[:CT], EM[:CT], MAX)
                        kb.recip(DEN[:CT], DEN[:CT])
                        yield
                        HH = v4(X1, CT, 64)
                        kb.tt(HH, TOT[:, :, 0:64], bc4(DEN, CT, 64), MUL)
                        kb.tt(X1[:CT, 0:256], X1[:CT, 0:256], R[:CT, R_MO:R_MO + 256], MUL)
                        kb.act(v4(mF, CT, 64), HH, AF.Square)
                        kb.red(SS4[:CT], v4(mF, CT, 64), ADD)
                        yield
                        kb.act(RSTD4[:CT], SS4[:CT], AF.Ln, bias=EPS, scale=1.0 / 64)
                        kb.act(RSTD4[:CT], RSTD4[:CT], AF.Exp, scale=-0.5)
                        kb.tt(HH, HH, bc4(RSTD4, CT, 64), MUL)
                        kb.tt(MIX[:CT, 512:768].re("p (h d) -> p h d", h=4), HH, MNb[:CT, l, :].re("p (o d) -> p o d", o=1).bc([CT, 4, 64]), MUL)


                    hoist = (not sample) and HOIST_SSD
                    XPs = P["XP2"] if hoist else P["XP"]
                    COs = P["CO2"] if hoist else CO

                    def ssd_front():
                        for g in range(2):
                            nb = 4 if g == 0 else 2
                            wss = [fm_pair(1536 + g * 512 + q * 256) for q in range(nb // 2)]
                            ps = bank()
                            for j in range(nb):
                                ws = wss[j // 2]
                                for k in range(8):
                                    kb.mm(ps[:, j * CT:(j + 1) * CT], ws[:, k, (j % 2) * 128:(j % 2 + 1) * 128], hnT[:, k, :CT], start=(k == 0), stop=(k == 7), inc=(k == 7 and j == nb - 1))
                            if sample:
                                kb.cp(XPs[:, g * 4:g * 4 + nb, :, 3:7], ps[:, :nb * CT].re("p (b s t) -> p b s t", b=nb, t=4), eng="act")
                            else:
                                kb.cp(XPs[:, g * 4:g * 4 + nb, 3:3 + CT], ps[:, :nb * CT].re("p (b t) -> p b t", b=nb), eng="act")
                        conv_blocks(P, XPs, 12, 6, CT, COs)

                    XP = P["XP"]
                    for g in range(3):
                        wss = [fm_pair(g * 512), fm_pair(g * 512 + 256)]
                        ps = bank()
                        for j in range(4):
                            ws = wss[j // 2]
                            for k in range(8):
                                kb.mm(ps[:, j * CT:(j + 1) * CT], ws[:, k, (j % 2) * 128:(j % 2 + 1) * 128], hnT[:, k, :CT], start=(k == 0), stop=(k == 7), inc=(k == 7 and j == 3))
                        if sample:
                            kb.cp(XP[:, g * 4:g * 4 + 4, :, 3:7], ps[:, :4 * CT].re("p (b s t) -> p b s t", b=4, t=4), eng="act")
                        else:
                            kb.cp(XP[:, g * 4:g * 4 + 4, 3:3 + CT], ps[:, :4 * CT].re("p (b t) -> p b t", b=4), eng="act")
                    if hoist:
                        ssd_front_inproj = True
                    conv_blocks(P, XP, 0, 12, CT)
                    if hoist:
                        ssd_front()
                    if CUT <= 4:
                        return
                    kb.act(SQa[:, :, :CT], CO[:, 0:8, :CT], AF.Square)
                    for half in range(2):
                        ps = bank()
                        for j in range(4):
                            kb.mm(ps[:, j * CT:(j + 1) * CT], ones, SQa[:, half * 4 + j, :CT], inc=(j == 3))
                        t = v4(X1, 128, CT)
                        kb.act(t, ps[:, :4 * CT].re("p (b t) -> p b t", b=4), AF.Ln, bias=EPS)
                        kb.act(t, t, AF.Exp, scale=-0.5)
                        if half == 0:
                            kb.ts(t, t, 128 ** -0.5, None, MUL)
                        kb.tt(CO[:, half * 4:half * 4 + 4, :CT], CO[:, half * 4:half * 4 + 4, :CT], t, MUL)
                    if CUT <= 5:
                        return
                    psR = rowbc(CS[:, 0:4], U)
                    kb.act(EG[:, :, :CT], psR[:, :4 * CT].re("p (h j) -> p h j", h=4), AF.Exp)
                    kb.tt(v4(X1, CT), bc4(GC[:, 0:4], CT, CT), psR[:CT, :4 * CT].re("p (h j) -> p h j", h=4), SUB)
                    kb.tt(v4(A, CT), v4(X1, CT), mbc(Ls), ADD)
                    kb.act(v4(A, CT), v4(A, CT), AF.Exp)
                    kb.tt(v4(Dt, CT), mbc(Um), v4(X1, CT), SUB)
                    kb.act(v4(Dt, CT), v4(Dt, CT), AF.Exp)
                    psK = bank()
                    for h in range(4):
                        kb.mm(psK[:CT, h * CT:(h + 1) * CT], CO[:, 4 + h, :CT], CO[:, 4 + h, :CT], inc=(h == 3))
                    kb.tt(v4(A, CT), v4(A, CT), psK[:CT, :4 * CT].re("p (h j) -> p h j", h=4), MUL)
                    kb.tt(v4(A, CT), v4(A, CT), bc4(NBETA, CT, CT), MUL)
                    psQ = bank()
                    for h in range(4):
                        kb.mm(psQ[:CT, h * CT:(h + 1) * CT], CO[:, 4 + h, :CT], CO[:, h, :CT], inc=(h == 3))
                    kb.tt(v4(Dt, CT), v4(Dt, CT), psQ[:CT, :4 * CT].re("p (h j) -> p h j", h=4), MUL)
                    if CUT <= 6:
                        return
                    ps = bank()
                    for h in range(4):
                        kb.tr(ps[:CT, h * CT:(h + 1) * CT], A[:CT, h * CT:(h + 1) * CT], ident[:CT, :CT], inc=(h == 3))
                    kb.cp(B[:CT, :4 * CT], ps[:CT, :4 * CT], eng="act")
                    kb.tt(v4(C, CT), v4(B, CT), mbc(ident[:CT, :CT]), ADD)
                    mg = mlstm_section(E, F, G2, H) if (ILV_ML and not sample) else None
                    for lev in range(1, P["lev"]):
                        last = (lev == P["lev"] - 1)
                        ps1 = bank()
                        for h in range(4):
                            hs = slice(h * CT, (h + 1) * CT)
                            kb.mm(ps1[:CT, hs], B[:CT, hs], A[:CT, hs], inc=(h == 3))
                        if not last:
                            ps2 = bank()
                            for h in range(4):
                                hs = slice(h * CT, (h + 1) * CT)
                                kb.mm(ps2[:CT, hs], A[:CT, hs], B[:CT, hs], inc=(h == 3))
                        if lev >= 2:
                            ps3 = bank()
                            for h in range(4):
                                hs = slice(h * CT, (h + 1) * CT)
                                kb.mm(ps3[:CT, hs], A[:CT, hs], C[:CT, hs], inc=(h == 3))
                        kb.cp(A[:CT, :4 * CT], ps1[:CT, :4 * CT])
                        if not last:
                            kb.cp(B[:CT, :4 * CT], ps2[:CT, :4 * CT], eng="act")
                        if lev >= 2:
                            kb.tt(C[:CT, :4 * CT], C[:CT, :4 * CT], ps3[:CT, :4 * CT], ADD)
                        if mg is not None:
                            _step(mg)
                            _step(mg)
                    ps3 = bank()
                    for h in range(4):
                        hs = slice(h * CT, (h + 1) * CT)
                        kb.mm(ps3[:CT, hs], A[:CT, hs], C[:CT, hs], inc=(h == 3))
                    kb.tt(C[:CT, :4 * CT], C[:CT, :4 * CT], ps3[:CT, :4 * CT], ADD)
                    if mg is not None:
                        while _step(mg):
                            pass
                    if CUT <= 7:
                        return
                    kb.act(BE[:CT], GC[:CT, 0:4], AF.Exp)
                    kb.tt(BE[:CT], BE[:CT], BETA[:CT], MUL)
                    kb.tt(KD[:CT], GL[:CT, 0:4], GC[:CT, 0:4], SUB)
                    kb.act(KD[:CT], KD[:CT], AF.Exp)
                    psKt = bank()
                    for h in range(4):
                        kb.tr(psKt[:CT, h * 128:(h + 1) * 128], CO[:, 4 + h, :CT], ident, inc=(h == 3))
                    pk = psKt[:CT, :].re("p (h d) -> p h d", h=4)
                    kb.tt(v4(F, CT, 128), pk, bc4(BE, CT, 128), MUL)
                    kb.tt(v4(G2, CT, 128), pk, bc4(KD, CT, 128), MUL)
                    psVt = bank()
                    for h in range(4):
                        kb.tr(psVt[:CT, h * 128:(h + 1) * 128], CO[:, 8 + h, :CT], ident, inc=(h == 3))
                    kb.tt(v4(E, CT, 128), psVt[:CT, :].re("p (h d) -> p h d", h=4), bc4(BETA, CT, 128), MUL)
                    psW = bank()
                    for h in range(4):
                        kb.mm(psW[:, h * CT:(h + 1) * CT], F[:CT, h * 128:(h + 1) * 128], C[:CT, h * CT:(h + 1) * CT], inc=(h == 3))
                    kb.ts(H[:, :4 * CT], psW[:, :4 * CT], -1.0, None, MUL)
                    kb.tt(v4(I, 128, CT), CO[:, 0:4, :CT], EG[:, :, :CT], MUL)
                    if CUT <= 8:
                        return
                    psV = bank(hold=True)
                    psO = bank(hold=True)
                    if not sample:
                        S = P["Sg"]
                        for h in range(4):
                            hs = slice(h * CT, (h + 1) * CT)
                            hd = slice(h * 128, (h + 1) * 128)
                            kb.mm(psV[:CT, hd], C[:CT, hs], E[:CT, hd], start=True, stop=False, inc=False)
                            kb.mm(psV[:CT, hd], H[:, hs], S[:, h, :], start=False, stop=True, inc=(h == 3))
                        kb.cp(E[:CT, :], psV[:CT, :])
                        for h in range(4):
                            hs = slice(h * CT, (h + 1) * CT)
                            hd = slice(h * 128, (h + 1) * 128)
                            kb.mm(psO[:CT, hd], I[:, hs], S[:, h, :], start=True, stop=False, inc=False)
                            kb.mm(psO[:CT, hd], Dt[:CT, hs], E[:CT, hd], start=False, stop=True, inc=(h == 3))
                        psS = bank()
                        for h in range(4):
                            hd = slice(h * 128, (h + 1) * 128)
                            kb.mm(psS[:, hd], G2[:CT, hd], E[:CT, hd], inc=(h == 3))
                        kb.tt(S[:, :, :], S[:, :, :], EG[:, :, CT - 1:CT].bc([128, 4, 128]), MUL)
                        kb.tt(S[:, :, :], S[:, :, :], psS[:, :].re("p (h d) -> p h d", h=4), ADD)
                    for h in range(4):
                        hs = slice(h * CT, (h + 1) * CT)
                        hd = slice(h * 128, (h + 1) * 128)
                        if not sample:
                            pass
                        else:
                            Ss = P["Ss"]
                            Zt = P["Zt"]
                            ZK = P["ZK"]
                            kb.dma("sp", Ss, st_gdn[l, :, h].rearrange("s k v -> k s v"))
                            zd = Zt[:, 0:nseq * (CT + 4)].re("p (s r) -> p s r", r=CT + 4)[:, :, 0:4]
                            kb.cp(zd, H[:, hs].re("p (s t) -> p s t", t=4))
                            kb.mm(psV[:CT, hd], C[:CT, hs], E[:CT, hd], start=True, stop=False, inc=False)
                            for s in range(nseq):
                                kb.mm(psV[:CT, hd], Zt[:, s * CT:(s + 1) * CT], Ss[:, s, :], start=False, stop=(s == nseq - 1), inc=(s == nseq - 1))
                            kb.cp(E[:CT, hd], psV[:CT, hd])
                            kb.cp(zd, I[:, hs].re("p (s t) -> p s t", t=4))
                            for s in range(nseq):
                                kb.mm(psO[:CT, hd], Zt[:, s * CT:(s + 1) * CT], Ss[:, s, :], start=(s == 0), stop=False, inc=False)
                            kb.mm(psO[:CT, hd], Dt[:CT, hs], E[:CT, hd], start=False, stop=True)
                            for g in range(nseq // 4):
                                kb.tt(ZK[:CT], G2[:CT, hd].re("p (o d) -> p o d", o=1).bc([CT, 4, 128]),
                                      rowmask[:CT, 4 * g:4 * g + 4].re("p (s o) -> p s o", o=1).bc([CT, 4, 128]), MUL)
                                psS = bank()
                                for s4 in range(4):
                                    kb.mm(psS[:, s4 * 128:(s4 + 1) * 128], ZK[:CT, s4, :], E[:CT, hd], inc=(s4 == 3))
                                egl = EG[:, h, 16 * g + 3:16 * g + 16:4].re("p (s o) -> p s o", o=1).bc([128, 4, 128])
                                kb.tt(Ss[:, 4 * g:4 * g + 4, :], Ss[:, 4 * g:4 * g + 4, :], egl, MUL)
                                kb.tt(Ss[:, 4 * g:4 * g + 4, :], Ss[:, 4 * g:4 * g + 4, :], psS[:, :].re("p (s d) -> p s d", s=4), ADD)
                            kb.dma("sp", o_sgdn[l, :, h].rearrange("s k v -> k s v"), Ss, out_dram=True)
                    po = psO[:CT, :].re("p (h d) -> p h d", h=4)
                    kb.act(v4(X1, CT, 128), po, AF.Square)
                    kb.red(SS4[:CT], v4(X1, CT, 128), ADD)
                    kb.act(RSTD4[:CT], SS4[:CT], AF.Ln, bias=EPS, scale=1.0 / 128)
                    kb.act(RSTD4[:CT], RSTD4[:CT], AF.Exp, scale=-0.5)
                    kb.tt(v4(X1, CT, 128), po, bc4(RSTD4, CT, 128), MUL)
                    kb.tt(v4(X1, CT, 128), v4(X1, CT, 128), GN[:CT, l, :].re("p (o d) -> p o d", o=1).bc([CT, 4, 128]), MUL)
                    kb.tt(MIX[:CT, 0:512], X1[:CT, :], R[:CT, R_GG:R_GG + 512], MUL)
                    release(psV, psO)

                    if CUT <= 9:
                        return
                    if not ILV_ML or sample:
                        for _ in mlstm_section(A, B, I, F):
                            pass

                    if not hoist:
                        ssd_front()
                    psR = rowbc(CS[:, 8:12], U)
                    kb.act(EG[:, :, :CT], psR[:, :4 * CT].re("p (h j) -> p h j", h=4), AF.Exp)
                    kb.tt(v4(X1, CT), bc4(GC[:, 8:12], CT, CT), psR[:CT, :4 * CT].re("p (h j) -> p h j", h=4), SUB)
                    kb.tt(v4(Dt, CT), mbc(Um), v4(X1, CT), SUB)
                    kb.act(v4(Dt, CT), v4(Dt, CT), AF.Exp)
                    psC = bank()
                    for g in range(2):
                        kb.mm(psC[:CT, g * CT:(g + 1) * CT], COs[:, 2 + g, :CT], COs[:, 4 + g, :CT], inc=(g == 1))
                    Dt4 = Dt[:CT, :4 * CT].re("p (g e j) -> p g e j", g=2, e=2)
                    kb.tt(Dt4, Dt4, psC[:CT, :2 * CT].re("p (g o j) -> p g o j", g=2, o=1).bc([CT, 2, 2, CT]), MUL)
                    kb.tt(v4(Dt, CT), v4(Dt, CT), bc4(SP[:, 8:12], CT, CT), MUL)
                    psX = bank()
                    for j in range(2):
                        kb.tr(psX[:CT, j * 128:(j + 1) * 128], COs[:, j, :CT], ident, inc=False)
                    for j in range(2):
                        kb.tr(psX[:CT, 256 + j * 128:256 + (j + 1) * 128], COs[:, 2 + j, :CT], ident, inc=(j == 1))
                    kb.cp(XT_[:CT, :], psX[:CT, 0:256])
                    kb.tt(KDS[:CT], GL[:CT, 8:12], GC[:CT, 8:12], SUB)
                    kb.act(KDS[:CT], KDS[:CT], AF.Exp)
                    kb.tt(KDS[:CT], KDS[:CT], SP[:CT, 8:12], MUL)
                    BW = F
                    kb.tt(BW[:CT, :].re("p (g e n) -> p g e n", g=2, e=2), psX[:CT, 256:512].re("p (g o n) -> p g o n", g=2, o=1).bc([CT, 2, 2, 128]),
                          KDS[:CT].re("p (g e o) -> p g e o", g=2, o=1).bc([CT, 2, 2, 128]), MUL)
                    CG = I
                    kb.tt(CG[:, :4 * CT].re("p (g e j) -> p g e j", g=2, e=2), COs[:, 4:6, :CT].re("p g (o j) -> p g o j", o=1).bc([128, 2, 2, CT]),
                          EG[:, :, :CT].re("p (g e) j -> p g e j", g=2), MUL)
                    psY = bank(hold=True)
                    if not sample:
                        HT = P["HT"]
                        for h in range(4):
                            hs = slice(h * CT, (h + 1) * CT)
                            hp_ = slice(h * 64, (h + 1) * 64)
                            kb.mm(psY[:CT, hp_], Dt[:CT, hs], XT_[:CT, hp_], start=True, stop=False, inc=False)
                            kb.mm(psY[:CT, hp_], CG[:, hs], HT[:, h, :], start=False, stop=True, inc=(h == 3))
                        psH = bank()
                        for h in range(4):
                            hp_ = slice(h * 64, (h + 1) * 64)
                            hn_ = slice(h * 128, (h + 1) * 128)
                            kb.mm(psH[:, hp_], BW[:CT, hn_], XT_[:CT, hp_], inc=(h == 3))
                        kb.tt(HT[:, :, :], HT[:, :, :], EG[:, :, CT - 1:CT].bc([128, 4, 64]), MUL)
                        kb.tt(HT[:, :, :], HT[:, :, :], psH[:, 0:256].re("p (h d) -> p h d", h=4), ADD)
                    for h in range(4):
                        hs = slice(h * CT, (h + 1) * CT)
                        hp_ = slice(h * 64, (h + 1) * 64)
                        hn_ = slice(h * 128, (h + 1) * 128)
                        if not sample:
                            pass
                        else:
                            HTs = P["HTs"]
                            STG = P["STG"]
                            Zt = P["Zt"]
                            ZK = P["ZK"]
                            for g in range(nseq // 4):
                                kb.dma("sp", STG, st_ssd[l, 4 * g:4 * g + 4, h].rearrange("s p n -> p s n"))
                                ps = bank()
                                for s4 in range(4):
                                    kb.tr(ps[:, s4 * 64:(s4 + 1) * 64], STG[:, s4, :], ident[:64, :64], inc=(s4 == 3))
                                kb.cp(HTs[:, 4 * g:4 * g + 4, :], ps[:, 0:256].re("p (s d) -> p s d", s=4))
                            zd = Zt[:, 0:nseq * (CT + 4)].re("p (s r) -> p s r", r=CT + 4)[:, :, 0:4]
                            kb.cp(zd, CG[:, hs].re("p (s t) -> p s t", t=4))
                            kb.mm(psY[:CT, hp_], Dt[:CT, hs], XT_[:CT, hp_], start=True, stop=False, inc=False)
                            for s in range(nseq):
                                kb.mm(psY[:CT, hp_], Zt[:, s * CT:(s + 1) * CT], HTs[:, s, :], start=False, stop=(s == nseq - 1), inc=(s == nseq - 1))
                            for g in range(nseq // 4):
                                kb.tt(ZK[:CT], BW[:CT, hn_].re("p (o d) -> p o d", o=1).bc([CT, 4, 128]),
                                      rowmask[:CT, 4 * g:4 * g + 4].re("p (s o) -> p s o", o=1).bc([CT, 4, 128]), MUL)
                                psH = bank()
                                for s4 in range(4):
                                    kb.mm(psH[:, s4 * 64:(s4 + 1) * 64], ZK[:CT, s4, :], XT_[:CT, hp_], inc=(s4 == 3))
                                egl = EG[:, h, 16 * g + 3:16 * g + 16:4].re("p (s o) -> p s o", o=1).bc([128, 4, 64])
                                kb.tt(HTs[:, 4 * g:4 * g + 4, :], HTs[:, 4 * g:4 * g + 4, :], egl, MUL)
                                kb.tt(HTs[:, 4 * g:4 * g + 4, :], HTs[:, 4 * g:4 * g + 4, :], psH[:, 0:256].re("p (s d) -> p s d", s=4), ADD)
                                ps = bank()
                                for s4 in range(4):
                                    kb.tr(ps[:64, s4 * 128:(s4 + 1) * 128], HTs[:, 4 * g + s4, :], ident, inc=(s4 == 3))
                                kb.cp(STG, ps[:64, :].re("p (s n) -> p s n", s=4))
                                kb.dma("sp", o_sssd[l, 4 * g:4 * g + 4, h].rearrange("s p n -> p s n"), STG, out_dram=True)
                    Y1 = X1
                    kb.tt(v4(Y1, CT, 64), XT_[:CT, :].re("p (h d) -> p h d", h=4), bc4(HP[:, 6, l, :], CT, 64), MUL)
                    kb.tt(Y1[:CT, 0:256], Y1[:CT, 0:256], psY[:CT, 0:256], ADD)
                    release(psY)
                    kb.tt(Y1[:CT, 0:256], Y1[:CT, 0:256], R[:CT, R_SZ:R_SZ + 256], MUL)
                    kb.act(H[:CT, 0:256], Y1[:CT, 0:256], AF.Square)
                    kb.red(SS1[:CT], H[:CT, 0:256], ADD)
                    kb.act(SS1[:CT], SS1[:CT], AF.Ln, bias=EPS, scale=1.0 / 256)
                    kb.act(SS1[:CT], SS1[:CT], AF.Exp, scale=-0.5)
                    kb.stt(MIX[:CT, 768:1024], Y1[:CT, 0:256], SS1[:CT, 0:1], SN[:CT, l, :], MUL, MUL)

                    if DEBUG and sample and l == 0:
                        kb.dma("sp", dbg_mix, MIX[:CT, :], out_dram=True)
                    if CUT <= 11:
                        return
                    yield "preout"
                    for half in range(2):
                        ps = bank()
                        for j in range(4):
                            k = half * 4 + j
                            kb.tr(ps[:, j * CT:(j + 1) * CT], MIX[:CT, k * 128:(k + 1) * 128], ident[:CT, :CT], inc=(j == 3))
                        kb.cp(mixT[:, half * 4:half * 4 + 4, :CT], ps[:, :4 * CT].re("p (k t) -> p k t", k=4), eng=("act" if half else "dve"))
                    for half in range(2):
                        ps = bank()
                        for j in range(4):
                            db = half * 4 + j
                            for k in range(8):
                                kb.mm(ps[:, j * CT:(j + 1) * CT], Wout[:, k, db * 128:(db + 1) * 128], mixT[:, k, :CT], start=(k == 0), stop=(k == 7), inc=(k == 7 and j == 3))
                        kb.cp(SQa[:, half * 4:half * 4 + 4, :CT], ps[:, :4 * CT].re("p (k t) -> p k t", k=4), eng=("act" if half else "dve"))
                    postnorm_add(SQa, c, 1, l, CT)

                def run_chunks(clist, P):
                    def step(g):
                        try:
                            next(g)
                            return True
                        except StopIteration:
                            return False
                    prev = None
                    for c in clist:
                        g = mixer_chunk(c, P)
                        alive = step(g)
                        if prev is not None:
                            while step(prev):
                                pass
                        if alive:
                            alive = step(g)
                        prev = g if alive else None
                    if prev is not None:
                        while step(prev):
                            pass

                def conv_blocks(P, XP, b0, nb, CT, CO=None):
                    CO = CO_main if CO is None else CO
                    sample = P["sample"]
                    nseq_ = P["nseq"]
                    if sample:
                        HS = P["HS"]
                        kb.cp(XP[:, 0:nb, :, 0:3], HS[:, b0:b0 + nb, :, :])
                    else:
                        Hh = P["Hh"]
                        kb.cp(XP[:, 0:nb, 0:3], Hh[:, b0:b0 + nb, :])
                    X1t = TMP["X1"]
                    for j0 in range(0, nb, 4):
                        gn = min(4, nb - j0)
                        if sample:
                            o = CO[:, j0:j0 + gn, :CT].re("p j (s t) -> p j s t", t=4)
                            tmpv = X1t[:, 0:gn * CT].re("p (j s t) -> p j s t", j=gn, t=4)
                            shp = [128, gn, nseq_, 4]
                        else:
                            o = CO[:, j0:j0 + gn, :CT]
                            tmpv = X1t[:, 0:gn * CT].re("p (j t) -> p j t", j=gn)
                            shp = [128, gn, CT]
                        for t in range(4):
                            if sample:
                                xi_ = XP[:, j0:j0 + gn, :, t:t + 4]
                                wv = CW[:, l, b0 + j0:b0 + j0 + gn, t:t + 1].re("p j (o q) -> p j o q", o=1).bc(shp)
                            else:
                                xi_ = XP[:, j0:j0 + gn, t:t + CT]
                                wv = CW[:, l, b0 + j0:b0 + j0 + gn, t:t + 1].bc(shp)
                            if t == 0:
                                kb.tt(o, xi_, wv, MUL)
                            else:
                                kb.tt(tmpv, xi_, wv, MUL)
                                kb.tt(o, o, tmpv, ADD)
                        if sample:
                            bv = CB[:, l, b0 + j0:b0 + j0 + gn].re("p (j o q) -> p j o q", o=1, q=1).bc(shp)
                        else:
                            bv = CB[:, l, b0 + j0:b0 + j0 + gn].re("p (j o) -> p j o", o=1).bc(shp)
                        kb.tt(o, o, bv, ADD)
                    if not sample:
                        kb.cp(P["Hh"][:, b0:b0 + nb, :], XP[:, 0:nb, CT:CT + 3])
                    kb.act(CO[:, 0:nb, :CT], CO[:, 0:nb, :CT], AF.Silu)

                with ExitStack() as es3:
                    kb.es = es3
                    Pp = dict(CT=128, nseq=1, sample=False, cm=cmp_, lastsel=lsp, lev=7)
                    Pp["XP"] = kb.sb("XPp", [128, 12, 131])
                    if HOIST_SSD:
                        Pp["XP2"] = kb.sb("XP2", [128, 6, 131])
                        Pp["CO2"] = kb.sb("CO2", [128, 6, 128])
                    Pp["WS"] = [kb.sb("WSp%d" % i, [128, 8, 256], BF16) for i in range(4)]
                    Pp["wsi"] = [0]
                    Pp["Hh"] = kb.sb("Hh", [128, 18, 3])
                    Pp["Sg"] = kb.sb("Sg", [128, 4, 128])
                    Pp["CME"] = kb.sb("CME", [64, 4, 65])
                    Pp["HT"] = kb.sb("HT", [128, 4, 64])
                    Pp["MB"] = kb.sb("MB", [128, 4])
                    Pp["BL"] = kb.sb("BLp", [128, 4, 1])
                    Pp["MN"] = kb.sb("MNp", [128, 4, 1])
                    Pp["WC"] = kb.sb("WCp", [128, 4, 1])
                    Pp["M0R"] = Pp["MB"][:, :].re("p (h o) -> p h o", o=1)
                    for nme in ("Hh", "Sg", "CME", "HT", "MB"):
                        kb.memset(Pp[nme], 0.0)
                    if STOP & 1:
                        run_chunks(list(range(NPC)), Pp)
                    kb.dma("sp", o_pgdn[l].rearrange("h k v -> k h v"), Pp["Sg"], out_dram=True)
                    for h in range(4):
                        kb.dma("sp", o_pmc[l, h], Pp["CME"][:, h, 0:64], out_dram=True)
                        kb.dma("sp", o_pmn[l, h].rearrange("(k o) -> k o", o=1), Pp["CME"][:, h, 64:65], out_dram=True, allow_slow_non_contiguous=True)
                    kb.dma("sp", o_pmm[l].rearrange("(o h) -> o h", o=1), Pp["MB"][0:1, :], out_dram=True)
                    ps = bank()
                    for h in range(4):
                        kb.tr(ps[:64, h * 128:(h + 1) * 128], Pp["HT"][:, h, :], ident, inc=(h == 3))
                    kb.cp(TMP["A"][:64, :], ps[:64, :])
                    kb.dma("sp", o_pssd[l].rearrange("h p n -> p h n"), TMP["A"][:64, :].re("p (h n) -> p h n", h=4), out_dram=True)
                    barrier()
                with ExitStack() as es3:
                    kb.es = es3
                    Psm = dict(CT=NST, nseq=NS, sample=True, cm=cms, lastsel=lss, lev=2)
                    Psm["XP"] = kb.sb("XPs", [128, 12, NS, 7])
                    Psm["WS"] = [kb.sb("WSs%d" % i, [128, 8, 256], BF16) for i in range(2)]
                    Psm["wsi"] = [0]
                    Psm["HS"] = kb.sb("HS", [128, 18, NS, 3])
                    SST = kb.sb("SST", [128, NS * 128])
                    Psm["Ss"] = SST[:, :].re("p (s d) -> p s d", s=NS)
                    Psm["Zt"] = kb.sb("Zt", [128, (NS + 1) * NST])
                    Psm["ZK"] = kb.sb("ZK", [64, 4, 128])
                    Psm["CMs"] = SST[:64, 0:NS * 65].re("p (s d) -> p s d", s=NS)
                    Psm["HTs"] = SST[:, 0:NS * 64].re("p (s d) -> p s d", s=NS)
                    Psm["STG"] = Psm["ZK"]
                    Psm["BL"] = kb.sb("BLs", [128, 4, NS])
                    Psm["MN"] = kb.sb("MNs", [128, 4, NS])
                    Psm["WC"] = kb.sb("WCs", [128, 4, NS])
                    m0r = kb.sb("M0Rs", [128, NS, 4])
                    Psm["m0"] = kb.sb("m0s", [NS, 4])
                    Psm["RMT"] = kb.sb("RMT", [NS, NST])
                    kb.memset(Psm["Zt"], 0.0)
                    kb.dma("sp", m0r, st_mm[l].partition_broadcast(128))
                    kb.dma("sp", Psm["m0"], st_mm[l])
                    Psm["M0R"] = m0r[:, :, :].re("p s h -> p h s")
                    ps = bank()
                    kb.tr(ps[:NS, :NST], rowmask[:NST, :NS], ident[:NST, :NST])
                    kb.cp(Psm["RMT"], ps[:NS, :NST])
                    for b in range(18):
                        stg_ = TMP["E"] if b % 2 else TMP["F"]
                        kb.dma("sp", stg_[:NS * 3, 0:128], st_conv[l, :, :, b * 128:(b + 1) * 128].rearrange("s r c -> (s r) c"))
                        ps = bank()
                        kb.tr(ps[:, :NS * 3], stg_[:NS * 3, 0:128], ident[:NS * 3, :NS * 3])
                        kb.cp(Psm["HS"][:, b, :, :], ps[:, :NS * 3].re("p (s r) -> p s r", r=3), eng=("act" if b % 2 else "dve"))
                    if STOP & 2:
                        run_chunks([NPC], Psm)
                    barrier()
            with ExitStack() as es2:
                kb.es = es2
                groups = []
                cs_ = list(range(NPC))
                GSZ = 6
                for g0 in range(0, NPC, GSZ):
                    groups.append(cs_[g0:g0 + GSZ])
                groups[-1] = groups[-1] + [NPC]
                MAXT = max(sum(ctok(c) for c in g) for g in groups)
                hn2 = kb.sb("hn2", [128, 8, MAXT], BF16)
                ACTT = kb.sb("ACTT", [128, NFB, MAXT], BF16)
                WU = [kb.sb("WU%d" % i, [128, 8, 256], BF16) for i in range(NWU)]
                WD = [kb.sb("WD%d" % i, [128, NFB, 128], BF16) for i in range(NWD)]
                GP = kb.sb("GP", [128, 2 + GSZ * 128])
                GS = kb.sb("GS", [128, NS, 6])
                VV = kb.sb("VV", [128, MAXT])
                ACC = kb.sb("ACC", [128, MAXT])
                T1 = kb.sb("T1", [128, MAXT])
                YF = kb.sb("YF", [128, 8, max(MAXT, 704)])
                HF = kb.sb("HF", [128, NFB, 2])
                HSf = kb.sb("HSf", [128, NFB, NS, 2])
                STGf = YF[:, :, :].re("p k t -> p (k t)")
                pass
                kb.memset(HF, 0.0)
                kb.dma("sp", STGf[:NS * 2, 0:DFF], st_ffn[l].rearrange("s r c -> (s r) c"))
                for b in range(NFB):
                    ps = bank()
                    kb.tr(ps[:, :NS * 2], STGf[:NS * 2, b * 128:(b + 1) * 128], ident[:NS * 2, :NS * 2])
                    kb.cp(HSf[:, b, :, :], ps[:, :NS * 2].re("p (s r) -> p s r", r=2), eng=("act" if b % 2 else "dve"))
                wu_i = 0
                wd_i = 0
                for gi, grp in enumerate(groups):
                    if not (STOP & 4):
                        continue
                    has_s = (grp[-1] == NPC)
                    pch = [c for c in grp if c < NPC]
                    npt = len(pch) * 128
                    ntg = npt + (NST if has_s else 0)
                    off = 0
                    offs = {}
                    for c in grp:
                        CT = ctok(c)
                        offs[c] = off
                        rmsnorm_to(hn2[:, :, off:off + CT], c, 2, l, CT)
                        off += CT
                    tbs = [(t0, min(512, ntg - t0)) for t0 in range(0, ntg, 512)]
                    for fb in range(NFB):
                        wu = WU[wu_i % NWU]
                        wu_i += 1
                        kb.dma("pool", wu[:, :, 0:128], w_up[l, :, fb * 128:(fb + 1) * 128].rearrange("(k p) c -> p k c", p=128))
                        kb.dma("pool", wu[:, :, 128:256], w_up[l, :, DFF + fb * 128:DFF + (fb + 1) * 128].rearrange("(k p) c -> p k c", p=128))
                        if (NPC - 1) in grp:
                            r0 = offs[NPC - 1] + 126
                            ps = bank()
                            for k in range(8):
                                kb.mm(ps[:2, 0:128], hn2[:, k, r0:r0 + 2], wu[:, k, 0:128], start=(k == 0), stop=(k == 7), inc=(k == 7))
                            kb.cp(STGf[:2, fb * 128:(fb + 1) * 128], ps[:2, 0:128], eng="act")
                        if has_s:
                            r0 = offs[NPC]
                            ps = bank()
                            for k in range(8):
                                kb.mm(ps[:NST, 0:128], hn2[:, k, r0:r0 + NST], wu[:, k, 0:128], start=(k == 0), stop=(k == 7), inc=(k == 7))
                            kb.cp(STGf[:NST, DFF + fb * 128:DFF + (fb + 1) * 128], ps[:NST, 0:128], eng="act")
                        for (t0, tw) in tbs:
                            psG = bank()
                            for k in range(8):
                                kb.mm(psG[:, :tw], wu[:, k, 0:128], hn2[:, k, t0:t0 + tw], start=(k == 0), stop=(k == 7), inc=(k == 7))
                            psV = bank()
                            for k in range(8):
                                kb.mm(psV[:, :tw], wu[:, k, 128:256], hn2[:, k, t0:t0 + tw], start=(k == 0), stop=(k == 7), inc=(k == 7))
                            p1 = min(t0 + tw, npt)
                            if p1 > t0:
                                kb.cp(GP[:, 2 + t0:2 + p1], psG[:, 0:p1 - t0], eng="act")
                            if t0 + tw > npt:
                                s0 = max(t0, npt) - t0
                                kb.cp(GS[:, :, 2:6], psG[:, s0:s0 + NST].re("p (s t) -> p s t", t=4), eng="act")
                            kb.cp(VV[:, t0:t0 + tw], psV[:, :tw], eng="act")
                        if npt > 0:
                            kb.cp(GP[:, 0:2], HF[:, fb, :])
                            kb.ts(ACC[:, 0:npt], GP[:, 0:npt], FCW[:, l, fb, 0:1], FCB[:, l, fb:fb + 1], MUL, ADD)
                            kb.stt(ACC[:, 0:npt], GP[:, 1:1 + npt], FCW[:, l, fb, 1:2], ACC[:, 0:npt], MUL, ADD)
                            kb.stt(ACC[:, 0:npt], GP[:, 2:2 + npt], FCW[:, l, fb, 2:3], ACC[:, 0:npt], MUL, ADD)
                            kb.cp(HF[:, fb, :], GP[:, npt:npt + 2])
                        if has_s:
                            kb.cp(GS[:, :, 0:2], HSf[:, fb, :, :])
                            a_s = ACC[:, npt:npt + NST].re("p (s t) -> p s t", t=4)
                            kb.ts(a_s, GS[:, :, 0:4], FCW[:, l, fb, 0:1], FCB[:, l, fb:fb + 1], MUL, ADD)
                            kb.stt(a_s, GS[:, :, 1:5], FCW[:, l, fb, 1:2], a_s, MUL, ADD)
                            kb.stt(a_s, GS[:, :, 2:6], FCW[:, l, fb, 2:3], a_s, MUL, ADD)
                        kb.act(T1[:, :ntg], ACC[:, :ntg], AF.Square, scale=0.044715 ** 0.5)
                        kb.stt(T1[:, :ntg], T1[:, :ntg], 1.0, ACC[:, :ntg], ADD, MUL)
                        kb.act(T1[:, :ntg], T1[:, :ntg], AF.Sigmoid, scale=1.5957691216057308)
                        kb.tt(ACC[:, :ntg], ACC[:, :ntg], VV[:, :ntg], MUL)
                        kb.tt(ACTT[:, fb, :ntg], T1[:, :ntg], ACC[:, :ntg], MUL)
                    if (NPC - 1) in grp:
                        kb.dma("sp", o_pffn[l], STGf[:2, 0:DFF], out_dram=True)
                    if has_s:
                        for r in range(2):
                            kb.dma("sp", o_sffn[l, :, r, :], STGf[2 + r:NST:4, DFF:2 * DFF], out_dram=True)
                    for db in range(8):
                        wd = WD[wd_i % NWD]
                        wd_i += 1
                        kb.dma("pool", wd, w_down[l, :, db * 128:(db + 1) * 128].rearrange("(f p) c -> p f c", p=128))
                        for (t0, tw) in tbs:
                            ps = bank()
                            for fb in range(NFB):
                                kb.mm(ps[:, :tw], wd[:, fb, :], ACTT[:, fb, t0:t0 + tw], start=(fb == 0), stop=(fb == NFB - 1), inc=(fb == NFB - 1))
                            kb.cp(YF[:, db, t0:t0 + tw], ps[:, :tw], eng=("act" if db % 2 else "dve"))
                    for c in grp:
                        CT = ctok(c)
                        kb.cp(SQa[:, :, :CT], YF[:, :, offs[c]:offs[c] + CT])
                        postnorm_add(SQa, c, 3, l, CT)
                barrier()

        with ExitStack() as es1:
            kb.es = es1
            yo = [kb.sb("yo%d" % i, [128, D]) for i in range(2)]
            for c in range(NCH):
                CT = ctok(c)
                yt = yo[c % 2]
                for half in range(2):
                    ps = bank()
                    for j in range(4):
                        k = half * 4 + j
                        kb.tr(ps[:CT, j * 128:(j + 1) * 128], xT[c][:, k, :CT], ident, inc=(j == 3))
                    kb.cp(yt[:CT, half * 512:(half + 1) * 512], ps[:CT, :], eng=("act" if half else "dve"))
                dst = y_p[c * 128:(c + 1) * 128, :] if c < NPC else y_s
                kb.dma("sp", dst, yt[:CT, :], out_dram=True)
        kb.finish()
    return nc


IN_NAMES = ["norm_mix_pre", "norm_mix_post", "norm_ffn_pre", "norm_ffn_post", "w_in", "conv_w", "conv_b",
            "gdn_a_log", "gdn_dt_bias", "gdn_norm", "mlstm_i_bias", "mlstm_f_bias", "mlstm_norm",
            "ssd_a_log", "ssd_dt_bias", "ssd_d", "ssd_norm", "w_out", "ffn_w_up", "ffn_conv_w", "ffn_conv_b", "ffn_w_down"]
ST_MAP = [("st_conv", "state_conv"), ("st_gdn", "state_gdn"), ("st_mc", "state_mlstm_c"), ("st_mn", "state_mlstm_n"),
          ("st_mm", "state_mlstm_m"), ("st_ssd", "state_ssd"), ("st_ffn", "state_ffn_conv")]
_NC_CACHE = {}


def make_in_maps(inputs, NPC, NS, DEPTH, ncores):
    consts = make_consts(NS)
    f = lambda a: np.ascontiguousarray(np.asarray(a, dtype=np.float32))
    shared = {n: f(inputs[n])[:DEPTH] for n in IN_NAMES}
    shared.update(consts)
    maps = []
    for c in range(ncores):
        m = dict(shared)
        m["xp"] = f(inputs["x_prompt"][c, :NPC * 128])
        m["xs"] = f(inputs["x_sample"][c * NS:(c + 1) * NS]).reshape(NS * 4, D)
        for k, n in ST_MAP:
            m[k] = f(inputs[n][:DEPTH, c * NS:(c + 1) * NS])
        maps.append(m)
    return maps


def assemble(results, NPC, NS, DEPTH):
    cat = lambda k, ax: np.concatenate([r[k] for r in results], axis=ax)
    y_p = np.stack([r["y_p"] for r in results], 0)
    y_s = np.concatenate([r["y_s"].reshape(NS, 4, D) for r in results], 0)
    outs = [y_p, y_s]
    for k in ("p_conv", "p_gdn", "p_mc", "p_mn", "p_mm", "p_ssd", "p_ffn"):
        outs.append(np.stack([r[k] for r in results], 1))
    for k in ("s_conv", "s_gdn", "s_mc", "s_mn", "s_mm", "s_ssd", "s_ffn"):
        outs.append(cat(k, 1))
    return tuple(np.ascontiguousarray(o, dtype=np.float32) for o in outs)


def kernel(**inputs):
    NPC, NS, DEPTH, ncores = 16, 16, 2, 8
    key = (NPC, NS, DEPTH)
    if key not in _NC_CACHE:
        _NC_CACHE[key] = build(NPC, NS, DEPTH)
    nc = _NC_CACHE[key]
    maps = make_in_maps(inputs, NPC, NS, DEPTH, ncores)
    res = run_bass_kernel_spmd(nc, maps, core_ids=list(range(ncores)))
    return assemble(res.results, NPC, NS, DEPTH)
```

```python
import numpy as np
import concourse.bass as bass
import concourse.mybir as mybir

F32 = mybir.dt.float32
BF16 = mybir.dt.bfloat16
ALU = mybir.AluOpType
AF = mybir.ActivationFunctionType
AX = mybir.AxisListType


class T:
    def __init__(self, ap, name):
        self.ap = ap
        self.name = name
        self.we = None
        self.wd = []
        self.re = {}
        self.rd = []

    def __getitem__(self, idx):
        return V(self, self.ap[idx])

    @property
    def t(self):
        return self


class V:
    def __init__(self, t, ap):
        self.t = t
        self.ap = ap

    def __getitem__(self, idx):
        return V(self.t, self.ap[idx])

    def re(self, pattern_, **kw):
        return V(self.t, self.ap.rearrange(pattern_, **kw))

    def bc(self, shape):
        return V(self.t, self.ap.to_broadcast(shape))


def _ap(x):
    return x.ap if isinstance(x, (T, V)) else x


class KB:
    def __init__(self, nc, es, n_dma_sems=6):
        self.nc = nc
        self.es = es
        self.E = {"pe": nc.tensor, "act": nc.scalar, "dve": nc.vector, "pool": nc.gpsimd, "sp": nc.sync}
        self.sem = {e: es.enter_context(nc.semaphore("s_" + e)) for e in ("pe", "act", "dve", "pool")}
        self.cnt = {e: 0 for e in self.sem}
        self.known = {e: {} for e in self.E}
        self.knownd = {e: {} for e in self.E}
        self.pend = {e: [] for e in self.E}
        self.dsems = {}
        for q in ("sp", "pool", "act"):
            self.dsems[q] = [[es.enter_context(nc.semaphore("d_%s%d" % (q, i))), 0] for i in range(n_dma_sems if q != "act" else 4)]
        self.dnext = {q: 0 for q in self.dsems}
        self.nbank = 0
        self.out_deps = []

    def sb(self, name, shape, dt=F32):
        self.nbank += 1
        name = "sb%d_%s" % (self.nbank, name)
        return T(self.es.enter_context(self.nc.sbuf_tensor(name, list(shape), dt)).ap(), name)

    def ps(self, name, shape, dt=F32):
        self.nbank += 1
        name = "ps%d_%s" % (self.nbank, name)
        t = T(self.es.enter_context(self.nc.psum_tensor(name, list(shape), dt)).ap(), name)
        t.psum = True
        return t

    def _wait_e(self, eng, dep):
        e2, c = dep
        if self.known[eng].get(e2, 0) >= c:
            return
        self.E[eng].wait_ge(self.sem[e2], c)
        self.known[eng][e2] = c

    def _wait_d(self, eng, dep):
        key, sem, tgt = dep
        if self.knownd[eng].get(key, 0) >= tgt:
            return
        self.E[eng].wait_ge(sem, tgt)
        self.knownd[eng][key] = tgt

    def _pre(self, eng, outs, ins):
        for e2, pl in self.pend.items():
            if e2 == eng:
                continue
            for (po, pi) in pl:
                for v in outs:
                    assert all(v.t is not x.t for x in po + pi), ("pending hazard", v.t.name, e2)
                for v in ins:
                    assert all(v.t is not x.t for x in po), ("pending hazard", v.t.name, e2)
        for v in ins:
            t = v.t
            if t.we is not None and not (eng == "pe" and t.we[0] == "pe"):
                self._wait_e(eng, t.we)
            for d in t.wd:
                self._wait_d(eng, d)
            if getattr(t, "psum", False):
                for e2, c in t.re.items():
                    if e2 != eng:
                        self._wait_e(eng, (e2, c))
        for v in outs:
            t = v.t
            if t.we is not None and not (eng == "pe" and t.we[0] == "pe"):
                self._wait_e(eng, t.we)
            for d in t.wd:
                self._wait_d(eng, d)
            for e2, c in t.re.items():
                if not (e2 == "pe" and eng == "pe"):
                    self._wait_e(eng, (e2, c))
            for d in t.rd:
                self._wait_d(eng, d)

    def op(self, eng, fn, outs, ins, inc=True):
        outs = [o for o in outs if o is not None]
        ins = [i for i in ins if isinstance(i, (T, V))]
        self._pre(eng, outs, ins)
        inst = fn()
        self.pend[eng].append((outs, ins))
        if inc:
            self.cnt[eng] += 1
            c = self.cnt[eng]
            inst.then_inc(self.sem[eng], 1)
            for (po, pi) in self.pend[eng]:
                for v in pi:
                    v.t.re[eng] = c
                for v in po:
                    v.t.we = (eng, c)
                    v.t.wd = []
                    v.t.re = {}
                    v.t.rd = []
            self.pend[eng] = []
        return inst

    def dma(self, q, out, in_, out_dram=False, **kw):
        assert not self.pend[q] if q in self.pend else True
        pool = self.dsems[q]
        i = self.dnext[q]
        self.dnext[q] = (i + 1) % len(pool)
        sem, prev = pool[i]
        key = (q, i)
        if prev > 0:
            self._wait_d(q, (key, sem, prev))
        self._pre(q, [o for o in [out] if isinstance(o, (T, V))], [o for o in [in_] if isinstance(o, (T, V))])
        tgt = prev + 16
        pool[i][1] = tgt
        self.E[q].dma_start(out=_ap(out), in_=_ap(in_), **kw).then_inc(sem, 16)
        dep = (key, sem, tgt)
        if isinstance(in_, (T, V)):
            in_.t.rd.append(dep)
        if isinstance(out, (T, V)):
            out.t.wd.append(dep)
        if out_dram:
            self.out_deps.append(dep)
        return dep

    def finish(self):
        for d in self.out_deps:
            self._wait_d("sp", d)

    def mm(self, out, lhsT, rhs, start=True, stop=True, inc=True):
        return self.op("pe", lambda: self.nc.tensor.matmul(_ap(out), _ap(lhsT), _ap(rhs), start=start, stop=stop),
                       [out], [lhsT, rhs], inc=inc)

    def tr(self, out, in_, ident, inc=True):
        return self.op("pe", lambda: self.nc.tensor.transpose(_ap(out), _ap(in_), _ap(ident)), [out], [in_, ident], inc=inc)

    def act(self, out, in_, func, bias=0.0, scale=1.0, eng="act"):
        return self.op("act", lambda: self.nc.scalar.activation(_ap(out), _ap(in_), func, bias=_ap(bias), scale=_ap(scale)),
                       [out], [in_, bias, scale])

    def ts(self, out, in0, s1, s2, op0, op1=None, eng="dve"):
        e = self.E[eng]
        if op1 is None:
            return self.op(eng, lambda: e.tensor_scalar(_ap(out), _ap(in0), _ap(s1), None, op0), [out], [in0, s1])
        return self.op(eng, lambda: e.tensor_scalar(_ap(out), _ap(in0), _ap(s1), _ap(s2), op0, op1), [out], [in0, s1, s2])

    def stt(self, out, in0, s, in1, op0, op1):
        return self.op("dve", lambda: self.nc.vector.scalar_tensor_tensor(_ap(out), _ap(in0), _ap(s), _ap(in1), op0, op1),
                       [out], [in0, s, in1])

    def tt(self, out, in0, in1, op, eng="dve"):
        e = self.E[eng]
        return self.op(eng, lambda: e.tensor_tensor(_ap(out), _ap(in0), _ap(in1), op), [out], [in0, in1])

    def cp(self, out, in_, eng="dve"):
        if eng == "act":
            return self.op("act", lambda: self.nc.scalar.copy(_ap(out), _ap(in_)), [out], [in_])
        e = self.E[eng]
        return self.op(eng, lambda: e.tensor_copy(_ap(out), _ap(in_)), [out], [in_])

    def recip(self, out, in_):
        return self.op("dve", lambda: self.nc.vector.reciprocal(_ap(out), _ap(in_)), [out], [in_])

    def red(self, out, in_, op, axis=AX.X):
        return self.op("dve", lambda: self.nc.vector.tensor_reduce(_ap(out), _ap(in_), axis, op), [out], [in_])

    def memset(self, out, val, eng="dve"):
        e = self.E[eng]
        return self.op(eng, lambda: e.memset(_ap(out), val), [out], [])


from contextlib import ExitStack
from concourse.bass_utils import run_bass_kernel_spmd

D = 1024
KC = 8
CONV_DIM = 2304
IN_COLS = 4116
NREST = 1812
DFF = 2816
NFB = 22
EPS = 1e-6
NEG = -30000.0
LNQ = float(np.log(128.0 ** -0.5))
DEBUG = False
HOIST_SSD = False
ILV_ML = True


def _step(g):
    try:
        next(g)
        return True
    except StopIteration:
        return False
NWU = 4
NWD = 3
STOP = 7
CUT = 99


class _Stop(Exception):
    pass
R_GG, R_GA, R_GB, R_MQ, R_MK, R_MV, R_MO, R_MI, R_MF, R_SZ, R_SDT = 0, 512, 516, 520, 776, 1032, 1288, 1544, 1548, 1552, 1808
MUL, ADD, SUB, MAX = ALU.mult, ALU.add, ALU.subtract, ALU.max


def make_consts(NS):
    c = {}
    c["ident"] = np.eye(128, dtype=np.float32)
    c["ones"] = np.ones((128, 128), np.float32)
    i = np.arange(128)[:, None]
    j = np.arange(128)[None, :]
    cm = np.zeros((128, 6, 128), np.float32)
    cm[:, 0] = (i <= j)
    cm[:, 1] = np.where(i > j, 0.0, NEG)
    cm[:, 2] = np.where(j >= i, 0.0, NEG)
    cm[:, 3] = np.where(j <= i, 0.0, NEG)
    cm[:, 4] = 1.0
    cm[:, 5] = (i == 127)
    c["cm_p"] = cm
    ls = np.zeros((128, 1), np.float32)
    ls[127, 0] = 1
    c["lastsel_p"] = ls
    n = NS * 4
    i = np.arange(n)[:, None]
    j = np.arange(n)[None, :]
    same = (i // 4) == (j // 4)
    cs = np.zeros((n, 6, n), np.float32)
    cs[:, 0] = same & (i <= j)
    cs[:, 1] = np.where(same & (i > j), 0.0, NEG)
    cs[:, 2] = np.where(same & (j >= i), 0.0, NEG)
    cs[:, 3] = np.where(same & (j <= i), 0.0, NEG)
    cs[:, 4] = same
    cs[:, 5] = same & (i % 4 == 3)
    c["cm_s"] = cs
    s = np.arange(NS)[None, :]
    k = np.arange(n)[:, None]
    c["lastsel_s"] = (k == 4 * s + 3).astype(np.float32)
    c["rowmask"] = ((k // 4) == s).astype(np.float32)
    return c


def build(NPC, NS, DEPTH):
    nc = bass.Bass("TRN2", target_bir_lowering=False)
    TP = NPC * 128
    NST = NS * 4
    assert NST <= 64

    def din(name, shape):
        return nc.dram_tensor(name, list(shape), F32, kind="ExternalInput").ap()

    def dout(name, shape):
        return nc.dram_tensor(name, list(shape), F32, kind="ExternalOutput").ap()

    xp = din("xp", [TP, D])
    xs = din("xs", [NST, D])
    st_conv = din("st_conv", [DEPTH, NS, 3, CONV_DIM])
    st_gdn = din("st_gdn", [DEPTH, NS, 4, 128, 128])
    st_mc = din("st_mc", [DEPTH, NS, 4, 64, 64])
    st_mn = din("st_mn", [DEPTH, NS, 4, 64])
    st_mm = din("st_mm", [DEPTH, NS, 4])
    st_ssd = din("st_ssd", [DEPTH, NS, 4, 64, 128])
    st_ffn = din("st_ffn", [DEPTH, NS, 2, DFF])
    nrm = [din(n, [DEPTH, D]) for n in ("norm_mix_pre", "norm_mix_post", "norm_ffn_pre", "norm_ffn_post")]
    w_in = din("w_in", [DEPTH, D, IN_COLS])
    conv_w = din("conv_w", [DEPTH, 4, CONV_DIM])
    conv_b = din("conv_b", [DEPTH, CONV_DIM])
    hp_names = ["gdn_a_log", "gdn_dt_bias", "mlstm_i_bias", "mlstm_f_bias", "ssd_a_log", "ssd_dt_bias", "ssd_d"]
    hp_d = [din(n, [DEPTH, 4]) for n in hp_names]
    gdn_norm = din("gdn_norm", [DEPTH, 128])
    mlstm_norm = din("mlstm_norm", [DEPTH, 64])
    ssd_norm = din("ssd_norm", [DEPTH, 256])
    w_out = din("w_out", [DEPTH, D, D])
    w_up = din("ffn_w_up", [DEPTH, D, 2 * DFF])
    fconv_w = din("ffn_conv_w", [DEPTH, 3, DFF])
    fconv_b = din("ffn_conv_b", [DEPTH, DFF])
    w_down = din("ffn_w_down", [DEPTH, DFF, D])
    c_ident = din("ident", [128, 128])
    c_ones = din("ones", [128, 128])
    c_cm_p = din("cm_p", [128, 6, 128])
    c_ls_p = din("lastsel_p", [128, 1])
    c_cm_s = din("cm_s", [NST, 6, NST])
    c_ls_s = din("lastsel_s", [NST, NS])
    c_rowmask = din("rowmask", [NST, NS])

    y_p = dout("y_p", [TP, D])
    y_s = dout("y_s", [NST, D])
    o_pconv = dout("p_conv", [DEPTH, 3, CONV_DIM])
    o_pgdn = dout("p_gdn", [DEPTH, 4, 128, 128])
    o_pmc = dout("p_mc", [DEPTH, 4, 64, 64])
    o_pmn = dout("p_mn", [DEPTH, 4, 64])
    o_pmm = dout("p_mm", [DEPTH, 4])
    o_pssd = dout("p_ssd", [DEPTH, 4, 64, 128])
    o_pffn = dout("p_ffn", [DEPTH, 2, DFF])
    o_sconv = dout("s_conv", [DEPTH, NS, 3, CONV_DIM])
    o_sgdn = dout("s_gdn", [DEPTH, NS, 4, 128, 128])
    o_smc = dout("s_mc", [DEPTH, NS, 4, 64, 64])
    o_smn = dout("s_mn", [DEPTH, NS, 4, 64])
    o_smm = dout("s_mm", [DEPTH, NS, 4])
    o_sssd = dout("s_ssd", [DEPTH, NS, 4, 64, 128])
    o_sffn = dout("s_ffn", [DEPTH, NS, 2, DFF])

    NCH = NPC + 1
    dbg_mix = dout("dbg_mix", [NST, D]) if DEBUG else None
    NSL = True

    with ExitStack() as es0:
        kb = KB(nc, es0)
        PS = [kb.ps("psb%d" % i, [128, 512]) for i in range(8)]
        bstate = [0]

        held = set()

        def bank(hold=False):
            while (bstate[0] % 8) in held:
                bstate[0] += 1
            i = bstate[0] % 8
            bstate[0] += 1
            if hold:
                held.add(i)
            return PS[i]

        def release(*bs):
            for b in bs:
                held.discard(PS.index(b))

        def barrier():
            for e in ("pe", "act", "dve", "pool", "sp"):
                for e2 in ("pe", "act", "dve", "pool"):
                    if e2 != e and kb.cnt[e2] > 0:
                        kb._wait_e(e, (e2, kb.cnt[e2]))
                for q in kb.dsems:
                    for i, (sem, tgt) in enumerate(kb.dsems[q]):
                        if tgt > 0:
                            kb._wait_d(e, ((q, i), sem, tgt))

        xT = [kb.sb("xT%d" % c, [128, 8, 128 if c < NPC else NST]) for c in range(NCH)]
        ident = kb.sb("ident", [128, 128])
        ones = kb.sb("ones", [128, 128])
        cmp_ = kb.sb("cm_p", [128, 6, 128])
        lsp = kb.sb("ls_p", [128, 1])
        cms = kb.sb("cm_s", [NST, 6, NST])
        lss = kb.sb("ls_s", [NST, NS])
        rowmask = kb.sb("rowmask", [NST, NS])
        NW = kb.sb("NW", [128, 4 * DEPTH * 8])
        CW = kb.sb("CW", [128, DEPTH, 18, 4])
        CB = kb.sb("CB", [128, DEPTH, 18])
        FCW = kb.sb("FCW", [128, DEPTH, NFB, 3])
        FCB = kb.sb("FCB", [128, DEPTH, NFB])
        HP = kb.sb("HP", [128, 7, DEPTH, 4])
        GN = kb.sb("GN", [128, DEPTH, 128])
        MNb = kb.sb("MNb", [128, DEPTH, 64])
        SN = kb.sb("SN", [128, DEPTH, 256])
        RS = kb.sb("RS", [128, 128])
        SQa = kb.sb("SQa", [128, 8, 128])
        SQb = kb.sb("SQb", [128, 8, 128])

        kb.dma("sp", ident, c_ident)
        kb.dma("sp", ones, c_ones)
        kb.dma("sp", cmp_, c_cm_p)
        kb.dma("sp", lsp, c_ls_p)
        kb.dma("sp", cms, c_cm_s)
        kb.dma("sp", lss, c_ls_s)
        kb.dma("sp", rowmask, c_rowmask)
        for w in range(4):
            for l in range(DEPTH):
                o = (w * DEPTH + l) * 8
                kb.dma("sp", NW[:, o:o + 8], nrm[w][l].rearrange("(k p) -> p k", p=128), allow_slow_non_contiguous=True)
        for l in range(DEPTH):
            for j in range(4):
                kb.dma("sp", CW[:, l, :, j], conv_w[l, j].rearrange("(b p) -> p b", p=128), allow_slow_non_contiguous=True)
            kb.dma("sp", CB[:, l, :], conv_b[l].rearrange("(b p) -> p b", p=128), allow_slow_non_contiguous=True)
            for j in range(3):
                kb.dma("sp", FCW[:, l, :, j], fconv_w[l, j].rearrange("(b p) -> p b", p=128), allow_slow_non_contiguous=True)
            kb.dma("sp", FCB[:, l, :], fconv_b[l].rearrange("(b p) -> p b", p=128), allow_slow_non_contiguous=True)
        for i in range(7):
            kb.dma("sp", HP[:, i], hp_d[i].partition_broadcast(128))
        kb.dma("sp", GN, gdn_norm.partition_broadcast(128))
        kb.dma("sp", MNb, mlstm_norm.partition_broadcast(128))
        kb.dma("sp", SN, ssd_norm.partition_broadcast(128))
        for i in (0, 4):
            kb.act(HP[:, i], HP[:, i], AF.Exp)
            kb.ts(HP[:, i], HP[:, i], -1.0, None, MUL)

        def nw(w, l, k):
            o = (w * DEPTH + l) * 8 + k
            return NW[:, o:o + 1]

        def ctok(c):
            return 128 if c < NPC else NST

        ones_bf = kb.sb("ones_bf", [128, 128], BF16)
        kb.cp(ones_bf, ones)

        def bfview(t3):
            v = t3[:, :, :].re("p k t -> p (k t)")
            return V(v.t, v.ap.bitcast(BF16)[:, 0:1024].rearrange("p (k t) -> p k t", k=8))

        def nwv(w, l, CT):
            o = (w * DEPTH + l) * 8
            return NW[:, o:o + 8].re("p (k o) -> p k o", o=1).bc([128, 8, CT])

        def rsv(CT):
            return RS[:, :CT].re("p (o t) -> p o t", o=1).bc([128, 8, CT])

        def rmsnorm_to(dst, c, w, l, CT):
            xc = xT[c]
            sqb = bfview(SQa)
            kb.act(sqb[:, :, :CT], xc[:, :, :CT], AF.Square)
            ps = bank()
            for k in range(8):
                kb.mm(ps[:, :CT], ones_bf, sqb[:, k, :CT], start=(k == 0), stop=(k == 7), inc=(k == 7))
            kb.act(RS[:, :CT], ps[:, :CT], AF.Ln, bias=EPS, scale=1.0 / D)
            kb.act(RS[:, :CT], RS[:, :CT], AF.Exp, scale=-0.5)
            kb.tt(SQa[:, :, :CT], xc[:, :, :CT], rsv(CT), MUL)
            kb.tt(dst[:, :, :CT], SQa[:, :, :CT], nwv(w, l, CT), MUL)

        def postnorm_add(Y, c, w, l, CT):
            sqb = bfview(SQb)
            kb.act(sqb[:, :, :CT], Y[:, :, :CT], AF.Square)
            ps = bank()
            for k in range(8):
                kb.mm(ps[:, :CT], ones_bf, sqb[:, k, :CT], start=(k == 0), stop=(k == 7), inc=(k == 7))
            kb.act(RS[:, :CT], ps[:, :CT], AF.Ln, bias=EPS, scale=1.0 / D)
            kb.act(RS[:, :CT], RS[:, :CT], AF.Exp, scale=-0.5)
            kb.tt(Y[:, :, :CT], Y[:, :, :CT], rsv(CT), MUL)
            kb.tt(Y[:, :, :CT], Y[:, :, :CT], nwv(w, l, CT), MUL)
            kb.tt(xT[c][:, :, :CT], xT[c][:, :, :CT], Y[:, :, :CT], ADD)

        with ExitStack() as es1:
            kb.es = es1
            xin = [kb.sb("xin%d" % i, [128, D]) for i in range(2)]
            for c in range(NCH):
                CT = ctok(c)
                xi = xin[c % 2]
                src = xp[c * 128:(c + 1) * 128, :] if c < NPC else xs
                kb.dma("sp", xi[:CT, :], src)
                for half in range(2):
                    ps = bank()
                    for j in range(4):
                        k = half * 4 + j
                        kb.tr(ps[:, j * CT:(j + 1) * CT], xi[:CT, k * 128:(k + 1) * 128], ident[:CT, :CT], inc=(j == 3))
                    kb.cp(xT[c][:, half * 4:half * 4 + 4, :CT], ps[:, :4 * CT].re("p (k t) -> p k t", k=4), eng=("dve" if half == 0 else "act"))
            barrier()

        for l in range(DEPTH):
            with ExitStack() as es2:
                kb.es = es2
                WinR = kb.sb("WinR", [128, 8, NREST], BF16)
                Wout = kb.sb("Wout", [128, 8, D], BF16)
                for k in range(8):
                    kb.dma("pool", WinR[:, k, :], w_in[l, k * 128:(k + 1) * 128, CONV_DIM:IN_COLS])
                for k in range(8):
                    kb.dma("pool", Wout[:, k, :], w_out[l, k * 128:(k + 1) * 128, :])
                hnT = kb.sb("hnT", [128, 8, 128], BF16)
                R = kb.sb("R", [128, NREST])
                STGc = [kb.sb("STGc%d" % i, [64, 256]) for i in range(2)]
                stgi = [0]
                QKm = kb.sb("QKm", [64, 8, 128])
                CO = kb.sb("CO", [128, 12, 128])
                CO_main = CO
                TMP = {n: kb.sb("tmp" + n, [128, 512]) for n in ("E", "F", "G2", "H", "I", "X1")}
                SQa2 = SQa[:, :, :].re("p k t -> p (k t)")
                SQb2 = SQb[:, :, :].re("p k t -> p (k t)")
                TMP["A"] = SQa2[:, 0:512]
                TMP["B"] = SQa2[:, 512:1024]
                TMP["C"] = SQb2[:, 0:512]
                TMP["Dt"] = SQb2[:, 512:1024]
                SM = kb.sb("SM", [128, 160])
                EG = kb.sb("EG", [128, 4, 128])
                MIX = kb.sb("MIX", [128, D])
                mixT = kb.sb("mixT", [128, 8, 128], BF16)
                VE = kb.sb("VE", [128, 4, 65])
                KW = kb.sb("KW", [128, 4, 64])
                XT_ = kb.sb("XTs", [128, 256])
                kb.memset(VE[:, :, 64:65], 1.0)
                Z = SM[:, 0:12]
                SP = SM[:, 12:24]
                CS = SM[:, 24:36]
                GC = SM[:, 36:48]
                BETA = SM[:, 48:52]
                NBETA = SM[:, 52:56]
                LI = SM[:, 56:60]
                BE = SM[:, 60:64]
                KD = SM[:, 64:68]
                GL = SM[:, 68:80]
                CC = SM[:, 80:84]
                MX = SM[:, 84:88]
                INT = SM[:, 88:92]
                MT = SM[:, 92:96]
                NMT = SM[:, 96:100]
                WI = SM[:, 100:104]
                DEN = SM[:, 104:108]
                EM = SM[:, 108:112]
                WK = SM[:, 112:116]
                SS4 = SM[:, 116:120]
                RSTD4 = SM[:, 120:124]
                KDS = SM[:, 124:128]
                M0C = SM[:, 128:132]
                SS1 = SM[:, 132:133]

                def v4(t, CT, w=None):
                    w = CT if w is None else w
                    return t[:CT, 0:4 * w].re("p (h j) -> p h j", h=4)

                def bc4(v, CT, w):
                    return v[:CT].re("p (h o) -> p h o", o=1).bc([CT, 4, w])

                def mixer_chunk(c, P):
                    CT, nseq, sample = P["CT"], P["nseq"], P["sample"]
                    cm = P["cm"]
                    U, Ls, Um, Lm, SSm, LSm = (cm[:CT, i, :CT] for i in range(6))
                    lastsel = P["lastsel"]

                    def mbc(m):
                        return m.re("p (o j) -> p o j", o=1).bc([CT, 4, CT])

                    rmsnorm_to(hnT, c, 0, l, CT)

                    if CUT <= 1:
                        return
                    need_cs = sample or c == NPC - 1
                    rows = slice(0, CT) if sample else slice(CT - 3, CT)
                    nr = CT if sample else 3

                    def fm_pair(col0, conv=True):
                        ws = P["WS"][P["wsi"][0] % len(P["WS"])]
                        P["wsi"][0] += 1
                        kb.dma("pool", ws, w_in[l, :, col0:col0 + 256].rearrange("(k p) c -> p k c", p=128))
                        if conv and need_cs:
                            psr = bank()
                            for k in range(8):
                                kb.mm(psr[:nr, 0:256], hnT[:, k, rows], ws[:, k, :], start=(k == 0), stop=(k == 7), inc=(k == 7))
                            stg = STGc[stgi[0] % 2]
                            stgi[0] += 1
                            kb.cp(stg[:nr, :], psr[:nr, 0:256], eng="act")
                            if sample:
                                for r in range(3):
                                    kb.dma("sp", o_sconv[l, :, r, col0:col0 + 256], stg[1 + r:CT:4, :], out_dram=True)
                            else:
                                kb.dma("sp", o_pconv[l, :, col0:col0 + 256], stg[:3, :], out_dram=True)
                        return ws

                    for pc in range(4):
                        ps = bank()
                        c0 = pc * 453
                        for k in range(8):
                            kb.mm(ps[:CT, :453], hnT[:, k, :CT], WinR[:, k, c0:c0 + 453], start=(k == 0), stop=(k == 7), inc=(k == 7))
                        kb.cp(R[:CT, pc * 453:(pc + 1) * 453], ps[:CT, :453], eng=("act" if pc % 2 else "dve"))
                    for pr in range(2):
                        ws = fm_pair(CONV_DIM + R_MQ + pr * 256, conv=False)
                        ps = bank()
                        for hh in range(4):
                            for k in range(8):
                                kb.mm(ps[:64, hh * CT:(hh + 1) * CT], ws[:, k, hh * 64:(hh + 1) * 64], hnT[:, k, :CT], start=(k == 0), stop=(k == 7), inc=(k == 7 and hh == 3))
                        if pr == 0:
                            kb.cp(QKm[:, 0:4, :CT], ps[:64, 0:4 * CT].re("p (b t) -> p b t", b=4))
                        else:
                            kb.ts(QKm[:, 4:8, :CT], ps[:64, 0:4 * CT].re("p (b t) -> p b t", b=4), 0.125, None, MUL)

                    if CUT <= 2:
                        return
                    def hp(i):
                        return HP[:CT, i, l, :]
                    kb.act(R[:CT, R_GG:R_GG + 512], R[:CT, R_GG:R_GG + 512], AF.Silu)
                    kb.act(R[:CT, R_SZ:R_SZ + 256], R[:CT, R_SZ:R_SZ + 256], AF.Silu)
                    kb.act(R[:CT, R_MO:R_MO + 256], R[:CT, R_MO:R_MO + 256], AF.Sigmoid)
                    kb.act(BETA[:CT], R[:CT, R_GB:R_GB + 4], AF.Sigmoid)
                    kb.tt(Z[:CT, 0:4], R[:CT, R_GA:R_GA + 4], hp(1), ADD)
                    kb.tt(Z[:CT, 4:8], R[:CT, R_MF:R_MF + 4], hp(3), ADD)
                    kb.ts(Z[:CT, 4:8], Z[:CT, 4:8], -1.0, None, MUL)
                    kb.tt(Z[:CT, 8:12], R[:CT, R_SDT:R_SDT + 4], hp(5), ADD)
                    kb.act(SP[:CT], Z[:CT], AF.Exp)
                    kb.act(SP[:CT], SP[:CT], AF.Ln, bias=1.0)
                    kb.tt(CS[:CT, 0:4], SP[:CT, 0:4], hp(0), MUL)
                    kb.ts(CS[:CT, 4:8], SP[:CT, 4:8], -1.0, None, MUL)
                    kb.tt(CS[:CT, 8:12], SP[:CT, 8:12], hp(4), MUL)
                    kb.ts(NBETA[:CT], BETA[:CT], -1.0, None, MUL)
                    kb.tt(LI[:CT], R[:CT, R_MI:R_MI + 4], hp(2), ADD)
                    ps = bank()
                    kb.mm(ps[:CT, 0:12], U, CS[:CT, 0:12], inc=False)
                    kb.mm(ps[:CT, 12:24], SSm, CS[:CT, 0:12])
                    kb.cp(GC[:CT], ps[:CT, 0:12])
                    kb.cp(GL[:CT], ps[:CT, 12:24])

                    if CUT <= 3:
                        return
                    yield "front"

                    def rowbc(colv, mat):
                        t = TMP["X1"]
                        kb.tt(v4(t, CT), mbc(mat), bc4(colv, CT, CT), MUL)
                        ps = bank()
                        kb.mm(ps[:, :4 * CT], ones[:CT, :], t[:CT, :4 * CT])
                        return ps

                    A, B, C, Dt, E, F, G2, H, I, X1 = (TMP[n] for n in ("A", "B", "C", "Dt", "E", "F", "G2", "H", "I", "X1"))

                    def mlstm_section(mA, mB, mI, mF):
                        psR = rowbc(CS[:, 4:8], U)
                        BL = P["BL"]
                        if sample:
                            kb.cp(BL, psR[:, :4 * CT].re("p (h s t) -> p h s t", h=4, t=4)[:, :, :, 3])
                        else:
                            kb.cp(BL, psR[:, :4 * CT].re("p (h j) -> p h j", h=4)[:, :, CT - 1:CT])
                        kb.tt(CC[:CT], LI[:CT], GC[:CT, 4:8], SUB)
                        yield
                        psC = rowbc(CC, ident[:CT, :CT])
                        Dm = v4(mA, CT)
                        kb.tt(Dm, psC[:CT, :4 * CT].re("p (h j) -> p h j", h=4), bc4(GC[:, 4:8], CT, CT), ADD)
                        kb.tt(Dm, Dm, mbc(Lm), ADD)
                        kb.red(MX[:CT], Dm, MAX)
                        yield
                        if sample:
                            ps = bank()
                            kb.mm(ps[:CT, 0:4], P["RMT"], P["m0"])
                            kb.cp(M0C[:CT], ps[:CT, 0:4])
                            kb.tt(INT[:CT], GC[:CT, 4:8], M0C[:CT], ADD)
                        else:
                            kb.tt(INT[:CT], GC[:CT, 4:8], P["MB"][:CT, :], ADD)
                        kb.tt(MT[:CT], INT[:CT], MX[:CT], MAX)
                        kb.ts(NMT[:CT], MT[:CT], -1.0, None, MUL)
                        yield
                        for h in range(4):
                            kb.act(mA[:CT, h * CT:(h + 1) * CT], mA[:CT, h * CT:(h + 1) * CT], AF.Exp, bias=NMT[:CT, h:h + 1])
                        kb.tt(WI[:CT], INT[:CT], MT[:CT], SUB)
                        kb.act(WI[:CT], WI[:CT], AF.Exp)
                        yield
                        psS_ = bank()
                        for h in range(4):
                            kb.mm(psS_[:CT, h * CT:(h + 1) * CT], QKm[:, h, :CT], QKm[:, 4 + h, :CT], inc=(h == 3))
                        kb.tt(Dm, Dm, psS_[:CT, :4 * CT].re("p (h j) -> p h j", h=4), MUL)
                        yield
                        ps = bank()
                        for h in range(4):
                            kb.tr(ps[:CT, h * CT:(h + 1) * CT], mA[:CT, h * CT:(h + 1) * CT], ident[:CT, :CT], inc=(h == 3))
                        kb.cp(mB[:CT, :4 * CT], ps[:CT, :4 * CT], eng="act")
                        yield
                        kb.cp(VE[:CT, :, 0:64], R[:CT, R_MV:R_MV + 256].re("p (h d) -> p h d", h=4))
                        psB = bank(hold=True)
                        for h in range(4):
                            kb.mm(psB[:CT, h * 65:(h + 1) * 65], mB[:CT, h * CT:(h + 1) * CT], VE[:CT, h, :], inc=(h == 3))
                        Xs = X1[:CT, 0:4 * nseq].re("p (h s) -> p h s", h=4)
                        kb.tt(Xs, MT[:CT].re("p (h o) -> p h o", o=1).bc([CT, 4, nseq]), lastsel[:CT, :].re("p (o s) -> p o s", o=1).bc([CT, 4, nseq]), MUL)
                        psM = bank()
                        kb.mm(psM[:, 0:4 * nseq], ones[:CT, :], X1[:CT, 0:4 * nseq])
                        MN_ = P["MN"]
                        WC = P["WC"]
                        kb.cp(MN_, psM[:, 0:4 * nseq].re("p (h s) -> p h s", h=4))
                        yield
                        kb.tt(WC, BL, P["M0R"], ADD)
                        kb.tt(WC, WC, MN_, SUB)
                        kb.act(WC, WC, AF.Exp)
                        yield
                        psBM = bank()
                        kb.mm(psBM[:CT, 0:4], LSm, NMT[:CT, 0:4])
                        kb.tt(WK[:CT], psBM[:CT, 0:4], GL[:CT, 4:8], ADD)
                        kb.tt(WK[:CT], WK[:CT], CC[:CT], ADD)
                        kb.act(WK[:CT], WK[:CT], AF.Exp)
                        kb.ts(WK[:CT], WK[:CT], 0.125, None, MUL)
                        kb.tt(KW[:CT], R[:CT, R_MK:R_MK + 256].re("p (h d) -> p h d", h=4), bc4(WK, CT, 64), MUL)
                        yield
                        psA = bank(hold=True)
                        if not sample:
                            CME = P["CME"]
                            for h in range(4):
                                kb.mm(psA[:CT, h * 65:(h + 1) * 65], QKm[:, h, :CT], CME[:, h, :], inc=(h == 3))
                            psN = bank()
                            for h in range(4):
                                kb.mm(psN[:64, h * 65:(h + 1) * 65], KW[:CT, h, :], VE[:CT, h, :], inc=(h == 3))
                            kb.tt(CME[:, :, :], CME[:, :, :], WC[:64, :, 0:1].bc([64, 4, 65]), MUL)
                            kb.tt(CME[:, :, :], CME[:, :, :], psN[:64, 0:260].re("p (h d) -> p h d", h=4), ADD)
                        for h in range(4):
                            if not sample:
                                pass
                            else:
                                CMs = P["CMs"]
                                Zt = P["Zt"]
                                ZK = P["ZK"]
                                kb.dma("sp", CMs[:, :, 0:64], st_mc[l, :, h].rearrange("s k v -> k s v"))
                                kb.dma("sp", CMs[:, :, 64:65], st_mn[l, :, h, :].rearrange("s (k o) -> k s o", o=1), allow_slow_non_contiguous=True)
                                zd = Zt[:64, 0:nseq * (CT + 4)].re("p (s r) -> p s r", r=CT + 4)[:, :, 0:4]
                                kb.cp(zd, QKm[:, h, :CT].re("p (s t) -> p s t", t=4))
                                for s_ in range(nseq):
                                    kb.mm(psA[:CT, h * 65:(h + 1) * 65], Zt[:64, s_ * CT:(s_ + 1) * CT], CMs[:, s_, :],
                                          start=(s_ == 0), stop=(s_ == nseq - 1), inc=(s_ == nseq - 1))
                                for g in range(nseq // 4):
                                    kb.tt(ZK[:CT, :, 0:64], KW[:CT, h, :].re("p (o d) -> p o d", o=1).bc([CT, 4, 64]),
                                          rowmask[:CT, 4 * g:4 * g + 4].re("p (s o) -> p s o", o=1).bc([CT, 4, 64]), MUL)
                                    psN = bank()
                                    for s4 in range(4):
                                        kb.mm(psN[:64, s4 * 65:(s4 + 1) * 65], ZK[:CT, s4, 0:64], VE[:CT, h, :], inc=(s4 == 3))
                                    wcb = WC[:64, h, 4 * g:4 * g + 4].re("p (s o) -> p s o", o=1).bc([64, 4, 65])
                                    kb.tt(CMs[:, 4 * g:4 * g + 4, :], CMs[:, 4 * g:4 * g + 4, :], wcb, MUL)
                                    kb.tt(CMs[:, 4 * g:4 * g + 4, :], CMs[:, 4 * g:4 * g + 4, :],
                                          psN[:64, 0:260].re("p (s d) -> p s d", s=4), ADD)
                                kb.dma("sp", o_smc[l, :, h].rearrange("s k v -> k s v"), CMs[:, :, 0:64], out_dram=True)
                                kb.dma("sp", o_smn[l, :, h, :].rearrange("s (k o) -> k s o", o=1), CMs[:, :, 64:65], out_dram=True, allow_slow_non_contiguous=True)
                        yield
                        if sample:
                            kb.dma("sp", o_smm[l].rearrange("(o s) h -> o h s", o=1), MN_[0:1, :, :], out_dram=True, allow_slow_non_contiguous=True)
                        else:
                            kb.cp(P["MB"], MN_[:, :, 0])
                        TOT = v4(mI, CT, 65)
                        kb.tt(TOT, psA[:CT, 0:260].re("p (h d) -> p h d", h=4), bc4(WI, CT, 65), MUL)
                        kb.tt(TOT, TOT, psB[:CT, 0:260].re("p (h d) -> p h d", h=4), ADD)
                        release(psA, psB)
                        yield
                        kb.act(DEN[:CT], mI[:CT, 64:260:65], AF.Abs)
                        kb.act(EM[:CT], NMT[:CT], AF.Exp)
                        kb.tt(DEN[:CT], DEN[:CT], EM[:CT], MAX)
                        kb.recip(DEN[:CT], DEN[:CT])
                        yield
                        HH = v4(X1, CT, 64)
                        kb.tt(HH, TOT[:, :, 0:64], bc4(DEN, CT, 64), MUL)
                        kb.tt(X1[:CT, 0:256], X1[:CT, 0:256], R[:CT, R_MO:R_MO + 256], MUL)
                        kb.act(v4(mF, CT, 64), HH, AF.Square)
                        kb.red(SS4[:CT], v4(mF, CT, 64), ADD)
                        yield
                        kb.act(RSTD4[:CT], SS4[:CT], AF.Ln, bias=EPS, scale=1.0 / 64)
                        kb.act(RSTD4[:CT], RSTD4[:CT], AF.Exp, scale=-0.5)
                        kb.tt(HH, HH, bc4(RSTD4, CT, 64), MUL)
                        kb.tt(MIX[:CT, 512:768].re("p (h d) -> p h d", h=4), HH, MNb[:CT, l, :].re("p (o d) -> p o d", o=1).bc([CT, 4, 64]), MUL)


                    hoist = (not sample) and HOIST_SSD
                    XPs = P["XP2"] if hoist else P["XP"]
                    COs = P["CO2"] if hoist else CO

                    def ssd_front():
                        for g in range(2):
                            nb = 4 if g == 0 else 2
                            wss = [fm_pair(1536 + g * 512 + q * 256) for q in range(nb // 2)]
                            ps = bank()
                            for j in range(nb):
                                ws = wss[j // 2]
                                for k in range(8):
                                    kb.mm(ps[:, j * CT:(j + 1) * CT], ws[:, k, (j % 2) * 128:(j % 2 + 1) * 128], hnT[:, k, :CT], start=(k == 0), stop=(k == 7), inc=(k == 7 and j == nb - 1))
                            if sample:
                                kb.cp(XPs[:, g * 4:g * 4 + nb, :, 3:7], ps[:, :nb * CT].re("p (b s t) -> p b s t", b=nb, t=4), eng="act")
                            else:
                                kb.cp(XPs[:, g * 4:g * 4 + nb, 3:3 + CT], ps[:, :nb * CT].re("p (b t) -> p b t", b=nb), eng="act")
                        conv_blocks(P, XPs, 12, 6, CT, COs)

                    XP = P["XP"]
                    for g in range(3):
                        wss = [fm_pair(g * 512), fm_pair(g * 512 + 256)]
                        ps = bank()
                        for j in range(4):
                            ws = wss[j // 2]
                            for k in range(8):
                                kb.mm(ps[:, j * CT:(j + 1) * CT], ws[:, k, (j % 2) * 128:(j % 2 + 1) * 128], hnT[:, k, :CT], start=(k == 0), stop=(k == 7), inc=(k == 7 and j == 3))
                        if sample:
                            kb.cp(XP[:, g * 4:g * 4 + 4, :, 3:7], ps[:, :4 * CT].re("p (b s t) -> p b s t", b=4, t=4), eng="act")
                        else:
                            kb.cp(XP[:, g * 4:g * 4 + 4, 3:3 + CT], ps[:, :4 * CT].re("p (b t) -> p b t", b=4), eng="act")
                    if hoist:
                        ssd_front_inproj = True
                    conv_blocks(P, XP, 0, 12, CT)
                    if hoist:
                        ssd_front()
                    if CUT <= 4:
                        return
                    sqb_ = bfview(SQa)
                    kb.act(sqb_[:, :, :CT], CO[:, 0:8, :CT], AF.Square)
                    for half in range(2):
                        ps = bank()
                        for j in range(4):
                            kb.mm(ps[:, j * CT:(j + 1) * CT], ones_bf, sqb_[:, half * 4 + j, :CT], inc=(j == 3))
                        t = v4(X1, 128, CT)
                        kb.act(t, ps[:, :4 * CT].re("p (b t) -> p b t", b=4), AF.Ln, bias=EPS)
                        kb.act(t, t, AF.Exp, scale=-0.5)
                        if half == 0:
                            kb.ts(t, t, 128 ** -0.5, None, MUL)
                        kb.tt(CO[:, half * 4:half * 4 + 4, :CT], CO[:, half * 4:half * 4 + 4, :CT], t, MUL)
                    if CUT <= 5:
                        return
                    psR = rowbc(CS[:, 0:4], U)
                    kb.act(EG[:, :, :CT], psR[:, :4 * CT].re("p (h j) -> p h j", h=4), AF.Exp)
                    kb.tt(v4(X1, CT), bc4(GC[:, 0:4], CT, CT), psR[:CT, :4 * CT].re("p (h j) -> p h j", h=4), SUB)
                    kb.tt(v4(A, CT), v4(X1, CT), mbc(Ls), ADD)
                    kb.act(v4(A, CT), v4(A, CT), AF.Exp)
                    kb.tt(v4(Dt, CT), mbc(Um), v4(X1, CT), SUB)
                    kb.act(v4(Dt, CT), v4(Dt, CT), AF.Exp)
                    psK = bank()
                    for h in range(4):
                        kb.mm(psK[:CT, h * CT:(h + 1) * CT], CO[:, 4 + h, :CT], CO[:, 4 + h, :CT], inc=(h == 3))
                    kb.tt(v4(A, CT), v4(A, CT), psK[:CT, :4 * CT].re("p (h j) -> p h j", h=4), MUL)
                    kb.tt(v4(A, CT), v4(A, CT), bc4(NBETA, CT, CT), MUL)
                    psQ = bank()
                    for h in range(4):
                        kb.mm(psQ[:CT, h * CT:(h + 1) * CT], CO[:, 4 + h, :CT], CO[:, h, :CT], inc=(h == 3))
                    kb.tt(v4(Dt, CT), v4(Dt, CT), psQ[:CT, :4 * CT].re("p (h j) -> p h j", h=4), MUL)
                    if CUT <= 6:
                        return
                    ps = bank()
                    for h in range(4):
                        kb.tr(ps[:CT, h * CT:(h + 1) * CT], A[:CT, h * CT:(h + 1) * CT], ident[:CT, :CT], inc=(h == 3))
                    kb.cp(B[:CT, :4 * CT], ps[:CT, :4 * CT], eng="act")
                    kb.tt(v4(C, CT), v4(B, CT), mbc(ident[:CT, :CT]), ADD)
                    mg = mlstm_section(E, F, G2, H) if (ILV_ML and not sample) else None
                    for lev in range(1, P["lev"]):
                        last = (lev == P["lev"] - 1)
                        ps1 = bank()
                        for h in range(4):
                            hs = slice(h * CT, (h + 1) * CT)
                            kb.mm(ps1[:CT, hs], B[:CT, hs], A[:CT, hs], inc=(h == 3))
                        if not last:
                            ps2 = bank()
                            for h in range(4):
                                hs = slice(h * CT, (h + 1) * CT)
                                kb.mm(ps2[:CT, hs], A[:CT, hs], B[:CT, hs], inc=(h == 3))
                        if lev >= 2:
                            ps3 = bank()
                            for h in range(4):
                                hs = slice(h * CT, (h + 1) * CT)
                                kb.mm(ps3[:CT, hs], A[:CT, hs], C[:CT, hs], inc=(h == 3))
                        kb.cp(A[:CT, :4 * CT], ps1[:CT, :4 * CT])
                        if not last:
                            kb.cp(B[:CT, :4 * CT], ps2[:CT, :4 * CT], eng="act")
                        if lev >= 2:
                            kb.tt(C[:CT, :4 * CT], C[:CT, :4 * CT], ps3[:CT, :4 * CT], ADD)
                        if mg is not None:
                            _step(mg)
                            _step(mg)
                    ps3 = bank()
                    for h in range(4):
                        hs = slice(h * CT, (h + 1) * CT)
                        kb.mm(ps3[:CT, hs], A[:CT, hs], C[:CT, hs], inc=(h == 3))
                    kb.tt(C[:CT, :4 * CT], C[:CT, :4 * CT], ps3[:CT, :4 * CT], ADD)
                    if mg is not None:
                        while _step(mg):
                            pass
                    if CUT <= 7:
                        return
                    kb.act(BE[:CT], GC[:CT, 0:4], AF.Exp)
                    kb.tt(BE[:CT], BE[:CT], BETA[:CT], MUL)
                    kb.tt(KD[:CT], GL[:CT, 0:4], GC[:CT, 0:4], SUB)
                    kb.act(KD[:CT], KD[:CT], AF.Exp)
                    psKt = bank()
                    for h in range(4):
                        kb.tr(psKt[:CT, h * 128:(h + 1) * 128], CO[:, 4 + h, :CT], ident, inc=(h == 3))
                    pk = psKt[:CT, :].re("p (h d) -> p h d", h=4)
                    kb.tt(v4(F, CT, 128), pk, bc4(BE, CT, 128), MUL)
                    kb.tt(v4(G2, CT, 128), pk, bc4(KD, CT, 128), MUL)
                    psVt = bank()
                    for h in range(4):
                        kb.tr(psVt[:CT, h * 128:(h + 1) * 128], CO[:, 8 + h, :CT], ident, inc=(h == 3))
                    kb.tt(v4(E, CT, 128), psVt[:CT, :].re("p (h d) -> p h d", h=4), bc4(BETA, CT, 128), MUL)
                    psW = bank()
                    for h in range(4):
                        kb.mm(psW[:, h * CT:(h + 1) * CT], F[:CT, h * 128:(h + 1) * 128], C[:CT, h * CT:(h + 1) * CT], inc=(h == 3))
                    kb.ts(H[:, :4 * CT], psW[:, :4 * CT], -1.0, None, MUL)
                    kb.tt(v4(I, 128, CT), CO[:, 0:4, :CT], EG[:, :, :CT], MUL)
                    if CUT <= 8:
                        return
                    psV = bank(hold=True)
                    psO = bank(hold=True)
                    if not sample:
                        S = P["Sg"]
                        for h in range(4):
                            hs = slice(h * CT, (h + 1) * CT)
                            hd = slice(h * 128, (h + 1) * 128)
                            kb.mm(psV[:CT, hd], C[:CT, hs], E[:CT, hd], start=True, stop=False, inc=False)
                            kb.mm(psV[:CT, hd], H[:, hs], S[:, h, :], start=False, stop=True, inc=(h == 3))
                        kb.cp(E[:CT, :], psV[:CT, :])
                        for h in range(4):
                            hs = slice(h * CT, (h + 1) * CT)
                            hd = slice(h * 128, (h + 1) * 128)
                            kb.mm(psO[:CT, hd], I[:, hs], S[:, h, :], start=True, stop=False, inc=False)
                            kb.mm(psO[:CT, hd], Dt[:CT, hs], E[:CT, hd], start=False, stop=True, inc=(h == 3))
                        psS = bank()
                        for h in range(4):
                            hd = slice(h * 128, (h + 1) * 128)
                            kb.mm(psS[:, hd], G2[:CT, hd], E[:CT, hd], inc=(h == 3))
                        kb.tt(S[:, :, :], S[:, :, :], EG[:, :, CT - 1:CT].bc([128, 4, 128]), MUL)
                        kb.tt(S[:, :, :], S[:, :, :], psS[:, :].re("p (h d) -> p h d", h=4), ADD)
                    for h in range(4):
                        hs = slice(h * CT, (h + 1) * CT)
                        hd = slice(h * 128, (h + 1) * 128)
                        if not sample:
                            pass
                        else:
                            Ss = P["Ss"]
                            Zt = P["Zt"]
                            ZK = P["ZK"]
                            kb.dma("sp", Ss, st_gdn[l, :, h].rearrange("s k v -> k s v"))
                            zd = Zt[:, 0:nseq * (CT + 4)].re("p (s r) -> p s r", r=CT + 4)[:, :, 0:4]
                            kb.cp(zd, H[:, hs].re("p (s t) -> p s t", t=4))
                            kb.mm(psV[:CT, hd], C[:CT, hs], E[:CT, hd], start=True, stop=False, inc=False)
                            for s in range(nseq):
                                kb.mm(psV[:CT, hd], Zt[:, s * CT:(s + 1) * CT], Ss[:, s, :], start=False, stop=(s == nseq - 1), inc=(s == nseq - 1))
                            kb.cp(E[:CT, hd], psV[:CT, hd])
                            kb.cp(zd, I[:, hs].re("p (s t) -> p s t", t=4))
                            for s in range(nseq):
                                kb.mm(psO[:CT, hd], Zt[:, s * CT:(s + 1) * CT], Ss[:, s, :], start=(s == 0), stop=False, inc=False)
                            kb.mm(psO[:CT, hd], Dt[:CT, hs], E[:CT, hd], start=False, stop=True)
                            for g in range(nseq // 4):
                                kb.tt(ZK[:CT], G2[:CT, hd].re("p (o d) -> p o d", o=1).bc([CT, 4, 128]),
                                      rowmask[:CT, 4 * g:4 * g + 4].re("p (s o) -> p s o", o=1).bc([CT, 4, 128]), MUL)
                                psS = bank()
                                for s4 in range(4):
                                    kb.mm(psS[:, s4 * 128:(s4 + 1) * 128], ZK[:CT, s4, :], E[:CT, hd], inc=(s4 == 3))
                                egl = EG[:, h, 16 * g + 3:16 * g + 16:4].re("p (s o) -> p s o", o=1).bc([128, 4, 128])
                                kb.tt(Ss[:, 4 * g:4 * g + 4, :], Ss[:, 4 * g:4 * g + 4, :], egl, MUL)
                                kb.tt(Ss[:, 4 * g:4 * g + 4, :], Ss[:, 4 * g:4 * g + 4, :], psS[:, :].re("p (s d) -> p s d", s=4), ADD)
                            kb.dma("sp", o_sgdn[l, :, h].rearrange("s k v -> k s v"), Ss, out_dram=True)
                    po = psO[:CT, :].re("p (h d) -> p h d", h=4)
                    kb.act(v4(X1, CT, 128), po, AF.Square)
                    kb.red(SS4[:CT], v4(X1, CT, 128), ADD)
                    kb.act(RSTD4[:CT], SS4[:CT], AF.Ln, bias=EPS, scale=1.0 / 128)
                    kb.act(RSTD4[:CT], RSTD4[:CT], AF.Exp, scale=-0.5)
                    kb.tt(v4(X1, CT, 128), po, bc4(RSTD4, CT, 128), MUL)
                    kb.tt(v4(X1, CT, 128), v4(X1, CT, 128), GN[:CT, l, :].re("p (o d) -> p o d", o=1).bc([CT, 4, 128]), MUL)
                    kb.tt(MIX[:CT, 0:512], X1[:CT, :], R[:CT, R_GG:R_GG + 512], MUL)
                    release(psV, psO)

                    if CUT <= 9:
                        return
                    if not ILV_ML or sample:
                        for _ in mlstm_section(A, B, I, F):
                            pass

                    if not hoist:
                        ssd_front()
                    psR = rowbc(CS[:, 8:12], U)
                    kb.act(EG[:, :, :CT], psR[:, :4 * CT].re("p (h j) -> p h j", h=4), AF.Exp)
                    kb.tt(v4(X1, CT), bc4(GC[:, 8:12], CT, CT), psR[:CT, :4 * CT].re("p (h j) -> p h j", h=4), SUB)
                    kb.tt(v4(Dt, CT), mbc(Um), v4(X1, CT), SUB)
                    kb.act(v4(Dt, CT), v4(Dt, CT), AF.Exp)
                    psC = bank()
                    for g in range(2):
                        kb.mm(psC[:CT, g * CT:(g + 1) * CT], COs[:, 2 + g, :CT], COs[:, 4 + g, :CT], inc=(g == 1))
                    Dt4 = Dt[:CT, :4 * CT].re("p (g e j) -> p g e j", g=2, e=2)
                    kb.tt(Dt4, Dt4, psC[:CT, :2 * CT].re("p (g o j) -> p g o j", g=2, o=1).bc([CT, 2, 2, CT]), MUL)
                    kb.tt(v4(Dt, CT), v4(Dt, CT), bc4(SP[:, 8:12], CT, CT), MUL)
                    psX = bank()
                    for j in range(2):
                        kb.tr(psX[:CT, j * 128:(j + 1) * 128], COs[:, j, :CT], ident, inc=False)
                    for j in range(2):
                        kb.tr(psX[:CT, 256 + j * 128:256 + (j + 1) * 128], COs[:, 2 + j, :CT], ident, inc=(j == 1))
                    kb.cp(XT_[:CT, :], psX[:CT, 0:256])
                    kb.tt(KDS[:CT], GL[:CT, 8:12], GC[:CT, 8:12], SUB)
                    kb.act(KDS[:CT], KDS[:CT], AF.Exp)
                    kb.tt(KDS[:CT], KDS[:CT], SP[:CT, 8:12], MUL)
                    BW = F
                    kb.tt(BW[:CT, :].re("p (g e n) -> p g e n", g=2, e=2), psX[:CT, 256:512].re("p (g o n) -> p g o n", g=2, o=1).bc([CT, 2, 2, 128]),
                          KDS[:CT].re("p (g e o) -> p g e o", g=2, o=1).bc([CT, 2, 2, 128]), MUL)
                    CG = I
                    kb.tt(CG[:, :4 * CT].re("p (g e j) -> p g e j", g=2, e=2), COs[:, 4:6, :CT].re("p g (o j) -> p g o j", o=1).bc([128, 2, 2, CT]),
                          EG[:, :, :CT].re("p (g e) j -> p g e j", g=2), MUL)
                    psY = bank(hold=True)
                    if not sample:
                        HT = P["HT"]
                        for h in range(4):
                            hs = slice(h * CT, (h + 1) * CT)
                            hp_ = slice(h * 64, (h + 1) * 64)
                            kb.mm(psY[:CT, hp_], Dt[:CT, hs], XT_[:CT, hp_], start=True, stop=False, inc=False)
                            kb.mm(psY[:CT, hp_], CG[:, hs], HT[:, h, :], start=False, stop=True, inc=(h == 3))
                        psH = bank()
                        for h in range(4):
                            hp_ = slice(h * 64, (h + 1) * 64)
                            hn_ = slice(h * 128, (h + 1) * 128)
                            kb.mm(psH[:, hp_], BW[:CT, hn_], XT_[:CT, hp_], inc=(h == 3))
                        kb.tt(HT[:, :, :], HT[:, :, :], EG[:, :, CT - 1:CT].bc([128, 4, 64]), MUL)
                        kb.tt(HT[:, :, :], HT[:, :, :], psH[:, 0:256].re("p (h d) -> p h d", h=4), ADD)
                    for h in range(4):
                        hs = slice(h * CT, (h + 1) * CT)
                        hp_ = slice(h * 64, (h + 1) * 64)
                        hn_ = slice(h * 128, (h + 1) * 128)
                        if not sample:
                            pass
                        else:
                            HTs = P["HTs"]
                            STG = P["STG"]
                            Zt = P["Zt"]
                            ZK = P["ZK"]
                            for g in range(nseq // 4):
                                kb.dma("sp", STG, st_ssd[l, 4 * g:4 * g + 4, h].rearrange("s p n -> p s n"))
                                ps = bank()
                                for s4 in range(4):
                                    kb.tr(ps[:, s4 * 64:(s4 + 1) * 64], STG[:, s4, :], ident[:64, :64], inc=(s4 == 3))
                                kb.cp(HTs[:, 4 * g:4 * g + 4, :], ps[:, 0:256].re("p (s d) -> p s d", s=4))
                            zd = Zt[:, 0:nseq * (CT + 4)].re("p (s r) -> p s r", r=CT + 4)[:, :, 0:4]
                            kb.cp(zd, CG[:, hs].re("p (s t) -> p s t", t=4))
                            kb.mm(psY[:CT, hp_], Dt[:CT, hs], XT_[:CT, hp_], start=True, stop=False, inc=False)
                            for s in range(nseq):
                                kb.mm(psY[:CT, hp_], Zt[:, s * CT:(s + 1) * CT], HTs[:, s, :], start=False, stop=(s == nseq - 1), inc=(s == nseq - 1))
                            for g in range(nseq // 4):
                                kb.tt(ZK[:CT], BW[:CT, hn_].re("p (o d) -> p o d", o=1).bc([CT, 4, 128]),
                                      rowmask[:CT, 4 * g:4 * g + 4].re("p (s o) -> p s o", o=1).bc([CT, 4, 128]), MUL)
                                psH = bank()
                                for s4 in range(4):
                                    kb.mm(psH[:, s4 * 64:(s4 + 1) * 64], ZK[:CT, s4, :], XT_[:CT, hp_], inc=(s4 == 3))
                                egl = EG[:, h, 16 * g + 3:16 * g + 16:4].re("p (s o) -> p s o", o=1).bc([128, 4, 64])
                                kb.tt(HTs[:, 4 * g:4 * g + 4, :], HTs[:, 4 * g:4 * g + 4, :], egl, MUL)
                                kb.tt(HTs[:, 4 * g:4 * g + 4, :], HTs[:, 4 * g:4 * g + 4, :], psH[:, 0:256].re("p (s d) -> p s d", s=4), ADD)
                                ps = bank()
                                for s4 in range(4):
                                    kb.tr(ps[:64, s4 * 128:(s4 + 1) * 128], HTs[:, 4 * g + s4, :], ident, inc=(s4 == 3))
                                kb.cp(STG, ps[:64, :].re("p (s n) -> p s n", s=4))
                                kb.dma("sp", o_sssd[l, 4 * g:4 * g + 4, h].rearrange("s p n -> p s n"), STG, out_dram=True)
                    Y1 = X1
                    kb.tt(v4(Y1, CT, 64), XT_[:CT, :].re("p (h d) -> p h d", h=4), bc4(HP[:, 6, l, :], CT, 64), MUL)
                    kb.tt(Y1[:CT, 0:256], Y1[:CT, 0:256], psY[:CT, 0:256], ADD)
                    release(psY)
                    kb.tt(Y1[:CT, 0:256], Y1[:CT, 0:256], R[:CT, R_SZ:R_SZ + 256], MUL)
                    kb.act(H[:CT, 0:256], Y1[:CT, 0:256], AF.Square)
                    kb.red(SS1[:CT], H[:CT, 0:256], ADD)
                    kb.act(SS1[:CT], SS1[:CT], AF.Ln, bias=EPS, scale=1.0 / 256)
                    kb.act(SS1[:CT], SS1[:CT], AF.Exp, scale=-0.5)
                    kb.stt(MIX[:CT, 768:1024], Y1[:CT, 0:256], SS1[:CT, 0:1], SN[:CT, l, :], MUL, MUL)

                    if DEBUG and sample and l == 0:
                        kb.dma("sp", dbg_mix, MIX[:CT, :], out_dram=True)
                    if CUT <= 11:
                        return
                    yield "preout"
                    for half in range(2):
                        ps = bank()
                        for j in range(4):
                            k = half * 4 + j
                            kb.tr(ps[:, j * CT:(j + 1) * CT], MIX[:CT, k * 128:(k + 1) * 128], ident[:CT, :CT], inc=(j == 3))
                        kb.cp(mixT[:, half * 4:half * 4 + 4, :CT], ps[:, :4 * CT].re("p (k t) -> p k t", k=4), eng=("act" if half else "dve"))
                    for half in range(2):
                        ps = bank()
                        for j in range(4):
                            db = half * 4 + j
                            for k in range(8):
                                kb.mm(ps[:, j * CT:(j + 1) * CT], Wout[:, k, db * 128:(db + 1) * 128], mixT[:, k, :CT], start=(k == 0), stop=(k == 7), inc=(k == 7 and j == 3))
                        kb.cp(SQa[:, half * 4:half * 4 + 4, :CT], ps[:, :4 * CT].re("p (k t) -> p k t", k=4), eng=("act" if half else "dve"))
                    postnorm_add(SQa, c, 1, l, CT)

                def run_chunks(clist, P):
                    def step(g):
                        try:
                            next(g)
                            return True
                        except StopIteration:
                            return False
                    prev = None
                    for c in clist:
                        g = mixer_chunk(c, P)
                        alive = step(g)
                        if prev is not None:
                            while step(prev):
                                pass
                        if alive:
                            alive = step(g)
                        prev = g if alive else None
                    if prev is not None:
                        while step(prev):
                            pass

                def conv_blocks(P, XP, b0, nb, CT, CO=None):
                    CO = CO_main if CO is None else CO
                    sample = P["sample"]
                    nseq_ = P["nseq"]
                    if sample:
                        HS = P["HS"]
                        kb.cp(XP[:, 0:nb, :, 0:3], HS[:, b0:b0 + nb, :, :])
                    else:
                        Hh = P["Hh"]
                        kb.cp(XP[:, 0:nb, 0:3], Hh[:, b0:b0 + nb, :])
                    for t in range(4):
                        for j in range(nb):
                            b = b0 + j
                            if sample:
                                o = CO[:, j, :CT].re("p (s t) -> p s t", t=4)
                                xi_ = XP[:, j, :, t:t + 4]
                            else:
                                o = CO[:, j, :CT]
                                xi_ = XP[:, j, t:t + CT]
                            if t == 0:
                                kb.act(o, xi_, AF.Identity, bias=CB[:, l, b:b + 1], scale=CW[:, l, b, 0:1])
                            else:
                                kb.stt(o, xi_, CW[:, l, b, t:t + 1], o, MUL, ADD)
                    if not sample:
                        kb.cp(P["Hh"][:, b0:b0 + nb, :], XP[:, 0:nb, CT:CT + 3])
                    kb.act(CO[:, 0:nb, :CT], CO[:, 0:nb, :CT], AF.Silu)

                with ExitStack() as es3:
                    kb.es = es3
                    Pp = dict(CT=128, nseq=1, sample=False, cm=cmp_, lastsel=lsp, lev=7)
                    Pp["XP"] = kb.sb("XPp", [128, 12, 131])
                    if HOIST_SSD:
                        Pp["XP2"] = kb.sb("XP2", [128, 6, 131])
                        Pp["CO2"] = kb.sb("CO2", [128, 6, 128])
                    Pp["WS"] = [kb.sb("WSp%d" % i, [128, 8, 256], BF16) for i in range(5)]
                    Pp["wsi"] = [0]
                    Pp["Hh"] = kb.sb("Hh", [128, 18, 3])
                    Pp["Sg"] = kb.sb("Sg", [128, 4, 128])
                    Pp["CME"] = kb.sb("CME", [64, 4, 65])
                    Pp["HT"] = kb.sb("HT", [128, 4, 64])
                    Pp["MB"] = kb.sb("MB", [128, 4])
                    Pp["BL"] = kb.sb("BLp", [128, 4, 1])
                    Pp["MN"] = kb.sb("MNp", [128, 4, 1])
                    Pp["WC"] = kb.sb("WCp", [128, 4, 1])
                    Pp["M0R"] = Pp["MB"][:, :].re("p (h o) -> p h o", o=1)
                    for nme in ("Hh", "Sg", "CME", "HT", "MB"):
                        kb.memset(Pp[nme], 0.0)
                    if STOP & 1:
                        run_chunks(list(range(NPC)), Pp)
                    kb.dma("sp", o_pgdn[l].rearrange("h k v -> k h v"), Pp["Sg"], out_dram=True)
                    for h in range(4):
                        kb.dma("sp", o_pmc[l, h], Pp["CME"][:, h, 0:64], out_dram=True)
                        kb.dma("sp", o_pmn[l, h].rearrange("(k o) -> k o", o=1), Pp["CME"][:, h, 64:65], out_dram=True, allow_slow_non_contiguous=True)
                    kb.dma("sp", o_pmm[l].rearrange("(o h) -> o h", o=1), Pp["MB"][0:1, :], out_dram=True)
                    ps = bank()
                    for h in range(4):
                        kb.tr(ps[:64, h * 128:(h + 1) * 128], Pp["HT"][:, h, :], ident, inc=(h == 3))
                    kb.cp(TMP["A"][:64, :], ps[:64, :])
                    kb.dma("sp", o_pssd[l].rearrange("h p n -> p h n"), TMP["A"][:64, :].re("p (h n) -> p h n", h=4), out_dram=True)
                    barrier()
                with ExitStack() as es3:
                    kb.es = es3
                    Psm = dict(CT=NST, nseq=NS, sample=True, cm=cms, lastsel=lss, lev=2)
                    Psm["XP"] = kb.sb("XPs", [128, 12, NS, 7])
                    Psm["WS"] = [kb.sb("WSs%d" % i, [128, 8, 256], BF16) for i in range(2)]
                    Psm["wsi"] = [0]
                    Psm["HS"] = kb.sb("HS", [128, 18, NS, 3])
                    SST = kb.sb("SST", [128, NS * 128])
                    Psm["Ss"] = SST[:, :].re("p (s d) -> p s d", s=NS)
                    Psm["Zt"] = kb.sb("Zt", [128, (NS + 1) * NST])
                    Psm["ZK"] = kb.sb("ZK", [64, 4, 128])
                    Psm["CMs"] = SST[:64, 0:NS * 65].re("p (s d) -> p s d", s=NS)
                    Psm["HTs"] = SST[:, 0:NS * 64].re("p (s d) -> p s d", s=NS)
                    Psm["STG"] = Psm["ZK"]
                    Psm["BL"] = kb.sb("BLs", [128, 4, NS])
                    Psm["MN"] = kb.sb("MNs", [128, 4, NS])
                    Psm["WC"] = kb.sb("WCs", [128, 4, NS])
                    m0r = kb.sb("M0Rs", [128, NS, 4])
                    Psm["m0"] = kb.sb("m0s", [NS, 4])
                    Psm["RMT"] = kb.sb("RMT", [NS, NST])
                    kb.memset(Psm["Zt"], 0.0)
                    kb.dma("sp", m0r, st_mm[l].partition_broadcast(128))
                    kb.dma("sp", Psm["m0"], st_mm[l])
                    Psm["M0R"] = m0r[:, :, :].re("p s h -> p h s")
                    ps = bank()
                    kb.tr(ps[:NS, :NST], rowmask[:NST, :NS], ident[:NST, :NST])
                    kb.cp(Psm["RMT"], ps[:NS, :NST])
                    for b in range(18):
                        stg_ = TMP["E"] if b % 2 else TMP["F"]
                        kb.dma("sp", stg_[:NS * 3, 0:128], st_conv[l, :, :, b * 128:(b + 1) * 128].rearrange("s r c -> (s r) c"))
                        ps = bank()
                        kb.tr(ps[:, :NS * 3], stg_[:NS * 3, 0:128], ident[:NS * 3, :NS * 3])
                        kb.cp(Psm["HS"][:, b, :, :], ps[:, :NS * 3].re("p (s r) -> p s r", r=3), eng=("act" if b % 2 else "dve"))
                    if STOP & 2:
                        run_chunks([NPC], Psm)
                    barrier()
            with ExitStack() as es2:
                kb.es = es2
                groups = []
                cs_ = list(range(NPC))
                GSZ = 6
                for g0 in range(0, NPC, GSZ):
                    groups.append(cs_[g0:g0 + GSZ])
                groups[-1] = groups[-1] + [NPC]
                MAXT = max(sum(ctok(c) for c in g) for g in groups)
                hn2 = kb.sb("hn2", [128, 8, MAXT], BF16)
                ACTT = kb.sb("ACTT", [128, NFB, MAXT], BF16)
                WU = [kb.sb("WU%d" % i, [128, 8, 256], BF16) for i in range(NWU)]
                WD = [kb.sb("WD%d" % i, [128, NFB, 128], BF16) for i in range(NWD)]
                GP = kb.sb("GP", [128, 2 + GSZ * 128])
                GS = kb.sb("GS", [128, NS, 6])
                VV = kb.sb("VV", [128, MAXT])
                ACC = kb.sb("ACC", [128, MAXT])
                T1 = kb.sb("T1", [128, MAXT])
                YF = kb.sb("YF", [128, 8, max(MAXT, 704)])
                HF = kb.sb("HF", [128, NFB, 2])
                HSf = kb.sb("HSf", [128, NFB, NS, 2])
                STGf = YF[:, :, :].re("p k t -> p (k t)")
                pass
                kb.memset(HF, 0.0)
                kb.dma("sp", STGf[:NS * 2, 0:DFF], st_ffn[l].rearrange("s r c -> (s r) c"))
                for b in range(NFB):
                    ps = bank()
                    kb.tr(ps[:, :NS * 2], STGf[:NS * 2, b * 128:(b + 1) * 128], ident[:NS * 2, :NS * 2])
                    kb.cp(HSf[:, b, :, :], ps[:, :NS * 2].re("p (s r) -> p s r", r=2), eng=("act" if b % 2 else "dve"))
                wu_i = 0
                wd_i = 0
                for gi, grp in enumerate(groups):
                    if not (STOP & 4):
                        continue
                    has_s = (grp[-1] == NPC)
                    pch = [c for c in grp if c < NPC]
                    npt = len(pch) * 128
                    ntg = npt + (NST if has_s else 0)
                    off = 0
                    offs = {}
                    for c in grp:
                        CT = ctok(c)
                        offs[c] = off
                        rmsnorm_to(hn2[:, :, off:off + CT], c, 2, l, CT)
                        off += CT
                    tbs = [(t0, min(512, ntg - t0)) for t0 in range(0, ntg, 512)]
                    for fb in range(NFB):
                        wu = WU[wu_i % NWU]
                        wu_i += 1
                        kb.dma("pool", wu[:, :, 0:128], w_up[l, :, fb * 128:(fb + 1) * 128].rearrange("(k p) c -> p k c", p=128))
                        kb.dma("pool", wu[:, :, 128:256], w_up[l, :, DFF + fb * 128:DFF + (fb + 1) * 128].rearrange("(k p) c -> p k c", p=128))
                        if (NPC - 1) in grp:
                            r0 = offs[NPC - 1] + 126
                            ps = bank()
                            for k in range(8):
                                kb.mm(ps[:2, 0:128], hn2[:, k, r0:r0 + 2], wu[:, k, 0:128], start=(k == 0), stop=(k == 7), inc=(k == 7))
                            kb.cp(STGf[:2, fb * 128:(fb + 1) * 128], ps[:2, 0:128], eng="act")
                        if has_s:
                            r0 = offs[NPC]
                            ps = bank()
                            for k in range(8):
                                kb.mm(ps[:NST, 0:128], hn2[:, k, r0:r0 + NST], wu[:, k, 0:128], start=(k == 0), stop=(k == 7), inc=(k == 7))
                            kb.cp(STGf[:NST, DFF + fb * 128:DFF + (fb + 1) * 128], ps[:NST, 0:128], eng="act")
                        for (t0, tw) in tbs:
                            psG = bank()
                            for k in range(8):
                                kb.mm(psG[:, :tw], wu[:, k, 0:128], hn2[:, k, t0:t0 + tw], start=(k == 0), stop=(k == 7), inc=(k == 7))
                            psV = bank()
                            for k in range(8):
                                kb.mm(psV[:, :tw], wu[:, k, 128:256], hn2[:, k, t0:t0 + tw], start=(k == 0), stop=(k == 7), inc=(k == 7))
                            p1 = min(t0 + tw, npt)
                            if p1 > t0:
                                kb.cp(GP[:, 2 + t0:2 + p1], psG[:, 0:p1 - t0], eng="act")
                            if t0 + tw > npt:
                                s0 = max(t0, npt) - t0
                                kb.cp(GS[:, :, 2:6], psG[:, s0:s0 + NST].re("p (s t) -> p s t", t=4), eng="act")
                            kb.cp(VV[:, t0:t0 + tw], psV[:, :tw], eng="act")
                        if npt > 0:
                            kb.cp(GP[:, 0:2], HF[:, fb, :])
                            kb.act(ACC[:, 0:npt], GP[:, 0:npt], AF.Identity, bias=FCB[:, l, fb:fb + 1], scale=FCW[:, l, fb, 0:1])
                            kb.stt(ACC[:, 0:npt], GP[:, 1:1 + npt], FCW[:, l, fb, 1:2], ACC[:, 0:npt], MUL, ADD)
                            kb.stt(ACC[:, 0:npt], GP[:, 2:2 + npt], FCW[:, l, fb, 2:3], ACC[:, 0:npt], MUL, ADD)
                            kb.cp(HF[:, fb, :], GP[:, npt:npt + 2])
                        if has_s:
                            kb.cp(GS[:, :, 0:2], HSf[:, fb, :, :])
                            a_s = ACC[:, npt:npt + NST].re("p (s t) -> p s t", t=4)
                            kb.act(a_s, GS[:, :, 0:4], AF.Identity, bias=FCB[:, l, fb:fb + 1], scale=FCW[:, l, fb, 0:1])
                            kb.stt(a_s, GS[:, :, 1:5], FCW[:, l, fb, 1:2], a_s, MUL, ADD)
                            kb.stt(a_s, GS[:, :, 2:6], FCW[:, l, fb, 2:3], a_s, MUL, ADD)
                        kb.act(T1[:, :ntg], ACC[:, :ntg], AF.Square, scale=0.044715 ** 0.5)
                        kb.stt(T1[:, :ntg], T1[:, :ntg], 1.0, ACC[:, :ntg], ADD, MUL)
                        kb.act(T1[:, :ntg], T1[:, :ntg], AF.Sigmoid, scale=1.5957691216057308)
                        kb.tt(ACC[:, :ntg], ACC[:, :ntg], VV[:, :ntg], MUL)
                        kb.tt(ACTT[:, fb, :ntg], T1[:, :ntg], ACC[:, :ntg], MUL)
                    if (NPC - 1) in grp:
                        kb.dma("sp", o_pffn[l], STGf[:2, 0:DFF], out_dram=True)
                    if has_s:
                        for r in range(2):
                            kb.dma("sp", o_sffn[l, :, r, :], STGf[2 + r:NST:4, DFF:2 * DFF], out_dram=True)
                    for db in range(8):
                        wd = WD[wd_i % NWD]
                        wd_i += 1
                        kb.dma("pool", wd, w_down[l, :, db * 128:(db + 1) * 128].rearrange("(f p) c -> p f c", p=128))
                        for (t0, tw) in tbs:
                            ps = bank()
                            for fb in range(NFB):
                                kb.mm(ps[:, :tw], wd[:, fb, :], ACTT[:, fb, t0:t0 + tw], start=(fb == 0), stop=(fb == NFB - 1), inc=(fb == NFB - 1))
                            kb.cp(YF[:, db, t0:t0 + tw], ps[:, :tw], eng=("act" if db % 2 else "dve"))
                    for c in grp:
                        CT = ctok(c)
                        kb.cp(SQa[:, :, :CT], YF[:, :, offs[c]:offs[c] + CT])
                        postnorm_add(SQa, c, 3, l, CT)
                barrier()

        with ExitStack() as es1:
            kb.es = es1
            yo = [kb.sb("yo%d" % i, [128, D]) for i in range(2)]
            for c in range(NCH):
                CT = ctok(c)
                yt = yo[c % 2]
                for half in range(2):
                    ps = bank()
                    for j in range(4):
                        k = half * 4 + j
                        kb.tr(ps[:CT, j * 128:(j + 1) * 128], xT[c][:, k, :CT], ident, inc=(j == 3))
                    kb.cp(yt[:CT, half * 512:(half + 1) * 512], ps[:CT, :], eng=("act" if half else "dve"))
                dst = y_p[c * 128:(c + 1) * 128, :] if c < NPC else y_s
                kb.dma("sp", dst, yt[:CT, :], out_dram=True)
        kb.finish()
    return nc


IN_NAMES = ["norm_mix_pre", "norm_mix_post", "norm_ffn_pre", "norm_ffn_post", "w_in", "conv_w", "conv_b",
            "gdn_a_log", "gdn_dt_bias", "gdn_norm", "mlstm_i_bias", "mlstm_f_bias", "mlstm_norm",
            "ssd_a_log", "ssd_dt_bias", "ssd_d", "ssd_norm", "w_out", "ffn_w_up", "ffn_conv_w", "ffn_conv_b", "ffn_w_down"]
ST_MAP = [("st_conv", "state_conv"), ("st_gdn", "state_gdn"), ("st_mc", "state_mlstm_c"), ("st_mn", "state_mlstm_n"),
          ("st_mm", "state_mlstm_m"), ("st_ssd", "state_ssd"), ("st_ffn", "state_ffn_conv")]
_NC_CACHE = {}


def make_in_maps(inputs, NPC, NS, DEPTH, ncores):
    consts = make_consts(NS)
    f = lambda a: np.ascontiguousarray(np.asarray(a, dtype=np.float32))
    shared = {n: f(inputs[n])[:DEPTH] for n in IN_NAMES}
    shared.update(consts)
    maps = []
    for c in range(ncores):
        m = dict(shared)
        m["xp"] = f(inputs["x_prompt"][c, :NPC * 128])
        m["xs"] = f(inputs["x_sample"][c * NS:(c + 1) * NS]).reshape(NS * 4, D)
        for k, n in ST_MAP:
            m[k] = f(inputs[n][:DEPTH, c * NS:(c + 1) * NS])
        maps.append(m)
    return maps


def assemble(results, NPC, NS, DEPTH):
    cat = lambda k, ax: np.concatenate([r[k] for r in results], axis=ax)
    y_p = np.stack([r["y_p"] for r in results], 0)
    y_s = np.concatenate([r["y_s"].reshape(NS, 4, D) for r in results], 0)
    outs = [y_p, y_s]
    for k in ("p_conv", "p_gdn", "p_mc", "p_mn", "p_mm", "p_ssd", "p_ffn"):
        outs.append(np.stack([r[k] for r in results], 1))
    for k in ("s_conv", "s_gdn", "s_mc", "s_mn", "s_mm", "s_ssd", "s_ffn"):
        outs.append(cat(k, 1))
    return tuple(np.ascontiguousarray(o, dtype=np.float32) for o in outs)


def kernel(**inputs):
    NPC, NS, DEPTH, ncores = 16, 16, 2, 8
    key = (NPC, NS, DEPTH)
    if key not in _NC_CACHE:
        _NC_CACHE[key] = build(NPC, NS, DEPTH)
    nc = _NC_CACHE[key]
    maps = make_in_maps(inputs, NPC, NS, DEPTH, ncores)
    res = run_bass_kernel_spmd(nc, maps, core_ids=list(range(ncores)))
    return assemble(res.results, NPC, NS, DEPTH)
```

```python
import numpy as np
import concourse.bass as bass
import concourse.mybir as mybir

F32 = mybir.dt.float32
BF16 = mybir.dt.bfloat16
ALU = mybir.AluOpType
AF = mybir.ActivationFunctionType
AX = mybir.AxisListType


class T:
    def __init__(self, ap, name):
        self.ap = ap
        self.name = name
        self.we = None
        self.wd = []
        self.re = {}
        self.rd = []

    def __getitem__(self, idx):
        return V(self, self.ap[idx])

    @property
    def t(self):
        return self


class V:
    def __init__(self, t, ap):
        self.t = t
        self.ap = ap

    def __getitem__(self, idx):
        return V(self.t, self.ap[idx])

    def re(self, pattern_, **kw):
        return V(self.t, self.ap.rearrange(pattern_, **kw))

    def bc(self, shape):
        return V(self.t, self.ap.to_broadcast(shape))


def _ap(x):
    return x.ap if isinstance(x, (T, V)) else x


class KB:
    def __init__(self, nc, es, n_dma_sems=6):
        self.nc = nc
        self.es = es
        self.E = {"pe": nc.tensor, "act": nc.scalar, "dve": nc.vector, "pool": nc.gpsimd, "sp": nc.sync}
        self.sem = {e: es.enter_context(nc.semaphore("s_" + e)) for e in ("pe", "act", "dve", "pool")}
        self.cnt = {e: 0 for e in self.sem}
        self.known = {e: {} for e in self.E}
        self.knownd = {e: {} for e in self.E}
        self.pend = {e: [] for e in self.E}
        self.dsems = {}
        for q in ("sp", "pool", "act"):
            self.dsems[q] = [[es.enter_context(nc.semaphore("d_%s%d" % (q, i))), 0] for i in range(n_dma_sems if q != "act" else 4)]
        self.dnext = {q: 0 for q in self.dsems}
        self.nbank = 0
        self.out_deps = []

    def sb(self, name, shape, dt=F32):
        self.nbank += 1
        name = "sb%d_%s" % (self.nbank, name)
        return T(self.es.enter_context(self.nc.sbuf_tensor(name, list(shape), dt)).ap(), name)

    def ps(self, name, shape, dt=F32):
        self.nbank += 1
        name = "ps%d_%s" % (self.nbank, name)
        t = T(self.es.enter_context(self.nc.psum_tensor(name, list(shape), dt)).ap(), name)
        t.psum = True
        return t

    def _wait_e(self, eng, dep):
        e2, c = dep
        if self.known[eng].get(e2, 0) >= c:
            return
        self.E[eng].wait_ge(self.sem[e2], c)
        self.known[eng][e2] = c

    def _wait_d(self, eng, dep):
        key, sem, tgt = dep
        if self.knownd[eng].get(key, 0) >= tgt:
            return
        self.E[eng].wait_ge(sem, tgt)
        self.knownd[eng][key] = tgt

    def _pre(self, eng, outs, ins):
        for e2, pl in self.pend.items():
            if e2 == eng:
                continue
            for (po, pi) in pl:
                for v in outs:
                    assert all(v.t is not x.t for x in po + pi), ("pending hazard", v.t.name, e2)
                for v in ins:
                    assert all(v.t is not x.t for x in po), ("pending hazard", v.t.name, e2)
        for v in ins:
            t = v.t
            if t.we is not None and not (eng == "pe" and t.we[0] == "pe"):
                self._wait_e(eng, t.we)
            for d in t.wd:
                self._wait_d(eng, d)
            if getattr(t, "psum", False):
                for e2, c in t.re.items():
                    if e2 != eng:
                        self._wait_e(eng, (e2, c))
        for v in outs:
            t = v.t
            if t.we is not None and not (eng == "pe" and t.we[0] == "pe"):
                self._wait_e(eng, t.we)
            for d in t.wd:
                self._wait_d(eng, d)
            for e2, c in t.re.items():
                if not (e2 == "pe" and eng == "pe"):
                    self._wait_e(eng, (e2, c))
            for d in t.rd:
                self._wait_d(eng, d)

    def op(self, eng, fn, outs, ins, inc=True):
        outs = [o for o in outs if o is not None]
        ins = [i for i in ins if isinstance(i, (T, V))]
        self._pre(eng, outs, ins)
        inst = fn()
        self.pend[eng].append((outs, ins))
        if inc:
            self.cnt[eng] += 1
            c = self.cnt[eng]
            inst.then_inc(self.sem[eng], 1)
            for (po, pi) in self.pend[eng]:
                for v in pi:
                    v.t.re[eng] = c
                for v in po:
                    v.t.we = (eng, c)
                    v.t.wd = []
                    v.t.re = {}
                    v.t.rd = []
            self.pend[eng] = []
        return inst

    def dma(self, q, out, in_, out_dram=False, **kw):
        assert not self.pend[q] if q in self.pend else True
        pool = self.dsems[q]
        i = self.dnext[q]
        self.dnext[q] = (i + 1) % len(pool)
        sem, prev = pool[i]
        key = (q, i)
        if prev > 0:
            self._wait_d(q, (key, sem, prev))
        self._pre(q, [o for o in [out] if isinstance(o, (T, V))], [o for o in [in_] if isinstance(o, (T, V))])
        tgt = prev + 16
        pool[i][1] = tgt
        self.E[q].dma_start(out=_ap(out), in_=_ap(in_), **kw).then_inc(sem, 16)
        dep = (key, sem, tgt)
        if isinstance(in_, (T, V)):
            in_.t.rd.append(dep)
        if isinstance(out, (T, V)):
            out.t.wd.append(dep)
        if out_dram:
            self.out_deps.append(dep)
        return dep

    def finish(self):
        for d in self.out_deps:
            self._wait_d("sp", d)

    def mm(self, out, lhsT, rhs, start=True, stop=True, inc=True):
        return self.op("pe", lambda: self.nc.tensor.matmul(_ap(out), _ap(lhsT), _ap(rhs), start=start, stop=stop),
                       [out], [lhsT, rhs], inc=inc)

    def tr(self, out, in_, ident, inc=True):
        return self.op("pe", lambda: self.nc.tensor.transpose(_ap(out), _ap(in_), _ap(ident)), [out], [in_, ident], inc=inc)

    def act(self, out, in_, func, bias=0.0, scale=1.0, eng="act"):
        return self.op("act", lambda: self.nc.scalar.activation(_ap(out), _ap(in_), func, bias=_ap(bias), scale=_ap(scale)),
                       [out], [in_, bias, scale])

    def ts(self, out, in0, s1, s2, op0, op1=None, eng="dve"):
        e = self.E[eng]
        if op1 is None:
            return self.op(eng, lambda: e.tensor_scalar(_ap(out), _ap(in0), _ap(s1), None, op0), [out], [in0, s1])
        return self.op(eng, lambda: e.tensor_scalar(_ap(out), _ap(in0), _ap(s1), _ap(s2), op0, op1), [out], [in0, s1, s2])

    def stt(self, out, in0, s, in1, op0, op1):
        return self.op("dve", lambda: self.nc.vector.scalar_tensor_tensor(_ap(out), _ap(in0), _ap(s), _ap(in1), op0, op1),
                       [out], [in0, s, in1])

    def tt(self, out, in0, in1, op, eng="dve"):
        e = self.E[eng]
        return self.op(eng, lambda: e.tensor_tensor(_ap(out), _ap(in0), _ap(in1), op), [out], [in0, in1])

    def cp(self, out, in_, eng="dve"):
        if eng == "act":
            return self.op("act", lambda: self.nc.scalar.copy(_ap(out), _ap(in_)), [out], [in_])
        e = self.E[eng]
        return self.op(eng, lambda: e.tensor_copy(_ap(out), _ap(in_)), [out], [in_])

    def recip(self, out, in_):
        return self.op("dve", lambda: self.nc.vector.reciprocal(_ap(out), _ap(in_)), [out], [in_])

    def red(self, out, in_, op, axis=AX.X):
        return self.op("dve", lambda: self.nc.vector.tensor_reduce(_ap(out), _ap(in_), axis, op), [out], [in_])

    def memset(self, out, val, eng="dve"):
        e = self.E[eng]
        return self.op(eng, lambda: e.memset(_ap(out), val), [out], [])


from contextlib import ExitStack
from concourse.bass_utils import run_bass_kernel_spmd

D = 1024
KC = 8
CONV_DIM = 2304
IN_COLS = 4116
NREST = 1812
DFF = 2816
NFB = 22
EPS = 1e-6
NEG = -30000.0
LNQ = float(np.log(128.0 ** -0.5))
DEBUG = False
PRENORM = True
HOIST_SSD = False
ILV_ML = True


def _step(g):
    try:
        next(g)
        return True
    except StopIteration:
        return False
NWU = 4
NWD = 3
STOP = 7
CUT = 99


class _Stop(Exception):
    pass
R_GG, R_GA, R_GB, R_MQ, R_MK, R_MV, R_MO, R_MI, R_MF, R_SZ, R_SDT = 0, 512, 516, 520, 776, 1032, 1288, 1544, 1548, 1552, 1808
MUL, ADD, SUB, MAX = ALU.mult, ALU.add, ALU.subtract, ALU.max


def make_consts(NS):
    c = {}
    c["ident"] = np.eye(128, dtype=np.float32)
    c["ones"] = np.ones((128, 128), np.float32)
    i = np.arange(128)[:, None]
    j = np.arange(128)[None, :]
    cm = np.zeros((128, 6, 128), np.float32)
    cm[:, 0] = (i <= j)
    cm[:, 1] = np.where(i > j, 0.0, NEG)
    cm[:, 2] = np.where(j >= i, 0.0, NEG)
    cm[:, 3] = np.where(j <= i, 0.0, NEG)
    cm[:, 4] = 1.0
    cm[:, 5] = (i == 127)
    c["cm_p"] = cm
    ls = np.zeros((128, 1), np.float32)
    ls[127, 0] = 1
    c["lastsel_p"] = ls
    n = NS * 4
    i = np.arange(n)[:, None]
    j = np.arange(n)[None, :]
    same = (i // 4) == (j // 4)
    cs = np.zeros((n, 6, n), np.float32)
    cs[:, 0] = same & (i <= j)
    cs[:, 1] = np.where(same & (i > j), 0.0, NEG)
    cs[:, 2] = np.where(same & (j >= i), 0.0, NEG)
    cs[:, 3] = np.where(same & (j <= i), 0.0, NEG)
    cs[:, 4] = same
    cs[:, 5] = same & (i % 4 == 3)
    c["cm_s"] = cs
    s = np.arange(NS)[None, :]
    k = np.arange(n)[:, None]
    c["lastsel_s"] = (k == 4 * s + 3).astype(np.float32)
    c["rowmask"] = ((k // 4) == s).astype(np.float32)
    return c


def build(NPC, NS, DEPTH):
    nc = bass.Bass("TRN2", target_bir_lowering=False)
    TP = NPC * 128
    NST = NS * 4
    assert NST <= 64

    def din(name, shape):
        return nc.dram_tensor(name, list(shape), F32, kind="ExternalInput").ap()

    def dout(name, shape):
        return nc.dram_tensor(name, list(shape), F32, kind="ExternalOutput").ap()

    xp = din("xp", [TP, D])
    xs = din("xs", [NST, D])
    st_conv = din("st_conv", [DEPTH, NS, 3, CONV_DIM])
    st_gdn = din("st_gdn", [DEPTH, NS, 4, 128, 128])
    st_mc = din("st_mc", [DEPTH, NS, 4, 64, 64])
    st_mn = din("st_mn", [DEPTH, NS, 4, 64])
    st_mm = din("st_mm", [DEPTH, NS, 4])
    st_ssd = din("st_ssd", [DEPTH, NS, 4, 64, 128])
    st_ffn = din("st_ffn", [DEPTH, NS, 2, DFF])
    nrm = [din(n, [DEPTH, D]) for n in ("norm_mix_pre", "norm_mix_post", "norm_ffn_pre", "norm_ffn_post")]
    w_in = din("w_in", [DEPTH, D, IN_COLS])
    conv_w = din("conv_w", [DEPTH, 4, CONV_DIM])
    conv_b = din("conv_b", [DEPTH, CONV_DIM])
    hp_names = ["gdn_a_log", "gdn_dt_bias", "mlstm_i_bias", "mlstm_f_bias", "ssd_a_log", "ssd_dt_bias", "ssd_d"]
    hp_d = [din(n, [DEPTH, 4]) for n in hp_names]
    gdn_norm = din("gdn_norm", [DEPTH, 128])
    mlstm_norm = din("mlstm_norm", [DEPTH, 64])
    ssd_norm = din("ssd_norm", [DEPTH, 256])
    w_out = din("w_out", [DEPTH, D, D])
    w_up = din("ffn_w_up", [DEPTH, D, 2 * DFF])
    fconv_w = din("ffn_conv_w", [DEPTH, 3, DFF])
    fconv_b = din("ffn_conv_b", [DEPTH, DFF])
    w_down = din("ffn_w_down", [DEPTH, DFF, D])
    c_ident = din("ident", [128, 128])
    c_ones = din("ones", [128, 128])
    c_cm_p = din("cm_p", [128, 6, 128])
    c_ls_p = din("lastsel_p", [128, 1])
    c_cm_s = din("cm_s", [NST, 6, NST])
    c_ls_s = din("lastsel_s", [NST, NS])
    c_rowmask = din("rowmask", [NST, NS])

    y_p = dout("y_p", [TP, D])
    y_s = dout("y_s", [NST, D])
    o_pconv = dout("p_conv", [DEPTH, 3, CONV_DIM])
    o_pgdn = dout("p_gdn", [DEPTH, 4, 128, 128])
    o_pmc = dout("p_mc", [DEPTH, 4, 64, 64])
    o_pmn = dout("p_mn", [DEPTH, 4, 64])
    o_pmm = dout("p_mm", [DEPTH, 4])
    o_pssd = dout("p_ssd", [DEPTH, 4, 64, 128])
    o_pffn = dout("p_ffn", [DEPTH, 2, DFF])
    o_sconv = dout("s_conv", [DEPTH, NS, 3, CONV_DIM])
    o_sgdn = dout("s_gdn", [DEPTH, NS, 4, 128, 128])
    o_smc = dout("s_mc", [DEPTH, NS, 4, 64, 64])
    o_smn = dout("s_mn", [DEPTH, NS, 4, 64])
    o_smm = dout("s_mm", [DEPTH, NS, 4])
    o_sssd = dout("s_ssd", [DEPTH, NS, 4, 64, 128])
    o_sffn = dout("s_ffn", [DEPTH, NS, 2, DFF])

    NCH = NPC + 1
    dbg_mix = dout("dbg_mix", [NST, D]) if DEBUG else None
    NSL = True

    with ExitStack() as es0:
        kb = KB(nc, es0)
        PS = [kb.ps("psb%d" % i, [128, 512]) for i in range(8)]
        bstate = [0]

        held = set()

        def bank(hold=False):
            while (bstate[0] % 8) in held:
                bstate[0] += 1
            i = bstate[0] % 8
            bstate[0] += 1
            if hold:
                held.add(i)
            return PS[i]

        def release(*bs):
            for b in bs:
                held.discard(PS.index(b))

        def barrier():
            for e in ("pe", "act", "dve", "pool", "sp"):
                for e2 in ("pe", "act", "dve", "pool"):
                    if e2 != e and kb.cnt[e2] > 0:
                        kb._wait_e(e, (e2, kb.cnt[e2]))
                for q in kb.dsems:
                    for i, (sem, tgt) in enumerate(kb.dsems[q]):
                        if tgt > 0:
                            kb._wait_d(e, ((q, i), sem, tgt))

        xT = [kb.sb("xT%d" % c, [128, 8, 128 if c < NPC else NST]) for c in range(NCH)]
        ident = kb.sb("ident", [128, 128])
        ones = kb.sb("ones", [128, 128])
        cmp_ = kb.sb("cm_p", [128, 6, 128])
        lsp = kb.sb("ls_p", [128, 1])
        cms = kb.sb("cm_s", [NST, 6, NST])
        lss = kb.sb("ls_s", [NST, NS])
        rowmask = kb.sb("rowmask", [NST, NS])
        NW = kb.sb("NW", [128, 4 * DEPTH * 8])
        CW = kb.sb("CW", [128, DEPTH, 18, 4])
        CB = kb.sb("CB", [128, DEPTH, 18])
        FCW = kb.sb("FCW", [128, DEPTH, NFB, 3])
        FCB = kb.sb("FCB", [128, DEPTH, NFB])
        HP = kb.sb("HP", [128, 7, DEPTH, 4])
        GN = kb.sb("GN", [128, DEPTH, 128])
        MNb = kb.sb("MNb", [128, DEPTH, 64])
        SN = kb.sb("SN", [128, DEPTH, 256])
        RS = kb.sb("RS", [128, 128])
        SQa = kb.sb("SQa", [128, 8, 128])
        SQb = kb.sb("SQb", [128, 8, 128])

        kb.dma("sp", ident, c_ident)
        kb.dma("sp", ones, c_ones)
        kb.dma("sp", cmp_, c_cm_p)
        kb.dma("sp", lsp, c_ls_p)
        kb.dma("sp", cms, c_cm_s)
        kb.dma("sp", lss, c_ls_s)
        kb.dma("sp", rowmask, c_rowmask)
        for w in range(4):
            for l in range(DEPTH):
                o = (w * DEPTH + l) * 8
                kb.dma("sp", NW[:, o:o + 8], nrm[w][l].rearrange("(k p) -> p k", p=128), allow_slow_non_contiguous=True)
        for l in range(DEPTH):
            for j in range(4):
                kb.dma("sp", CW[:, l, :, j], conv_w[l, j].rearrange("(b p) -> p b", p=128), allow_slow_non_contiguous=True)
            kb.dma("sp", CB[:, l, :], conv_b[l].rearrange("(b p) -> p b", p=128), allow_slow_non_contiguous=True)
            for j in range(3):
                kb.dma("sp", FCW[:, l, :, j], fconv_w[l, j].rearrange("(b p) -> p b", p=128), allow_slow_non_contiguous=True)
            kb.dma("sp", FCB[:, l, :], fconv_b[l].rearrange("(b p) -> p b", p=128), allow_slow_non_contiguous=True)
        for i in range(7):
            kb.dma("sp", HP[:, i], hp_d[i].partition_broadcast(128))
        kb.dma("sp", GN, gdn_norm.partition_broadcast(128))
        kb.dma("sp", MNb, mlstm_norm.partition_broadcast(128))
        kb.dma("sp", SN, ssd_norm.partition_broadcast(128))
        for i in (0, 4):
            kb.act(HP[:, i], HP[:, i], AF.Exp)
            kb.ts(HP[:, i], HP[:, i], -1.0, None, MUL)

        def nw(w, l, k):
            o = (w * DEPTH + l) * 8 + k
            return NW[:, o:o + 1]

        def ctok(c):
            return 128 if c < NPC else NST

        ones_bf = kb.sb("ones_bf", [128, 128], BF16)
        kb.cp(ones_bf, ones)

        def bfview(t3):
            v = t3[:, :, :].re("p k t -> p (k t)")
            return V(v.t, v.ap.bitcast(BF16)[:, 0:1024].rearrange("p (k t) -> p k t", k=8))

        def nwv(w, l, CT):
            o = (w * DEPTH + l) * 8
            return NW[:, o:o + 8].re("p (k o) -> p k o", o=1).bc([128, 8, CT])

        def rsv(CT):
            return RS[:, :CT].re("p (o t) -> p o t", o=1).bc([128, 8, CT])

        def rmsnorm_to(dst, c, w, l, CT):
            xc = xT[c]
            sqb = bfview(SQa)
            kb.act(sqb[:, :, :CT], xc[:, :, :CT], AF.Square)
            ps = bank()
            for k in range(8):
                kb.mm(ps[:, :CT], ones_bf, sqb[:, k, :CT], start=(k == 0), stop=(k == 7), inc=(k == 7))
            kb.act(RS[:, :CT], ps[:, :CT], AF.Ln, bias=EPS, scale=1.0 / D)
            kb.act(RS[:, :CT], RS[:, :CT], AF.Exp, scale=-0.5)
            kb.tt(SQa[:, :, :CT], xc[:, :, :CT], rsv(CT), MUL)
            kb.tt(dst[:, :, :CT], SQa[:, :, :CT], nwv(w, l, CT), MUL)

        def postnorm_add(Y, c, w, l, CT):
            sqb = bfview(SQb)
            kb.act(sqb[:, :, :CT], Y[:, :, :CT], AF.Square)
            ps = bank()
            for k in range(8):
                kb.mm(ps[:, :CT], ones_bf, sqb[:, k, :CT], start=(k == 0), stop=(k == 7), inc=(k == 7))
            kb.act(RS[:, :CT], ps[:, :CT], AF.Ln, bias=EPS, scale=1.0 / D)
            kb.act(RS[:, :CT], RS[:, :CT], AF.Exp, scale=-0.5)
            kb.tt(Y[:, :, :CT], Y[:, :, :CT], rsv(CT), MUL)
            kb.tt(Y[:, :, :CT], Y[:, :, :CT], nwv(w, l, CT), MUL)
            kb.tt(xT[c][:, :, :CT], xT[c][:, :, :CT], Y[:, :, :CT], ADD)

        with ExitStack() as es1:
            kb.es = es1
            xin = [kb.sb("xin%d" % i, [128, D]) for i in range(2)]
            for c in range(NCH):
                CT = ctok(c)
                xi = xin[c % 2]
                src = xp[c * 128:(c + 1) * 128, :] if c < NPC else xs
                kb.dma("sp", xi[:CT, :], src)
                for half in range(2):
                    ps = bank()
                    for j in range(4):
                        k = half * 4 + j
                        kb.tr(ps[:, j * CT:(j + 1) * CT], xi[:CT, k * 128:(k + 1) * 128], ident[:CT, :CT], inc=(j == 3))
                    kb.cp(xT[c][:, half * 4:half * 4 + 4, :CT], ps[:, :4 * CT].re("p (k t) -> p k t", k=4), eng=("dve" if half == 0 else "act"))
            barrier()

        for l in range(DEPTH):
            with ExitStack() as es2:
                kb.es = es2
                WinR = kb.sb("WinR", [128, 8, NREST], BF16)
                Wout = kb.sb("Wout", [128, 8, D], BF16)
                for k in range(8):
                    kb.dma("pool", WinR[:, k, :], w_in[l, k * 128:(k + 1) * 128, CONV_DIM:IN_COLS])
                for k in range(8):
                    kb.dma("pool", Wout[:, k, :], w_out[l, k * 128:(k + 1) * 128, :])
                HN = [kb.sb("hnT0", [128, 8, 128], BF16), None]
                normed = {}
                R = kb.sb("R", [128, NREST])
                STGc = [kb.sb("STGc%d" % i, [64, 256]) for i in range(2)]
                stgi = [0]
                QKm = kb.sb("QKm", [64, 8, 128])
                CO = kb.sb("CO", [128, 12, 128])
                CO_main = CO
                TMP = {n: kb.sb("tmp" + n, [128, 512]) for n in ("E", "F", "G2", "H", "I", "X1")}
                SQa2 = SQa[:, :, :].re("p k t -> p (k t)")
                SQb2 = SQb[:, :, :].re("p k t -> p (k t)")
                TMP["A"] = SQa2[:, 0:512]
                TMP["B"] = SQa2[:, 512:1024]
                TMP["C"] = SQb2[:, 0:512]
                TMP["Dt"] = SQb2[:, 512:1024]
                SM = kb.sb("SM", [128, 160])
                EG = kb.sb("EG", [128, 4, 128])
                MIX = kb.sb("MIX", [128, D])
                mixT = kb.sb("mixT", [128, 8, 128], BF16)
                VE = kb.sb("VE", [128, 4, 65])
                KW = kb.sb("KW", [128, 4, 64])
                XT_ = kb.sb("XTs", [128, 256])
                kb.memset(VE[:, :, 64:65], 1.0)
                Z = SM[:, 0:12]
                SP = SM[:, 12:24]
                CS = SM[:, 24:36]
                GC = SM[:, 36:48]
                BETA = SM[:, 48:52]
                NBETA = SM[:, 52:56]
                LI = SM[:, 56:60]
                BE = SM[:, 60:64]
                KD = SM[:, 64:68]
                GL = SM[:, 68:80]
                CC = SM[:, 80:84]
                MX = SM[:, 84:88]
                INT = SM[:, 88:92]
                MT = SM[:, 92:96]
                NMT = SM[:, 96:100]
                WI = SM[:, 100:104]
                DEN = SM[:, 104:108]
                EM = SM[:, 108:112]
                WK = SM[:, 112:116]
                SS4 = SM[:, 116:120]
                RSTD4 = SM[:, 120:124]
                KDS = SM[:, 124:128]
                M0C = SM[:, 128:132]
                SS1 = SM[:, 132:133]

                def v4(t, CT, w=None):
                    w = CT if w is None else w
                    return t[:CT, 0:4 * w].re("p (h j) -> p h j", h=4)

                def bc4(v, CT, w):
                    return v[:CT].re("p (h o) -> p h o", o=1).bc([CT, 4, w])

                def mixer_chunk(c, P, nxt=None):
                    hnT = HN[0] if P["sample"] else HN[c % 2]
                    CT, nseq, sample = P["CT"], P["nseq"], P["sample"]
                    cm = P["cm"]
                    U, Ls, Um, Lm, SSm, LSm = (cm[:CT, i, :CT] for i in range(6))
                    lastsel = P["lastsel"]

                    def mbc(m):
                        return m.re("p (o j) -> p o j", o=1).bc([CT, 4, CT])

                    if normed.get("c") != c:
                        rmsnorm_to(hnT, c, 0, l, CT)

                    if CUT <= 1:
                        return
                    need_cs = sample or c == NPC - 1
                    rows = slice(0, CT) if sample else slice(CT - 3, CT)
                    nr = CT if sample else 3

                    def fm_pair(col0, conv=True):
                        ws = P["WS"][P["wsi"][0] % len(P["WS"])]
                        P["wsi"][0] += 1
                        kb.dma("pool", ws, w_in[l, :, col0:col0 + 256].rearrange("(k p) c -> p k c", p=128))
                        if conv and need_cs:
                            psr = bank()
                            for k in range(8):
                                kb.mm(psr[:nr, 0:256], hnT[:, k, rows], ws[:, k, :], start=(k == 0), stop=(k == 7), inc=(k == 7))
                            stg = STGc[stgi[0] % 2]
                            stgi[0] += 1
                            kb.cp(stg[:nr, :], psr[:nr, 0:256], eng="act")
                            if sample:
                                for r in range(3):
                                    kb.dma("sp", o_sconv[l, :, r, col0:col0 + 256], stg[1 + r:CT:4, :], out_dram=True)
                            else:
                                kb.dma("sp", o_pconv[l, :, col0:col0 + 256], stg[:3, :], out_dram=True)
                        return ws

                    for pc in range(4):
                        ps = bank()
                        c0 = pc * 453
                        for k in range(8):
                            kb.mm(ps[:CT, :453], hnT[:, k, :CT], WinR[:, k, c0:c0 + 453], start=(k == 0), stop=(k == 7), inc=(k == 7))
                        kb.cp(R[:CT, pc * 453:(pc + 1) * 453], ps[:CT, :453], eng=("act" if pc % 2 else "dve"))
                    for pr in range(2):
                        ws = fm_pair(CONV_DIM + R_MQ + pr * 256, conv=False)
                        ps = bank()
                        for hh in range(4):
                            for k in range(8):
                                kb.mm(ps[:64, hh * CT:(hh + 1) * CT], ws[:, k, hh * 64:(hh + 1) * 64], hnT[:, k, :CT], start=(k == 0), stop=(k == 7), inc=(k == 7 and hh == 3))
                        if pr == 0:
                            kb.cp(QKm[:, 0:4, :CT], ps[:64, 0:4 * CT].re("p (b t) -> p b t", b=4))
                        else:
                            kb.ts(QKm[:, 4:8, :CT], ps[:64, 0:4 * CT].re("p (b t) -> p b t", b=4), 0.125, None, MUL)

                    if CUT <= 2:
                        return
                    def hp(i):
                        return HP[:CT, i, l, :]
                    kb.act(R[:CT, R_GG:R_GG + 512], R[:CT, R_GG:R_GG + 512], AF.Silu)
                    kb.act(R[:CT, R_SZ:R_SZ + 256], R[:CT, R_SZ:R_SZ + 256], AF.Silu)
                    kb.act(R[:CT, R_MO:R_MO + 256], R[:CT, R_MO:R_MO + 256], AF.Sigmoid)
                    kb.act(BETA[:CT], R[:CT, R_GB:R_GB + 4], AF.Sigmoid)
                    kb.tt(Z[:CT, 0:4], R[:CT, R_GA:R_GA + 4], hp(1), ADD)
                    kb.tt(Z[:CT, 4:8], R[:CT, R_MF:R_MF + 4], hp(3), ADD)
                    kb.ts(Z[:CT, 4:8], Z[:CT, 4:8], -1.0, None, MUL)
                    kb.tt(Z[:CT, 8:12], R[:CT, R_SDT:R_SDT + 4], hp(5), ADD)
                    kb.act(SP[:CT], Z[:CT], AF.Exp)
                    kb.act(SP[:CT], SP[:CT], AF.Ln, bias=1.0)
                    kb.tt(CS[:CT, 0:4], SP[:CT, 0:4], hp(0), MUL)
                    kb.ts(CS[:CT, 4:8], SP[:CT, 4:8], -1.0, None, MUL)
                    kb.tt(CS[:CT, 8:12], SP[:CT, 8:12], hp(4), MUL)
                    kb.ts(NBETA[:CT], BETA[:CT], -1.0, None, MUL)
                    kb.tt(LI[:CT], R[:CT, R_MI:R_MI + 4], hp(2), ADD)
                    def gates_mm():
                        ps = bank()
                        kb.mm(ps[:CT, 0:12], U, CS[:CT, 0:12], inc=False)
                        kb.mm(ps[:CT, 12:24], SSm, CS[:CT, 0:12])
                        kb.cp(GC[:CT], ps[:CT, 0:12])
                        kb.cp(GL[:CT], ps[:CT, 12:24])

                    yield "front"

                    def rowbc(colv, mat):
                        t = TMP["X1"]
                        kb.tt(v4(t, CT), mbc(mat), bc4(colv, CT, CT), MUL)
                        ps = bank()
                        kb.mm(ps[:, :4 * CT], ones[:CT, :], t[:CT, :4 * CT])
                        return ps

                    A, B, C, Dt, E, F, G2, H, I, X1 = (TMP[n] for n in ("A", "B", "C", "Dt", "E", "F", "G2", "H", "I", "X1"))

                    def mlstm_section(mA, mB, mI, mF):
                        psR = rowbc(CS[:, 4:8], U)
                        BL = P["BL"]
                        if sample:
                            kb.cp(BL, psR[:, :4 * CT].re("p (h s t) -> p h s t", h=4, t=4)[:, :, :, 3])
                        else:
                            kb.cp(BL, psR[:, :4 * CT].re("p (h j) -> p h j", h=4)[:, :, CT - 1:CT])
                        kb.tt(CC[:CT], LI[:CT], GC[:CT, 4:8], SUB)
                        yield
                        psC = rowbc(CC, ident[:CT, :CT])
                        Dm = v4(mA, CT)
                        kb.tt(Dm, psC[:CT, :4 * CT].re("p (h j) -> p h j", h=4), bc4(GC[:, 4:8], CT, CT), ADD)
                        kb.tt(Dm, Dm, mbc(Lm), ADD)
                        kb.red(MX[:CT], Dm, MAX)
                        yield
                        if sample:
                            ps = bank()
                            kb.mm(ps[:CT, 0:4], P["RMT"], P["m0"])
                            kb.cp(M0C[:CT], ps[:CT, 0:4])
                            kb.tt(INT[:CT], GC[:CT, 4:8], M0C[:CT], ADD)
                        else:
                            kb.tt(INT[:CT], GC[:CT, 4:8], P["MB"][:CT, :], ADD)
                        kb.tt(MT[:CT], INT[:CT], MX[:CT], MAX)
                        kb.ts(NMT[:CT], MT[:CT], -1.0, None, MUL)
                        yield
                        for h in range(4):
                            kb.act(mA[:CT, h * CT:(h + 1) * CT], mA[:CT, h * CT:(h + 1) * CT], AF.Exp, bias=NMT[:CT, h:h + 1])
                        kb.tt(WI[:CT], INT[:CT], MT[:CT], SUB)
                        kb.act(WI[:CT], WI[:CT], AF.Exp)
                        yield
                        psS_ = bank()
                        for h in range(4):
                            kb.mm(psS_[:CT, h * CT:(h + 1) * CT], QKm[:, h, :CT], QKm[:, 4 + h, :CT], inc=(h == 3))
                        kb.tt(Dm, Dm, psS_[:CT, :4 * CT].re("p (h j) -> p h j", h=4), MUL)
                        yield
                        ps = bank()
                        for h in range(4):
                            kb.tr(ps[:CT, h * CT:(h + 1) * CT], mA[:CT, h * CT:(h + 1) * CT], ident[:CT, :CT], inc=(h == 3))
                        kb.cp(mB[:CT, :4 * CT], ps[:CT, :4 * CT], eng="act")
                        yield
                        kb.cp(VE[:CT, :, 0:64], R[:CT, R_MV:R_MV + 256].re("p (h d) -> p h d", h=4))
                        psB = bank(hold=True)
                        for h in range(4):
                            kb.mm(psB[:CT, h * 65:(h + 1) * 65], mB[:CT, h * CT:(h + 1) * CT], VE[:CT, h, :], inc=(h == 3))
                        Xs = X1[:CT, 0:4 * nseq].re("p (h s) -> p h s", h=4)
                        kb.tt(Xs, MT[:CT].re("p (h o) -> p h o", o=1).bc([CT, 4, nseq]), lastsel[:CT, :].re("p (o s) -> p o s", o=1).bc([CT, 4, nseq]), MUL)
                        psM = bank()
                        kb.mm(psM[:, 0:4 * nseq], ones[:CT, :], X1[:CT, 0:4 * nseq])
                        MN_ = P["MN"]
                        WC = P["WC"]
                        kb.cp(MN_, psM[:, 0:4 * nseq].re("p (h s) -> p h s", h=4))
                        yield
                        kb.tt(WC, BL, P["M0R"], ADD)
                        kb.tt(WC, WC, MN_, SUB)
                        kb.act(WC, WC, AF.Exp)
                        yield
                        psBM = bank()
                        kb.mm(psBM[:CT, 0:4], LSm, NMT[:CT, 0:4])
                        kb.tt(WK[:CT], psBM[:CT, 0:4], GL[:CT, 4:8], ADD)
                        kb.tt(WK[:CT], WK[:CT], CC[:CT], ADD)
                        kb.act(WK[:CT], WK[:CT], AF.Exp)
                        kb.ts(WK[:CT], WK[:CT], 0.125, None, MUL)
                        kb.tt(KW[:CT], R[:CT, R_MK:R_MK + 256].re("p (h d) -> p h d", h=4), bc4(WK, CT, 64), MUL)
                        yield
                        psA = bank(hold=True)
                        if not sample:
                            CME = P["CME"]
                            for h in range(4):
                                kb.mm(psA[:CT, h * 65:(h + 1) * 65], QKm[:, h, :CT], CME[:, h, :], inc=(h == 3))
                            psN = bank()
                            for h in range(4):
                                kb.mm(psN[:64, h * 65:(h + 1) * 65], KW[:CT, h, :], VE[:CT, h, :], inc=(h == 3))
                            kb.tt(CME[:, :, :], CME[:, :, :], WC[:64, :, 0:1].bc([64, 4, 65]), MUL)
                            kb.tt(CME[:, :, :], CME[:, :, :], psN[:64, 0:260].re("p (h d) -> p h d", h=4), ADD)
                        for h in range(4):
                            if not sample:
                                pass
                            else:
                                CMs = P["CMs"]
                                Zt = P["Zt"]
                                ZK = P["ZK"]
                                kb.dma("sp", CMs[:, :, 0:64], st_mc[l, :, h].rearrange("s k v -> k s v"))
                                kb.dma("sp", CMs[:, :, 64:65], st_mn[l, :, h, :].rearrange("s (k o) -> k s o", o=1), allow_slow_non_contiguous=True)
                                zd = Zt[:64, 0:nseq * (CT + 4)].re("p (s r) -> p s r", r=CT + 4)[:, :, 0:4]
                                kb.cp(zd, QKm[:, h, :CT].re("p (s t) -> p s t", t=4))
                                for s_ in range(nseq):
                                    kb.mm(psA[:CT, h * 65:(h + 1) * 65], Zt[:64, s_ * CT:(s_ + 1) * CT], CMs[:, s_, :],
                                          start=(s_ == 0), stop=(s_ == nseq - 1), inc=(s_ == nseq - 1))
                                for g in range(nseq // 4):
                                    kb.tt(ZK[:CT, :, 0:64], KW[:CT, h, :].re("p (o d) -> p o d", o=1).bc([CT, 4, 64]),
                                          rowmask[:CT, 4 * g:4 * g + 4].re("p (s o) -> p s o", o=1).bc([CT, 4, 64]), MUL)
                                    psN = bank()
                                    for s4 in range(4):
                                        kb.mm(psN[:64, s4 * 65:(s4 + 1) * 65], ZK[:CT, s4, 0:64], VE[:CT, h, :], inc=(s4 == 3))
                                    wcb = WC[:64, h, 4 * g:4 * g + 4].re("p (s o) -> p s o", o=1).bc([64, 4, 65])
                                    kb.tt(CMs[:, 4 * g:4 * g + 4, :], CMs[:, 4 * g:4 * g + 4, :], wcb, MUL)
                                    kb.tt(CMs[:, 4 * g:4 * g + 4, :], CMs[:, 4 * g:4 * g + 4, :],
                                          psN[:64, 0:260].re("p (s d) -> p s d", s=4), ADD)
                                kb.dma("sp", o_smc[l, :, h].rearrange("s k v -> k s v"), CMs[:, :, 0:64], out_dram=True)
                                kb.dma("sp", o_smn[l, :, h, :].rearrange("s (k o) -> k s o", o=1), CMs[:, :, 64:65], out_dram=True, allow_slow_non_contiguous=True)
                        yield
                        if sample:
                            kb.dma("sp", o_smm[l].rearrange("(o s) h -> o h s", o=1), MN_[0:1, :, :], out_dram=True, allow_slow_non_contiguous=True)
                        else:
                            kb.cp(P["MB"], MN_[:, :, 0])
                        TOT = v4(mI, CT, 65)
                        kb.tt(TOT, psA[:CT, 0:260].re("p (h d) -> p h d", h=4), bc4(WI, CT, 65), MUL)
                        kb.tt(TOT, TOT, psB[:CT, 0:260].re("p (h d) -> p h d", h=4), ADD)
                        release(psA, psB)
                        yield
                        kb.act(DEN[:CT], mI[:CT, 64:260:65], AF.Abs)
                        kb.act(EM[:CT], NMT[:CT], AF.Exp)
                        kb.tt(DEN[:CT], DEN[:CT], EM[:CT], MAX)
                        kb.recip(DEN[:CT], DEN[:CT])
                        yield
                        HH = v4(X1, CT, 64)
                        kb.tt(HH, TOT[:, :, 0:64], bc4(DEN, CT, 64), MUL)
                        kb.tt(X1[:CT, 0:256], X1[:CT, 0:256], R[:CT, R_MO:R_MO + 256], MUL)
                        kb.act(v4(mF, CT, 64), HH, AF.Square)
                        kb.red(SS4[:CT], v4(mF, CT, 64), ADD)
                        yield
                        kb.act(RSTD4[:CT], SS4[:CT], AF.Ln, bias=EPS, scale=1.0 / 64)
                        kb.act(RSTD4[:CT], RSTD4[:CT], AF.Exp, scale=-0.5)
                        kb.tt(HH, HH, bc4(RSTD4, CT, 64), MUL)
                        kb.tt(MIX[:CT, 512:768].re("p (h d) -> p h d", h=4), HH, MNb[:CT, l, :].re("p (o d) -> p o d", o=1).bc([CT, 4, 64]), MUL)


                    hoist = (not sample) and HOIST_SSD
                    XPs = P["XP2"] if hoist else P["XP"]
                    COs = P["CO2"] if hoist else CO

                    def ssd_front():
                        for g in range(2):
                            nb = 4 if g == 0 else 2
                            wss = [fm_pair(1536 + g * 512 + q * 256) for q in range(nb // 2)]
                            ps = bank()
                            for j in range(nb):
                                ws = wss[j // 2]
                                for k in range(8):
                                    kb.mm(ps[:, j * CT:(j + 1) * CT], ws[:, k, (j % 2) * 128:(j % 2 + 1) * 128], hnT[:, k, :CT], start=(k == 0), stop=(k == 7), inc=(k == 7 and j == nb - 1))
                            if sample:
                                kb.cp(XPs[:, g * 4:g * 4 + nb, :, 3:7], ps[:, :nb * CT].re("p (b s t) -> p b s t", b=nb, t=4), eng="act")
                            else:
                                kb.cp(XPs[:, g * 4:g * 4 + nb, 3:3 + CT], ps[:, :nb * CT].re("p (b t) -> p b t", b=nb), eng="act")
                        conv_blocks(P, XPs, 12, 6, CT, COs)

                    XP = P["XP"]
                    for g in range(3):
                        wss = [fm_pair(g * 512), fm_pair(g * 512 + 256)]
                        ps = bank()
                        for j in range(4):
                            ws = wss[j // 2]
                            for k in range(8):
                                kb.mm(ps[:, j * CT:(j + 1) * CT], ws[:, k, (j % 2) * 128:(j % 2 + 1) * 128], hnT[:, k, :CT], start=(k == 0), stop=(k == 7), inc=(k == 7 and j == 3))
                        if sample:
                            kb.cp(XP[:, g * 4:g * 4 + 4, :, 3:7], ps[:, :4 * CT].re("p (b s t) -> p b s t", b=4, t=4), eng="act")
                        else:
                            kb.cp(XP[:, g * 4:g * 4 + 4, 3:3 + CT], ps[:, :4 * CT].re("p (b t) -> p b t", b=4), eng="act")
                    gates_mm()
                    conv_blocks(P, XP, 0, 12, CT)
                    if hoist:
                        ssd_front()
                    if CUT <= 4:
                        return
                    sqb_ = bfview(SQa)
                    kb.act(sqb_[:, :, :CT], CO[:, 0:8, :CT], AF.Square)
                    for half in range(2):
                        ps = bank()
                        for j in range(4):
                            kb.mm(ps[:, j * CT:(j + 1) * CT], ones_bf, sqb_[:, half * 4 + j, :CT], inc=(j == 3))
                        t = v4(X1, 128, CT)
                        kb.act(t, ps[:, :4 * CT].re("p (b t) -> p b t", b=4), AF.Ln, bias=EPS)
                        kb.act(t, t, AF.Exp, scale=-0.5)
                        if half == 0:
                            kb.ts(t, t, 128 ** -0.5, None, MUL)
                        kb.tt(CO[:, half * 4:half * 4 + 4, :CT], CO[:, half * 4:half * 4 + 4, :CT], t, MUL)
                    if CUT <= 5:
                        return
                    psR = rowbc(CS[:, 0:4], U)
                    kb.act(EG[:, :, :CT], psR[:, :4 * CT].re("p (h j) -> p h j", h=4), AF.Exp)
                    kb.tt(v4(X1, CT), bc4(GC[:, 0:4], CT, CT), psR[:CT, :4 * CT].re("p (h j) -> p h j", h=4), SUB)
                    kb.tt(v4(A, CT), v4(X1, CT), mbc(Ls), ADD)
                    kb.act(v4(A, CT), v4(A, CT), AF.Exp)
                    kb.tt(v4(Dt, CT), mbc(Um), v4(X1, CT), SUB)
                    kb.act(v4(Dt, CT), v4(Dt, CT), AF.Exp)
                    psK = bank()
                    for h in range(4):
                        kb.mm(psK[:CT, h * CT:(h + 1) * CT], CO[:, 4 + h, :CT], CO[:, 4 + h, :CT], inc=(h == 3))
                    kb.tt(v4(A, CT), v4(A, CT), psK[:CT, :4 * CT].re("p (h j) -> p h j", h=4), MUL)
                    kb.tt(v4(A, CT), v4(A, CT), bc4(NBETA, CT, CT), MUL)
                    psQ = bank()
                    for h in range(4):
                        kb.mm(psQ[:CT, h * CT:(h + 1) * CT], CO[:, 4 + h, :CT], CO[:, h, :CT], inc=(h == 3))
                    kb.tt(v4(Dt, CT), v4(Dt, CT), psQ[:CT, :4 * CT].re("p (h j) -> p h j", h=4), MUL)
                    if CUT <= 6:
                        return
                    ps = bank()
                    for h in range(4):
                        kb.tr(ps[:CT, h * CT:(h + 1) * CT], A[:CT, h * CT:(h + 1) * CT], ident[:CT, :CT], inc=(h == 3))
                    kb.cp(B[:CT, :4 * CT], ps[:CT, :4 * CT], eng="act")
                    kb.tt(v4(C, CT), v4(B, CT), mbc(ident[:CT, :CT]), ADD)
                    mg = mlstm_section(E, F, G2, H) if (ILV_ML and not sample) else None
                    for lev in range(1, P["lev"]):
                        last = (lev == P["lev"] - 1)
                        ps1 = bank()
                        for h in range(4):
                            hs = slice(h * CT, (h + 1) * CT)
                            kb.mm(ps1[:CT, hs], B[:CT, hs], A[:CT, hs], inc=(h == 3))
                        if not last:
                            ps2 = bank()
                            for h in range(4):
                                hs = slice(h * CT, (h + 1) * CT)
                                kb.mm(ps2[:CT, hs], A[:CT, hs], B[:CT, hs], inc=(h == 3))
                        if lev >= 2:
                            ps3 = bank()
                            for h in range(4):
                                hs = slice(h * CT, (h + 1) * CT)
                                kb.mm(ps3[:CT, hs], A[:CT, hs], C[:CT, hs], inc=(h == 3))
                        kb.cp(A[:CT, :4 * CT], ps1[:CT, :4 * CT])
                        if not last:
                            kb.cp(B[:CT, :4 * CT], ps2[:CT, :4 * CT], eng="act")
                        if lev >= 2:
                            kb.tt(C[:CT, :4 * CT], C[:CT, :4 * CT], ps3[:CT, :4 * CT], ADD)
                        if mg is not None:
                            _step(mg)
                            _step(mg)
                    ps3 = bank()
                    for h in range(4):
                        hs = slice(h * CT, (h + 1) * CT)
                        kb.mm(ps3[:CT, hs], A[:CT, hs], C[:CT, hs], inc=(h == 3))
                    kb.tt(C[:CT, :4 * CT], C[:CT, :4 * CT], ps3[:CT, :4 * CT], ADD)
                    if mg is not None:
                        while _step(mg):
                            pass
                    if CUT <= 7:
                        return
                    kb.act(BE[:CT], GC[:CT, 0:4], AF.Exp)
                    kb.tt(BE[:CT], BE[:CT], BETA[:CT], MUL)
                    kb.tt(KD[:CT], GL[:CT, 0:4], GC[:CT, 0:4], SUB)
                    kb.act(KD[:CT], KD[:CT], AF.Exp)
                    psKt = bank()
                    for h in range(4):
                        kb.tr(psKt[:CT, h * 128:(h + 1) * 128], CO[:, 4 + h, :CT], ident, inc=(h == 3))
                    pk = psKt[:CT, :].re("p (h d) -> p h d", h=4)
                    kb.tt(v4(F, CT, 128), pk, bc4(BE, CT, 128), MUL)
                    kb.tt(v4(G2, CT, 128), pk, bc4(KD, CT, 128), MUL)
                    psVt = bank()
                    for h in range(4):
                        kb.tr(psVt[:CT, h * 128:(h + 1) * 128], CO[:, 8 + h, :CT], ident, inc=(h == 3))
                    kb.tt(v4(E, CT, 128), psVt[:CT, :].re("p (h d) -> p h d", h=4), bc4(BETA, CT, 128), MUL)
                    psW = bank()
                    for h in range(4):
                        kb.mm(psW[:, h * CT:(h + 1) * CT], F[:CT, h * 128:(h + 1) * 128], C[:CT, h * CT:(h + 1) * CT], inc=(h == 3))
                    kb.ts(H[:, :4 * CT], psW[:, :4 * CT], -1.0, None, MUL)
                    kb.tt(v4(I, 128, CT), CO[:, 0:4, :CT], EG[:, :, :CT], MUL)
                    if CUT <= 8:
                        return
                    psV = bank(hold=True)
                    psO = bank(hold=True)
                    if not sample:
                        S = P["Sg"]
                        for h in range(4):
                            hs = slice(h * CT, (h + 1) * CT)
                            hd = slice(h * 128, (h + 1) * 128)
                            kb.mm(psV[:CT, hd], C[:CT, hs], E[:CT, hd], start=True, stop=False, inc=False)
                            kb.mm(psV[:CT, hd], H[:, hs], S[:, h, :], start=False, stop=True, inc=(h == 3))
                        kb.cp(E[:CT, :], psV[:CT, :])
                        for h in range(4):
                            hs = slice(h * CT, (h + 1) * CT)
                            hd = slice(h * 128, (h + 1) * 128)
                            kb.mm(psO[:CT, hd], I[:, hs], S[:, h, :], start=True, stop=False, inc=False)
                            kb.mm(psO[:CT, hd], Dt[:CT, hs], E[:CT, hd], start=False, stop=True, inc=(h == 3))
                        psS = bank()
                        for h in range(4):
                            hd = slice(h * 128, (h + 1) * 128)
                            kb.mm(psS[:, hd], G2[:CT, hd], E[:CT, hd], inc=(h == 3))
                        kb.tt(S[:, :, :], S[:, :, :], EG[:, :, CT - 1:CT].bc([128, 4, 128]), MUL)
                        kb.tt(S[:, :, :], S[:, :, :], psS[:, :].re("p (h d) -> p h d", h=4), ADD)
                    for h in range(4):
                        hs = slice(h * CT, (h + 1) * CT)
                        hd = slice(h * 128, (h + 1) * 128)
                        if not sample:
                            pass
                        else:
                            Ss = P["Ss"]
                            Zt = P["Zt"]
                            ZK = P["ZK"]
                            kb.dma("sp", Ss, st_gdn[l, :, h].rearrange("s k v -> k s v"))
                            zd = Zt[:, 0:nseq * (CT + 4)].re("p (s r) -> p s r", r=CT + 4)[:, :, 0:4]
                            kb.cp(zd, H[:, hs].re("p (s t) -> p s t", t=4))
                            kb.mm(psV[:CT, hd], C[:CT, hs], E[:CT, hd], start=True, stop=False, inc=False)
                            for s in range(nseq):
                                kb.mm(psV[:CT, hd], Zt[:, s * CT:(s + 1) * CT], Ss[:, s, :], start=False, stop=(s == nseq - 1), inc=(s == nseq - 1))
                            kb.cp(E[:CT, hd], psV[:CT, hd])
                            kb.cp(zd, I[:, hs].re("p (s t) -> p s t", t=4))
                            for s in range(nseq):
                                kb.mm(psO[:CT, hd], Zt[:, s * CT:(s + 1) * CT], Ss[:, s, :], start=(s == 0), stop=False, inc=False)
                            kb.mm(psO[:CT, hd], Dt[:CT, hs], E[:CT, hd], start=False, stop=True)
                            for g in range(nseq // 4):
                                kb.tt(ZK[:CT], G2[:CT, hd].re("p (o d) -> p o d", o=1).bc([CT, 4, 128]),
                                      rowmask[:CT, 4 * g:4 * g + 4].re("p (s o) -> p s o", o=1).bc([CT, 4, 128]), MUL)
                                psS = bank()
                                for s4 in range(4):
                                    kb.mm(psS[:, s4 * 128:(s4 + 1) * 128], ZK[:CT, s4, :], E[:CT, hd], inc=(s4 == 3))
                                egl = EG[:, h, 16 * g + 3:16 * g + 16:4].re("p (s o) -> p s o", o=1).bc([128, 4, 128])
                                kb.tt(Ss[:, 4 * g:4 * g + 4, :], Ss[:, 4 * g:4 * g + 4, :], egl, MUL)
                                kb.tt(Ss[:, 4 * g:4 * g + 4, :], Ss[:, 4 * g:4 * g + 4, :], psS[:, :].re("p (s d) -> p s d", s=4), ADD)
                            kb.dma("sp", o_sgdn[l, :, h].rearrange("s k v -> k s v"), Ss, out_dram=True)
                    po = psO[:CT, :].re("p (h d) -> p h d", h=4)
                    kb.act(v4(X1, CT, 128), po, AF.Square)
                    kb.red(SS4[:CT], v4(X1, CT, 128), ADD)
                    kb.act(RSTD4[:CT], SS4[:CT], AF.Ln, bias=EPS, scale=1.0 / 128)
                    kb.act(RSTD4[:CT], RSTD4[:CT], AF.Exp, scale=-0.5)
                    kb.tt(v4(X1, CT, 128), po, bc4(RSTD4, CT, 128), MUL)
                    kb.tt(v4(X1, CT, 128), v4(X1, CT, 128), GN[:CT, l, :].re("p (o d) -> p o d", o=1).bc([CT, 4, 128]), MUL)
                    kb.tt(MIX[:CT, 0:512], X1[:CT, :], R[:CT, R_GG:R_GG + 512], MUL)
                    release(psV, psO)

                    if CUT <= 9:
                        return
                    if not ILV_ML or sample:
                        for _ in mlstm_section(A, B, I, F):
                            pass

                    if nxt is not None and PRENORM:
                        rmsnorm_to(HN[nxt % 2], nxt, 0, l, ctok(nxt))
                        normed["c"] = nxt
                    if not hoist:
                        ssd_front()
                    psR = rowbc(CS[:, 8:12], U)
                    kb.act(EG[:, :, :CT], psR[:, :4 * CT].re("p (h j) -> p h j", h=4), AF.Exp)
                    kb.tt(v4(X1, CT), bc4(GC[:, 8:12], CT, CT), psR[:CT, :4 * CT].re("p (h j) -> p h j", h=4), SUB)
                    kb.tt(v4(Dt, CT), mbc(Um), v4(X1, CT), SUB)
                    kb.act(v4(Dt, CT), v4(Dt, CT), AF.Exp)
                    psC = bank()
                    for g in range(2):
                        kb.mm(psC[:CT, g * CT:(g + 1) * CT], COs[:, 2 + g, :CT], COs[:, 4 + g, :CT], inc=(g == 1))
                    Dt4 = Dt[:CT, :4 * CT].re("p (g e j) -> p g e j", g=2, e=2)
                    kb.tt(Dt4, Dt4, psC[:CT, :2 * CT].re("p (g o j) -> p g o j", g=2, o=1).bc([CT, 2, 2, CT]), MUL)
                    kb.tt(v4(Dt, CT), v4(Dt, CT), bc4(SP[:, 8:12], CT, CT), MUL)
                    psX = bank()
                    for j in range(2):
                        kb.tr(psX[:CT, j * 128:(j + 1) * 128], COs[:, j, :CT], ident, inc=False)
                    for j in range(2):
                        kb.tr(psX[:CT, 256 + j * 128:256 + (j + 1) * 128], COs[:, 2 + j, :CT], ident, inc=(j == 1))
                    kb.cp(XT_[:CT, :], psX[:CT, 0:256])
                    kb.tt(KDS[:CT], GL[:CT, 8:12], GC[:CT, 8:12], SUB)
                    kb.act(KDS[:CT], KDS[:CT], AF.Exp)
                    kb.tt(KDS[:CT], KDS[:CT], SP[:CT, 8:12], MUL)
                    BW = F
                    kb.tt(BW[:CT, :].re("p (g e n) -> p g e n", g=2, e=2), psX[:CT, 256:512].re("p (g o n) -> p g o n", g=2, o=1).bc([CT, 2, 2, 128]),
                          KDS[:CT].re("p (g e o) -> p g e o", g=2, o=1).bc([CT, 2, 2, 128]), MUL)
                    CG = I
                    kb.tt(CG[:, :4 * CT].re("p (g e j) -> p g e j", g=2, e=2), COs[:, 4:6, :CT].re("p g (o j) -> p g o j", o=1).bc([128, 2, 2, CT]),
                          EG[:, :, :CT].re("p (g e) j -> p g e j", g=2), MUL)
                    psY = bank(hold=True)
                    if not sample:
                        HT = P["HT"]
                        for h in range(4):
                            hs = slice(h * CT, (h + 1) * CT)
                            hp_ = slice(h * 64, (h + 1) * 64)
                            kb.mm(psY[:CT, hp_], Dt[:CT, hs], XT_[:CT, hp_], start=True, stop=False, inc=False)
                            kb.mm(psY[:CT, hp_], CG[:, hs], HT[:, h, :], start=False, stop=True, inc=(h == 3))
                        psH = bank()
                        for h in range(4):
                            hp_ = slice(h * 64, (h + 1) * 64)
                            hn_ = slice(h * 128, (h + 1) * 128)
                            kb.mm(psH[:, hp_], BW[:CT, hn_], XT_[:CT, hp_], inc=(h == 3))
                        kb.tt(HT[:, :, :], HT[:, :, :], EG[:, :, CT - 1:CT].bc([128, 4, 64]), MUL)
                        kb.tt(HT[:, :, :], HT[:, :, :], psH[:, 0:256].re("p (h d) -> p h d", h=4), ADD)
                    for h in range(4):
                        hs = slice(h * CT, (h + 1) * CT)
                        hp_ = slice(h * 64, (h + 1) * 64)
                        hn_ = slice(h * 128, (h + 1) * 128)
                        if not sample:
                            pass
                        else:
                            HTs = P["HTs"]
                            STG = P["STG"]
                            Zt = P["Zt"]
                            ZK = P["ZK"]
                            for g in range(nseq // 4):
                                kb.dma("sp", STG, st_ssd[l, 4 * g:4 * g + 4, h].rearrange("s p n -> p s n"))
                                ps = bank()
                                for s4 in range(4):
                                    kb.tr(ps[:, s4 * 64:(s4 + 1) * 64], STG[:, s4, :], ident[:64, :64], inc=(s4 == 3))
                                kb.cp(HTs[:, 4 * g:4 * g + 4, :], ps[:, 0:256].re("p (s d) -> p s d", s=4))
                            zd = Zt[:, 0:nseq * (CT + 4)].re("p (s r) -> p s r", r=CT + 4)[:, :, 0:4]
                            kb.cp(zd, CG[:, hs].re("p (s t) -> p s t", t=4))
                            kb.mm(psY[:CT, hp_], Dt[:CT, hs], XT_[:CT, hp_], start=True, stop=False, inc=False)
                            for s in range(nseq):
                                kb.mm(psY[:CT, hp_], Zt[:, s * CT:(s + 1) * CT], HTs[:, s, :], start=False, stop=(s == nseq - 1), inc=(s == nseq - 1))
                            for g in range(nseq // 4):
                                kb.tt(ZK[:CT], BW[:CT, hn_].re("p (o d) -> p o d", o=1).bc([CT, 4, 128]),
                                      rowmask[:CT, 4 * g:4 * g + 4].re("p (s o) -> p s o", o=1).bc([CT, 4, 128]), MUL)
                                psH = bank()
                                for s4 in range(4):
                                    kb.mm(psH[:, s4 * 64:(s4 + 1) * 64], ZK[:CT, s4, :], XT_[:CT, hp_], inc=(s4 == 3))
                                egl = EG[:, h, 16 * g + 3:16 * g + 16:4].re("p (s o) -> p s o", o=1).bc([128, 4, 64])
                                kb.tt(HTs[:, 4 * g:4 * g + 4, :], HTs[:, 4 * g:4 * g + 4, :], egl, MUL)
                                kb.tt(HTs[:, 4 * g:4 * g + 4, :], HTs[:, 4 * g:4 * g + 4, :], psH[:, 0:256].re("p (s d) -> p s d", s=4), ADD)
                                ps = bank()
                                for s4 in range(4):
                                    kb.tr(ps[:64, s4 * 128:(s4 + 1) * 128], HTs[:, 4 * g + s4, :], ident, inc=(s4 == 3))
                                kb.cp(STG, ps[:64, :].re("p (s n) -> p s n", s=4))
                                kb.dma("sp", o_sssd[l, 4 * g:4 * g + 4, h].rearrange("s p n -> p s n"), STG, out_dram=True)
                    Y1 = X1
                    kb.tt(v4(Y1, CT, 64), XT_[:CT, :].re("p (h d) -> p h d", h=4), bc4(HP[:, 6, l, :], CT, 64), MUL)
                    kb.tt(Y1[:CT, 0:256], Y1[:CT, 0:256], psY[:CT, 0:256], ADD)
                    release(psY)
                    kb.tt(Y1[:CT, 0:256], Y1[:CT, 0:256], R[:CT, R_SZ:R_SZ + 256], MUL)
                    kb.act(H[:CT, 0:256], Y1[:CT, 0:256], AF.Square)
                    kb.red(SS1[:CT], H[:CT, 0:256], ADD)
                    kb.act(SS1[:CT], SS1[:CT], AF.Ln, bias=EPS, scale=1.0 / 256)
                    kb.act(SS1[:CT], SS1[:CT], AF.Exp, scale=-0.5)
                    kb.stt(MIX[:CT, 768:1024], Y1[:CT, 0:256], SS1[:CT, 0:1], SN[:CT, l, :], MUL, MUL)

                    if DEBUG and sample and l == 0:
                        kb.dma("sp", dbg_mix, MIX[:CT, :], out_dram=True)
                    if CUT <= 11:
                        return
                    yield "preout"
                    for half in range(2):
                        ps = bank()
                        for j in range(4):
                            k = half * 4 + j
                            kb.tr(ps[:, j * CT:(j + 1) * CT], MIX[:CT, k * 128:(k + 1) * 128], ident[:CT, :CT], inc=(j == 3))
                        kb.cp(mixT[:, half * 4:half * 4 + 4, :CT], ps[:, :4 * CT].re("p (k t) -> p k t", k=4), eng=("act" if half else "dve"))
                    for half in range(2):
                        ps = bank()
                        for j in range(4):
                            db = half * 4 + j
                            for k in range(8):
                                kb.mm(ps[:, j * CT:(j + 1) * CT], Wout[:, k, db * 128:(db + 1) * 128], mixT[:, k, :CT], start=(k == 0), stop=(k == 7), inc=(k == 7 and j == 3))
                        kb.cp(SQa[:, half * 4:half * 4 + 4, :CT], ps[:, :4 * CT].re("p (k t) -> p k t", k=4), eng=("act" if half else "dve"))
                    postnorm_add(SQa, c, 1, l, CT)

                def run_chunks(clist, P):
                    def step(g):
                        try:
                            next(g)
                            return True
                        except StopIteration:
                            return False
                    prev = None
                    for c in clist:
                        ci = clist.index(c)
                        g = mixer_chunk(c, P, clist[ci + 1] if ci + 1 < len(clist) else None)
                        alive = step(g)
                        if prev is not None:
                            while step(prev):
                                pass
                        if alive:
                            alive = step(g)
                        prev = g if alive else None
                    if prev is not None:
                        while step(prev):
                            pass

                def conv_blocks(P, XP, b0, nb, CT, CO=None):
                    CO = CO_main if CO is None else CO
                    sample = P["sample"]
                    nseq_ = P["nseq"]
                    if sample:
                        HS = P["HS"]
                        kb.cp(XP[:, 0:nb, :, 0:3], HS[:, b0:b0 + nb, :, :])
                    else:
                        Hh = P["Hh"]
                        kb.cp(XP[:, 0:nb, 0:3], Hh[:, b0:b0 + nb, :])
                    for t in range(4):
                        for j in range(nb):
                            b = b0 + j
                            if sample:
                                o = CO[:, j, :CT].re("p (s t) -> p s t", t=4)
                                xi_ = XP[:, j, :, t:t + 4]
                            else:
                                o = CO[:, j, :CT]
                                xi_ = XP[:, j, t:t + CT]
                            if t == 0:
                                kb.act(o, xi_, AF.Identity, bias=CB[:, l, b:b + 1], scale=CW[:, l, b, 0:1])
                            else:
                                kb.stt(o, xi_, CW[:, l, b, t:t + 1], o, MUL, ADD)
                    if not sample:
                        kb.cp(P["Hh"][:, b0:b0 + nb, :], XP[:, 0:nb, CT:CT + 3])
                    kb.act(CO[:, 0:nb, :CT], CO[:, 0:nb, :CT], AF.Silu)

                with ExitStack() as es3:
                    kb.es = es3
                    Pp = dict(CT=128, nseq=1, sample=False, cm=cmp_, lastsel=lsp, lev=7)
                    Pp["XP"] = kb.sb("XPp", [128, 12, 131])
                    HN[1] = kb.sb("hnT1", [128, 8, 128], BF16)
                    if HOIST_SSD:
                        Pp["XP2"] = kb.sb("XP2", [128, 6, 131])
                        Pp["CO2"] = kb.sb("CO2", [128, 6, 128])
                    Pp["WS"] = [kb.sb("WSp%d" % i, [128, 8, 256], BF16) for i in range(5)]
                    Pp["wsi"] = [0]
                    Pp["Hh"] = kb.sb("Hh", [128, 18, 3])
                    Pp["Sg"] = kb.sb("Sg", [128, 4, 128])
                    Pp["CME"] = kb.sb("CME", [64, 4, 65])
                    Pp["HT"] = kb.sb("HT", [128, 4, 64])
                    Pp["MB"] = kb.sb("MB", [128, 4])
                    Pp["BL"] = kb.sb("BLp", [128, 4, 1])
                    Pp["MN"] = kb.sb("MNp", [128, 4, 1])
                    Pp["WC"] = kb.sb("WCp", [128, 4, 1])
                    Pp["M0R"] = Pp["MB"][:, :].re("p (h o) -> p h o", o=1)
                    for nme in ("Hh", "Sg", "CME", "HT", "MB"):
                        kb.memset(Pp[nme], 0.0)
                    if STOP & 1:
                        run_chunks(list(range(NPC)), Pp)
                    kb.dma("sp", o_pgdn[l].rearrange("h k v -> k h v"), Pp["Sg"], out_dram=True)
                    for h in range(4):
                        kb.dma("sp", o_pmc[l, h], Pp["CME"][:, h, 0:64], out_dram=True)
                        kb.dma("sp", o_pmn[l, h].rearrange("(k o) -> k o", o=1), Pp["CME"][:, h, 64:65], out_dram=True, allow_slow_non_contiguous=True)
                    kb.dma("sp", o_pmm[l].rearrange("(o h) -> o h", o=1), Pp["MB"][0:1, :], out_dram=True)
                    ps = bank()
                    for h in range(4):
                        kb.tr(ps[:64, h * 128:(h + 1) * 128], Pp["HT"][:, h, :], ident, inc=(h == 3))
                    kb.cp(TMP["A"][:64, :], ps[:64, :])
                    kb.dma("sp", o_pssd[l].rearrange("h p n -> p h n"), TMP["A"][:64, :].re("p (h n) -> p h n", h=4), out_dram=True)
                    barrier()
                with ExitStack() as es3:
                    kb.es = es3
                    Psm = dict(CT=NST, nseq=NS, sample=True, cm=cms, lastsel=lss, lev=2)
                    Psm["XP"] = kb.sb("XPs", [128, 12, NS, 7])
                    Psm["WS"] = [kb.sb("WSs%d" % i, [128, 8, 256], BF16) for i in range(2)]
                    Psm["wsi"] = [0]
                    Psm["HS"] = kb.sb("HS", [128, 18, NS, 3])
                    SST = kb.sb("SST", [128, NS * 128])
                    Psm["Ss"] = SST[:, :].re("p (s d) -> p s d", s=NS)
                    Psm["Zt"] = kb.sb("Zt", [128, (NS + 1) * NST])
                    Psm["ZK"] = kb.sb("ZK", [64, 4, 128])
                    Psm["CMs"] = SST[:64, 0:NS * 65].re("p (s d) -> p s d", s=NS)
                    Psm["HTs"] = SST[:, 0:NS * 64].re("p (s d) -> p s d", s=NS)
                    Psm["STG"] = Psm["ZK"]
                    Psm["BL"] = kb.sb("BLs", [128, 4, NS])
                    Psm["MN"] = kb.sb("MNs", [128, 4, NS])
                    Psm["WC"] = kb.sb("WCs", [128, 4, NS])
                    m0r = kb.sb("M0Rs", [128, NS, 4])
                    Psm["m0"] = kb.sb("m0s", [NS, 4])
                    Psm["RMT"] = kb.sb("RMT", [NS, NST])
                    kb.memset(Psm["Zt"], 0.0)
                    kb.dma("sp", m0r, st_mm[l].partition_broadcast(128))
                    kb.dma("sp", Psm["m0"], st_mm[l])
                    Psm["M0R"] = m0r[:, :, :].re("p s h -> p h s")
                    ps = bank()
                    kb.tr(ps[:NS, :NST], rowmask[:NST, :NS], ident[:NST, :NST])
                    kb.cp(Psm["RMT"], ps[:NS, :NST])
                    for b in range(18):
                        stg_ = TMP["E"] if b % 2 else TMP["F"]
                        kb.dma("sp", stg_[:NS * 3, 0:128], st_conv[l, :, :, b * 128:(b + 1) * 128].rearrange("s r c -> (s r) c"))
                        ps = bank()
                        kb.tr(ps[:, :NS * 3], stg_[:NS * 3, 0:128], ident[:NS * 3, :NS * 3])
                        kb.cp(Psm["HS"][:, b, :, :], ps[:, :NS * 3].re("p (s r) -> p s r", r=3), eng=("act" if b % 2 else "dve"))
                    if STOP & 2:
                        run_chunks([NPC], Psm)
                    barrier()
            with ExitStack() as es2:
                kb.es = es2
                groups = []
                cs_ = list(range(NPC))
                GSZ = 6
                for g0 in range(0, NPC, GSZ):
                    groups.append(cs_[g0:g0 + GSZ])
                groups[-1] = groups[-1] + [NPC]
                MAXT = max(sum(ctok(c) for c in g) for g in groups)
                hn2 = kb.sb("hn2", [128, 8, MAXT], BF16)
                ACTT = kb.sb("ACTT", [128, NFB, MAXT], BF16)
                WU = [kb.sb("WU%d" % i, [128, 8, 256], BF16) for i in range(NWU)]
                WD = [kb.sb("WD%d" % i, [128, NFB, 128], BF16) for i in range(NWD)]
                GP = kb.sb("GP", [128, 2 + GSZ * 128])
                GS = kb.sb("GS", [128, NS, 6])
                VV = kb.sb("VV", [128, MAXT])
                ACC = kb.sb("ACC", [128, MAXT])
                T1 = kb.sb("T1", [128, MAXT])
                YF = kb.sb("YF", [128, 8, max(MAXT, 704)])
                HF = kb.sb("HF", [128, NFB, 2])
                HSf = kb.sb("HSf", [128, NFB, NS, 2])
                STGf = YF[:, :, :].re("p k t -> p (k t)")
                pass
                kb.memset(HF, 0.0)
                kb.dma("sp", STGf[:NS * 2, 0:DFF], st_ffn[l].rearrange("s r c -> (s r) c"))
                for b in range(NFB):
                    ps = bank()
                    kb.tr(ps[:, :NS * 2], STGf[:NS * 2, b * 128:(b + 1) * 128], ident[:NS * 2, :NS * 2])
                    kb.cp(HSf[:, b, :, :], ps[:, :NS * 2].re("p (s r) -> p s r", r=2), eng=("act" if b % 2 else "dve"))
                wu_i = 0
                wd_i = 0
                for gi, grp in enumerate(groups):
                    if not (STOP & 4):
                        continue
                    has_s = (grp[-1] == NPC)
                    pch = [c for c in grp if c < NPC]
                    npt = len(pch) * 128
                    ntg = npt + (NST if has_s else 0)
                    off = 0
                    offs = {}
                    for c in grp:
                        CT = ctok(c)
                        offs[c] = off
                        rmsnorm_to(hn2[:, :, off:off + CT], c, 2, l, CT)
                        off += CT
                    tbs = [(t0, min(512, ntg - t0)) for t0 in range(0, ntg, 512)]
                    for fb in range(NFB):
                        wu = WU[wu_i % NWU]
                        wu_i += 1
                        kb.dma("pool", wu[:, :, 0:128], w_up[l, :, fb * 128:(fb + 1) * 128].rearrange("(k p) c -> p k c", p=128))
                        kb.dma("pool", wu[:, :, 128:256], w_up[l, :, DFF + fb * 128:DFF + (fb + 1) * 128].rearrange("(k p) c -> p k c", p=128))
                        if (NPC - 1) in grp:
                            r0 = offs[NPC - 1] + 126
                            ps = bank()
                            for k in range(8):
                                kb.mm(ps[:2, 0:128], hn2[:, k, r0:r0 + 2], wu[:, k, 0:128], start=(k == 0), stop=(k == 7), inc=(k == 7))
                            kb.cp(STGf[:2, fb * 128:(fb + 1) * 128], ps[:2, 0:128], eng="act")
                        if has_s:
                            r0 = offs[NPC]
                            ps = bank()
                            for k in range(8):
                                kb.mm(ps[:NST, 0:128], hn2[:, k, r0:r0 + NST], wu[:, k, 0:128], start=(k == 0), stop=(k == 7), inc=(k == 7))
                            kb.cp(STGf[:NST, DFF + fb * 128:DFF + (fb + 1) * 128], ps[:NST, 0:128], eng="act")
                        for (t0, tw) in tbs:
                            psG = bank()
                            for k in range(8):
                                kb.mm(psG[:, :tw], wu[:, k, 0:128], hn2[:, k, t0:t0 + tw], start=(k == 0), stop=(k == 7), inc=(k == 7))
                            psV = bank()
                            for k in range(8):
                                kb.mm(psV[:, :tw], wu[:, k, 128:256], hn2[:, k, t0:t0 + tw], start=(k == 0), stop=(k == 7), inc=(k == 7))
                            p1 = min(t0 + tw, npt)
                            if p1 > t0:
                                kb.cp(GP[:, 2 + t0:2 + p1], psG[:, 0:p1 - t0], eng="act")
                            if t0 + tw > npt:
                                s0 = max(t0, npt) - t0
                                kb.cp(GS[:, :, 2:6], psG[:, s0:s0 + NST].re("p (s t) -> p s t", t=4), eng="act")
                            kb.cp(VV[:, t0:t0 + tw], psV[:, :tw], eng="act")
                        if npt > 0:
                            kb.cp(GP[:, 0:2], HF[:, fb, :])
                            kb.act(ACC[:, 0:npt], GP[:, 0:npt], AF.Identity, bias=FCB[:, l, fb:fb + 1], scale=FCW[:, l, fb, 0:1])
                            kb.stt(ACC[:, 0:npt], GP[:, 1:1 + npt], FCW[:, l, fb, 1:2], ACC[:, 0:npt], MUL, ADD)
                            kb.stt(ACC[:, 0:npt], GP[:, 2:2 + npt], FCW[:, l, fb, 2:3], ACC[:, 0:npt], MUL, ADD)
                            kb.cp(HF[:, fb, :], GP[:, npt:npt + 2])
                        if has_s:
                            kb.cp(GS[:, :, 0:2], HSf[:, fb, :, :])
                            a_s = ACC[:, npt:npt + NST].re("p (s t) -> p s t", t=4)
                            kb.act(a_s, GS[:, :, 0:4], AF.Identity, bias=FCB[:, l, fb:fb + 1], scale=FCW[:, l, fb, 0:1])
                            kb.stt(a_s, GS[:, :, 1:5], FCW[:, l, fb, 1:2], a_s, MUL, ADD)
                            kb.stt(a_s, GS[:, :, 2:6], FCW[:, l, fb, 2:3], a_s, MUL, ADD)
                        kb.act(T1[:, :ntg], ACC[:, :ntg], AF.Square, scale=0.044715 ** 0.5)
                        kb.stt(T1[:, :ntg], T1[:, :ntg], 1.0, ACC[:, :ntg], ADD, MUL)
                        kb.act(T1[:, :ntg], T1[:, :ntg], AF.Sigmoid, scale=1.5957691216057308)
                        kb.tt(ACC[:, :ntg], ACC[:, :ntg], VV[:, :ntg], MUL)
                        kb.tt(ACTT[:, fb, :ntg], T1[:, :ntg], ACC[:, :ntg], MUL)
                    if (NPC - 1) in grp:
                        kb.dma("sp", o_pffn[l], STGf[:2, 0:DFF], out_dram=True)
                    if has_s:
                        for r in range(2):
                            kb.dma("sp", o_sffn[l, :, r, :], STGf[2 + r:NST:4, DFF:2 * DFF], out_dram=True)
                    for db in range(8):
                        wd = WD[wd_i % NWD]
                        wd_i += 1
                        kb.dma("pool", wd, w_down[l, :, db * 128:(db + 1) * 128].rearrange("(f p) c -> p f c", p=128))
                        for (t0, tw) in tbs:
                            ps = bank()
                            for fb in range(NFB):
                                kb.mm(ps[:, :tw], wd[:, fb, :], ACTT[:, fb, t0:t0 + tw], start=(fb == 0), stop=(fb == NFB - 1), inc=(fb == NFB - 1))
                            kb.cp(YF[:, db, t0:t0 + tw], ps[:, :tw], eng=("act" if db % 2 else "dve"))
                    for c in grp:
                        CT = ctok(c)
                        kb.cp(SQa[:, :, :CT], YF[:, :, offs[c]:offs[c] + CT])
                        postnorm_add(SQa, c, 3, l, CT)
                barrier()

        with ExitStack() as es1:
            kb.es = es1
            yo = [kb.sb("yo%d" % i, [128, D]) for i in range(2)]
            for c in range(NCH):
                CT = ctok(c)
                yt = yo[c % 2]
                for half in range(2):
                    ps = bank()
                    for j in range(4):
                        k = half * 4 + j
                        kb.tr(ps[:CT, j * 128:(j + 1) * 128], xT[c][:, k, :CT], ident, inc=(j == 3))
                    kb.cp(yt[:CT, half * 512:(half + 1) * 512], ps[:CT, :], eng=("act" if half else "dve"))
                dst = y_p[c * 128:(c + 1) * 128, :] if c < NPC else y_s
                kb.dma("sp", dst, yt[:CT, :], out_dram=True)
        kb.finish()
    return nc


IN_NAMES = ["norm_mix_pre", "norm_mix_post", "norm_ffn_pre", "norm_ffn_post", "w_in", "conv_w", "conv_b",
            "gdn_a_log", "gdn_dt_bias", "gdn_norm", "mlstm_i_bias", "mlstm_f_bias", "mlstm_norm",
            "ssd_a_log", "ssd_dt_bias", "ssd_d", "ssd_norm", "w_out", "ffn_w_up", "ffn_conv_w", "ffn_conv_b", "ffn_w_down"]
ST_MAP = [("st_conv", "state_conv"), ("st_gdn", "state_gdn"), ("st_mc", "state_mlstm_c"), ("st_mn", "state_mlstm_n"),
          ("st_mm", "state_mlstm_m"), ("st_ssd", "state_ssd"), ("st_ffn", "state_ffn_conv")]
_NC_CACHE = {}


def make_in_maps(inputs, NPC, NS, DEPTH, ncores):
    consts = make_consts(NS)
    f = lambda a: np.ascontiguousarray(np.asarray(a, dtype=np.float32))
    shared = {n: f(inputs[n])[:DEPTH] for n in IN_NAMES}
    shared.update(consts)
    maps = []
    for c in range(ncores):
        m = dict(shared)
        m["xp"] = f(inputs["x_prompt"][c, :NPC * 128])
        m["xs"] = f(inputs["x_sample"][c * NS:(c + 1) * NS]).reshape(NS * 4, D)
        for k, n in ST_MAP:
            m[k] = f(inputs[n][:DEPTH, c * NS:(c + 1) * NS])
        maps.append(m)
    return maps


def assemble(results, NPC, NS, DEPTH):
    cat = lambda k, ax: np.concatenate([r[k] for r in results], axis=ax)
    y_p = np.stack([r["y_p"] for r in results], 0)
    y_s = np.concatenate([r["y_s"].reshape(NS, 4, D) for r in results], 0)
    outs = [y_p, y_s]
    for k in ("p_conv", "p_gdn", "p_mc", "p_mn", "p_mm", "p_ssd", "p_ffn"):
        outs.append(np.stack([r[k] for r in results], 1))
    for k in ("s_conv", "s_gdn", "s_mc", "s_mn", "s_mm", "s_ssd", "s_ffn"):
        outs.append(cat(k, 1))
    return tuple(np.ascontiguousarray(o, dtype=np.float32) for o in outs)


def kernel(**inputs):
    NPC, NS, DEPTH, ncores = 16, 16, 2, 8
    key = (NPC, NS, DEPTH)
    if key not in _NC_CACHE:
        _NC_CACHE[key] = build(NPC, NS, DEPTH)
    nc = _NC_CACHE[key]
    maps = make_in_maps(inputs, NPC, NS, DEPTH, ncores)
    res = run_bass_kernel_spmd(nc, maps, core_ids=list(range(ncores)))
    return assemble(res.results, NPC, NS, DEPTH)
```

```python
import numpy as np
import concourse.bass as bass
import concourse.mybir as mybir

F32 = mybir.dt.float32
BF16 = mybir.dt.bfloat16
ALU = mybir.AluOpType
AF = mybir.ActivationFunctionType
AX = mybir.AxisListType


class T:
    def __init__(self, ap, name):
        self.ap = ap
        self.name = name
        self.we = None
        self.wd = []
        self.re = {}
        self.rd = []

    def __getitem__(self, idx):
        return V(self, self.ap[idx])

    @property
    def t(self):
        return self


class V:
    def __init__(self, t, ap):
        self.t = t
        self.ap = ap

    def __getitem__(self, idx):
        return V(self.t, self.ap[idx])

    def re(self, pattern_, **kw):
        return V(self.t, self.ap.rearrange(pattern_, **kw))

    def bc(self, shape):
        return V(self.t, self.ap.to_broadcast(shape))


def _ap(x):
    return x.ap if isinstance(x, (T, V)) else x


class KB:
    def __init__(self, nc, es, n_dma_sems=6):
        self.nc = nc
        self.es = es
        self.E = {"pe": nc.tensor, "act": nc.scalar, "dve": nc.vector, "pool": nc.gpsimd, "sp": nc.sync}
        self.sem = {e: es.enter_context(nc.semaphore("s_" + e)) for e in ("pe", "act", "dve", "pool")}
        self.cnt = {e: 0 for e in self.sem}
        self.known = {e: {} for e in self.E}
        self.knownd = {e: {} for e in self.E}
        self.pend = {e: [] for e in self.E}
        self.dsems = {}
        for q in ("sp", "pool", "act"):
            self.dsems[q] = [[es.enter_context(nc.semaphore("d_%s%d" % (q, i))), 0] for i in range(n_dma_sems if q != "act" else 4)]
        self.dnext = {q: 0 for q in self.dsems}
        self.nbank = 0
        self.out_deps = []

    def sb(self, name, shape, dt=F32):
        self.nbank += 1
        name = "sb%d_%s" % (self.nbank, name)
        return T(self.es.enter_context(self.nc.sbuf_tensor(name, list(shape), dt)).ap(), name)

    def ps(self, name, shape, dt=F32):
        self.nbank += 1
        name = "ps%d_%s" % (self.nbank, name)
        t = T(self.es.enter_context(self.nc.psum_tensor(name, list(shape), dt)).ap(), name)
        t.psum = True
        return t

    def _wait_e(self, eng, dep):
        e2, c = dep
        if self.known[eng].get(e2, 0) >= c:
            return
        self.E[eng].wait_ge(self.sem[e2], c)
        self.known[eng][e2] = c

    def _wait_d(self, eng, dep):
        key, sem, tgt = dep
        if self.knownd[eng].get(key, 0) >= tgt:
            return
        self.E[eng].wait_ge(sem, tgt)
        self.knownd[eng][key] = tgt

    def _pre(self, eng, outs, ins):
        for e2, pl in self.pend.items():
            if e2 == eng:
                continue
            for (po, pi) in pl:
                for v in outs:
                    assert all(v.t is not x.t for x in po + pi), ("pending hazard", v.t.name, e2)
                for v in ins:
                    assert all(v.t is not x.t for x in po), ("pending hazard", v.t.name, e2)
        for v in ins:
            t = v.t
            if t.we is not None and not (eng == "pe" and t.we[0] == "pe"):
                self._wait_e(eng, t.we)
            for d in t.wd:
                self._wait_d(eng, d)
            if getattr(t, "psum", False):
                for e2, c in t.re.items():
                    if e2 != eng:
                        self._wait_e(eng, (e2, c))
        for v in outs:
            t = v.t
            if t.we is not None and not (eng == "pe" and t.we[0] == "pe"):
                self._wait_e(eng, t.we)
            for d in t.wd:
                self._wait_d(eng, d)
            for e2, c in t.re.items():
                if not (e2 == "pe" and eng == "pe"):
                    self._wait_e(eng, (e2, c))
            for d in t.rd:
                self._wait_d(eng, d)

    def op(self, eng, fn, outs, ins, inc=True):
        outs = [o for o in outs if o is not None]
        ins = [i for i in ins if isinstance(i, (T, V))]
        self._pre(eng, outs, ins)
        inst = fn()
        self.pend[eng].append((outs, ins))
        if inc:
            self.cnt[eng] += 1
            c = self.cnt[eng]
            inst.then_inc(self.sem[eng], 1)
            for (po, pi) in self.pend[eng]:
                for v in pi:
                    v.t.re[eng] = c
                for v in po:
                    v.t.we = (eng, c)
                    v.t.wd = []
                    v.t.re = {}
                    v.t.rd = []
            self.pend[eng] = []
        return inst

    def dma(self, q, out, in_, out_dram=False, **kw):
        assert not self.pend[q] if q in self.pend else True
        pool = self.dsems[q]
        i = self.dnext[q]
        self.dnext[q] = (i + 1) % len(pool)
        sem, prev = pool[i]
        key = (q, i)
        if prev > 0:
            self._wait_d(q, (key, sem, prev))
        self._pre(q, [o for o in [out] if isinstance(o, (T, V))], [o for o in [in_] if isinstance(o, (T, V))])
        tgt = prev + 16
        pool[i][1] = tgt
        self.E[q].dma_start(out=_ap(out), in_=_ap(in_), **kw).then_inc(sem, 16)
        dep = (key, sem, tgt)
        if isinstance(in_, (T, V)):
            in_.t.rd.append(dep)
        if isinstance(out, (T, V)):
            out.t.wd.append(dep)
        if out_dram:
            self.out_deps.append(dep)
        return dep

    def finish(self):
        for d in self.out_deps:
            self._wait_d("sp", d)

    def mm(self, out, lhsT, rhs, start=True, stop=True, inc=True):
        return self.op("pe", lambda: self.nc.tensor.matmul(_ap(out), _ap(lhsT), _ap(rhs), start=start, stop=stop),
                       [out], [lhsT, rhs], inc=inc)

    def tr(self, out, in_, ident, inc=True):
        return self.op("pe", lambda: self.nc.tensor.transpose(_ap(out), _ap(in_), _ap(ident)), [out], [in_, ident], inc=inc)

    def act(self, out, in_, func, bias=0.0, scale=1.0, eng="act"):
        return self.op("act", lambda: self.nc.scalar.activation(_ap(out), _ap(in_), func, bias=_ap(bias), scale=_ap(scale)),
                       [out], [in_, bias, scale])

    def ts(self, out, in0, s1, s2, op0, op1=None, eng="dve"):
        e = self.E[eng]
        if op1 is None:
            return self.op(eng, lambda: e.tensor_scalar(_ap(out), _ap(in0), _ap(s1), None, op0), [out], [in0, s1])
        return self.op(eng, lambda: e.tensor_scalar(_ap(out), _ap(in0), _ap(s1), _ap(s2), op0, op1), [out], [in0, s1, s2])

    def stt(self, out, in0, s, in1, op0, op1):
        return self.op("dve", lambda: self.nc.vector.scalar_tensor_tensor(_ap(out), _ap(in0), _ap(s), _ap(in1), op0, op1),
                       [out], [in0, s, in1])

    def tt(self, out, in0, in1, op, eng="dve"):
        e = self.E[eng]
        return self.op(eng, lambda: e.tensor_tensor(_ap(out), _ap(in0), _ap(in1), op), [out], [in0, in1])

    def cp(self, out, in_, eng="dve"):
        if eng == "act":
            return self.op("act", lambda: self.nc.scalar.copy(_ap(out), _ap(in_)), [out], [in_])
        e = self.E[eng]
        return self.op(eng, lambda: e.tensor_copy(_ap(out), _ap(in_)), [out], [in_])

    def recip(self, out, in_):
        return self.op("dve", lambda: self.nc.vector.reciprocal(_ap(out), _ap(in_)), [out], [in_])

    def red(self, out, in_, op, axis=AX.X):
        return self.op("dve", lambda: self.nc.vector.tensor_reduce(_ap(out), _ap(in_), axis, op), [out], [in_])

    def memset(self, out, val, eng="dve"):
        e = self.E[eng]
        return self.op(eng, lambda: e.memset(_ap(out), val), [out], [])


from contextlib import ExitStack
from concourse.bass_utils import run_bass_kernel_spmd

D = 1024
KC = 8
CONV_DIM = 2304
IN_COLS = 4116
NREST = 1812
DFF = 2816
NFB = 22
EPS = 1e-6
NEG = -30000.0
LNQ = float(np.log(128.0 ** -0.5))
DEBUG = False
PRENORM = True
HOIST_SSD = False
ILV_ML = True


def _step(g):
    try:
        next(g)
        return True
    except StopIteration:
        return False
NWU = 3
NWD = 3
STOP = 7
CUT = 99


class _Stop(Exception):
    pass
R_GG, R_GA, R_GB, R_MQ, R_MK, R_MV, R_MO, R_MI, R_MF, R_SZ, R_SDT = 0, 512, 516, 520, 776, 1032, 1288, 1544, 1548, 1552, 1808
MUL, ADD, SUB, MAX = ALU.mult, ALU.add, ALU.subtract, ALU.max


def make_consts(NS):
    c = {}
    c["ident"] = np.eye(128, dtype=np.float32)
    c["ones"] = np.ones((128, 128), np.float32)
    i = np.arange(128)[:, None]
    j = np.arange(128)[None, :]
    cm = np.zeros((128, 6, 128), np.float32)
    cm[:, 0] = (i <= j)
    cm[:, 1] = np.where(i > j, 0.0, NEG)
    cm[:, 2] = np.where(j >= i, 0.0, NEG)
    cm[:, 3] = np.where(j <= i, 0.0, NEG)
    cm[:, 4] = 1.0
    cm[:, 5] = (i == 127)
    c["cm_p"] = cm
    ls = np.zeros((128, 1), np.float32)
    ls[127, 0] = 1
    c["lastsel_p"] = ls
    n = NS * 4
    i = np.arange(n)[:, None]
    j = np.arange(n)[None, :]
    same = (i // 4) == (j // 4)
    cs = np.zeros((n, 6, n), np.float32)
    cs[:, 0] = same & (i <= j)
    cs[:, 1] = np.where(same & (i > j), 0.0, NEG)
    cs[:, 2] = np.where(same & (j >= i), 0.0, NEG)
    cs[:, 3] = np.where(same & (j <= i), 0.0, NEG)
    cs[:, 4] = same
    cs[:, 5] = same & (i % 4 == 3)
    c["cm_s"] = cs
    s = np.arange(NS)[None, :]
    k = np.arange(n)[:, None]
    c["lastsel_s"] = (k == 4 * s + 3).astype(np.float32)
    c["rowmask"] = ((k // 4) == s).astype(np.float32)
    return c


def build(NPC, NS, DEPTH):
    nc = bass.Bass("TRN2", target_bir_lowering=False)
    TP = NPC * 128
    NST = NS * 4
    assert NST <= 64

    def din(name, shape):
        return nc.dram_tensor(name, list(shape), F32, kind="ExternalInput").ap()

    def dout(name, shape):
        return nc.dram_tensor(name, list(shape), F32, kind="ExternalOutput").ap()

    xp = din("xp", [TP, D])
    xs = din("xs", [NST, D])
    st_conv = din("st_conv", [DEPTH, NS, 3, CONV_DIM])
    st_gdn = din("st_gdn", [DEPTH, NS, 4, 128, 128])
    st_mc = din("st_mc", [DEPTH, NS, 4, 64, 64])
    st_mn = din("st_mn", [DEPTH, NS, 4, 64])
    st_mm = din("st_mm", [DEPTH, NS, 4])
    st_ssd = din("st_ssd", [DEPTH, NS, 4, 64, 128])
    st_ffn = din("st_ffn", [DEPTH, NS, 2, DFF])
    nrm = [din(n, [DEPTH, D]) for n in ("norm_mix_pre", "norm_mix_post", "norm_ffn_pre", "norm_ffn_post")]
    w_in = din("w_in", [DEPTH, D, IN_COLS])
    conv_w = din("conv_w", [DEPTH, 4, CONV_DIM])
    conv_b = din("conv_b", [DEPTH, CONV_DIM])
    hp_names = ["gdn_a_log", "gdn_dt_bias", "mlstm_i_bias", "mlstm_f_bias", "ssd_a_log", "ssd_dt_bias", "ssd_d"]
    hp_d = [din(n, [DEPTH, 4]) for n in hp_names]
    gdn_norm = din("gdn_norm", [DEPTH, 128])
    mlstm_norm = din("mlstm_norm", [DEPTH, 64])
    ssd_norm = din("ssd_norm", [DEPTH, 256])
    w_out = din("w_out", [DEPTH, D, D])
    w_up = din("ffn_w_up", [DEPTH, D, 2 * DFF])
    fconv_w = din("ffn_conv_w", [DEPTH, 3, DFF])
    fconv_b = din("ffn_conv_b", [DEPTH, DFF])
    w_down = din("ffn_w_down", [DEPTH, DFF, D])
    c_ident = din("ident", [128, 128])
    c_ones = din("ones", [128, 128])
    c_cm_p = din("cm_p", [128, 6, 128])
    c_ls_p = din("lastsel_p", [128, 1])
    c_cm_s = din("cm_s", [NST, 6, NST])
    c_ls_s = din("lastsel_s", [NST, NS])
    c_rowmask = din("rowmask", [NST, NS])

    y_p = dout("y_p", [TP, D])
    y_s = dout("y_s", [NST, D])
    o_pconv = dout("p_conv", [DEPTH, 3, CONV_DIM])
    o_pgdn = dout("p_gdn", [DEPTH, 4, 128, 128])
    o_pmc = dout("p_mc", [DEPTH, 4, 64, 64])
    o_pmn = dout("p_mn", [DEPTH, 4, 64])
    o_pmm = dout("p_mm", [DEPTH, 4])
    o_pssd = dout("p_ssd", [DEPTH, 4, 64, 128])
    o_pffn = dout("p_ffn", [DEPTH, 2, DFF])
    o_sconv = dout("s_conv", [DEPTH, NS, 3, CONV_DIM])
    o_sgdn = dout("s_gdn", [DEPTH, NS, 4, 128, 128])
    o_smc = dout("s_mc", [DEPTH, NS, 4, 64, 64])
    o_smn = dout("s_mn", [DEPTH, NS, 4, 64])
    o_smm = dout("s_mm", [DEPTH, NS, 4])
    o_sssd = dout("s_ssd", [DEPTH, NS, 4, 64, 128])
    o_sffn = dout("s_ffn", [DEPTH, NS, 2, DFF])

    NCH = NPC + 1
    dbg_mix = dout("dbg_mix", [NST, D]) if DEBUG else None
    NSL = True

    with ExitStack() as es0:
        kb = KB(nc, es0)
        PS = [kb.ps("psb%d" % i, [128, 512]) for i in range(8)]
        bstate = [0]

        held = set()

        def bank(hold=False):
            while (bstate[0] % 8) in held:
                bstate[0] += 1
            i = bstate[0] % 8
            bstate[0] += 1
            if hold:
                held.add(i)
            return PS[i]

        def release(*bs):
            for b in bs:
                held.discard(PS.index(b))

        def barrier():
            for e in ("pe", "act", "dve", "pool", "sp"):
                for e2 in ("pe", "act", "dve", "pool"):
                    if e2 != e and kb.cnt[e2] > 0:
                        kb._wait_e(e, (e2, kb.cnt[e2]))
                for q in kb.dsems:
                    for i, (sem, tgt) in enumerate(kb.dsems[q]):
                        if tgt > 0:
                            kb._wait_d(e, ((q, i), sem, tgt))

        xT = [kb.sb("xT%d" % c, [128, 8, 128 if c < NPC else NST]) for c in range(NCH)]
        ident = kb.sb("ident", [128, 128])
        ones = kb.sb("ones", [128, 128])
        cmp_ = kb.sb("cm_p", [128, 6, 128])
        lsp = kb.sb("ls_p", [128, 1])
        cms = kb.sb("cm_s", [NST, 6, NST])
        lss = kb.sb("ls_s", [NST, NS])
        rowmask = kb.sb("rowmask", [NST, NS])
        NW = kb.sb("NW", [128, 4 * DEPTH * 8])
        CW = kb.sb("CW", [128, DEPTH, 18, 4])
        CB = kb.sb("CB", [128, DEPTH, 18])
        FCW = kb.sb("FCW", [128, DEPTH, NFB, 3])
        FCB = kb.sb("FCB", [128, DEPTH, NFB])
        HP = kb.sb("HP", [128, 7, DEPTH, 4])
        GN = kb.sb("GN", [128, DEPTH, 128])
        MNb = kb.sb("MNb", [128, DEPTH, 64])
        SN = kb.sb("SN", [128, DEPTH, 256])
        RS = kb.sb("RS", [128, 128])
        SQa = kb.sb("SQa", [128, 8, 128])
        SQb = kb.sb("SQb", [128, 8, 128])

        kb.dma("sp", ident, c_ident)
        kb.dma("sp", ones, c_ones)
        kb.dma("sp", cmp_, c_cm_p)
        kb.dma("sp", lsp, c_ls_p)
        kb.dma("sp", cms, c_cm_s)
        kb.dma("sp", lss, c_ls_s)
        kb.dma("sp", rowmask, c_rowmask)
        for w in range(4):
            for l in range(DEPTH):
                o = (w * DEPTH + l) * 8
                kb.dma("sp", NW[:, o:o + 8], nrm[w][l].rearrange("(k p) -> p k", p=128), allow_slow_non_contiguous=True)
        for l in range(DEPTH):
            for j in range(4):
                kb.dma("sp", CW[:, l, :, j], conv_w[l, j].rearrange("(b p) -> p b", p=128), allow_slow_non_contiguous=True)
            kb.dma("sp", CB[:, l, :], conv_b[l].rearrange("(b p) -> p b", p=128), allow_slow_non_contiguous=True)
            for j in range(3):
                kb.dma("sp", FCW[:, l, :, j], fconv_w[l, j].rearrange("(b p) -> p b", p=128), allow_slow_non_contiguous=True)
            kb.dma("sp", FCB[:, l, :], fconv_b[l].rearrange("(b p) -> p b", p=128), allow_slow_non_contiguous=True)
        for i in range(7):
            kb.dma("sp", HP[:, i], hp_d[i].partition_broadcast(128))
        kb.dma("sp", GN, gdn_norm.partition_broadcast(128))
        kb.dma("sp", MNb, mlstm_norm.partition_broadcast(128))
        kb.dma("sp", SN, ssd_norm.partition_broadcast(128))
        for i in (0, 4):
            kb.act(HP[:, i], HP[:, i], AF.Exp)
            kb.ts(HP[:, i], HP[:, i], -1.0, None, MUL)

        def nw(w, l, k):
            o = (w * DEPTH + l) * 8 + k
            return NW[:, o:o + 1]

        def ctok(c):
            return 128 if c < NPC else NST

        ones_bf = kb.sb("ones_bf", [128, 128], BF16)
        kb.cp(ones_bf, ones)

        def bfview(t3):
            v = t3[:, :, :].re("p k t -> p (k t)")
            return V(v.t, v.ap.bitcast(BF16)[:, 0:1024].rearrange("p (k t) -> p k t", k=8))

        def nwv(w, l, CT):
            o = (w * DEPTH + l) * 8
            return NW[:, o:o + 8].re("p (k o) -> p k o", o=1).bc([128, 8, CT])

        def rsv(CT):
            return RS[:, :CT].re("p (o t) -> p o t", o=1).bc([128, 8, CT])

        def rmsnorm_to(dst, c, w, l, CT):
            xc = xT[c]
            sqb = bfview(SQa)
            kb.act(sqb[:, :, :CT], xc[:, :, :CT], AF.Square)
            ps = bank()
            for k in range(8):
                kb.mm(ps[:, :CT], ones_bf, sqb[:, k, :CT], start=(k == 0), stop=(k == 7), inc=(k == 7))
            kb.act(RS[:, :CT], ps[:, :CT], AF.Ln, bias=EPS, scale=1.0 / D)
            kb.act(RS[:, :CT], RS[:, :CT], AF.Exp, scale=-0.5)
            kb.tt(SQa[:, :, :CT], xc[:, :, :CT], rsv(CT), MUL)
            kb.tt(dst[:, :, :CT], SQa[:, :, :CT], nwv(w, l, CT), MUL)

        def postnorm_add(Y, c, w, l, CT):
            sqb = bfview(SQb)
            kb.act(sqb[:, :, :CT], Y[:, :, :CT], AF.Square)
            ps = bank()
            for k in range(8):
                kb.mm(ps[:, :CT], ones_bf, sqb[:, k, :CT], start=(k == 0), stop=(k == 7), inc=(k == 7))
            kb.act(RS[:, :CT], ps[:, :CT], AF.Ln, bias=EPS, scale=1.0 / D)
            kb.act(RS[:, :CT], RS[:, :CT], AF.Exp, scale=-0.5)
            kb.tt(Y[:, :, :CT], Y[:, :, :CT], rsv(CT), MUL)
            kb.tt(Y[:, :, :CT], Y[:, :, :CT], nwv(w, l, CT), MUL)
            kb.tt(xT[c][:, :, :CT], xT[c][:, :, :CT], Y[:, :, :CT], ADD)

        with ExitStack() as es1:
            kb.es = es1
            xin = [kb.sb("xin%d" % i, [128, D]) for i in range(2)]
            for c in range(NCH):
                CT = ctok(c)
                xi = xin[c % 2]
                src = xp[c * 128:(c + 1) * 128, :] if c < NPC else xs
                kb.dma("sp", xi[:CT, :], src)
                for half in range(2):
                    ps = bank()
                    for j in range(4):
                        k = half * 4 + j
                        kb.tr(ps[:, j * CT:(j + 1) * CT], xi[:CT, k * 128:(k + 1) * 128], ident[:CT, :CT], inc=(j == 3))
                    kb.cp(xT[c][:, half * 4:half * 4 + 4, :CT], ps[:, :4 * CT].re("p (k t) -> p k t", k=4), eng=("dve" if half == 0 else "act"))
            barrier()

        for l in range(DEPTH):
            with ExitStack() as es2:
                kb.es = es2
                WinR = kb.sb("WinR", [128, 8, NREST], BF16)
                Wout = kb.sb("Wout", [128, 8, D], BF16)
                for k in range(8):
                    kb.dma("pool", WinR[:, k, :], w_in[l, k * 128:(k + 1) * 128, CONV_DIM:IN_COLS])
                for k in range(8):
                    kb.dma("pool", Wout[:, k, :], w_out[l, k * 128:(k + 1) * 128, :])
                HN = [kb.sb("hnT0", [128, 8, 128], BF16), None]
                normed = {}
                R = kb.sb("R", [128, NREST])
                STGc = [kb.sb("STGc%d" % i, [64, 256]) for i in range(2)]
                stgi = [0]
                QKm = kb.sb("QKm", [64, 8, 128])
                CO = kb.sb("CO", [128, 12, 128])
                CO_main = CO
                TMP = {n: kb.sb("tmp" + n, [128, 512]) for n in ("E", "F", "G2", "H", "I", "X1")}
                SQa2 = SQa[:, :, :].re("p k t -> p (k t)")
                SQb2 = SQb[:, :, :].re("p k t -> p (k t)")
                TMP["A"] = SQa2[:, 0:512]
                TMP["B"] = SQa2[:, 512:1024]
                TMP["C"] = SQb2[:, 0:512]
                TMP["Dt"] = SQb2[:, 512:1024]
                SM = kb.sb("SM", [128, 160])
                EG = kb.sb("EG", [128, 4, 128])
                MIX = kb.sb("MIX", [128, D])
                mixT = kb.sb("mixT", [128, 8, 128], BF16)
                VE = kb.sb("VE", [128, 4, 65])
                KW = kb.sb("KW", [128, 4, 64])
                XT_ = kb.sb("XTs", [128, 256])
                kb.memset(VE[:, :, 64:65], 1.0)
                Z = SM[:, 0:12]
                SP = SM[:, 12:24]
                CS = SM[:, 24:36]
                GC = SM[:, 36:48]
                BETA = SM[:, 48:52]
                NBETA = SM[:, 52:56]
                LI = SM[:, 56:60]
                BE = SM[:, 60:64]
                KD = SM[:, 64:68]
                GL = SM[:, 68:80]
                CC = SM[:, 80:84]
                MX = SM[:, 84:88]
                INT = SM[:, 88:92]
                MT = SM[:, 92:96]
                NMT = SM[:, 96:100]
                WI = SM[:, 100:104]
                DEN = SM[:, 104:108]
                EM = SM[:, 108:112]
                WK = SM[:, 112:116]
                SS4 = SM[:, 116:120]
                RSTD4 = SM[:, 120:124]
                KDS = SM[:, 124:128]
                M0C = SM[:, 128:132]
                SS1 = SM[:, 132:133]

                def v4(t, CT, w=None):
                    w = CT if w is None else w
                    return t[:CT, 0:4 * w].re("p (h j) -> p h j", h=4)

                def bc4(v, CT, w):
                    return v[:CT].re("p (h o) -> p h o", o=1).bc([CT, 4, w])

                def mixer_chunk(c, P, nxt=None):
                    hnT = HN[0] if P["sample"] else HN[c % 2]
                    CT, nseq, sample = P["CT"], P["nseq"], P["sample"]
                    cm = P["cm"]
                    U, Ls, Um, Lm, SSm, LSm = (cm[:CT, i, :CT] for i in range(6))
                    lastsel = P["lastsel"]

                    def mbc(m):
                        return m.re("p (o j) -> p o j", o=1).bc([CT, 4, CT])

                    if normed.get("c") != c:
                        rmsnorm_to(hnT, c, 0, l, CT)

                    if CUT <= 1:
                        return
                    need_cs = sample or c == NPC - 1
                    rows = slice(0, CT) if sample else slice(CT - 3, CT)
                    nr = CT if sample else 3

                    def fm_pair(col0, conv=True):
                        ws = P["WS"][P["wsi"][0] % len(P["WS"])]
                        P["wsi"][0] += 1
                        kb.dma("pool", ws, w_in[l, :, col0:col0 + 256].rearrange("(k p) c -> p k c", p=128))
                        if conv and need_cs:
                            psr = bank()
                            for k in range(8):
                                kb.mm(psr[:nr, 0:256], hnT[:, k, rows], ws[:, k, :], start=(k == 0), stop=(k == 7), inc=(k == 7))
                            stg = STGc[stgi[0] % 2]
                            stgi[0] += 1
                            kb.cp(stg[:nr, :], psr[:nr, 0:256], eng="act")
                            if sample:
                                for r in range(3):
                                    kb.dma("sp", o_sconv[l, :, r, col0:col0 + 256], stg[1 + r:CT:4, :], out_dram=True)
                            else:
                                kb.dma("sp", o_pconv[l, :, col0:col0 + 256], stg[:3, :], out_dram=True)
                        return ws

                    for pc in range(4):
                        ps = bank()
                        c0 = pc * 453
                        for k in range(8):
                            kb.mm(ps[:CT, :453], hnT[:, k, :CT], WinR[:, k, c0:c0 + 453], start=(k == 0), stop=(k == 7), inc=(k == 7))
                        kb.cp(R[:CT, pc * 453:(pc + 1) * 453], ps[:CT, :453], eng=("act" if pc % 2 else "dve"))
                    for pr in range(2):
                        ws = fm_pair(CONV_DIM + R_MQ + pr * 256, conv=False)
                        ps = bank()
                        for hh in range(4):
                            for k in range(8):
                                kb.mm(ps[:64, hh * CT:(hh + 1) * CT], ws[:, k, hh * 64:(hh + 1) * 64], hnT[:, k, :CT], start=(k == 0), stop=(k == 7), inc=(k == 7 and hh == 3))
                        if pr == 0:
                            kb.cp(QKm[:, 0:4, :CT], ps[:64, 0:4 * CT].re("p (b t) -> p b t", b=4))
                        else:
                            kb.ts(QKm[:, 4:8, :CT], ps[:64, 0:4 * CT].re("p (b t) -> p b t", b=4), 0.125, None, MUL)

                    if CUT <= 2:
                        return
                    def hp(i):
                        return HP[:CT, i, l, :]
                    kb.act(R[:CT, R_GG:R_GG + 512], R[:CT, R_GG:R_GG + 512], AF.Silu)
                    kb.act(R[:CT, R_SZ:R_SZ + 256], R[:CT, R_SZ:R_SZ + 256], AF.Silu)
                    kb.act(R[:CT, R_MO:R_MO + 256], R[:CT, R_MO:R_MO + 256], AF.Sigmoid)
                    kb.act(BETA[:CT], R[:CT, R_GB:R_GB + 4], AF.Sigmoid)
                    kb.tt(Z[:CT, 0:4], R[:CT, R_GA:R_GA + 4], hp(1), ADD)
                    kb.tt(Z[:CT, 4:8], R[:CT, R_MF:R_MF + 4], hp(3), ADD)
                    kb.ts(Z[:CT, 4:8], Z[:CT, 4:8], -1.0, None, MUL)
                    kb.tt(Z[:CT, 8:12], R[:CT, R_SDT:R_SDT + 4], hp(5), ADD)
                    kb.act(SP[:CT], Z[:CT], AF.Exp)
                    kb.act(SP[:CT], SP[:CT], AF.Ln, bias=1.0)
                    kb.tt(CS[:CT, 0:4], SP[:CT, 0:4], hp(0), MUL)
                    kb.ts(CS[:CT, 4:8], SP[:CT, 4:8], -1.0, None, MUL)
                    kb.tt(CS[:CT, 8:12], SP[:CT, 8:12], hp(4), MUL)
                    kb.ts(NBETA[:CT], BETA[:CT], -1.0, None, MUL)
                    kb.tt(LI[:CT], R[:CT, R_MI:R_MI + 4], hp(2), ADD)
                    def gates_mm():
                        ps = bank()
                        kb.mm(ps[:CT, 0:12], U, CS[:CT, 0:12], inc=False)
                        kb.mm(ps[:CT, 12:24], SSm, CS[:CT, 0:12])
                        kb.cp(GC[:CT], ps[:CT, 0:12])
                        kb.cp(GL[:CT], ps[:CT, 12:24])

                    yield "front"

                    def rowbc(colv, mat):
                        t = TMP["X1"]
                        kb.tt(v4(t, CT), mbc(mat), bc4(colv, CT, CT), MUL)
                        ps = bank()
                        kb.mm(ps[:, :4 * CT], ones[:CT, :], t[:CT, :4 * CT])
                        return ps

                    A, B, C, Dt, E, F, G2, H, I, X1 = (TMP[n] for n in ("A", "B", "C", "Dt", "E", "F", "G2", "H", "I", "X1"))

                    def mlstm_section(mA, mB, mI, mF):
                        psR = rowbc(CS[:, 4:8], U)
                        BL = P["BL"]
                        if sample:
                            kb.cp(BL, psR[:, :4 * CT].re("p (h s t) -> p h s t", h=4, t=4)[:, :, :, 3])
                        else:
                            kb.cp(BL, psR[:, :4 * CT].re("p (h j) -> p h j", h=4)[:, :, CT - 1:CT])
                        kb.tt(CC[:CT], LI[:CT], GC[:CT, 4:8], SUB)
                        yield
                        psC = rowbc(CC, ident[:CT, :CT])
                        Dm = v4(mA, CT)
                        kb.tt(Dm, psC[:CT, :4 * CT].re("p (h j) -> p h j", h=4), bc4(GC[:, 4:8], CT, CT), ADD)
                        kb.tt(Dm, Dm, mbc(Lm), ADD)
                        kb.red(MX[:CT], Dm, MAX)
                        yield
                        if sample:
                            ps = bank()
                            kb.mm(ps[:CT, 0:4], P["RMT"], P["m0"])
                            kb.cp(M0C[:CT], ps[:CT, 0:4])
                            kb.tt(INT[:CT], GC[:CT, 4:8], M0C[:CT], ADD)
                        else:
                            kb.tt(INT[:CT], GC[:CT, 4:8], P["MB"][:CT, :], ADD)
                        kb.tt(MT[:CT], INT[:CT], MX[:CT], MAX)
                        kb.ts(NMT[:CT], MT[:CT], -1.0, None, MUL)
                        yield
                        for h in range(4):
                            kb.act(mA[:CT, h * CT:(h + 1) * CT], mA[:CT, h * CT:(h + 1) * CT], AF.Exp, bias=NMT[:CT, h:h + 1])
                        kb.tt(WI[:CT], INT[:CT], MT[:CT], SUB)
                        kb.act(WI[:CT], WI[:CT], AF.Exp)
                        yield
                        psS_ = bank()
                        for h in range(4):
                            kb.mm(psS_[:CT, h * CT:(h + 1) * CT], QKm[:, h, :CT], QKm[:, 4 + h, :CT], inc=(h == 3))
                        kb.tt(Dm, Dm, psS_[:CT, :4 * CT].re("p (h j) -> p h j", h=4), MUL)
                        yield
                        ps = bank()
                        for h in range(4):
                            kb.tr(ps[:CT, h * CT:(h + 1) * CT], mA[:CT, h * CT:(h + 1) * CT], ident[:CT, :CT], inc=(h == 3))
                        kb.cp(mB[:CT, :4 * CT], ps[:CT, :4 * CT], eng="act")
                        yield
                        kb.cp(VE[:CT, :, 0:64], R[:CT, R_MV:R_MV + 256].re("p (h d) -> p h d", h=4))
                        psB = bank(hold=True)
                        for h in range(4):
                            kb.mm(psB[:CT, h * 65:(h + 1) * 65], mB[:CT, h * CT:(h + 1) * CT], VE[:CT, h, :], inc=(h == 3))
                        Xs = X1[:CT, 0:4 * nseq].re("p (h s) -> p h s", h=4)
                        kb.tt(Xs, MT[:CT].re("p (h o) -> p h o", o=1).bc([CT, 4, nseq]), lastsel[:CT, :].re("p (o s) -> p o s", o=1).bc([CT, 4, nseq]), MUL)
                        psM = bank()
                        kb.mm(psM[:, 0:4 * nseq], ones[:CT, :], X1[:CT, 0:4 * nseq])
                        MN_ = P["MN"]
                        WC = P["WC"]
                        kb.cp(MN_, psM[:, 0:4 * nseq].re("p (h s) -> p h s", h=4))
                        yield
                        kb.tt(WC, BL, P["M0R"], ADD)
                        kb.tt(WC, WC, MN_, SUB)
                        kb.act(WC, WC, AF.Exp)
                        yield
                        psBM = bank()
                        kb.mm(psBM[:CT, 0:4], LSm, NMT[:CT, 0:4])
                        kb.tt(WK[:CT], psBM[:CT, 0:4], GL[:CT, 4:8], ADD)
                        kb.tt(WK[:CT], WK[:CT], CC[:CT], ADD)
                        kb.act(WK[:CT], WK[:CT], AF.Exp)
                        kb.ts(WK[:CT], WK[:CT], 0.125, None, MUL)
                        kb.tt(KW[:CT], R[:CT, R_MK:R_MK + 256].re("p (h d) -> p h d", h=4), bc4(WK, CT, 64), MUL)
                        yield
                        psA = bank(hold=True)
                        if not sample:
                            CME = P["CME"]
                            for h in range(4):
                                kb.mm(psA[:CT, h * 65:(h + 1) * 65], QKm[:, h, :CT], CME[:, h, :], inc=(h == 3))
                            psN = bank()
                            for h in range(4):
                                kb.mm(psN[:64, h * 65:(h + 1) * 65], KW[:CT, h, :], VE[:CT, h, :], inc=(h == 3))
                            kb.tt(CME[:, :, :], CME[:, :, :], WC[:64, :, 0:1].bc([64, 4, 65]), MUL)
                            kb.tt(CME[:, :, :], CME[:, :, :], psN[:64, 0:260].re("p (h d) -> p h d", h=4), ADD)
                        for h in range(4):
                            if not sample:
                                pass
                            else:
                                CMs = P["CMs"]
                                Zt = P["Zt"]
                                ZK = P["ZK"]
                                kb.dma("sp", CMs[:, :, 0:64], st_mc[l, :, h].rearrange("s k v -> k s v"))
                                kb.dma("sp", CMs[:, :, 64:65], st_mn[l, :, h, :].rearrange("s (k o) -> k s o", o=1), allow_slow_non_contiguous=True)
                                zd = Zt[:64, 0:nseq * (CT + 4)].re("p (s r) -> p s r", r=CT + 4)[:, :, 0:4]
                                kb.cp(zd, QKm[:, h, :CT].re("p (s t) -> p s t", t=4))
                                for s_ in range(nseq):
                                    kb.mm(psA[:CT, h * 65:(h + 1) * 65], Zt[:64, s_ * CT:(s_ + 1) * CT], CMs[:, s_, :],
                                          start=(s_ == 0), stop=(s_ == nseq - 1), inc=(s_ == nseq - 1))
                                for g in range(nseq // 4):
                                    kb.tt(ZK[:CT, :, 0:64], KW[:CT, h, :].re("p (o d) -> p o d", o=1).bc([CT, 4, 64]),
                                          rowmask[:CT, 4 * g:4 * g + 4].re("p (s o) -> p s o", o=1).bc([CT, 4, 64]), MUL)
                                    psN = bank()
                                    for s4 in range(4):
                                        kb.mm(psN[:64, s4 * 65:(s4 + 1) * 65], ZK[:CT, s4, 0:64], VE[:CT, h, :], inc=(s4 == 3))
                                    wcb = WC[:64, h, 4 * g:4 * g + 4].re("p (s o) -> p s o", o=1).bc([64, 4, 65])
                                    kb.tt(CMs[:, 4 * g:4 * g + 4, :], CMs[:, 4 * g:4 * g + 4, :], wcb, MUL)
                                    kb.tt(CMs[:, 4 * g:4 * g + 4, :], CMs[:, 4 * g:4 * g + 4, :],
                                          psN[:64, 0:260].re("p (s d) -> p s d", s=4), ADD)
                                kb.dma("sp", o_smc[l, :, h].rearrange("s k v -> k s v"), CMs[:, :, 0:64], out_dram=True)
                                kb.dma("sp", o_smn[l, :, h, :].rearrange("s (k o) -> k s o", o=1), CMs[:, :, 64:65], out_dram=True, allow_slow_non_contiguous=True)
                        yield
                        if sample:
                            kb.dma("sp", o_smm[l].rearrange("(o s) h -> o h s", o=1), MN_[0:1, :, :], out_dram=True, allow_slow_non_contiguous=True)
                        else:
                            kb.cp(P["MB"], MN_[:, :, 0])
                        TOT = v4(mI, CT, 65)
                        kb.tt(TOT, psA[:CT, 0:260].re("p (h d) -> p h d", h=4), bc4(WI, CT, 65), MUL)
                        kb.tt(TOT, TOT, psB[:CT, 0:260].re("p (h d) -> p h d", h=4), ADD)
                        release(psA, psB)
                        yield
                        kb.act(DEN[:CT], mI[:CT, 64:260:65], AF.Abs)
                        kb.act(EM[:CT], NMT[:CT], AF.Exp)
                        kb.tt(DEN[:CT], DEN[:CT], EM[:CT], MAX)
                        kb.recip(DEN[:CT], DEN[:CT])
                        yield
                        HH = v4(X1, CT, 64)
                        kb.tt(HH, TOT[:, :, 0:64], bc4(DEN, CT, 64), MUL)
                        kb.tt(X1[:CT, 0:256], X1[:CT, 0:256], R[:CT, R_MO:R_MO + 256], MUL)
                        kb.act(v4(mF, CT, 64), HH, AF.Square)
                        kb.red(SS4[:CT], v4(mF, CT, 64), ADD)
                        yield
                        kb.act(RSTD4[:CT], SS4[:CT], AF.Ln, bias=EPS, scale=1.0 / 64)
                        kb.act(RSTD4[:CT], RSTD4[:CT], AF.Exp, scale=-0.5)
                        kb.tt(HH, HH, bc4(RSTD4, CT, 64), MUL)
                        kb.tt(MIX[:CT, 512:768].re("p (h d) -> p h d", h=4), HH, MNb[:CT, l, :].re("p (o d) -> p o d", o=1).bc([CT, 4, 64]), MUL)


                    hoist = (not sample) and HOIST_SSD
                    XPs = P["XP2"] if hoist else P["XP"]
                    COs = P["CO2"] if hoist else CO

                    def ssd_front():
                        for g in range(2):
                            nb = 4 if g == 0 else 2
                            wss = [fm_pair(1536 + g * 512 + q * 256) for q in range(nb // 2)]
                            ps = bank()
                            for j in range(nb):
                                ws = wss[j // 2]
                                for k in range(8):
                                    kb.mm(ps[:, j * CT:(j + 1) * CT], ws[:, k, (j % 2) * 128:(j % 2 + 1) * 128], hnT[:, k, :CT], start=(k == 0), stop=(k == 7), inc=(k == 7 and j == nb - 1))
                            if sample:
                                kb.cp(XPs[:, g * 4:g * 4 + nb, :, 3:7], ps[:, :nb * CT].re("p (b s t) -> p b s t", b=nb, t=4), eng="act")
                            else:
                                kb.cp(XPs[:, g * 4:g * 4 + nb, 3:3 + CT], ps[:, :nb * CT].re("p (b t) -> p b t", b=nb), eng="act")
                        conv_blocks(P, XPs, 12, 6, CT, COs)

                    XP = P["XP"]
                    for g in range(3):
                        if g == 2:
                            yield "mid"
                        wss = [fm_pair(g * 512), fm_pair(g * 512 + 256)]
                        ps = bank()
                        for j in range(4):
                            ws = wss[j // 2]
                            for k in range(8):
                                kb.mm(ps[:, j * CT:(j + 1) * CT], ws[:, k, (j % 2) * 128:(j % 2 + 1) * 128], hnT[:, k, :CT], start=(k == 0), stop=(k == 7), inc=(k == 7 and j == 3))
                        if sample:
                            kb.cp(XP[:, g * 4:g * 4 + 4, :, 3:7], ps[:, :4 * CT].re("p (b s t) -> p b s t", b=4, t=4), eng="act")
                        else:
                            kb.cp(XP[:, g * 4:g * 4 + 4, 3:3 + CT], ps[:, :4 * CT].re("p (b t) -> p b t", b=4), eng="act")
                    gates_mm()
                    conv_blocks(P, XP, 0, 12, CT)
                    if hoist:
                        ssd_front()
                    if CUT <= 4:
                        return
                    sqb_ = bfview(SQa)
                    kb.act(sqb_[:, :, :CT], CO[:, 0:8, :CT], AF.Square)
                    for half in range(2):
                        ps = bank()
                        for j in range(4):
                            kb.mm(ps[:, j * CT:(j + 1) * CT], ones_bf, sqb_[:, half * 4 + j, :CT], inc=(j == 3))
                        t = v4(X1, 128, CT)
                        kb.act(t, ps[:, :4 * CT].re("p (b t) -> p b t", b=4), AF.Ln, bias=EPS)
                        kb.act(t, t, AF.Exp, scale=-0.5)
                        if half == 0:
                            kb.ts(t, t, 128 ** -0.5, None, MUL)
                        kb.tt(CO[:, half * 4:half * 4 + 4, :CT], CO[:, half * 4:half * 4 + 4, :CT], t, MUL)
                    if CUT <= 5:
                        return
                    psR = rowbc(CS[:, 0:4], U)
                    kb.act(EG[:, :, :CT], psR[:, :4 * CT].re("p (h j) -> p h j", h=4), AF.Exp)
                    kb.tt(v4(X1, CT), bc4(GC[:, 0:4], CT, CT), psR[:CT, :4 * CT].re("p (h j) -> p h j", h=4), SUB)
                    kb.tt(v4(A, CT), v4(X1, CT), mbc(Ls), ADD)
                    kb.act(v4(A, CT), v4(A, CT), AF.Exp)
                    kb.tt(v4(Dt, CT), mbc(Um), v4(X1, CT), SUB)
                    kb.act(v4(Dt, CT), v4(Dt, CT), AF.Exp)
                    psK = bank()
                    for h in range(4):
                        kb.mm(psK[:CT, h * CT:(h + 1) * CT], CO[:, 4 + h, :CT], CO[:, 4 + h, :CT], inc=(h == 3))
                    kb.tt(v4(A, CT), v4(A, CT), psK[:CT, :4 * CT].re("p (h j) -> p h j", h=4), MUL)
                    kb.tt(v4(A, CT), v4(A, CT), bc4(NBETA, CT, CT), MUL)
                    psQ = bank()
                    for h in range(4):
                        kb.mm(psQ[:CT, h * CT:(h + 1) * CT], CO[:, 4 + h, :CT], CO[:, h, :CT], inc=(h == 3))
                    kb.tt(v4(Dt, CT), v4(Dt, CT), psQ[:CT, :4 * CT].re("p (h j) -> p h j", h=4), MUL)
                    if CUT <= 6:
                        return
                    ps = bank()
                    for h in range(4):
                        kb.tr(ps[:CT, h * CT:(h + 1) * CT], A[:CT, h * CT:(h + 1) * CT], ident[:CT, :CT], inc=(h == 3))
                    kb.cp(B[:CT, :4 * CT], ps[:CT, :4 * CT], eng="act")
                    kb.tt(v4(C, CT), v4(B, CT), mbc(ident[:CT, :CT]), ADD)
                    mg = mlstm_section(E, F, G2, H) if (ILV_ML and not sample) else None
                    for lev in range(1, P["lev"]):
                        last = (lev == P["lev"] - 1)
                        ps1 = bank()
                        for h in range(4):
                            hs = slice(h * CT, (h + 1) * CT)
                            kb.mm(ps1[:CT, hs], B[:CT, hs], A[:CT, hs], inc=(h == 3))
                        if not last:
                            ps2 = bank()
                            for h in range(4):
                                hs = slice(h * CT, (h + 1) * CT)
                                kb.mm(ps2[:CT, hs], A[:CT, hs], B[:CT, hs], inc=(h == 3))
                        if lev >= 2:
                            ps3 = bank()
                            for h in range(4):
                                hs = slice(h * CT, (h + 1) * CT)
                                kb.mm(ps3[:CT, hs], A[:CT, hs], C[:CT, hs], inc=(h == 3))
                        kb.cp(A[:CT, :4 * CT], ps1[:CT, :4 * CT])
                        if not last:
                            kb.cp(B[:CT, :4 * CT], ps2[:CT, :4 * CT], eng="act")
                        if lev >= 2:
                            kb.tt(C[:CT, :4 * CT], C[:CT, :4 * CT], ps3[:CT, :4 * CT], ADD)
                        if mg is not None:
                            _step(mg)
                            _step(mg)
                    ps3 = bank()
                    for h in range(4):
                        hs = slice(h * CT, (h + 1) * CT)
                        kb.mm(ps3[:CT, hs], A[:CT, hs], C[:CT, hs], inc=(h == 3))
                    kb.tt(C[:CT, :4 * CT], C[:CT, :4 * CT], ps3[:CT, :4 * CT], ADD)
                    if mg is not None:
                        while _step(mg):
                            pass
                    if CUT <= 7:
                        return
                    kb.act(BE[:CT], GC[:CT, 0:4], AF.Exp)
                    kb.tt(BE[:CT], BE[:CT], BETA[:CT], MUL)
                    kb.tt(KD[:CT], GL[:CT, 0:4], GC[:CT, 0:4], SUB)
                    kb.act(KD[:CT], KD[:CT], AF.Exp)
                    psKt = bank()
                    for h in range(4):
                        kb.tr(psKt[:CT, h * 128:(h + 1) * 128], CO[:, 4 + h, :CT], ident, inc=(h == 3))
                    pk = psKt[:CT, :].re("p (h d) -> p h d", h=4)
                    kb.tt(v4(F, CT, 128), pk, bc4(BE, CT, 128), MUL)
                    kb.tt(v4(G2, CT, 128), pk, bc4(KD, CT, 128), MUL)
                    psVt = bank()
                    for h in range(4):
                        kb.tr(psVt[:CT, h * 128:(h + 1) * 128], CO[:, 8 + h, :CT], ident, inc=(h == 3))
                    kb.tt(v4(E, CT, 128), psVt[:CT, :].re("p (h d) -> p h d", h=4), bc4(BETA, CT, 128), MUL)
                    psW = bank()
                    for h in range(4):
                        kb.mm(psW[:, h * CT:(h + 1) * CT], F[:CT, h * 128:(h + 1) * 128], C[:CT, h * CT:(h + 1) * CT], inc=(h == 3))
                    kb.ts(H[:, :4 * CT], psW[:, :4 * CT], -1.0, None, MUL)
                    kb.tt(v4(I, 128, CT), CO[:, 0:4, :CT], EG[:, :, :CT], MUL)
                    if CUT <= 8:
                        return
                    psV = bank(hold=True)
                    psO = bank(hold=True)
                    if not sample:
                        S = P["Sg"]
                        for h in range(4):
                            hs = slice(h * CT, (h + 1) * CT)
                            hd = slice(h * 128, (h + 1) * 128)
                            kb.mm(psV[:CT, hd], C[:CT, hs], E[:CT, hd], start=True, stop=False, inc=False)
                            kb.mm(psV[:CT, hd], H[:, hs], S[:, h, :], start=False, stop=True, inc=(h == 3))
                        kb.cp(E[:CT, :], psV[:CT, :])
                        for h in range(4):
                            hs = slice(h * CT, (h + 1) * CT)
                            hd = slice(h * 128, (h + 1) * 128)
                            kb.mm(psO[:CT, hd], I[:, hs], S[:, h, :], start=True, stop=False, inc=False)
                            kb.mm(psO[:CT, hd], Dt[:CT, hs], E[:CT, hd], start=False, stop=True, inc=(h == 3))
                        psS = bank()
                        for h in range(4):
                            hd = slice(h * 128, (h + 1) * 128)
                            kb.mm(psS[:, hd], G2[:CT, hd], E[:CT, hd], inc=(h == 3))
                        kb.tt(S[:, :, :], S[:, :, :], EG[:, :, CT - 1:CT].bc([128, 4, 128]), MUL)
                        kb.tt(S[:, :, :], S[:, :, :], psS[:, :].re("p (h d) -> p h d", h=4), ADD)
                    for h in range(4):
                        hs = slice(h * CT, (h + 1) * CT)
                        hd = slice(h * 128, (h + 1) * 128)
                        if not sample:
                            pass
                        else:
                            Ss = P["Ss"]
                            Zt = P["Zt"]
                            ZK = P["ZK"]
                            kb.dma("sp", Ss, st_gdn[l, :, h].rearrange("s k v -> k s v"))
                            zd = Zt[:, 0:nseq * (CT + 4)].re("p (s r) -> p s r", r=CT + 4)[:, :, 0:4]
                            kb.cp(zd, H[:, hs].re("p (s t) -> p s t", t=4))
                            kb.mm(psV[:CT, hd], C[:CT, hs], E[:CT, hd], start=True, stop=False, inc=False)
                            for s in range(nseq):
                                kb.mm(psV[:CT, hd], Zt[:, s * CT:(s + 1) * CT], Ss[:, s, :], start=False, stop=(s == nseq - 1), inc=(s == nseq - 1))
                            kb.cp(E[:CT, hd], psV[:CT, hd])
                            kb.cp(zd, I[:, hs].re("p (s t) -> p s t", t=4))
                            for s in range(nseq):
                                kb.mm(psO[:CT, hd], Zt[:, s * CT:(s + 1) * CT], Ss[:, s, :], start=(s == 0), stop=False, inc=False)
                            kb.mm(psO[:CT, hd], Dt[:CT, hs], E[:CT, hd], start=False, stop=True)
                            for g in range(nseq // 4):
                                kb.tt(ZK[:CT], G2[:CT, hd].re("p (o d) -> p o d", o=1).bc([CT, 4, 128]),
                                      rowmask[:CT, 4 * g:4 * g + 4].re("p (s o) -> p s o", o=1).bc([CT, 4, 128]), MUL)
                                psS = bank()
                                for s4 in range(4):
                                    kb.mm(psS[:, s4 * 128:(s4 + 1) * 128], ZK[:CT, s4, :], E[:CT, hd], inc=(s4 == 3))
                                egl = EG[:, h, 16 * g + 3:16 * g + 16:4].re("p (s o) -> p s o", o=1).bc([128, 4, 128])
                                kb.tt(Ss[:, 4 * g:4 * g + 4, :], Ss[:, 4 * g:4 * g + 4, :], egl, MUL)
                                kb.tt(Ss[:, 4 * g:4 * g + 4, :], Ss[:, 4 * g:4 * g + 4, :], psS[:, :].re("p (s d) -> p s d", s=4), ADD)
                            kb.dma("sp", o_sgdn[l, :, h].rearrange("s k v -> k s v"), Ss, out_dram=True)
                    po = psO[:CT, :].re("p (h d) -> p h d", h=4)
                    kb.act(v4(X1, CT, 128), po, AF.Square)
                    kb.red(SS4[:CT], v4(X1, CT, 128), ADD)
                    kb.act(RSTD4[:CT], SS4[:CT], AF.Ln, bias=EPS, scale=1.0 / 128)
                    kb.act(RSTD4[:CT], RSTD4[:CT], AF.Exp, scale=-0.5)
                    kb.tt(v4(X1, CT, 128), po, bc4(RSTD4, CT, 128), MUL)
                    kb.tt(v4(X1, CT, 128), v4(X1, CT, 128), GN[:CT, l, :].re("p (o d) -> p o d", o=1).bc([CT, 4, 128]), MUL)
                    kb.tt(MIX[:CT, 0:512], X1[:CT, :], R[:CT, R_GG:R_GG + 512], MUL)
                    release(psV, psO)

                    if CUT <= 9:
                        return
                    if not ILV_ML or sample:
                        for _ in mlstm_section(A, B, I, F):
                            pass

                    if nxt is not None and PRENORM:
                        rmsnorm_to(HN[nxt % 2], nxt, 0, l, ctok(nxt))
                        normed["c"] = nxt
                    if not hoist:
                        ssd_front()
                    psR = rowbc(CS[:, 8:12], U)
                    kb.act(EG[:, :, :CT], psR[:, :4 * CT].re("p (h j) -> p h j", h=4), AF.Exp)
                    kb.tt(v4(X1, CT), bc4(GC[:, 8:12], CT, CT), psR[:CT, :4 * CT].re("p (h j) -> p h j", h=4), SUB)
                    kb.tt(v4(Dt, CT), mbc(Um), v4(X1, CT), SUB)
                    kb.act(v4(Dt, CT), v4(Dt, CT), AF.Exp)
                    psC = bank()
                    for g in range(2):
                        kb.mm(psC[:CT, g * CT:(g + 1) * CT], COs[:, 2 + g, :CT], COs[:, 4 + g, :CT], inc=(g == 1))
                    Dt4 = Dt[:CT, :4 * CT].re("p (g e j) -> p g e j", g=2, e=2)
                    kb.tt(Dt4, Dt4, psC[:CT, :2 * CT].re("p (g o j) -> p g o j", g=2, o=1).bc([CT, 2, 2, CT]), MUL)
                    kb.tt(v4(Dt, CT), v4(Dt, CT), bc4(SP[:, 8:12], CT, CT), MUL)
                    psX = bank()
                    for j in range(2):
                        kb.tr(psX[:CT, j * 128:(j + 1) * 128], COs[:, j, :CT], ident, inc=False)
                    for j in range(2):
                        kb.tr(psX[:CT, 256 + j * 128:256 + (j + 1) * 128], COs[:, 2 + j, :CT], ident, inc=(j == 1))
                    kb.cp(XT_[:CT, :], psX[:CT, 0:256])
                    kb.tt(KDS[:CT], GL[:CT, 8:12], GC[:CT, 8:12], SUB)
                    kb.act(KDS[:CT], KDS[:CT], AF.Exp)
                    kb.tt(KDS[:CT], KDS[:CT], SP[:CT, 8:12], MUL)
                    BW = F
                    kb.tt(BW[:CT, :].re("p (g e n) -> p g e n", g=2, e=2), psX[:CT, 256:512].re("p (g o n) -> p g o n", g=2, o=1).bc([CT, 2, 2, 128]),
                          KDS[:CT].re("p (g e o) -> p g e o", g=2, o=1).bc([CT, 2, 2, 128]), MUL)
                    CG = I
                    kb.tt(CG[:, :4 * CT].re("p (g e j) -> p g e j", g=2, e=2), COs[:, 4:6, :CT].re("p g (o j) -> p g o j", o=1).bc([128, 2, 2, CT]),
                          EG[:, :, :CT].re("p (g e) j -> p g e j", g=2), MUL)
                    psY = bank(hold=True)
                    if not sample:
                        HT = P["HT"]
                        for h in range(4):
                            hs = slice(h * CT, (h + 1) * CT)
                            hp_ = slice(h * 64, (h + 1) * 64)
                            kb.mm(psY[:CT, hp_], Dt[:CT, hs], XT_[:CT, hp_], start=True, stop=False, inc=False)
                            kb.mm(psY[:CT, hp_], CG[:, hs], HT[:, h, :], start=False, stop=True, inc=(h == 3))
                        psH = bank()
                        for h in range(4):
                            hp_ = slice(h * 64, (h + 1) * 64)
                            hn_ = slice(h * 128, (h + 1) * 128)
                            kb.mm(psH[:, hp_], BW[:CT, hn_], XT_[:CT, hp_], inc=(h == 3))
                        kb.tt(HT[:, :, :], HT[:, :, :], EG[:, :, CT - 1:CT].bc([128, 4, 64]), MUL)
                        kb.tt(HT[:, :, :], HT[:, :, :], psH[:, 0:256].re("p (h d) -> p h d", h=4), ADD)
                    for h in range(4):
                        hs = slice(h * CT, (h + 1) * CT)
                        hp_ = slice(h * 64, (h + 1) * 64)
                        hn_ = slice(h * 128, (h + 1) * 128)
                        if not sample:
                            pass
                        else:
                            HTs = P["HTs"]
                            STG = P["STG"]
                            Zt = P["Zt"]
                            ZK = P["ZK"]
                            for g in range(nseq // 4):
                                kb.dma("sp", STG, st_ssd[l, 4 * g:4 * g + 4, h].rearrange("s p n -> p s n"))
                                ps = bank()
                                for s4 in range(4):
                                    kb.tr(ps[:, s4 * 64:(s4 + 1) * 64], STG[:, s4, :], ident[:64, :64], inc=(s4 == 3))
                                kb.cp(HTs[:, 4 * g:4 * g + 4, :], ps[:, 0:256].re("p (s d) -> p s d", s=4))
                            zd = Zt[:, 0:nseq * (CT + 4)].re("p (s r) -> p s r", r=CT + 4)[:, :, 0:4]
                            kb.cp(zd, CG[:, hs].re("p (s t) -> p s t", t=4))
                            kb.mm(psY[:CT, hp_], Dt[:CT, hs], XT_[:CT, hp_], start=True, stop=False, inc=False)
                            for s in range(nseq):
                                kb.mm(psY[:CT, hp_], Zt[:, s * CT:(s + 1) * CT], HTs[:, s, :], start=False, stop=(s == nseq - 1), inc=(s == nseq - 1))
                            for g in range(nseq // 4):
                                kb.tt(ZK[:CT], BW[:CT, hn_].re("p (o d) -> p o d", o=1).bc([CT, 4, 128]),
                                      rowmask[:CT, 4 * g:4 * g + 4].re("p (s o) -> p s o", o=1).bc([CT, 4, 128]), MUL)
                                psH = bank()
                                for s4 in range(4):
                                    kb.mm(psH[:, s4 * 64:(s4 + 1) * 64], ZK[:CT, s4, :], XT_[:CT, hp_], inc=(s4 == 3))
                                egl = EG[:, h, 16 * g + 3:16 * g + 16:4].re("p (s o) -> p s o", o=1).bc([128, 4, 64])
                                kb.tt(HTs[:, 4 * g:4 * g + 4, :], HTs[:, 4 * g:4 * g + 4, :], egl, MUL)
                                kb.tt(HTs[:, 4 * g:4 * g + 4, :], HTs[:, 4 * g:4 * g + 4, :], psH[:, 0:256].re("p (s d) -> p s d", s=4), ADD)
                                ps = bank()
                                for s4 in range(4):
                                    kb.tr(ps[:64, s4 * 128:(s4 + 1) * 128], HTs[:, 4 * g + s4, :], ident, inc=(s4 == 3))
                                kb.cp(STG, ps[:64, :].re("p (s n) -> p s n", s=4))
                                kb.dma("sp", o_sssd[l, 4 * g:4 * g + 4, h].rearrange("s p n -> p s n"), STG, out_dram=True)
                    Y1 = X1
                    kb.tt(v4(Y1, CT, 64), XT_[:CT, :].re("p (h d) -> p h d", h=4), bc4(HP[:, 6, l, :], CT, 64), MUL)
                    kb.tt(Y1[:CT, 0:256], Y1[:CT, 0:256], psY[:CT, 0:256], ADD)
                    release(psY)
                    kb.tt(Y1[:CT, 0:256], Y1[:CT, 0:256], R[:CT, R_SZ:R_SZ + 256], MUL)
                    kb.act(H[:CT, 0:256], Y1[:CT, 0:256], AF.Square)
                    kb.red(SS1[:CT], H[:CT, 0:256], ADD)
                    kb.act(SS1[:CT], SS1[:CT], AF.Ln, bias=EPS, scale=1.0 / 256)
                    kb.act(SS1[:CT], SS1[:CT], AF.Exp, scale=-0.5)
                    kb.stt(MIX[:CT, 768:1024], Y1[:CT, 0:256], SS1[:CT, 0:1], SN[:CT, l, :], MUL, MUL)

                    if DEBUG and sample and l == 0:
                        kb.dma("sp", dbg_mix, MIX[:CT, :], out_dram=True)
                    if CUT <= 11:
                        return
                    yield "preout"
                    for half in range(2):
                        ps = bank()
                        for j in range(4):
                            k = half * 4 + j
                            kb.tr(ps[:, j * CT:(j + 1) * CT], MIX[:CT, k * 128:(k + 1) * 128], ident[:CT, :CT], inc=(j == 3))
                        kb.cp(mixT[:, half * 4:half * 4 + 4, :CT], ps[:, :4 * CT].re("p (k t) -> p k t", k=4), eng=("act" if half else "dve"))
                    for half in range(2):
                        ps = bank()
                        for j in range(4):
                            db = half * 4 + j
                            for k in range(8):
                                kb.mm(ps[:, j * CT:(j + 1) * CT], Wout[:, k, db * 128:(db + 1) * 128], mixT[:, k, :CT], start=(k == 0), stop=(k == 7), inc=(k == 7 and j == 3))
                        kb.cp(SQa[:, half * 4:half * 4 + 4, :CT], ps[:, :4 * CT].re("p (k t) -> p k t", k=4), eng=("act" if half else "dve"))
                    postnorm_add(SQa, c, 1, l, CT)

                def run_chunks(clist, P):
                    def step(g):
                        try:
                            next(g)
                            return True
                        except StopIteration:
                            return False
                    prev = None
                    for c in clist:
                        ci = clist.index(c)
                        g = mixer_chunk(c, P, clist[ci + 1] if ci + 1 < len(clist) else None)
                        alive = step(g)
                        if alive:
                            alive = step(g)
                        if prev is not None:
                            while step(prev):
                                pass
                        if alive:
                            alive = step(g)
                        prev = g if alive else None
                    if prev is not None:
                        while step(prev):
                            pass

                def conv_blocks(P, XP, b0, nb, CT, CO=None):
                    CO = CO_main if CO is None else CO
                    sample = P["sample"]
                    nseq_ = P["nseq"]
                    if sample:
                        HS = P["HS"]
                        kb.cp(XP[:, 0:nb, :, 0:3], HS[:, b0:b0 + nb, :, :])
                    else:
                        Hh = P["Hh"]
                        kb.cp(XP[:, 0:nb, 0:3], Hh[:, b0:b0 + nb, :])
                    for t in range(4):
                        for j in range(nb):
                            b = b0 + j
                            if sample:
                                o = CO[:, j, :CT].re("p (s t) -> p s t", t=4)
                                xi_ = XP[:, j, :, t:t + 4]
                            else:
                                o = CO[:, j, :CT]
                                xi_ = XP[:, j, t:t + CT]
                            if t == 0:
                                kb.act(o, xi_, AF.Identity, bias=CB[:, l, b:b + 1], scale=CW[:, l, b, 0:1])
                            else:
                                kb.stt(o, xi_, CW[:, l, b, t:t + 1], o, MUL, ADD)
                    if not sample:
                        kb.cp(P["Hh"][:, b0:b0 + nb, :], XP[:, 0:nb, CT:CT + 3])
                    kb.act(CO[:, 0:nb, :CT], CO[:, 0:nb, :CT], AF.Silu)

                with ExitStack() as es3:
                    kb.es = es3
                    Pp = dict(CT=128, nseq=1, sample=False, cm=cmp_, lastsel=lsp, lev=7)
                    Pp["XP"] = kb.sb("XPp", [128, 12, 131])
                    HN[1] = kb.sb("hnT1", [128, 8, 128], BF16)
                    if HOIST_SSD:
                        Pp["XP2"] = kb.sb("XP2", [128, 6, 131])
                        Pp["CO2"] = kb.sb("CO2", [128, 6, 128])
                    Pp["WS"] = [kb.sb("WSp%d" % i, [128, 8, 256], BF16) for i in range(5)]
                    Pp["wsi"] = [0]
                    Pp["Hh"] = kb.sb("Hh", [128, 18, 3])
                    Pp["Sg"] = kb.sb("Sg", [128, 4, 128])
                    Pp["CME"] = kb.sb("CME", [64, 4, 65])
                    Pp["HT"] = kb.sb("HT", [128, 4, 64])
                    Pp["MB"] = kb.sb("MB", [128, 4])
                    Pp["BL"] = kb.sb("BLp", [128, 4, 1])
                    Pp["MN"] = kb.sb("MNp", [128, 4, 1])
                    Pp["WC"] = kb.sb("WCp", [128, 4, 1])
                    Pp["M0R"] = Pp["MB"][:, :].re("p (h o) -> p h o", o=1)
                    for nme in ("Hh", "Sg", "CME", "HT", "MB"):
                        kb.memset(Pp[nme], 0.0)
                    if STOP & 1:
                        run_chunks(list(range(NPC)), Pp)
                    kb.dma("sp", o_pgdn[l].rearrange("h k v -> k h v"), Pp["Sg"], out_dram=True)
                    for h in range(4):
                        kb.dma("sp", o_pmc[l, h], Pp["CME"][:, h, 0:64], out_dram=True)
                        kb.dma("sp", o_pmn[l, h].rearrange("(k o) -> k o", o=1), Pp["CME"][:, h, 64:65], out_dram=True, allow_slow_non_contiguous=True)
                    kb.dma("sp", o_pmm[l].rearrange("(o h) -> o h", o=1), Pp["MB"][0:1, :], out_dram=True)
                    ps = bank()
                    for h in range(4):
                        kb.tr(ps[:64, h * 128:(h + 1) * 128], Pp["HT"][:, h, :], ident, inc=(h == 3))
                    kb.cp(TMP["A"][:64, :], ps[:64, :])
                    kb.dma("sp", o_pssd[l].rearrange("h p n -> p h n"), TMP["A"][:64, :].re("p (h n) -> p h n", h=4), out_dram=True)
                    barrier()
                with ExitStack() as es3:
                    kb.es = es3
                    Psm = dict(CT=NST, nseq=NS, sample=True, cm=cms, lastsel=lss, lev=2)
                    Psm["XP"] = kb.sb("XPs", [128, 12, NS, 7])
                    Psm["WS"] = [kb.sb("WSs%d" % i, [128, 8, 256], BF16) for i in range(2)]
                    Psm["wsi"] = [0]
                    Psm["HS"] = kb.sb("HS", [128, 18, NS, 3])
                    SST = kb.sb("SST", [128, NS * 128])
                    Psm["Ss"] = SST[:, :].re("p (s d) -> p s d", s=NS)
                    Psm["Zt"] = kb.sb("Zt", [128, (NS + 1) * NST])
                    Psm["ZK"] = kb.sb("ZK", [64, 4, 128])
                    Psm["CMs"] = SST[:64, 0:NS * 65].re("p (s d) -> p s d", s=NS)
                    Psm["HTs"] = SST[:, 0:NS * 64].re("p (s d) -> p s d", s=NS)
                    Psm["STG"] = Psm["ZK"]
                    Psm["BL"] = kb.sb("BLs", [128, 4, NS])
                    Psm["MN"] = kb.sb("MNs", [128, 4, NS])
                    Psm["WC"] = kb.sb("WCs", [128, 4, NS])
                    m0r = kb.sb("M0Rs", [128, NS, 4])
                    Psm["m0"] = kb.sb("m0s", [NS, 4])
                    Psm["RMT"] = kb.sb("RMT", [NS, NST])
                    kb.memset(Psm["Zt"], 0.0)
                    kb.dma("sp", m0r, st_mm[l].partition_broadcast(128))
                    kb.dma("sp", Psm["m0"], st_mm[l])
                    Psm["M0R"] = m0r[:, :, :].re("p s h -> p h s")
                    ps = bank()
                    kb.tr(ps[:NS, :NST], rowmask[:NST, :NS], ident[:NST, :NST])
                    kb.cp(Psm["RMT"], ps[:NS, :NST])
                    for b in range(18):
                        stg_ = TMP["E"] if b % 2 else TMP["F"]
                        kb.dma("sp", stg_[:NS * 3, 0:128], st_conv[l, :, :, b * 128:(b + 1) * 128].rearrange("s r c -> (s r) c"))
                        ps = bank()
                        kb.tr(ps[:, :NS * 3], stg_[:NS * 3, 0:128], ident[:NS * 3, :NS * 3])
                        kb.cp(Psm["HS"][:, b, :, :], ps[:, :NS * 3].re("p (s r) -> p s r", r=3), eng=("act" if b % 2 else "dve"))
                    if STOP & 2:
                        run_chunks([NPC], Psm)
                    barrier()
            with ExitStack() as es2:
                kb.es = es2
                groups = []
                cs_ = list(range(NPC))
                GSZ = 6
                for g0 in range(0, NPC, GSZ):
                    groups.append(cs_[g0:g0 + GSZ])
                groups[-1] = groups[-1] + [NPC]
                MAXT = max(sum(ctok(c) for c in g) for g in groups)
                hn2 = kb.sb("hn2", [128, 8, MAXT], BF16)
                ACTT = kb.sb("ACTT", [128, NFB, MAXT], BF16)
                WU = [kb.sb("WU%d" % i, [128, 8, 256], BF16) for i in range(NWU)]
                WD = [kb.sb("WD%d" % i, [128, NFB, 128], BF16) for i in range(NWD)]
                GP = kb.sb("GP", [128, 2 + GSZ * 128])
                GS = kb.sb("GS", [128, NS, 6])
                VV2 = [kb.sb("VV%d" % i, [128, MAXT]) for i in range(2)]
                ACC2 = [kb.sb("ACC%d" % i, [128, MAXT]) for i in range(2)]
                T12 = [kb.sb("T1%d" % i, [128, MAXT]) for i in range(2)]
                YF = kb.sb("YF", [128, 8, max(MAXT, 704)])
                HF = kb.sb("HF", [128, NFB, 2])
                HSf = kb.sb("HSf", [128, NFB, NS, 2])
                STGf = YF[:, :, :].re("p k t -> p (k t)")
                pass
                kb.memset(HF, 0.0)
                kb.dma("sp", STGf[:NS * 2, 0:DFF], st_ffn[l].rearrange("s r c -> (s r) c"))
                for b in range(NFB):
                    ps = bank()
                    kb.tr(ps[:, :NS * 2], STGf[:NS * 2, b * 128:(b + 1) * 128], ident[:NS * 2, :NS * 2])
                    kb.cp(HSf[:, b, :, :], ps[:, :NS * 2].re("p (s r) -> p s r", r=2), eng=("act" if b % 2 else "dve"))
                wu_i = 0
                wd_i = 0
                for gi, grp in enumerate(groups):
                    if not (STOP & 4):
                        continue
                    has_s = (grp[-1] == NPC)
                    pch = [c for c in grp if c < NPC]
                    npt = len(pch) * 128
                    ntg = npt + (NST if has_s else 0)
                    off = 0
                    offs = {}
                    for c in grp:
                        CT = ctok(c)
                        offs[c] = off
                        rmsnorm_to(hn2[:, :, off:off + CT], c, 2, l, CT)
                        off += CT
                    tbs = [(t0, min(512, ntg - t0)) for t0 in range(0, ntg, 512)]
                    for fb in range(NFB):
                        wu = WU[wu_i % NWU]
                        VV, ACC, T1 = VV2[fb % 2], ACC2[fb % 2], T12[fb % 2]
                        wu_i += 1
                        kb.dma("pool", wu[:, :, 0:128], w_up[l, :, fb * 128:(fb + 1) * 128].rearrange("(k p) c -> p k c", p=128))
                        kb.dma("pool", wu[:, :, 128:256], w_up[l, :, DFF + fb * 128:DFF + (fb + 1) * 128].rearrange("(k p) c -> p k c", p=128))
                        if (NPC - 1) in grp:
                            r0 = offs[NPC - 1] + 126
                            ps = bank()
                            for k in range(8):
                                kb.mm(ps[:2, 0:128], hn2[:, k, r0:r0 + 2], wu[:, k, 0:128], start=(k == 0), stop=(k == 7), inc=(k == 7))
                            kb.cp(STGf[:2, fb * 128:(fb + 1) * 128], ps[:2, 0:128], eng="act")
                        if has_s:
                            r0 = offs[NPC]
                            ps = bank()
                            for k in range(8):
                                kb.mm(ps[:NST, 0:128], hn2[:, k, r0:r0 + NST], wu[:, k, 0:128], start=(k == 0), stop=(k == 7), inc=(k == 7))
                            kb.cp(STGf[:NST, DFF + fb * 128:DFF + (fb + 1) * 128], ps[:NST, 0:128], eng="act")
                        for (t0, tw) in tbs:
                            psG = bank()
                            for k in range(8):
                                kb.mm(psG[:, :tw], wu[:, k, 0:128], hn2[:, k, t0:t0 + tw], start=(k == 0), stop=(k == 7), inc=(k == 7))
                            psV = bank()
                            for k in range(8):
                                kb.mm(psV[:, :tw], wu[:, k, 128:256], hn2[:, k, t0:t0 + tw], start=(k == 0), stop=(k == 7), inc=(k == 7))
                            p1 = min(t0 + tw, npt)
                            if p1 > t0:
                                kb.cp(GP[:, 2 + t0:2 + p1], psG[:, 0:p1 - t0], eng="act")
                            if t0 + tw > npt:
                                s0 = max(t0, npt) - t0
                                kb.cp(GS[:, :, 2:6], psG[:, s0:s0 + NST].re("p (s t) -> p s t", t=4), eng="act")
                            kb.cp(VV[:, t0:t0 + tw], psV[:, :tw], eng="act")
                        if npt > 0:
                            kb.cp(GP[:, 0:2], HF[:, fb, :])
                            kb.act(ACC[:, 0:npt], GP[:, 0:npt], AF.Identity, bias=FCB[:, l, fb:fb + 1], scale=FCW[:, l, fb, 0:1])
                            kb.stt(ACC[:, 0:npt], GP[:, 1:1 + npt], FCW[:, l, fb, 1:2], ACC[:, 0:npt], MUL, ADD)
                            kb.stt(ACC[:, 0:npt], GP[:, 2:2 + npt], FCW[:, l, fb, 2:3], ACC[:, 0:npt], MUL, ADD)
                            kb.cp(HF[:, fb, :], GP[:, npt:npt + 2])
                        if has_s:
                            kb.cp(GS[:, :, 0:2], HSf[:, fb, :, :])
                            a_s = ACC[:, npt:npt + NST].re("p (s t) -> p s t", t=4)
                            kb.act(a_s, GS[:, :, 0:4], AF.Identity, bias=FCB[:, l, fb:fb + 1], scale=FCW[:, l, fb, 0:1])
                            kb.stt(a_s, GS[:, :, 1:5], FCW[:, l, fb, 1:2], a_s, MUL, ADD)
                            kb.stt(a_s, GS[:, :, 2:6], FCW[:, l, fb, 2:3], a_s, MUL, ADD)
                        kb.act(T1[:, :ntg], ACC[:, :ntg], AF.Square, scale=0.044715 ** 0.5)
                        kb.stt(T1[:, :ntg], T1[:, :ntg], 1.0, ACC[:, :ntg], ADD, MUL)
                        kb.act(T1[:, :ntg], T1[:, :ntg], AF.Sigmoid, scale=1.5957691216057308)
                        kb.tt(ACC[:, :ntg], ACC[:, :ntg], VV[:, :ntg], MUL)
                        kb.tt(ACTT[:, fb, :ntg], T1[:, :ntg], ACC[:, :ntg], MUL)
                    if (NPC - 1) in grp:
                        kb.dma("sp", o_pffn[l], STGf[:2, 0:DFF], out_dram=True)
                    if has_s:
                        for r in range(2):
                            kb.dma("sp", o_sffn[l, :, r, :], STGf[2 + r:NST:4, DFF:2 * DFF], out_dram=True)
                    for db in range(8):
                        wd = WD[wd_i % NWD]
                        wd_i += 1
                        kb.dma("pool", wd, w_down[l, :, db * 128:(db + 1) * 128].rearrange("(f p) c -> p f c", p=128))
                        for (t0, tw) in tbs:
                            ps = bank()
                            for fb in range(NFB):
                                kb.mm(ps[:, :tw], wd[:, fb, :], ACTT[:, fb, t0:t0 + tw], start=(fb == 0), stop=(fb == NFB - 1), inc=(fb == NFB - 1))
                            kb.cp(YF[:, db, t0:t0 + tw], ps[:, :tw], eng=("act" if db % 2 else "dve"))
                    for c in grp:
                        CT = ctok(c)
                        kb.cp(SQa[:, :, :CT], YF[:, :, offs[c]:offs[c] + CT])
                        postnorm_add(SQa, c, 3, l, CT)
                barrier()

        with ExitStack() as es1:
            kb.es = es1
            yo = [kb.sb("yo%d" % i, [128, D]) for i in range(2)]
            for c in range(NCH):
                CT = ctok(c)
                yt = yo[c % 2]
                for half in range(2):
                    ps = bank()
                    for j in range(4):
                        k = half * 4 + j
                        kb.tr(ps[:CT, j * 128:(j + 1) * 128], xT[c][:, k, :CT], ident, inc=(j == 3))
                    kb.cp(yt[:CT, half * 512:(half + 1) * 512], ps[:CT, :], eng=("act" if half else "dve"))
                dst = y_p[c * 128:(c + 1) * 128, :] if c < NPC else y_s
                kb.dma("sp", dst, yt[:CT, :], out_dram=True)
        kb.finish()
    return nc


IN_NAMES = ["norm_mix_pre", "norm_mix_post", "norm_ffn_pre", "norm_ffn_post", "w_in", "conv_w", "conv_b",
            "gdn_a_log", "gdn_dt_bias", "gdn_norm", "mlstm_i_bias", "mlstm_f_bias", "mlstm_norm",
            "ssd_a_log", "ssd_dt_bias", "ssd_d", "ssd_norm", "w_out", "ffn_w_up", "ffn_conv_w", "ffn_conv_b", "ffn_w_down"]
ST_MAP = [("st_conv", "state_conv"), ("st_gdn", "state_gdn"), ("st_mc", "state_mlstm_c"), ("st_mn", "state_mlstm_n"),
          ("st_mm", "state_mlstm_m"), ("st_ssd", "state_ssd"), ("st_ffn", "state_ffn_conv")]
_NC_CACHE = {}


def make_in_maps(inputs, NPC, NS, DEPTH, ncores):
    consts = make_consts(NS)
    f = lambda a: np.ascontiguousarray(np.asarray(a, dtype=np.float32))
    shared = {n: f(inputs[n])[:DEPTH] for n in IN_NAMES}
    shared.update(consts)
    maps = []
    for c in range(ncores):
        m = dict(shared)
        m["xp"] = f(inputs["x_prompt"][c, :NPC * 128])
        m["xs"] = f(inputs["x_sample"][c * NS:(c + 1) * NS]).reshape(NS * 4, D)
        for k, n in ST_MAP:
            m[k] = f(inputs[n][:DEPTH, c * NS:(c + 1) * NS])
        maps.append(m)
    return maps


def assemble(results, NPC, NS, DEPTH):
    cat = lambda k, ax: np.concatenate([r[k] for r in results], axis=ax)
    y_p = np.stack([r["y_p"] for r in results], 0)
    y_s = np.concatenate([r["y_s"].reshape(NS, 4, D) for r in results], 0)
    outs = [y_p, y_s]
    for k in ("p_conv", "p_gdn", "p_mc", "p_mn", "p_mm", "p_ssd", "p_ffn"):
        outs.append(np.stack([r[k] for r in results], 1))
    for k in ("s_conv", "s_gdn", "s_mc", "s_mn", "s_mm", "s_ssd", "s_ffn"):
        outs.append(cat(k, 1))
    return tuple(np.ascontiguousarray(o, dtype=np.float32) for o in outs)


def kernel(**inputs):
    NPC, NS, DEPTH, ncores = 16, 16, 2, 8
    key = (NPC, NS, DEPTH)
    if key not in _NC_CACHE:
        _NC_CACHE[key] = build(NPC, NS, DEPTH)
    nc = _NC_CACHE[key]
    maps = make_in_maps(inputs, NPC, NS, DEPTH, ncores)
    res = run_bass_kernel_spmd(nc, maps, core_ids=list(range(ncores)))
    return assemble(res.results, NPC, NS, DEPTH)
```

```python
import numpy as np
import concourse.bass as bass
import concourse.mybir as mybir

F32 = mybir.dt.float32
BF16 = mybir.dt.bfloat16
ALU = mybir.AluOpType
AF = mybir.ActivationFunctionType
AX = mybir.AxisListType


class T:
    def __init__(self, ap, name):
        self.ap = ap
        self.name = name
        self.we = None
        self.wd = []
        self.re = {}
        self.rd = []

    def __getitem__(self, idx):
        return V(self, self.ap[idx])

    @property
    def t(self):
        return self


class V:
    def __init__(self, t, ap):
        self.t = t
        self.ap = ap

    def __getitem__(self, idx):
        return V(self.t, self.ap[idx])

    def re(self, pattern_, **kw):
        return V(self.t, self.ap.rearrange(pattern_, **kw))

    def bc(self, shape):
        return V(self.t, self.ap.to_broadcast(shape))


def _ap(x):
    return x.ap if isinstance(x, (T, V)) else x


class KB:
    def __init__(self, nc, es, n_dma_sems=6):
        self.nc = nc
        self.es = es
        self.E = {"pe": nc.tensor, "act": nc.scalar, "dve": nc.vector, "pool": nc.gpsimd, "sp": nc.sync}
        self.sem = {e: es.enter_context(nc.semaphore("s_" + e)) for e in ("pe", "act", "dve", "pool")}
        self.cnt = {e: 0 for e in self.sem}
        self.known = {e: {} for e in self.E}
        self.knownd = {e: {} for e in self.E}
        self.pend = {e: [] for e in self.E}
        self.dsems = {}
        for q in ("sp", "pool", "act"):
            self.dsems[q] = [[es.enter_context(nc.semaphore("d_%s%d" % (q, i))), 0] for i in range(n_dma_sems if q != "act" else 4)]
        self.dnext = {q: 0 for q in self.dsems}
        self.nbank = 0
        self.out_deps = []

    def sb(self, name, shape, dt=F32):
        self.nbank += 1
        name = "sb%d_%s" % (self.nbank, name)
        return T(self.es.enter_context(self.nc.sbuf_tensor(name, list(shape), dt)).ap(), name)

    def ps(self, name, shape, dt=F32):
        self.nbank += 1
        name = "ps%d_%s" % (self.nbank, name)
        t = T(self.es.enter_context(self.nc.psum_tensor(name, list(shape), dt)).ap(), name)
        t.psum = True
        return t

    def _wait_e(self, eng, dep):
        e2, c = dep
        if self.known[eng].get(e2, 0) >= c:
            return
        self.E[eng].wait_ge(self.sem[e2], c)
        self.known[eng][e2] = c

    def _wait_d(self, eng, dep):
        key, sem, tgt = dep
        if self.knownd[eng].get(key, 0) >= tgt:
            return
        self.E[eng].wait_ge(sem, tgt)
        self.knownd[eng][key] = tgt

    def _pre(self, eng, outs, ins):
        for e2, pl in self.pend.items():
            if e2 == eng:
                continue
            for (po, pi) in pl:
                for v in outs:
                    assert all(v.t is not x.t for x in po + pi), ("pending hazard", v.t.name, e2)
                for v in ins:
                    assert all(v.t is not x.t for x in po), ("pending hazard", v.t.name, e2)
        for v in ins:
            t = v.t
            if t.we is not None and not (eng == "pe" and t.we[0] == "pe"):
                self._wait_e(eng, t.we)
            for d in t.wd:
                self._wait_d(eng, d)
            if getattr(t, "psum", False):
                for e2, c in t.re.items():
                    if e2 != eng:
                        self._wait_e(eng, (e2, c))
        for v in outs:
            t = v.t
            if t.we is not None and not (eng == "pe" and t.we[0] == "pe"):
                self._wait_e(eng, t.we)
            for d in t.wd:
                self._wait_d(eng, d)
            for e2, c in t.re.items():
                if not (e2 == "pe" and eng == "pe"):
                    self._wait_e(eng, (e2, c))
            for d in t.rd:
                self._wait_d(eng, d)

    def op(self, eng, fn, outs, ins, inc=True):
        outs = [o for o in outs if o is not None]
        ins = [i for i in ins if isinstance(i, (T, V))]
        self._pre(eng, outs, ins)
        inst = fn()
        self.pend[eng].append((outs, ins))
        if inc:
            self.cnt[eng] += 1
            c = self.cnt[eng]
            inst.then_inc(self.sem[eng], 1)
            for (po, pi) in self.pend[eng]:
                for v in pi:
                    v.t.re[eng] = c
                for v in po:
                    v.t.we = (eng, c)
                    v.t.wd = []
                    v.t.re = {}
                    v.t.rd = []
            self.pend[eng] = []
        return inst

    def dma(self, q, out, in_, out_dram=False, **kw):
        assert not self.pend[q] if q in self.pend else True
        pool = self.dsems[q]
        i = self.dnext[q]
        self.dnext[q] = (i + 1) % len(pool)
        sem, prev = pool[i]
        key = (q, i)
        if prev > 0:
            self._wait_d(q, (key, sem, prev))
        self._pre(q, [o for o in [out] if isinstance(o, (T, V))], [o for o in [in_] if isinstance(o, (T, V))])
        tgt = prev + 16
        pool[i][1] = tgt
        self.E[q].dma_start(out=_ap(out), in_=_ap(in_), **kw).then_inc(sem, 16)
        dep = (key, sem, tgt)
        if isinstance(in_, (T, V)):
            in_.t.rd.append(dep)
        if isinstance(out, (T, V)):
            out.t.wd.append(dep)
        if out_dram:
            self.out_deps.append(dep)
        return dep

    def finish(self):
        for d in self.out_deps:
            self._wait_d("sp", d)

    def mm(self, out, lhsT, rhs, start=True, stop=True, inc=True):
        return self.op("pe", lambda: self.nc.tensor.matmul(_ap(out), _ap(lhsT), _ap(rhs), start=start, stop=stop),
                       [out], [lhsT, rhs], inc=inc)

    def tr(self, out, in_, ident, inc=True):
        return self.op("pe", lambda: self.nc.tensor.transpose(_ap(out), _ap(in_), _ap(ident)), [out], [in_, ident], inc=inc)

    def act(self, out, in_, func, bias=0.0, scale=1.0, eng="act"):
        return self.op("act", lambda: self.nc.scalar.activation(_ap(out), _ap(in_), func, bias=_ap(bias), scale=_ap(scale)),
                       [out], [in_, bias, scale])

    def ts(self, out, in0, s1, s2, op0, op1=None, eng="dve"):
        e = self.E[eng]
        if op1 is None:
            return self.op(eng, lambda: e.tensor_scalar(_ap(out), _ap(in0), _ap(s1), None, op0), [out], [in0, s1])
        return self.op(eng, lambda: e.tensor_scalar(_ap(out), _ap(in0), _ap(s1), _ap(s2), op0, op1), [out], [in0, s1, s2])

    def stt(self, out, in0, s, in1, op0, op1):
        return self.op("dve", lambda: self.nc.vector.scalar_tensor_tensor(_ap(out), _ap(in0), _ap(s), _ap(in1), op0, op1),
                       [out], [in0, s, in1])

    def tt(self, out, in0, in1, op, eng="dve"):
        e = self.E[eng]
        return self.op(eng, lambda: e.tensor_tensor(_ap(out), _ap(in0), _ap(in1), op), [out], [in0, in1])

    def cp(self, out, in_, eng="dve"):
        if eng == "act":
            return self.op("act", lambda: self.nc.scalar.copy(_ap(out), _ap(in_)), [out], [in_])
        e = self.E[eng]
        return self.op(eng, lambda: e.tensor_copy(_ap(out), _ap(in_)), [out], [in_])

    def recip(self, out, in_):
        return self.op("dve", lambda: self.nc.vector.reciprocal(_ap(out), _ap(in_)), [out], [in_])

    def red(self, out, in_, op, axis=AX.X):
        return self.op("dve", lambda: self.nc.vector.tensor_reduce(_ap(out), _ap(in_), axis, op), [out], [in_])

    def memset(self, out, val, eng="dve"):
        e = self.E[eng]
        return self.op(eng, lambda: e.memset(_ap(out), val), [out], [])


from contextlib import ExitStack
from concourse.bass_utils import run_bass_kernel_spmd

D = 1024
KC = 8
CONV_DIM = 2304
IN_COLS = 4116
NREST = 1812
DFF = 2816
NFB = 22
EPS = 1e-6
NEG = -30000.0
LNQ = float(np.log(128.0 ** -0.5))
DEBUG = False
GSZ_G = 6
FFN_ILV = True
PRENORM = True
HOIST_SSD = False
ILV_ML = True


def _step(g):
    try:
        next(g)
        return True
    except StopIteration:
        return False
NWU = 3
NWD = 3
STOP = 7
CUT = 99


class _Stop(Exception):
    pass
R_GG, R_GA, R_GB, R_MQ, R_MK, R_MV, R_MO, R_MI, R_MF, R_SZ, R_SDT = 0, 512, 516, 520, 776, 1032, 1288, 1544, 1548, 1552, 1808
MUL, ADD, SUB, MAX = ALU.mult, ALU.add, ALU.subtract, ALU.max


def make_consts(NS):
    c = {}
    c["ident"] = np.eye(128, dtype=np.float32)
    c["ones"] = np.ones((128, 128), np.float32)
    i = np.arange(128)[:, None]
    j = np.arange(128)[None, :]
    cm = np.zeros((128, 6, 128), np.float32)
    cm[:, 0] = (i <= j)
    cm[:, 1] = np.where(i > j, 0.0, NEG)
    cm[:, 2] = np.where(j >= i, 0.0, NEG)
    cm[:, 3] = np.where(j <= i, 0.0, NEG)
    cm[:, 4] = 1.0
    cm[:, 5] = (i == 127)
    c["cm_p"] = cm
    ls = np.zeros((128, 1), np.float32)
    ls[127, 0] = 1
    c["lastsel_p"] = ls
    n = NS * 4
    i = np.arange(n)[:, None]
    j = np.arange(n)[None, :]
    same = (i // 4) == (j // 4)
    cs = np.zeros((n, 6, n), np.float32)
    cs[:, 0] = same & (i <= j)
    cs[:, 1] = np.where(same & (i > j), 0.0, NEG)
    cs[:, 2] = np.where(same & (j >= i), 0.0, NEG)
    cs[:, 3] = np.where(same & (j <= i), 0.0, NEG)
    cs[:, 4] = same
    cs[:, 5] = same & (i % 4 == 3)
    c["cm_s"] = cs
    s = np.arange(NS)[None, :]
    k = np.arange(n)[:, None]
    c["lastsel_s"] = (k == 4 * s + 3).astype(np.float32)
    c["rowmask"] = ((k // 4) == s).astype(np.float32)
    return c


def build(NPC, NS, DEPTH):
    nc = bass.Bass("TRN2", target_bir_lowering=False)
    TP = NPC * 128
    NST = NS * 4
    assert NST <= 64

    def din(name, shape):
        return nc.dram_tensor(name, list(shape), F32, kind="ExternalInput").ap()

    def dout(name, shape):
        return nc.dram_tensor(name, list(shape), F32, kind="ExternalOutput").ap()

    xp = din("xp", [TP, D])
    xs = din("xs", [NST, D])
    st_conv = din("st_conv", [DEPTH, NS, 3, CONV_DIM])
    st_gdn = din("st_gdn", [DEPTH, NS, 4, 128, 128])
    st_mc = din("st_mc", [DEPTH, NS, 4, 64, 64])
    st_mn = din("st_mn", [DEPTH, NS, 4, 64])
    st_mm = din("st_mm", [DEPTH, NS, 4])
    st_ssd = din("st_ssd", [DEPTH, NS, 4, 64, 128])
    st_ffn = din("st_ffn", [DEPTH, NS, 2, DFF])
    nrm = [din(n, [DEPTH, D]) for n in ("norm_mix_pre", "norm_mix_post", "norm_ffn_pre", "norm_ffn_post")]
    w_in = din("w_in", [DEPTH, D, IN_COLS])
    conv_w = din("conv_w", [DEPTH, 4, CONV_DIM])
    conv_b = din("conv_b", [DEPTH, CONV_DIM])
    hp_names = ["gdn_a_log", "gdn_dt_bias", "mlstm_i_bias", "mlstm_f_bias", "ssd_a_log", "ssd_dt_bias", "ssd_d"]
    hp_d = [din(n, [DEPTH, 4]) for n in hp_names]
    gdn_norm = din("gdn_norm", [DEPTH, 128])
    mlstm_norm = din("mlstm_norm", [DEPTH, 64])
    ssd_norm = din("ssd_norm", [DEPTH, 256])
    w_out = din("w_out", [DEPTH, D, D])
    w_up = din("ffn_w_up", [DEPTH, D, 2 * DFF])
    fconv_w = din("ffn_conv_w", [DEPTH, 3, DFF])
    fconv_b = din("ffn_conv_b", [DEPTH, DFF])
    w_down = din("ffn_w_down", [DEPTH, DFF, D])
    c_ident = din("ident", [128, 128])
    c_ones = din("ones", [128, 128])
    c_cm_p = din("cm_p", [128, 6, 128])
    c_ls_p = din("lastsel_p", [128, 1])
    c_cm_s = din("cm_s", [NST, 6, NST])
    c_ls_s = din("lastsel_s", [NST, NS])
    c_rowmask = din("rowmask", [NST, NS])

    y_p = dout("y_p", [TP, D])
    y_s = dout("y_s", [NST, D])
    o_pconv = dout("p_conv", [DEPTH, 3, CONV_DIM])
    o_pgdn = dout("p_gdn", [DEPTH, 4, 128, 128])
    o_pmc = dout("p_mc", [DEPTH, 4, 64, 64])
    o_pmn = dout("p_mn", [DEPTH, 4, 64])
    o_pmm = dout("p_mm", [DEPTH, 4])
    o_pssd = dout("p_ssd", [DEPTH, 4, 64, 128])
    o_pffn = dout("p_ffn", [DEPTH, 2, DFF])
    o_sconv = dout("s_conv", [DEPTH, NS, 3, CONV_DIM])
    o_sgdn = dout("s_gdn", [DEPTH, NS, 4, 128, 128])
    o_smc = dout("s_mc", [DEPTH, NS, 4, 64, 64])
    o_smn = dout("s_mn", [DEPTH, NS, 4, 64])
    o_smm = dout("s_mm", [DEPTH, NS, 4])
    o_sssd = dout("s_ssd", [DEPTH, NS, 4, 64, 128])
    o_sffn = dout("s_ffn", [DEPTH, NS, 2, DFF])

    NCH = NPC + 1
    dbg_mix = dout("dbg_mix", [NST, D]) if DEBUG else None
    NSL = True

    with ExitStack() as es0:
        kb = KB(nc, es0)
        PS = [kb.ps("psb%d" % i, [128, 512]) for i in range(8)]
        bstate = [0]

        held = set()

        def bank(hold=False):
            while (bstate[0] % 8) in held:
                bstate[0] += 1
            i = bstate[0] % 8
            bstate[0] += 1
            if hold:
                held.add(i)
            return PS[i]

        def release(*bs):
            for b in bs:
                held.discard(PS.index(b))

        def barrier():
            for e in ("pe", "act", "dve", "pool", "sp"):
                for e2 in ("pe", "act", "dve", "pool"):
                    if e2 != e and kb.cnt[e2] > 0:
                        kb._wait_e(e, (e2, kb.cnt[e2]))
                for q in kb.dsems:
                    for i, (sem, tgt) in enumerate(kb.dsems[q]):
                        if tgt > 0:
                            kb._wait_d(e, ((q, i), sem, tgt))

        xT = [kb.sb("xT%d" % c, [128, 8, 128 if c < NPC else NST]) for c in range(NCH)]
        ident = kb.sb("ident", [128, 128])
        ones = kb.sb("ones", [128, 128])
        cmp_ = kb.sb("cm_p", [128, 6, 128])
        lsp = kb.sb("ls_p", [128, 1])
        cms = kb.sb("cm_s", [NST, 6, NST])
        lss = kb.sb("ls_s", [NST, NS])
        rowmask = kb.sb("rowmask", [NST, NS])
        NW = kb.sb("NW", [128, 4 * DEPTH * 8])
        CW = kb.sb("CW", [128, DEPTH, 18, 4])
        CB = kb.sb("CB", [128, DEPTH, 18])
        FCW = kb.sb("FCW", [128, DEPTH, NFB, 3])
        FCB = kb.sb("FCB", [128, DEPTH, NFB])
        HP = kb.sb("HP", [128, 7, DEPTH, 4])
        GN = kb.sb("GN", [128, DEPTH, 128])
        MNb = kb.sb("MNb", [128, DEPTH, 64])
        SN = kb.sb("SN", [128, DEPTH, 256])
        RS = kb.sb("RS", [128, 128])
        SQa = kb.sb("SQa", [128, 8, 128])
        SQb = kb.sb("SQb", [128, 8, 128])

        kb.dma("sp", ident, c_ident)
        kb.dma("sp", ones, c_ones)
        kb.dma("sp", cmp_, c_cm_p)
        kb.dma("sp", lsp, c_ls_p)
        kb.dma("sp", cms, c_cm_s)
        kb.dma("sp", lss, c_ls_s)
        kb.dma("sp", rowmask, c_rowmask)
        for w in range(4):
            for l in range(DEPTH):
                o = (w * DEPTH + l) * 8
                kb.dma("sp", NW[:, o:o + 8], nrm[w][l].rearrange("(k p) -> p k", p=128), allow_slow_non_contiguous=True)
        for l in range(DEPTH):
            for j in range(4):
                kb.dma("sp", CW[:, l, :, j], conv_w[l, j].rearrange("(b p) -> p b", p=128), allow_slow_non_contiguous=True)
            kb.dma("sp", CB[:, l, :], conv_b[l].rearrange("(b p) -> p b", p=128), allow_slow_non_contiguous=True)
            for j in range(3):
                kb.dma("sp", FCW[:, l, :, j], fconv_w[l, j].rearrange("(b p) -> p b", p=128), allow_slow_non_contiguous=True)
            kb.dma("sp", FCB[:, l, :], fconv_b[l].rearrange("(b p) -> p b", p=128), allow_slow_non_contiguous=True)
        for i in range(7):
            kb.dma("sp", HP[:, i], hp_d[i].partition_broadcast(128))
        kb.dma("sp", GN, gdn_norm.partition_broadcast(128))
        kb.dma("sp", MNb, mlstm_norm.partition_broadcast(128))
        kb.dma("sp", SN, ssd_norm.partition_broadcast(128))
        for i in (0, 4):
            kb.act(HP[:, i], HP[:, i], AF.Exp)
            kb.ts(HP[:, i], HP[:, i], -1.0, None, MUL)

        def nw(w, l, k):
            o = (w * DEPTH + l) * 8 + k
            return NW[:, o:o + 1]

        def ctok(c):
            return 128 if c < NPC else NST

        ones_bf = kb.sb("ones_bf", [128, 128], BF16)
        kb.cp(ones_bf, ones)

        def bfview(t3):
            v = t3[:, :, :].re("p k t -> p (k t)")
            return V(v.t, v.ap.bitcast(BF16)[:, 0:1024].rearrange("p (k t) -> p k t", k=8))

        def nwv(w, l, CT):
            o = (w * DEPTH + l) * 8
            return NW[:, o:o + 8].re("p (k o) -> p k o", o=1).bc([128, 8, CT])

        def rsv(CT):
            return RS[:, :CT].re("p (o t) -> p o t", o=1).bc([128, 8, CT])

        def rmsnorm_to(dst, c, w, l, CT):
            xc = xT[c]
            sqb = bfview(SQa)
            kb.act(sqb[:, :, :CT], xc[:, :, :CT], AF.Square)
            ps = bank()
            for k in range(8):
                kb.mm(ps[:, :CT], ones_bf, sqb[:, k, :CT], start=(k == 0), stop=(k == 7), inc=(k == 7))
            kb.act(RS[:, :CT], ps[:, :CT], AF.Ln, bias=EPS, scale=1.0 / D)
            kb.act(RS[:, :CT], RS[:, :CT], AF.Exp, scale=-0.5)
            kb.tt(SQa[:, :, :CT], xc[:, :, :CT], rsv(CT), MUL)
            kb.tt(dst[:, :, :CT], SQa[:, :, :CT], nwv(w, l, CT), MUL)

        def postnorm_add(Y, c, w, l, CT):
            sqb = bfview(SQb)
            kb.act(sqb[:, :, :CT], Y[:, :, :CT], AF.Square)
            ps = bank()
            for k in range(8):
                kb.mm(ps[:, :CT], ones_bf, sqb[:, k, :CT], start=(k == 0), stop=(k == 7), inc=(k == 7))
            kb.act(RS[:, :CT], ps[:, :CT], AF.Ln, bias=EPS, scale=1.0 / D)
            kb.act(RS[:, :CT], RS[:, :CT], AF.Exp, scale=-0.5)
            kb.tt(Y[:, :, :CT], Y[:, :, :CT], rsv(CT), MUL)
            kb.tt(Y[:, :, :CT], Y[:, :, :CT], nwv(w, l, CT), MUL)
            kb.tt(xT[c][:, :, :CT], xT[c][:, :, :CT], Y[:, :, :CT], ADD)

        with ExitStack() as es1:
            kb.es = es1
            xin = [kb.sb("xin%d" % i, [128, D]) for i in range(2)]
            for c in range(NCH):
                CT = ctok(c)
                xi = xin[c % 2]
                src = xp[c * 128:(c + 1) * 128, :] if c < NPC else xs
                kb.dma("sp", xi[:CT, :], src)
                for half in range(2):
                    ps = bank()
                    for j in range(4):
                        k = half * 4 + j
                        kb.tr(ps[:, j * CT:(j + 1) * CT], xi[:CT, k * 128:(k + 1) * 128], ident[:CT, :CT], inc=(j == 3))
                    kb.cp(xT[c][:, half * 4:half * 4 + 4, :CT], ps[:, :4 * CT].re("p (k t) -> p k t", k=4), eng=("dve" if half == 0 else "act"))
            barrier()

        for l in range(DEPTH):
            with ExitStack() as es2:
                kb.es = es2
                WinR = kb.sb("WinR", [128, 8, NREST], BF16)
                Wout = kb.sb("Wout", [128, 8, D], BF16)
                for k in range(8):
                    kb.dma("pool", WinR[:, k, :], w_in[l, k * 128:(k + 1) * 128, CONV_DIM:IN_COLS])
                for k in range(8):
                    kb.dma("pool", Wout[:, k, :], w_out[l, k * 128:(k + 1) * 128, :])
                HN = [kb.sb("hnT0", [128, 8, 128], BF16), None]
                normed = {}
                R = kb.sb("R", [128, NREST])
                STGc = [kb.sb("STGc%d" % i, [64, 256]) for i in range(2)]
                stgi = [0]
                QKm = kb.sb("QKm", [64, 8, 128])
                CO = kb.sb("CO", [128, 12, 128])
                CO_main = CO
                TMP = {n: kb.sb("tmp" + n, [128, 512]) for n in ("E", "F", "G2", "H", "I", "X1")}
                SQa2 = SQa[:, :, :].re("p k t -> p (k t)")
                SQb2 = SQb[:, :, :].re("p k t -> p (k t)")
                TMP["A"] = SQa2[:, 0:512]
                TMP["B"] = SQa2[:, 512:1024]
                TMP["C"] = SQb2[:, 0:512]
                TMP["Dt"] = SQb2[:, 512:1024]
                SM = kb.sb("SM", [128, 160])
                EG = kb.sb("EG", [128, 4, 128])
                MIX = kb.sb("MIX", [128, D])
                mixT = kb.sb("mixT", [128, 8, 128], BF16)
                VE = kb.sb("VE", [128, 4, 65])
                KW = kb.sb("KW", [128, 4, 64])
                XT_ = kb.sb("XTs", [128, 256])
                kb.memset(VE[:, :, 64:65], 1.0)
                Z = SM[:, 0:12]
                SP = SM[:, 12:24]
                CS = SM[:, 24:36]
                GC = SM[:, 36:48]
                BETA = SM[:, 48:52]
                NBETA = SM[:, 52:56]
                LI = SM[:, 56:60]
                BE = SM[:, 60:64]
                KD = SM[:, 64:68]
                GL = SM[:, 68:80]
                CC = SM[:, 80:84]
                MX = SM[:, 84:88]
                INT = SM[:, 88:92]
                MT = SM[:, 92:96]
                NMT = SM[:, 96:100]
                WI = SM[:, 100:104]
                DEN = SM[:, 104:108]
                EM = SM[:, 108:112]
                WK = SM[:, 112:116]
                SS4 = SM[:, 116:120]
                RSTD4 = SM[:, 120:124]
                KDS = SM[:, 124:128]
                M0C = SM[:, 128:132]
                SS1 = SM[:, 132:133]

                def v4(t, CT, w=None):
                    w = CT if w is None else w
                    return t[:CT, 0:4 * w].re("p (h j) -> p h j", h=4)

                def bc4(v, CT, w):
                    return v[:CT].re("p (h o) -> p h o", o=1).bc([CT, 4, w])

                def mixer_chunk(c, P, nxt=None):
                    hnT = HN[0] if P["sample"] else HN[c % 2]
                    CT, nseq, sample = P["CT"], P["nseq"], P["sample"]
                    cm = P["cm"]
                    U, Ls, Um, Lm, SSm, LSm = (cm[:CT, i, :CT] for i in range(6))
                    lastsel = P["lastsel"]

                    def mbc(m):
                        return m.re("p (o j) -> p o j", o=1).bc([CT, 4, CT])

                    if normed.get("c") != c:
                        rmsnorm_to(hnT, c, 0, l, CT)

                    if CUT <= 1:
                        return
                    need_cs = sample or c == NPC - 1
                    rows = slice(0, CT) if sample else slice(CT - 3, CT)
                    nr = CT if sample else 3

                    def fm_pair(col0, conv=True):
                        ws = P["WS"][P["wsi"][0] % len(P["WS"])]
                        P["wsi"][0] += 1
                        kb.dma("pool", ws, w_in[l, :, col0:col0 + 256].rearrange("(k p) c -> p k c", p=128))
                        if conv and need_cs:
                            psr = bank()
                            for k in range(8):
                                kb.mm(psr[:nr, 0:256], hnT[:, k, rows], ws[:, k, :], start=(k == 0), stop=(k == 7), inc=(k == 7))
                            stg = STGc[stgi[0] % 2]
                            stgi[0] += 1
                            kb.cp(stg[:nr, :], psr[:nr, 0:256], eng="act")
                            if sample:
                                for r in range(3):
                                    kb.dma("sp", o_sconv[l, :, r, col0:col0 + 256], stg[1 + r:CT:4, :], out_dram=True)
                            else:
                                kb.dma("sp", o_pconv[l, :, col0:col0 + 256], stg[:3, :], out_dram=True)
                        return ws

                    for pc in range(4):
                        ps = bank()
                        c0 = pc * 453
                        for k in range(8):
                            kb.mm(ps[:CT, :453], hnT[:, k, :CT], WinR[:, k, c0:c0 + 453], start=(k == 0), stop=(k == 7), inc=(k == 7))
                        kb.cp(R[:CT, pc * 453:(pc + 1) * 453], ps[:CT, :453], eng=("act" if pc % 2 else "dve"))
                    for pr in range(2):
                        ws = fm_pair(CONV_DIM + R_MQ + pr * 256, conv=False)
                        ps = bank()
                        for hh in range(4):
                            for k in range(8):
                                kb.mm(ps[:64, hh * CT:(hh + 1) * CT], ws[:, k, hh * 64:(hh + 1) * 64], hnT[:, k, :CT], start=(k == 0), stop=(k == 7), inc=(k == 7 and hh == 3))
                        if pr == 0:
                            kb.cp(QKm[:, 0:4, :CT], ps[:64, 0:4 * CT].re("p (b t) -> p b t", b=4))
                        else:
                            kb.ts(QKm[:, 4:8, :CT], ps[:64, 0:4 * CT].re("p (b t) -> p b t", b=4), 0.125, None, MUL)

                    if CUT <= 2:
                        return
                    def hp(i):
                        return HP[:CT, i, l, :]
                    kb.act(R[:CT, R_GG:R_GG + 512], R[:CT, R_GG:R_GG + 512], AF.Silu)
                    kb.act(R[:CT, R_SZ:R_SZ + 256], R[:CT, R_SZ:R_SZ + 256], AF.Silu)
                    kb.act(R[:CT, R_MO:R_MO + 256], R[:CT, R_MO:R_MO + 256], AF.Sigmoid)
                    kb.act(BETA[:CT], R[:CT, R_GB:R_GB + 4], AF.Sigmoid)
                    kb.tt(Z[:CT, 0:4], R[:CT, R_GA:R_GA + 4], hp(1), ADD)
                    kb.tt(Z[:CT, 4:8], R[:CT, R_MF:R_MF + 4], hp(3), ADD)
                    kb.ts(Z[:CT, 4:8], Z[:CT, 4:8], -1.0, None, MUL)
                    kb.tt(Z[:CT, 8:12], R[:CT, R_SDT:R_SDT + 4], hp(5), ADD)
                    kb.act(SP[:CT], Z[:CT], AF.Exp)
                    kb.act(SP[:CT], SP[:CT], AF.Ln, bias=1.0)
                    kb.tt(CS[:CT, 0:4], SP[:CT, 0:4], hp(0), MUL)
                    kb.ts(CS[:CT, 4:8], SP[:CT, 4:8], -1.0, None, MUL)
                    kb.tt(CS[:CT, 8:12], SP[:CT, 8:12], hp(4), MUL)
                    kb.ts(NBETA[:CT], BETA[:CT], -1.0, None, MUL)
                    kb.tt(LI[:CT], R[:CT, R_MI:R_MI + 4], hp(2), ADD)
                    def gates_mm():
                        ps = bank()
                        kb.mm(ps[:CT, 0:12], U, CS[:CT, 0:12], inc=False)
                        kb.mm(ps[:CT, 12:24], SSm, CS[:CT, 0:12])
                        kb.cp(GC[:CT], ps[:CT, 0:12])
                        kb.cp(GL[:CT], ps[:CT, 12:24])

                    yield "front"

                    def rowbc(colv, mat):
                        t = TMP["X1"]
                        kb.tt(v4(t, CT), mbc(mat), bc4(colv, CT, CT), MUL)
                        ps = bank()
                        kb.mm(ps[:, :4 * CT], ones[:CT, :], t[:CT, :4 * CT])
                        return ps

                    A, B, C, Dt, E, F, G2, H, I, X1 = (TMP[n] for n in ("A", "B", "C", "Dt", "E", "F", "G2", "H", "I", "X1"))

                    def mlstm_section(mA, mB, mI, mF):
                        psR = rowbc(CS[:, 4:8], U)
                        BL = P["BL"]
                        if sample:
                            kb.cp(BL, psR[:, :4 * CT].re("p (h s t) -> p h s t", h=4, t=4)[:, :, :, 3])
                        else:
                            kb.cp(BL, psR[:, :4 * CT].re("p (h j) -> p h j", h=4)[:, :, CT - 1:CT])
                        kb.tt(CC[:CT], LI[:CT], GC[:CT, 4:8], SUB)
                        yield
                        psC = rowbc(CC, ident[:CT, :CT])
                        Dm = v4(mA, CT)
                        kb.tt(Dm, psC[:CT, :4 * CT].re("p (h j) -> p h j", h=4), bc4(GC[:, 4:8], CT, CT), ADD)
                        kb.tt(Dm, Dm, mbc(Lm), ADD)
                        kb.red(MX[:CT], Dm, MAX)
                        yield
                        if sample:
                            ps = bank()
                            kb.mm(ps[:CT, 0:4], P["RMT"], P["m0"])
                            kb.cp(M0C[:CT], ps[:CT, 0:4])
                            kb.tt(INT[:CT], GC[:CT, 4:8], M0C[:CT], ADD)
                        else:
                            kb.tt(INT[:CT], GC[:CT, 4:8], P["MB"][:CT, :], ADD)
                        kb.tt(MT[:CT], INT[:CT], MX[:CT], MAX)
                        kb.ts(NMT[:CT], MT[:CT], -1.0, None, MUL)
                        yield
                        for h in range(4):
                            kb.act(mA[:CT, h * CT:(h + 1) * CT], mA[:CT, h * CT:(h + 1) * CT], AF.Exp, bias=NMT[:CT, h:h + 1])
                        kb.tt(WI[:CT], INT[:CT], MT[:CT], SUB)
                        kb.act(WI[:CT], WI[:CT], AF.Exp)
                        yield
                        psS_ = bank()
                        for h in range(4):
                            kb.mm(psS_[:CT, h * CT:(h + 1) * CT], QKm[:, h, :CT], QKm[:, 4 + h, :CT], inc=(h == 3))
                        kb.tt(Dm, Dm, psS_[:CT, :4 * CT].re("p (h j) -> p h j", h=4), MUL)
                        yield
                        ps = bank()
                        for h in range(4):
                            kb.tr(ps[:CT, h * CT:(h + 1) * CT], mA[:CT, h * CT:(h + 1) * CT], ident[:CT, :CT], inc=(h == 3))
                        kb.cp(mB[:CT, :4 * CT], ps[:CT, :4 * CT], eng="act")
                        yield
                        kb.cp(VE[:CT, :, 0:64], R[:CT, R_MV:R_MV + 256].re("p (h d) -> p h d", h=4))
                        psB = bank(hold=True)
                        for h in range(4):
                            kb.mm(psB[:CT, h * 65:(h + 1) * 65], mB[:CT, h * CT:(h + 1) * CT], VE[:CT, h, :], inc=(h == 3))
                        Xs = X1[:CT, 0:4 * nseq].re("p (h s) -> p h s", h=4)
                        kb.tt(Xs, MT[:CT].re("p (h o) -> p h o", o=1).bc([CT, 4, nseq]), lastsel[:CT, :].re("p (o s) -> p o s", o=1).bc([CT, 4, nseq]), MUL)
                        psM = bank()
                        kb.mm(psM[:, 0:4 * nseq], ones[:CT, :], X1[:CT, 0:4 * nseq])
                        MN_ = P["MN"]
                        WC = P["WC"]
                        kb.cp(MN_, psM[:, 0:4 * nseq].re("p (h s) -> p h s", h=4))
                        yield
                        kb.tt(WC, BL, P["M0R"], ADD)
                        kb.tt(WC, WC, MN_, SUB)
                        kb.act(WC, WC, AF.Exp)
                        yield
                        psBM = bank()
                        kb.mm(psBM[:CT, 0:4], LSm, NMT[:CT, 0:4])
                        kb.tt(WK[:CT], psBM[:CT, 0:4], GL[:CT, 4:8], ADD)
                        kb.tt(WK[:CT], WK[:CT], CC[:CT], ADD)
                        kb.act(WK[:CT], WK[:CT], AF.Exp)
                        kb.ts(WK[:CT], WK[:CT], 0.125, None, MUL)
                        kb.tt(KW[:CT], R[:CT, R_MK:R_MK + 256].re("p (h d) -> p h d", h=4), bc4(WK, CT, 64), MUL)
                        yield
                        psA = bank(hold=True)
                        if not sample:
                            CME = P["CME"]
                            for h in range(4):
                                kb.mm(psA[:CT, h * 65:(h + 1) * 65], QKm[:, h, :CT], CME[:, h, :], inc=(h == 3))
                            psN = bank()
                            for h in range(4):
                                kb.mm(psN[:64, h * 65:(h + 1) * 65], KW[:CT, h, :], VE[:CT, h, :], inc=(h == 3))
                            kb.tt(CME[:, :, :], CME[:, :, :], WC[:64, :, 0:1].bc([64, 4, 65]), MUL)
                            kb.tt(CME[:, :, :], CME[:, :, :], psN[:64, 0:260].re("p (h d) -> p h d", h=4), ADD)
                        for h in range(4):
                            if not sample:
                                pass
                            else:
                                CMs = P["CMs"]
                                Zt = P["Zt"]
                                ZK = P["ZK"]
                                kb.dma("sp", CMs[:, :, 0:64], st_mc[l, :, h].rearrange("s k v -> k s v"))
                                kb.dma("sp", CMs[:, :, 64:65], st_mn[l, :, h, :].rearrange("s (k o) -> k s o", o=1), allow_slow_non_contiguous=True)
                                zd = Zt[:64, 0:nseq * (CT + 4)].re("p (s r) -> p s r", r=CT + 4)[:, :, 0:4]
                                kb.cp(zd, QKm[:, h, :CT].re("p (s t) -> p s t", t=4))
                                for s_ in range(nseq):
                                    kb.mm(psA[:CT, h * 65:(h + 1) * 65], Zt[:64, s_ * CT:(s_ + 1) * CT], CMs[:, s_, :],
                                          start=(s_ == 0), stop=(s_ == nseq - 1), inc=(s_ == nseq - 1))
                                for g in range(nseq // 4):
                                    kb.tt(ZK[:CT, :, 0:64], KW[:CT, h, :].re("p (o d) -> p o d", o=1).bc([CT, 4, 64]),
                                          rowmask[:CT, 4 * g:4 * g + 4].re("p (s o) -> p s o", o=1).bc([CT, 4, 64]), MUL)
                                    psN = bank()
                                    for s4 in range(4):
                                        kb.mm(psN[:64, s4 * 65:(s4 + 1) * 65], ZK[:CT, s4, 0:64], VE[:CT, h, :], inc=(s4 == 3))
                                    wcb = WC[:64, h, 4 * g:4 * g + 4].re("p (s o) -> p s o", o=1).bc([64, 4, 65])
                                    kb.tt(CMs[:, 4 * g:4 * g + 4, :], CMs[:, 4 * g:4 * g + 4, :], wcb, MUL)
                                    kb.tt(CMs[:, 4 * g:4 * g + 4, :], CMs[:, 4 * g:4 * g + 4, :],
                                          psN[:64, 0:260].re("p (s d) -> p s d", s=4), ADD)
                                kb.dma("sp", o_smc[l, :, h].rearrange("s k v -> k s v"), CMs[:, :, 0:64], out_dram=True)
                                kb.dma("sp", o_smn[l, :, h, :].rearrange("s (k o) -> k s o", o=1), CMs[:, :, 64:65], out_dram=True, allow_slow_non_contiguous=True)
                        yield
                        if sample:
                            kb.dma("sp", o_smm[l].rearrange("(o s) h -> o h s", o=1), MN_[0:1, :, :], out_dram=True, allow_slow_non_contiguous=True)
                        else:
                            kb.cp(P["MB"], MN_[:, :, 0])
                        TOT = v4(mI, CT, 65)
                        kb.tt(TOT, psA[:CT, 0:260].re("p (h d) -> p h d", h=4), bc4(WI, CT, 65), MUL)
                        kb.tt(TOT, TOT, psB[:CT, 0:260].re("p (h d) -> p h d", h=4), ADD)
                        release(psA, psB)
                        yield
                        kb.act(DEN[:CT], mI[:CT, 64:260:65], AF.Abs)
                        kb.act(EM[:CT], NMT[:CT], AF.Exp)
                        kb.tt(DEN[:CT], DEN[:CT], EM[:CT], MAX)
                        kb.recip(DEN[:CT], DEN[:CT])
                        yield
                        HH = v4(X1, CT, 64)
                        kb.tt(HH, TOT[:, :, 0:64], bc4(DEN, CT, 64), MUL)
                        kb.tt(X1[:CT, 0:256], X1[:CT, 0:256], R[:CT, R_MO:R_MO + 256], MUL)
                        kb.act(v4(mF, CT, 64), HH, AF.Square)
                        kb.red(SS4[:CT], v4(mF, CT, 64), ADD)
                        yield
                        kb.act(RSTD4[:CT], SS4[:CT], AF.Ln, bias=EPS, scale=1.0 / 64)
                        kb.act(RSTD4[:CT], RSTD4[:CT], AF.Exp, scale=-0.5)
                        kb.tt(HH, HH, bc4(RSTD4, CT, 64), MUL)
                        kb.tt(MIX[:CT, 512:768].re("p (h d) -> p h d", h=4), HH, MNb[:CT, l, :].re("p (o d) -> p o d", o=1).bc([CT, 4, 64]), MUL)


                    hoist = (not sample) and HOIST_SSD
                    XPs = P["XP2"] if hoist else P["XP"]
                    COs = P["CO2"] if hoist else CO

                    def ssd_front():
                        for g in range(2):
                            nb = 4 if g == 0 else 2
                            wss = [fm_pair(1536 + g * 512 + q * 256) for q in range(nb // 2)]
                            ps = bank()
                            for j in range(nb):
                                ws = wss[j // 2]
                                for k in range(8):
                                    kb.mm(ps[:, j * CT:(j + 1) * CT], ws[:, k, (j % 2) * 128:(j % 2 + 1) * 128], hnT[:, k, :CT], start=(k == 0), stop=(k == 7), inc=(k == 7 and j == nb - 1))
                            if sample:
                                kb.cp(XPs[:, g * 4:g * 4 + nb, :, 3:7], ps[:, :nb * CT].re("p (b s t) -> p b s t", b=nb, t=4), eng="act")
                            else:
                                kb.cp(XPs[:, g * 4:g * 4 + nb, 3:3 + CT], ps[:, :nb * CT].re("p (b t) -> p b t", b=nb), eng="act")
                        conv_blocks(P, XPs, 12, 6, CT, COs)

                    XP = P["XP"]
                    for g in range(3):
                        if g == 2:
                            yield "mid"
                        wss = [fm_pair(g * 512), fm_pair(g * 512 + 256)]
                        ps = bank()
                        for j in range(4):
                            ws = wss[j // 2]
                            for k in range(8):
                                kb.mm(ps[:, j * CT:(j + 1) * CT], ws[:, k, (j % 2) * 128:(j % 2 + 1) * 128], hnT[:, k, :CT], start=(k == 0), stop=(k == 7), inc=(k == 7 and j == 3))
                        if sample:
                            kb.cp(XP[:, g * 4:g * 4 + 4, :, 3:7], ps[:, :4 * CT].re("p (b s t) -> p b s t", b=4, t=4), eng="act")
                        else:
                            kb.cp(XP[:, g * 4:g * 4 + 4, 3:3 + CT], ps[:, :4 * CT].re("p (b t) -> p b t", b=4), eng="act")
                    gates_mm()
                    conv_blocks(P, XP, 0, 12, CT)
                    if hoist:
                        ssd_front()
                    if CUT <= 4:
                        return
                    sqb_ = bfview(SQa)
                    kb.act(sqb_[:, :, :CT], CO[:, 0:8, :CT], AF.Square)
                    for half in range(2):
                        ps = bank()
                        for j in range(4):
                            kb.mm(ps[:, j * CT:(j + 1) * CT], ones_bf, sqb_[:, half * 4 + j, :CT], inc=(j == 3))
                        t = v4(X1, 128, CT)
                        kb.act(t, ps[:, :4 * CT].re("p (b t) -> p b t", b=4), AF.Ln, bias=EPS)
                        kb.act(t, t, AF.Exp, scale=-0.5)
                        if half == 0:
                            kb.ts(t, t, 128 ** -0.5, None, MUL)
                        kb.tt(CO[:, half * 4:half * 4 + 4, :CT], CO[:, half * 4:half * 4 + 4, :CT], t, MUL)
                    if CUT <= 5:
                        return
                    psR = rowbc(CS[:, 0:4], U)
                    kb.act(EG[:, :, :CT], psR[:, :4 * CT].re("p (h j) -> p h j", h=4), AF.Exp)
                    kb.tt(v4(X1, CT), bc4(GC[:, 0:4], CT, CT), psR[:CT, :4 * CT].re("p (h j) -> p h j", h=4), SUB)
                    kb.tt(v4(A, CT), v4(X1, CT), mbc(Ls), ADD)
                    kb.act(v4(A, CT), v4(A, CT), AF.Exp)
                    kb.tt(v4(Dt, CT), mbc(Um), v4(X1, CT), SUB)
                    kb.act(v4(Dt, CT), v4(Dt, CT), AF.Exp)
                    psK = bank()
                    for h in range(4):
                        kb.mm(psK[:CT, h * CT:(h + 1) * CT], CO[:, 4 + h, :CT], CO[:, 4 + h, :CT], inc=(h == 3))
                    kb.tt(v4(A, CT), v4(A, CT), psK[:CT, :4 * CT].re("p (h j) -> p h j", h=4), MUL)
                    kb.tt(v4(A, CT), v4(A, CT), bc4(NBETA, CT, CT), MUL)
                    psQ = bank()
                    for h in range(4):
                        kb.mm(psQ[:CT, h * CT:(h + 1) * CT], CO[:, 4 + h, :CT], CO[:, h, :CT], inc=(h == 3))
                    kb.tt(v4(Dt, CT), v4(Dt, CT), psQ[:CT, :4 * CT].re("p (h j) -> p h j", h=4), MUL)
                    if CUT <= 6:
                        return
                    ps = bank()
                    for h in range(4):
                        kb.tr(ps[:CT, h * CT:(h + 1) * CT], A[:CT, h * CT:(h + 1) * CT], ident[:CT, :CT], inc=(h == 3))
                    kb.cp(B[:CT, :4 * CT], ps[:CT, :4 * CT], eng="act")
                    kb.tt(v4(C, CT), v4(B, CT), mbc(ident[:CT, :CT]), ADD)
                    mg = mlstm_section(E, F, G2, H) if (ILV_ML and not sample) else None
                    for lev in range(1, P["lev"]):
                        last = (lev == P["lev"] - 1)
                        ps1 = bank()
                        for h in range(4):
                            hs = slice(h * CT, (h + 1) * CT)
                            kb.mm(ps1[:CT, hs], B[:CT, hs], A[:CT, hs], inc=(h == 3))
                        if not last:
                            ps2 = bank()
                            for h in range(4):
                                hs = slice(h * CT, (h + 1) * CT)
                                kb.mm(ps2[:CT, hs], A[:CT, hs], B[:CT, hs], inc=(h == 3))
                        if lev >= 2:
                            ps3 = bank()
                            for h in range(4):
                                hs = slice(h * CT, (h + 1) * CT)
                                kb.mm(ps3[:CT, hs], A[:CT, hs], C[:CT, hs], inc=(h == 3))
                        kb.cp(A[:CT, :4 * CT], ps1[:CT, :4 * CT])
                        if not last:
                            kb.cp(B[:CT, :4 * CT], ps2[:CT, :4 * CT], eng="act")
                        if lev >= 2:
                            kb.tt(C[:CT, :4 * CT], C[:CT, :4 * CT], ps3[:CT, :4 * CT], ADD)
                        if mg is not None:
                            _step(mg)
                            _step(mg)
                    ps3 = bank()
                    for h in range(4):
                        hs = slice(h * CT, (h + 1) * CT)
                        kb.mm(ps3[:CT, hs], A[:CT, hs], C[:CT, hs], inc=(h == 3))
                    kb.tt(C[:CT, :4 * CT], C[:CT, :4 * CT], ps3[:CT, :4 * CT], ADD)
                    if mg is not None:
                        while _step(mg):
                            pass
                    if CUT <= 7:
                        return
                    kb.act(BE[:CT], GC[:CT, 0:4], AF.Exp)
                    kb.tt(BE[:CT], BE[:CT], BETA[:CT], MUL)
                    kb.tt(KD[:CT], GL[:CT, 0:4], GC[:CT, 0:4], SUB)
                    kb.act(KD[:CT], KD[:CT], AF.Exp)
                    psKt = bank()
                    for h in range(4):
                        kb.tr(psKt[:CT, h * 128:(h + 1) * 128], CO[:, 4 + h, :CT], ident, inc=(h == 3))
                    pk = psKt[:CT, :].re("p (h d) -> p h d", h=4)
                    kb.tt(v4(F, CT, 128), pk, bc4(BE, CT, 128), MUL)
                    kb.tt(v4(G2, CT, 128), pk, bc4(KD, CT, 128), MUL)
                    psVt = bank()
                    for h in range(4):
                        kb.tr(psVt[:CT, h * 128:(h + 1) * 128], CO[:, 8 + h, :CT], ident, inc=(h == 3))
                    kb.tt(v4(E, CT, 128), psVt[:CT, :].re("p (h d) -> p h d", h=4), bc4(BETA, CT, 128), MUL)
                    psW = bank()
                    for h in range(4):
                        kb.mm(psW[:, h * CT:(h + 1) * CT], F[:CT, h * 128:(h + 1) * 128], C[:CT, h * CT:(h + 1) * CT], inc=(h == 3))
                    kb.ts(H[:, :4 * CT], psW[:, :4 * CT], -1.0, None, MUL)
                    kb.tt(v4(I, 128, CT), CO[:, 0:4, :CT], EG[:, :, :CT], MUL)
                    if CUT <= 8:
                        return
                    psV = bank(hold=True)
                    psO = bank(hold=True)
                    if not sample:
                        S = P["Sg"]
                        for h in range(4):
                            hs = slice(h * CT, (h + 1) * CT)
                            hd = slice(h * 128, (h + 1) * 128)
                            kb.mm(psV[:CT, hd], C[:CT, hs], E[:CT, hd], start=True, stop=False, inc=False)
                            kb.mm(psV[:CT, hd], H[:, hs], S[:, h, :], start=False, stop=True, inc=(h == 3))
                        kb.cp(E[:CT, :], psV[:CT, :])
                        for h in range(4):
                            hs = slice(h * CT, (h + 1) * CT)
                            hd = slice(h * 128, (h + 1) * 128)
                            kb.mm(psO[:CT, hd], I[:, hs], S[:, h, :], start=True, stop=False, inc=False)
                            kb.mm(psO[:CT, hd], Dt[:CT, hs], E[:CT, hd], start=False, stop=True, inc=(h == 3))
                        psS = bank()
                        for h in range(4):
                            hd = slice(h * 128, (h + 1) * 128)
                            kb.mm(psS[:, hd], G2[:CT, hd], E[:CT, hd], inc=(h == 3))
                        kb.tt(S[:, :, :], S[:, :, :], EG[:, :, CT - 1:CT].bc([128, 4, 128]), MUL)
                        kb.tt(S[:, :, :], S[:, :, :], psS[:, :].re("p (h d) -> p h d", h=4), ADD)
                    for h in range(4):
                        hs = slice(h * CT, (h + 1) * CT)
                        hd = slice(h * 128, (h + 1) * 128)
                        if not sample:
                            pass
                        else:
                            Ss = P["Ss"]
                            Zt = P["Zt"]
                            ZK = P["ZK"]
                            kb.dma("sp", Ss, st_gdn[l, :, h].rearrange("s k v -> k s v"))
                            zd = Zt[:, 0:nseq * (CT + 4)].re("p (s r) -> p s r", r=CT + 4)[:, :, 0:4]
                            kb.cp(zd, H[:, hs].re("p (s t) -> p s t", t=4))
                            kb.mm(psV[:CT, hd], C[:CT, hs], E[:CT, hd], start=True, stop=False, inc=False)
                            for s in range(nseq):
                                kb.mm(psV[:CT, hd], Zt[:, s * CT:(s + 1) * CT], Ss[:, s, :], start=False, stop=(s == nseq - 1), inc=(s == nseq - 1))
                            kb.cp(E[:CT, hd], psV[:CT, hd])
                            kb.cp(zd, I[:, hs].re("p (s t) -> p s t", t=4))
                            for s in range(nseq):
                                kb.mm(psO[:CT, hd], Zt[:, s * CT:(s + 1) * CT], Ss[:, s, :], start=(s == 0), stop=False, inc=False)
                            kb.mm(psO[:CT, hd], Dt[:CT, hs], E[:CT, hd], start=False, stop=True)
                            for g in range(nseq // 4):
                                kb.tt(ZK[:CT], G2[:CT, hd].re("p (o d) -> p o d", o=1).bc([CT, 4, 128]),
                                      rowmask[:CT, 4 * g:4 * g + 4].re("p (s o) -> p s o", o=1).bc([CT, 4, 128]), MUL)
                                psS = bank()
                                for s4 in range(4):
                                    kb.mm(psS[:, s4 * 128:(s4 + 1) * 128], ZK[:CT, s4, :], E[:CT, hd], inc=(s4 == 3))
                                egl = EG[:, h, 16 * g + 3:16 * g + 16:4].re("p (s o) -> p s o", o=1).bc([128, 4, 128])
                                kb.tt(Ss[:, 4 * g:4 * g + 4, :], Ss[:, 4 * g:4 * g + 4, :], egl, MUL)
                                kb.tt(Ss[:, 4 * g:4 * g + 4, :], Ss[:, 4 * g:4 * g + 4, :], psS[:, :].re("p (s d) -> p s d", s=4), ADD)
                            kb.dma("sp", o_sgdn[l, :, h].rearrange("s k v -> k s v"), Ss, out_dram=True)
                    po = psO[:CT, :].re("p (h d) -> p h d", h=4)
                    kb.act(v4(X1, CT, 128), po, AF.Square)
                    kb.red(SS4[:CT], v4(X1, CT, 128), ADD)
                    kb.act(RSTD4[:CT], SS4[:CT], AF.Ln, bias=EPS, scale=1.0 / 128)
                    kb.act(RSTD4[:CT], RSTD4[:CT], AF.Exp, scale=-0.5)
                    kb.tt(v4(X1, CT, 128), po, bc4(RSTD4, CT, 128), MUL)
                    kb.tt(v4(X1, CT, 128), v4(X1, CT, 128), GN[:CT, l, :].re("p (o d) -> p o d", o=1).bc([CT, 4, 128]), MUL)
                    kb.tt(MIX[:CT, 0:512], X1[:CT, :], R[:CT, R_GG:R_GG + 512], MUL)
                    release(psV, psO)

                    if CUT <= 9:
                        return
                    if not ILV_ML or sample:
                        for _ in mlstm_section(A, B, I, F):
                            pass

                    if nxt is not None and PRENORM:
                        rmsnorm_to(HN[nxt % 2], nxt, 0, l, ctok(nxt))
                        normed["c"] = nxt
                    if not hoist:
                        ssd_front()
                    psR = rowbc(CS[:, 8:12], U)
                    kb.act(EG[:, :, :CT], psR[:, :4 * CT].re("p (h j) -> p h j", h=4), AF.Exp)
                    kb.tt(v4(X1, CT), bc4(GC[:, 8:12], CT, CT), psR[:CT, :4 * CT].re("p (h j) -> p h j", h=4), SUB)
                    kb.tt(v4(Dt, CT), mbc(Um), v4(X1, CT), SUB)
                    kb.act(v4(Dt, CT), v4(Dt, CT), AF.Exp)
                    psC = bank()
                    for g in range(2):
                        kb.mm(psC[:CT, g * CT:(g + 1) * CT], COs[:, 2 + g, :CT], COs[:, 4 + g, :CT], inc=(g == 1))
                    Dt4 = Dt[:CT, :4 * CT].re("p (g e j) -> p g e j", g=2, e=2)
                    kb.tt(Dt4, Dt4, psC[:CT, :2 * CT].re("p (g o j) -> p g o j", g=2, o=1).bc([CT, 2, 2, CT]), MUL)
                    kb.tt(v4(Dt, CT), v4(Dt, CT), bc4(SP[:, 8:12], CT, CT), MUL)
                    psX = bank()
                    for j in range(2):
                        kb.tr(psX[:CT, j * 128:(j + 1) * 128], COs[:, j, :CT], ident, inc=False)
                    for j in range(2):
                        kb.tr(psX[:CT, 256 + j * 128:256 + (j + 1) * 128], COs[:, 2 + j, :CT], ident, inc=(j == 1))
                    kb.cp(XT_[:CT, :], psX[:CT, 0:256])
                    kb.tt(KDS[:CT], GL[:CT, 8:12], GC[:CT, 8:12], SUB)
                    kb.act(KDS[:CT], KDS[:CT], AF.Exp)
                    kb.tt(KDS[:CT], KDS[:CT], SP[:CT, 8:12], MUL)
                    BW = F
                    kb.tt(BW[:CT, :].re("p (g e n) -> p g e n", g=2, e=2), psX[:CT, 256:512].re("p (g o n) -> p g o n", g=2, o=1).bc([CT, 2, 2, 128]),
                          KDS[:CT].re("p (g e o) -> p g e o", g=2, o=1).bc([CT, 2, 2, 128]), MUL)
                    CG = I
                    kb.tt(CG[:, :4 * CT].re("p (g e j) -> p g e j", g=2, e=2), COs[:, 4:6, :CT].re("p g (o j) -> p g o j", o=1).bc([128, 2, 2, CT]),
                          EG[:, :, :CT].re("p (g e) j -> p g e j", g=2), MUL)
                    psY = bank(hold=True)
                    if not sample:
                        HT = P["HT"]
                        for h in range(4):
                            hs = slice(h * CT, (h + 1) * CT)
                            hp_ = slice(h * 64, (h + 1) * 64)
                            kb.mm(psY[:CT, hp_], Dt[:CT, hs], XT_[:CT, hp_], start=True, stop=False, inc=False)
                            kb.mm(psY[:CT, hp_], CG[:, hs], HT[:, h, :], start=False, stop=True, inc=(h == 3))
                        psH = bank()
                        for h in range(4):
                            hp_ = slice(h * 64, (h + 1) * 64)
                            hn_ = slice(h * 128, (h + 1) * 128)
                            kb.mm(psH[:, hp_], BW[:CT, hn_], XT_[:CT, hp_], inc=(h == 3))
                        kb.tt(HT[:, :, :], HT[:, :, :], EG[:, :, CT - 1:CT].bc([128, 4, 64]), MUL)
                        kb.tt(HT[:, :, :], HT[:, :, :], psH[:, 0:256].re("p (h d) -> p h d", h=4), ADD)
                    for h in range(4):
                        hs = slice(h * CT, (h + 1) * CT)
                        hp_ = slice(h * 64, (h + 1) * 64)
                        hn_ = slice(h * 128, (h + 1) * 128)
                        if not sample:
                            pass
                        else:
                            HTs = P["HTs"]
                            STG = P["STG"]
                            Zt = P["Zt"]
                            ZK = P["ZK"]
                            for g in range(nseq // 4):
                                kb.dma("sp", STG, st_ssd[l, 4 * g:4 * g + 4, h].rearrange("s p n -> p s n"))
                                ps = bank()
                                for s4 in range(4):
                                    kb.tr(ps[:, s4 * 64:(s4 + 1) * 64], STG[:, s4, :], ident[:64, :64], inc=(s4 == 3))
                                kb.cp(HTs[:, 4 * g:4 * g + 4, :], ps[:, 0:256].re("p (s d) -> p s d", s=4))
                            zd = Zt[:, 0:nseq * (CT + 4)].re("p (s r) -> p s r", r=CT + 4)[:, :, 0:4]
                            kb.cp(zd, CG[:, hs].re("p (s t) -> p s t", t=4))
                            kb.mm(psY[:CT, hp_], Dt[:CT, hs], XT_[:CT, hp_], start=True, stop=False, inc=False)
                            for s in range(nseq):
                                kb.mm(psY[:CT, hp_], Zt[:, s * CT:(s + 1) * CT], HTs[:, s, :], start=False, stop=(s == nseq - 1), inc=(s == nseq - 1))
                            for g in range(nseq // 4):
                                kb.tt(ZK[:CT], BW[:CT, hn_].re("p (o d) -> p o d", o=1).bc([CT, 4, 128]),
                                      rowmask[:CT, 4 * g:4 * g + 4].re("p (s o) -> p s o", o=1).bc([CT, 4, 128]), MUL)
                                psH = bank()
                                for s4 in range(4):
                                    kb.mm(psH[:, s4 * 64:(s4 + 1) * 64], ZK[:CT, s4, :], XT_[:CT, hp_], inc=(s4 == 3))
                                egl = EG[:, h, 16 * g + 3:16 * g + 16:4].re("p (s o) -> p s o", o=1).bc([128, 4, 64])
                                kb.tt(HTs[:, 4 * g:4 * g + 4, :], HTs[:, 4 * g:4 * g + 4, :], egl, MUL)
                                kb.tt(HTs[:, 4 * g:4 * g + 4, :], HTs[:, 4 * g:4 * g + 4, :], psH[:, 0:256].re("p (s d) -> p s d", s=4), ADD)
                                ps = bank()
                                for s4 in range(4):
                                    kb.tr(ps[:64, s4 * 128:(s4 + 1) * 128], HTs[:, 4 * g + s4, :], ident, inc=(s4 == 3))
                                kb.cp(STG, ps[:64, :].re("p (s n) -> p s n", s=4))
                                kb.dma("sp", o_sssd[l, 4 * g:4 * g + 4, h].rearrange("s p n -> p s n"), STG, out_dram=True)
                    Y1 = X1
                    kb.tt(v4(Y1, CT, 64), XT_[:CT, :].re("p (h d) -> p h d", h=4), bc4(HP[:, 6, l, :], CT, 64), MUL)
                    kb.tt(Y1[:CT, 0:256], Y1[:CT, 0:256], psY[:CT, 0:256], ADD)
                    release(psY)
                    kb.tt(Y1[:CT, 0:256], Y1[:CT, 0:256], R[:CT, R_SZ:R_SZ + 256], MUL)
                    kb.act(H[:CT, 0:256], Y1[:CT, 0:256], AF.Square)
                    kb.red(SS1[:CT], H[:CT, 0:256], ADD)
                    kb.act(SS1[:CT], SS1[:CT], AF.Ln, bias=EPS, scale=1.0 / 256)
                    kb.act(SS1[:CT], SS1[:CT], AF.Exp, scale=-0.5)
                    kb.stt(MIX[:CT, 768:1024], Y1[:CT, 0:256], SS1[:CT, 0:1], SN[:CT, l, :], MUL, MUL)

                    if DEBUG and sample and l == 0:
                        kb.dma("sp", dbg_mix, MIX[:CT, :], out_dram=True)
                    if CUT <= 11:
                        return
                    yield "preout"
                    for half in range(2):
                        ps = bank()
                        for j in range(4):
                            k = half * 4 + j
                            kb.tr(ps[:, j * CT:(j + 1) * CT], MIX[:CT, k * 128:(k + 1) * 128], ident[:CT, :CT], inc=(j == 3))
                        kb.cp(mixT[:, half * 4:half * 4 + 4, :CT], ps[:, :4 * CT].re("p (k t) -> p k t", k=4), eng=("act" if half else "dve"))
                    for half in range(2):
                        ps = bank()
                        for j in range(4):
                            db = half * 4 + j
                            for k in range(8):
                                kb.mm(ps[:, j * CT:(j + 1) * CT], Wout[:, k, db * 128:(db + 1) * 128], mixT[:, k, :CT], start=(k == 0), stop=(k == 7), inc=(k == 7 and j == 3))
                        kb.cp(SQa[:, half * 4:half * 4 + 4, :CT], ps[:, :4 * CT].re("p (k t) -> p k t", k=4), eng=("act" if half else "dve"))
                    postnorm_add(SQa, c, 1, l, CT)

                def run_chunks(clist, P):
                    def step(g):
                        try:
                            next(g)
                            return True
                        except StopIteration:
                            return False
                    prev = None
                    for c in clist:
                        ci = clist.index(c)
                        g = mixer_chunk(c, P, clist[ci + 1] if ci + 1 < len(clist) else None)
                        alive = step(g)
                        if alive:
                            alive = step(g)
                        if prev is not None:
                            while step(prev):
                                pass
                        if alive:
                            alive = step(g)
                        prev = g if alive else None
                    if prev is not None:
                        while step(prev):
                            pass

                def conv_blocks(P, XP, b0, nb, CT, CO=None):
                    CO = CO_main if CO is None else CO
                    sample = P["sample"]
                    nseq_ = P["nseq"]
                    if sample:
                        HS = P["HS"]
                        kb.cp(XP[:, 0:nb, :, 0:3], HS[:, b0:b0 + nb, :, :])
                    else:
                        Hh = P["Hh"]
                        kb.cp(XP[:, 0:nb, 0:3], Hh[:, b0:b0 + nb, :])
                    for t in range(4):
                        for j in range(nb):
                            b = b0 + j
                            if sample:
                                o = CO[:, j, :CT].re("p (s t) -> p s t", t=4)
                                xi_ = XP[:, j, :, t:t + 4]
                            else:
                                o = CO[:, j, :CT]
                                xi_ = XP[:, j, t:t + CT]
                            if t == 0:
                                kb.act(o, xi_, AF.Identity, bias=CB[:, l, b:b + 1], scale=CW[:, l, b, 0:1])
                            else:
                                kb.stt(o, xi_, CW[:, l, b, t:t + 1], o, MUL, ADD)
                    if not sample:
                        kb.cp(P["Hh"][:, b0:b0 + nb, :], XP[:, 0:nb, CT:CT + 3])
                    kb.act(CO[:, 0:nb, :CT], CO[:, 0:nb, :CT], AF.Silu)

                with ExitStack() as es3:
                    kb.es = es3
                    Pp = dict(CT=128, nseq=1, sample=False, cm=cmp_, lastsel=lsp, lev=7)
                    Pp["XP"] = kb.sb("XPp", [128, 12, 131])
                    HN[1] = kb.sb("hnT1", [128, 8, 128], BF16)
                    if HOIST_SSD:
                        Pp["XP2"] = kb.sb("XP2", [128, 6, 131])
                        Pp["CO2"] = kb.sb("CO2", [128, 6, 128])
                    Pp["WS"] = [kb.sb("WSp%d" % i, [128, 8, 256], BF16) for i in range(5)]
                    Pp["wsi"] = [0]
                    Pp["Hh"] = kb.sb("Hh", [128, 18, 3])
                    Pp["Sg"] = kb.sb("Sg", [128, 4, 128])
                    Pp["CME"] = kb.sb("CME", [64, 4, 65])
                    Pp["HT"] = kb.sb("HT", [128, 4, 64])
                    Pp["MB"] = kb.sb("MB", [128, 4])
                    Pp["BL"] = kb.sb("BLp", [128, 4, 1])
                    Pp["MN"] = kb.sb("MNp", [128, 4, 1])
                    Pp["WC"] = kb.sb("WCp", [128, 4, 1])
                    Pp["M0R"] = Pp["MB"][:, :].re("p (h o) -> p h o", o=1)
                    for nme in ("Hh", "Sg", "CME", "HT", "MB"):
                        kb.memset(Pp[nme], 0.0)
                    if STOP & 1:
                        run_chunks(list(range(NPC)), Pp)
                    kb.dma("sp", o_pgdn[l].rearrange("h k v -> k h v"), Pp["Sg"], out_dram=True)
                    for h in range(4):
                        kb.dma("sp", o_pmc[l, h], Pp["CME"][:, h, 0:64], out_dram=True)
                        kb.dma("sp", o_pmn[l, h].rearrange("(k o) -> k o", o=1), Pp["CME"][:, h, 64:65], out_dram=True, allow_slow_non_contiguous=True)
                    kb.dma("sp", o_pmm[l].rearrange("(o h) -> o h", o=1), Pp["MB"][0:1, :], out_dram=True)
                    ps = bank()
                    for h in range(4):
                        kb.tr(ps[:64, h * 128:(h + 1) * 128], Pp["HT"][:, h, :], ident, inc=(h == 3))
                    kb.cp(TMP["A"][:64, :], ps[:64, :])
                    kb.dma("sp", o_pssd[l].rearrange("h p n -> p h n"), TMP["A"][:64, :].re("p (h n) -> p h n", h=4), out_dram=True)
                    barrier()
                with ExitStack() as es3:
                    kb.es = es3
                    Psm = dict(CT=NST, nseq=NS, sample=True, cm=cms, lastsel=lss, lev=2)
                    Psm["XP"] = kb.sb("XPs", [128, 12, NS, 7])
                    Psm["WS"] = [kb.sb("WSs%d" % i, [128, 8, 256], BF16) for i in range(2)]
                    Psm["wsi"] = [0]
                    Psm["HS"] = kb.sb("HS", [128, 18, NS, 3])
                    SST = kb.sb("SST", [128, NS * 128])
                    Psm["Ss"] = SST[:, :].re("p (s d) -> p s d", s=NS)
                    Psm["Zt"] = kb.sb("Zt", [128, (NS + 1) * NST])
                    Psm["ZK"] = kb.sb("ZK", [64, 4, 128])
                    Psm["CMs"] = SST[:64, 0:NS * 65].re("p (s d) -> p s d", s=NS)
                    Psm["HTs"] = SST[:, 0:NS * 64].re("p (s d) -> p s d", s=NS)
                    Psm["STG"] = Psm["ZK"]
                    Psm["BL"] = kb.sb("BLs", [128, 4, NS])
                    Psm["MN"] = kb.sb("MNs", [128, 4, NS])
                    Psm["WC"] = kb.sb("WCs", [128, 4, NS])
                    m0r = kb.sb("M0Rs", [128, NS, 4])
                    Psm["m0"] = kb.sb("m0s", [NS, 4])
                    Psm["RMT"] = kb.sb("RMT", [NS, NST])
                    kb.memset(Psm["Zt"], 0.0)
                    kb.dma("sp", m0r, st_mm[l].partition_broadcast(128))
                    kb.dma("sp", Psm["m0"], st_mm[l])
                    Psm["M0R"] = m0r[:, :, :].re("p s h -> p h s")
                    ps = bank()
                    kb.tr(ps[:NS, :NST], rowmask[:NST, :NS], ident[:NST, :NST])
                    kb.cp(Psm["RMT"], ps[:NS, :NST])
                    for b in range(18):
                        stg_ = TMP["E"] if b % 2 else TMP["F"]
                        kb.dma("sp", stg_[:NS * 3, 0:128], st_conv[l, :, :, b * 128:(b + 1) * 128].rearrange("s r c -> (s r) c"))
                        ps = bank()
                        kb.tr(ps[:, :NS * 3], stg_[:NS * 3, 0:128], ident[:NS * 3, :NS * 3])
                        kb.cp(Psm["HS"][:, b, :, :], ps[:, :NS * 3].re("p (s r) -> p s r", r=3), eng=("act" if b % 2 else "dve"))
                    if STOP & 2:
                        run_chunks([NPC], Psm)
                    barrier()
            with ExitStack() as es2:
                kb.es = es2
                groups = []
                cs_ = list(range(NPC))
                GSZ = GSZ_G
                for g0 in range(0, NPC, GSZ):
                    groups.append(cs_[g0:g0 + GSZ])
                groups[-1] = groups[-1] + [NPC]
                MAXT = max(sum(ctok(c) for c in g) for g in groups)
                hn2 = kb.sb("hn2", [128, 8, MAXT], BF16)
                ACTT = kb.sb("ACTT", [128, NFB, MAXT], BF16)
                WU = [kb.sb("WU%d" % i, [128, 8, 256], BF16) for i in range(NWU)]
                WD = [kb.sb("WD%d" % i, [128, NFB, 128], BF16) for i in range(NWD)]
                GP = kb.sb("GP", [128, 2 + GSZ * 128])
                GS = kb.sb("GS", [128, NS, 6])
                VV2 = [kb.sb("VV%d" % i, [128, MAXT]) for i in range(2)]
                ACC2 = [kb.sb("ACC%d" % i, [128, MAXT]) for i in range(2)]
                T12 = [kb.sb("T1%d" % i, [128, MAXT]) for i in range(2)]
                YF = kb.sb("YF", [128, 8, max(MAXT, 704)])
                HF = kb.sb("HF", [128, NFB, 2])
                HSf = kb.sb("HSf", [128, NFB, NS, 2])
                STGf = YF[:, :, :].re("p k t -> p (k t)")
                pass
                kb.memset(HF, 0.0)
                kb.dma("sp", STGf[:NS * 2, 0:DFF], st_ffn[l].rearrange("s r c -> (s r) c"))
                for b in range(NFB):
                    ps = bank()
                    kb.tr(ps[:, :NS * 2], STGf[:NS * 2, b * 128:(b + 1) * 128], ident[:NS * 2, :NS * 2])
                    kb.cp(HSf[:, b, :, :], ps[:, :NS * 2].re("p (s r) -> p s r", r=2), eng=("act" if b % 2 else "dve"))
                wu_i = 0
                wd_i = 0
                def make_post(grp_, offs_):
                    for c_ in grp_:
                        CT_ = ctok(c_)
                        kb.cp(SQa[:, :, :CT_], YF[:, :, offs_[c_]:offs_[c_] + CT_])
                        postnorm_add(SQa, c_, 3, l, CT_)
                        yield

                post_gen = None
                for gi, grp in enumerate(groups):
                    if not (STOP & 4):
                        continue
                    has_s = (grp[-1] == NPC)
                    if post_gen is not None and (has_s or (NPC - 1) in grp or not FFN_ILV):
                        while _step(post_gen):
                            pass
                        post_gen = None
                    pch = [c for c in grp if c < NPC]
                    npt = len(pch) * 128
                    ntg = npt + (NST if has_s else 0)
                    off = 0
                    offs = {}
                    for c in grp:
                        CT = ctok(c)
                        offs[c] = off
                        rmsnorm_to(hn2[:, :, off:off + CT], c, 2, l, CT)
                        off += CT
                    tbs = [(t0, min(512, ntg - t0)) for t0 in range(0, ntg, 512)]
                    for fb in range(NFB):
                        wu = WU[wu_i % NWU]
                        VV, ACC, T1 = VV2[fb % 2], ACC2[fb % 2], T12[fb % 2]
                        wu_i += 1
                        kb.dma("pool", wu[:, :, 0:128], w_up[l, :, fb * 128:(fb + 1) * 128].rearrange("(k p) c -> p k c", p=128))
                        kb.dma("pool", wu[:, :, 128:256], w_up[l, :, DFF + fb * 128:DFF + (fb + 1) * 128].rearrange("(k p) c -> p k c", p=128))
                        if (NPC - 1) in grp:
                            r0 = offs[NPC - 1] + 126
                            ps = bank()
                            for k in range(8):
                                kb.mm(ps[:2, 0:128], hn2[:, k, r0:r0 + 2], wu[:, k, 0:128], start=(k == 0), stop=(k == 7), inc=(k == 7))
                            kb.cp(STGf[:2, fb * 128:(fb + 1) * 128], ps[:2, 0:128], eng="act")
                        if has_s:
                            r0 = offs[NPC]
                            ps = bank()
                            for k in range(8):
                                kb.mm(ps[:NST, 0:128], hn2[:, k, r0:r0 + NST], wu[:, k, 0:128], start=(k == 0), stop=(k == 7), inc=(k == 7))
                            kb.cp(STGf[:NST, DFF + fb * 128:DFF + (fb + 1) * 128], ps[:NST, 0:128], eng="act")
                        for (t0, tw) in tbs:
                            psG = bank()
                            for k in range(8):
                                kb.mm(psG[:, :tw], wu[:, k, 0:128], hn2[:, k, t0:t0 + tw], start=(k == 0), stop=(k == 7), inc=(k == 7))
                            psV = bank()
                            for k in range(8):
                                kb.mm(psV[:, :tw], wu[:, k, 128:256], hn2[:, k, t0:t0 + tw], start=(k == 0), stop=(k == 7), inc=(k == 7))
                            p1 = min(t0 + tw, npt)
                            if p1 > t0:
                                kb.cp(GP[:, 2 + t0:2 + p1], psG[:, 0:p1 - t0], eng="act")
                            if t0 + tw > npt:
                                s0 = max(t0, npt) - t0
                                kb.cp(GS[:, :, 2:6], psG[:, s0:s0 + NST].re("p (s t) -> p s t", t=4), eng="act")
                            kb.cp(VV[:, t0:t0 + tw], psV[:, :tw], eng="act")
                        if npt > 0:
                            kb.cp(GP[:, 0:2], HF[:, fb, :])
                            kb.act(ACC[:, 0:npt], GP[:, 0:npt], AF.Identity, bias=FCB[:, l, fb:fb + 1], scale=FCW[:, l, fb, 0:1])
                            kb.stt(ACC[:, 0:npt], GP[:, 1:1 + npt], FCW[:, l, fb, 1:2], ACC[:, 0:npt], MUL, ADD)
                            kb.stt(ACC[:, 0:npt], GP[:, 2:2 + npt], FCW[:, l, fb, 2:3], ACC[:, 0:npt], MUL, ADD)
                            kb.cp(HF[:, fb, :], GP[:, npt:npt + 2])
                        if has_s:
                            kb.cp(GS[:, :, 0:2], HSf[:, fb, :, :])
                            a_s = ACC[:, npt:npt + NST].re("p (s t) -> p s t", t=4)
                            kb.act(a_s, GS[:, :, 0:4], AF.Identity, bias=FCB[:, l, fb:fb + 1], scale=FCW[:, l, fb, 0:1])
                            kb.stt(a_s, GS[:, :, 1:5], FCW[:, l, fb, 1:2], a_s, MUL, ADD)
                            kb.stt(a_s, GS[:, :, 2:6], FCW[:, l, fb, 2:3], a_s, MUL, ADD)
                        kb.act(T1[:, :ntg], ACC[:, :ntg], AF.Square, scale=0.044715 ** 0.5)
                        kb.stt(T1[:, :ntg], T1[:, :ntg], 1.0, ACC[:, :ntg], ADD, MUL)
                        kb.act(T1[:, :ntg], T1[:, :ntg], AF.Sigmoid, scale=1.5957691216057308)
                        kb.tt(ACC[:, :ntg], ACC[:, :ntg], VV[:, :ntg], MUL)
                        kb.tt(ACTT[:, fb, :ntg], T1[:, :ntg], ACC[:, :ntg], MUL)
                        if post_gen is not None and fb % 3 == 2:
                            _step(post_gen)
                    if post_gen is not None:
                        while _step(post_gen):
                            pass
                        post_gen = None
                    if (NPC - 1) in grp:
                        kb.dma("sp", o_pffn[l], STGf[:2, 0:DFF], out_dram=True)
                    if has_s:
                        for r in range(2):
                            kb.dma("sp", o_sffn[l, :, r, :], STGf[2 + r:NST:4, DFF:2 * DFF], out_dram=True)
                    for db in range(8):
                        wd = WD[wd_i % NWD]
                        wd_i += 1
                        kb.dma("pool", wd, w_down[l, :, db * 128:(db + 1) * 128].rearrange("(f p) c -> p f c", p=128))
                        for (t0, tw) in tbs:
                            ps = bank()
                            for fb in range(NFB):
                                kb.mm(ps[:, :tw], wd[:, fb, :], ACTT[:, fb, t0:t0 + tw], start=(fb == 0), stop=(fb == NFB - 1), inc=(fb == NFB - 1))
                            kb.cp(YF[:, db, t0:t0 + tw], ps[:, :tw], eng=("act" if db % 2 else "dve"))
                    post_gen = make_post(list(grp), dict(offs))
                if post_gen is not None:
                    while _step(post_gen):
                        pass
                    post_gen = None
                barrier()

        with ExitStack() as es1:
            kb.es = es1
            yo = [kb.sb("yo%d" % i, [128, D]) for i in range(2)]
            for c in range(NCH):
                CT = ctok(c)
                yt = yo[c % 2]
                for half in range(2):
                    ps = bank()
                    for j in range(4):
                        k = half * 4 + j
                        kb.tr(ps[:CT, j * 128:(j + 1) * 128], xT[c][:, k, :CT], ident, inc=(j == 3))
                    kb.cp(yt[:CT, half * 512:(half + 1) * 512], ps[:CT, :], eng=("act" if half else "dve"))
                dst = y_p[c * 128:(c + 1) * 128, :] if c < NPC else y_s
                kb.dma("sp", dst, yt[:CT, :], out_dram=True)
        kb.finish()
    return nc


IN_NAMES = ["norm_mix_pre", "norm_mix_post", "norm_ffn_pre", "norm_ffn_post", "w_in", "conv_w", "conv_b",
            "gdn_a_log", "gdn_dt_bias", "gdn_norm", "mlstm_i_bias", "mlstm_f_bias", "mlstm_norm",
            "ssd_a_log", "ssd_dt_bias", "ssd_d", "ssd_norm", "w_out", "ffn_w_up", "ffn_conv_w", "ffn_conv_b", "ffn_w_down"]
ST_MAP = [("st_conv", "state_conv"), ("st_gdn", "state_gdn"), ("st_mc", "state_mlstm_c"), ("st_mn", "state_mlstm_n"),
          ("st_mm", "state_mlstm_m"), ("st_ssd", "state_ssd"), ("st_ffn", "state_ffn_conv")]
_NC_CACHE = {}


def make_in_maps(inputs, NPC, NS, DEPTH, ncores):
    consts = make_consts(NS)
    f = lambda a: np.ascontiguousarray(np.asarray(a, dtype=np.float32))
    shared = {n: f(inputs[n])[:DEPTH] for n in IN_NAMES}
    shared.update(consts)
    maps = []
    for c in range(ncores):
        m = dict(shared)
        m["xp"] = f(inputs["x_prompt"][c, :NPC * 128])
        m["xs"] = f(inputs["x_sample"][c * NS:(c + 1) * NS]).reshape(NS * 4, D)
        for k, n in ST_MAP:
            m[k] = f(inputs[n][:DEPTH, c * NS:(c + 1) * NS])
        maps.append(m)
    return maps


def assemble(results, NPC, NS, DEPTH):
    cat = lambda k, ax: np.concatenate([r[k] for r in results], axis=ax)
    y_p = np.stack([r["y_p"] for r in results], 0)
    y_s = np.concatenate([r["y_s"].reshape(NS, 4, D) for r in results], 0)
    outs = [y_p, y_s]
    for k in ("p_conv", "p_gdn", "p_mc", "p_mn", "p_mm", "p_ssd", "p_ffn"):
        outs.append(np.stack([r[k] for r in results], 1))
    for k in ("s_conv", "s_gdn", "s_mc", "s_mn", "s_mm", "s_ssd", "s_ffn"):
        outs.append(cat(k, 1))
    return tuple(np.ascontiguousarray(o, dtype=np.float32) for o in outs)


def kernel(**inputs):
    NPC, NS, DEPTH, ncores = 16, 16, 2, 8
    key = (NPC, NS, DEPTH)
    if key not in _NC_CACHE:
        _NC_CACHE[key] = build(NPC, NS, DEPTH)
    nc = _NC_CACHE[key]
    maps = make_in_maps(inputs, NPC, NS, DEPTH, ncores)
    res = run_bass_kernel_spmd(nc, maps, core_ids=list(range(ncores)))
    return assemble(res.results, NPC, NS, DEPTH)
```
